# Optimizing a Trainium2 kernel written in Bass

```python
import math
import jax
import jax.numpy as jnp
from jax import lax
import numpy as np

D_MODEL = 1024
BATCH = 4
SEQ = 8192
DEPTH = 2

GRID_W = 64
CTX_LEN = 256
EPS = 1e-6
SUBLN_EPS = 1e-5

FN_GROUPS = 4
FN_GROUP_DIM = 64
FN_WIDTH = FN_GROUPS * FN_GROUP_DIM

HY_WIDTH = 256
HY_ORDER = 2
HY_SHORT = 3
HY_EMB_BANDS = 16
HY_EMB_DIM = 1 + 2 * HY_EMB_BANDS
HY_FILTER_ORDER = 64
HY_DECAY_TARGET = 1e-2
HY_FAST_DECAY = 0.3
HY_SLOW_DECAY = 1.5

DA_HEADS = 4
DA_QK_DIM = 64
DA_V_DIM = 2 * DA_QK_DIM
DA_WIDTH = DA_HEADS * DA_V_DIM
ROPE_BASE = 10000.0
Q_BLOCK = 128

N_BRANCHES = 3
COL_F = FN_WIDTH
COL_HY = (HY_ORDER + 1) * HY_WIDTH
COL_QK = DA_HEADS * 2 * DA_QK_DIM
COL_V = DA_WIDTH
COL_G = N_BRANCHES * D_MODEL
OFF_F = 0
OFF_HY = OFF_F + COL_F
OFF_Q = OFF_HY + COL_HY
OFF_K = OFF_Q + COL_QK
OFF_V = OFF_K + COL_QK
OFF_G = OFF_V + COL_V
IN_COLS = OFF_G + COL_G

N_EXPERTS = 32
TOP_K = 4
D_EXPERT = 1024
SWIGLU_ALPHA = 1.702
SWIGLU_LIMIT = 7.0
MOE_BLOCK = 256

kernel_name = 'hybrid_fourier_hyena_diffattn_moe_dit'


def rms_norm(x, g, eps=EPS):
    x32 = x.astype(jnp.float32)
    y = x32 * lax.rsqrt(jnp.mean(x32 * x32, axis=-1, keepdims=True) + eps)
    return (y * g.astype(jnp.float32)).astype(x.dtype)


def modulate(h, shift, scale):
    return h * (1 + scale) + shift


def fourier_mix(u):
    b, l, _ = u.shape
    ug = u.astype(jnp.float32).reshape(b, l, FN_GROUPS, FN_GROUP_DIM)
    y = jnp.fft.fftn(ug, axes=(1, 3), norm='ortho').real
    return y.reshape(b, l, FN_WIDTH).astype(u.dtype)


def short_conv(u, w, b):
    l = u.shape[1]
    r = HY_SHORT // 2
    up = jnp.pad(u, ((0, 0), (r, r), (0, 0)))
    return sum(up[:, j:j + l] * w[j] for j in range(HY_SHORT)) + b


def hyena_filter_spectra(l, p):
    t = jnp.linspace(0.0, 1.0, l, dtype=jnp.float32)[:, None]
    ang = (2.0 * math.pi / l) * jnp.arange(l, dtype=jnp.float32)[:, None]
    bands = jnp.linspace(1e-4, HY_EMB_BANDS - 1, HY_EMB_BANDS, dtype=jnp.float32)[None, :]
    emb = jnp.concatenate([t, jnp.cos(bands * ang), -jnp.sin(bands * ang)], axis=-1)
    z = jnp.sin(p['hy_freq1'] * (emb @ p['hy_w1'] + p['hy_b1']))
    z = jnp.sin(p['hy_freq2'] * (z @ p['hy_w2'] + p['hy_b2']))
    h = (z @ p['hy_w3'] + p['hy_b3']).astype(jnp.float32).reshape(l, HY_ORDER, 2, HY_WIDTH)
    deltas = jnp.abs(jnp.linspace(math.log(HY_DECAY_TARGET) / HY_SLOW_DECAY,
                                  math.log(HY_DECAY_TARGET) / HY_FAST_DECAY, HY_WIDTH, dtype=jnp.float32))
    h = h * jnp.exp(-t * deltas)[:, None, None, :]
    h_fwd, h_bwd = h[:, :, 0], h[:, :, 1]
    filt = jnp.concatenate([h_fwd, jnp.zeros((1, HY_ORDER, HY_WIDTH), jnp.float32), h_bwd[1:][::-1]], axis=0)
    filt = filt / jnp.sum(jnp.abs(filt), axis=0, keepdims=True)
    return jnp.fft.rfft(filt, axis=0)


def long_conv(u, spec, bias):
    l = u.shape[1]
    u32 = u.astype(jnp.float32)
    uf = jnp.fft.rfft(u32, n=2 * l, axis=1)
    y = jnp.fft.irfft(uf * spec[None], n=2 * l, axis=1)[:, :l]
    return (y + u32 * bias).astype(u.dtype)


def hyena_mix(z, p, spec):
    z = short_conv(z, p['hy_conv_w'], p['hy_conv_b'])
    v, x1, x2 = jnp.split(z, HY_ORDER + 1, axis=-1)
    y = x1 * long_conv(v, spec[:, 0], p['hy_bias'][0])
    return x2 * long_conv(y, spec[:, 1], p['hy_bias'][1])


def axial_rope_tables(n_lat):
    rows = n_lat // GRID_W
    row = jnp.repeat(jnp.arange(rows), GRID_W)
    col = jnp.tile(jnp.arange(GRID_W), rows)
    pos = jnp.stack([row, col], axis=-1).astype(jnp.float32)
    n_freq = DA_QK_DIM // 4
    inv = ROPE_BASE ** (-jnp.arange(n_freq, dtype=jnp.float32) / n_freq)
    ang = pos[:, :, None] * inv
    return jnp.cos(ang), jnp.sin(ang)


def apply_axial_rope(t, cos, sin):
    tr = t.reshape(t.shape[:-1] + (2, 2, DA_QK_DIM // 4))
    a, b = tr[..., 0, :], tr[..., 1, :]
    c = cos[None, :, None, None]
    s = sin[None, :, None, None]
    out = jnp.stack([a * c - b * s, b * c + a * s], axis=-2)
    return out.reshape(t.shape).astype(t.dtype)


def diff_attend(q, k, v, lam):
    s = jnp.einsum('bqhmd,bkhmd->bhmqk', q, k).astype(jnp.float32) * (DA_QK_DIM ** -0.5)
    pr = jax.nn.softmax(s, axis=-1)
    a = pr[:, :, 0] - lam * pr[:, :, 1]
    return jnp.einsum('bhqk,bkhe->bqhe', a.astype(v.dtype), v)


def latent_diff_attention(q, k_all, v_all, lam):
    b, l = q.shape[:2]
    nb = l // Q_BLOCK
    qb = q.reshape((b, nb, Q_BLOCK) + q.shape[2:]).swapaxes(0, 1)
    o = lax.map(lambda blk: diff_attend(blk, k_all, v_all, lam), qb)
    return o.swapaxes(0, 1).reshape(b, l, DA_HEADS, DA_V_DIM)


def mixer_merge(proj, attn_heads, spec, p, lam_init):
    y_f = fourier_mix(proj[..., OFF_F:OFF_HY]) @ p['w_f']
    y_h = hyena_mix(proj[..., OFF_HY:OFF_Q], p, spec) @ p['w_h']
    o = rms_norm(attn_heads, p['subln_g'], SUBLN_EPS) * (1.0 - lam_init)
    y_a = o.reshape(o.shape[:2] + (DA_WIDTH,)) @ p['w_a']
    g_f, g_h, g_a = jnp.split(jax.nn.sigmoid(proj[..., OFF_G:]), N_BRANCHES, axis=-1)
    return (g_f * y_f + g_h * y_h + g_a * y_a) @ p['w_o']


def swiglu_clamped(u):
    x_glu, x_lin = u[..., ::2], u[..., 1::2]
    x_glu = jnp.minimum(x_glu, SWIGLU_LIMIT)
    x_lin = jnp.clip(x_lin, -SWIGLU_LIMIT, SWIGLU_LIMIT)
    return x_glu * jax.nn.sigmoid(SWIGLU_ALPHA * x_glu) * (x_lin + 1)


def moe_ffn(h, p):
    t_tok, d = h.shape
    logits = (h @ p['w_router'] + p['b_router']).astype(jnp.float32)
    top_v, top_i = lax.top_k(logits, TOP_K)
    gate = jax.nn.softmax(top_v, axis=-1)
    n_assign = t_tok * TOP_K
    flat_e = top_i.reshape(-1)
    order = jnp.argsort(flat_e)
    sorted_e = flat_e[order]
    sorted_tok = order // TOP_K
    sorted_gate = gate.reshape(-1)[order]
    counts = jnp.bincount(flat_e, length=N_EXPERTS)
    padded = (counts + MOE_BLOCK - 1) // MOE_BLOCK * MOE_BLOCK
    start = jnp.cumsum(counts) - counts
    pad_end = jnp.cumsum(padded)
    pad_start = pad_end - padded
    dest = pad_start[sorted_e] + jnp.arange(n_assign) - start[sorted_e]
    n_blocks = -(-n_assign // MOE_BLOCK) + N_EXPERTS
    n_rows = n_blocks * MOE_BLOCK
    row_tok = jnp.full((n_rows,), t_tok, jnp.int32).at[dest].set(sorted_tok.astype(jnp.int32))
    row_gate = jnp.zeros((n_rows,), jnp.float32).at[dest].set(sorted_gate)
    blk_exp = jnp.minimum(jnp.searchsorted(pad_end, jnp.arange(n_blocks) * MOE_BLOCK, side='right'), N_EXPERTS - 1)
    h_pad = jnp.concatenate([h, jnp.zeros((1, d), h.dtype)], axis=0)
    w1, b1, w2, b2 = p['w_e1'], p['b_e1'], p['w_e2'], p['b_e2']

    def expert_block(args):
        tok, e = args
        u = h_pad[tok] @ w1[e] + b1[e]
        return swiglu_clamped(u) @ w2[e] + b2[e]

    y = lax.map(expert_block, (row_tok.reshape(n_blocks, MOE_BLOCK), blk_exp))
    y = y.reshape(n_rows, d) * row_gate[:, None].astype(y.dtype)
    return jax.ops.segment_sum(y, row_tok, num_segments=t_tok + 1)[:t_tok]


def hybrid_layer(l, x, xc, c, c_ctx, p, ctx_out):
    b, n_lat, d = x.shape
    n_ctx = xc.shape[1]
    lam_init = 0.8 - 0.6 * math.exp(-0.3 * l)
    lam = (jnp.exp(jnp.sum(p['lam_q'][0] * p['lam_k'][0]).astype(jnp.float32))
           - jnp.exp(jnp.sum(p['lam_q'][1] * p['lam_k'][1]).astype(jnp.float32)) + lam_init)
    mod = jnp.split((jax.nn.silu(c) @ p['w_mod'] + p['b_mod'])[:, None, :], 6, axis=-1)
    mod_c = jnp.split(jax.nn.silu(c_ctx) @ p['w_mod'] + p['b_mod'], 6, axis=-1)

    h = modulate(rms_norm(x, p['norm1_g']), mod[0], mod[1])
    hc = modulate(rms_norm(xc, p['norm1_g']), mod_c[0], mod_c[1])
    proj = h @ p['w_in']
    if ctx_out:
        proj_c = hc @ p['w_in']
        kv_c = proj_c[..., OFF_K:OFF_G]
    else:
        kv_c = hc @ p['w_in'][:, OFF_K:OFF_G]
    k_c = rms_norm(kv_c[..., :COL_QK].reshape(b, n_ctx, DA_HEADS, 2, DA_QK_DIM), p['k_norm_g'])
    v_c = kv_c[..., COL_QK:].reshape(b, n_ctx, DA_HEADS, DA_V_DIM)

    cos, sin = axial_rope_tables(n_lat)
    q = apply_axial_rope(rms_norm(proj[..., OFF_Q:OFF_K].reshape(b, n_lat, DA_HEADS, 2, DA_QK_DIM), p['q_norm_g']), cos, sin)
    k = apply_axial_rope(rms_norm(proj[..., OFF_K:OFF_V].reshape(b, n_lat, DA_HEADS, 2, DA_QK_DIM), p['k_norm_g']), cos, sin)
    v = proj[..., OFF_V:OFF_G].reshape(b, n_lat, DA_HEADS, DA_V_DIM)
    o = latent_diff_attention(q, jnp.concatenate([k, k_c], axis=1), jnp.concatenate([v, v_c], axis=1), lam)
    x_mixed = x + mod[2] * mixer_merge(proj, o, hyena_filter_spectra(n_lat, p), p, lam_init)
    if ctx_out:
        q_c = rms_norm(proj_c[..., OFF_Q:OFF_K].reshape(b, n_ctx, DA_HEADS, 2, DA_QK_DIM), p['q_norm_g'])
        o_c = diff_attend(q_c, k_c, v_c, lam)
        xc = xc + mod_c[2] * mixer_merge(proj_c, o_c, hyena_filter_spectra(n_ctx, p), p, lam_init)
    x = x_mixed

    h2 = modulate(rms_norm(x, p['norm2_g']), mod[3], mod[4]).reshape(b * n_lat, d)
    if ctx_out:
        h2c = modulate(rms_norm(xc, p['norm2_g']), mod_c[3], mod_c[4]).reshape(b * n_ctx, d)
        y = moe_ffn(jnp.concatenate([h2c, h2], axis=0), p)
        xc = xc + mod_c[5] * y[:b * n_ctx].reshape(b, n_ctx, d)
        y_lat = y[b * n_ctx:]
    else:
        y_lat = moe_ffn(h2, p)
    x = x + mod[5] * y_lat.reshape(b, n_lat, d)
    return x, xc


def setup_inputs(seed: int = 0) -> dict:
    key = jax.random.key(seed)
    ks = jax.random.split(key, 35)
    D = D_MODEL
    hy_cols = (HY_ORDER + 1) * HY_WIDTH
    hy_out = HY_ORDER * 2 * HY_WIDTH

    def nrm(k, shape, scale):
        return jax.random.normal(k, shape, jnp.float32) * scale

    return {
        'x': nrm(ks[0], (BATCH, SEQ, D), 1.0),
        'c': nrm(ks[1], (BATCH, D), 1.0),
        'ctx': nrm(ks[2], (BATCH, CTX_LEN, D), 1.0),
        'c_ctx': nrm(ks[3], (D,), 1.0),
        'w_mod': nrm(ks[4], (DEPTH, D, 6 * D), 0.5 * D ** -0.5),
        'b_mod': nrm(ks[5], (DEPTH, 6 * D), 0.02),
        'norm1_g': 1.0 + nrm(ks[6], (DEPTH, D), 0.02),
        'norm2_g': 1.0 + nrm(ks[7], (DEPTH, D), 0.02),
        'w_in': nrm(ks[8], (DEPTH, D, IN_COLS), D ** -0.5),
        'hy_conv_w': nrm(ks[9], (DEPTH, HY_SHORT, hy_cols), HY_SHORT ** -0.5),
        'hy_conv_b': nrm(ks[10], (DEPTH, hy_cols), 0.02),
        'hy_w1': nrm(ks[11], (DEPTH, HY_EMB_DIM, HY_FILTER_ORDER), HY_EMB_DIM ** -0.5),
        'hy_b1': nrm(ks[12], (DEPTH, HY_FILTER_ORDER), 0.02),
        'hy_freq1': 1.0 + nrm(ks[13], (DEPTH, HY_FILTER_ORDER), 0.02),
        'hy_w2': nrm(ks[14], (DEPTH, HY_FILTER_ORDER, HY_FILTER_ORDER), HY_FILTER_ORDER ** -0.5),
        'hy_b2': nrm(ks[15], (DEPTH, HY_FILTER_ORDER), 0.02),
        'hy_freq2': 1.0 + nrm(ks[16], (DEPTH, HY_FILTER_ORDER), 0.02),
        'hy_w3': nrm(ks[17], (DEPTH, HY_FILTER_ORDER, hy_out), HY_FILTER_ORDER ** -0.5),
        'hy_b3': nrm(ks[18], (DEPTH, hy_out), 0.02),
        'hy_bias': nrm(ks[19], (DEPTH, HY_ORDER, HY_WIDTH), 1.0),
        'q_norm_g': 1.0 + nrm(ks[20], (DEPTH, DA_QK_DIM), 0.02),
        'k_norm_g': 1.0 + nrm(ks[21], (DEPTH, DA_QK_DIM), 0.02),
        'lam_q': nrm(ks[22], (DEPTH, 2, DA_QK_DIM), 0.1),
        'lam_k': nrm(ks[23], (DEPTH, 2, DA_QK_DIM), 0.1),
        'subln_g': 1.0 + nrm(ks[24], (DEPTH, DA_V_DIM), 0.02),
        'w_f': nrm(ks[25], (DEPTH, FN_WIDTH, D), FN_WIDTH ** -0.5),
        'w_h': nrm(ks[26], (DEPTH, HY_WIDTH, D), HY_WIDTH ** -0.5),
        'w_a': nrm(ks[27], (DEPTH, DA_WIDTH, D), DA_WIDTH ** -0.5),
        'w_o': nrm(ks[28], (DEPTH, D, D), D ** -0.5),
        'w_router': nrm(ks[29], (DEPTH, D, N_EXPERTS), D ** -0.5),
        'b_router': nrm(ks[30], (DEPTH, N_EXPERTS), 0.01),
        'w_e1': nrm(ks[31], (DEPTH, N_EXPERTS, D, 2 * D_EXPERT), D ** -0.5),
        'b_e1': nrm(ks[32], (DEPTH, N_EXPERTS, 2 * D_EXPERT), 0.02),
        'w_e2': nrm(ks[33], (DEPTH, N_EXPERTS, D_EXPERT, D), D_EXPERT ** -0.5),
        'b_e2': nrm(ks[34], (DEPTH, N_EXPERTS, D), 0.02),
    }


def reference(x, c, ctx, c_ctx, w_mod, b_mod, norm1_g, norm2_g, w_in, hy_conv_w, hy_conv_b,
              hy_w1, hy_b1, hy_freq1, hy_w2, hy_b2, hy_freq2, hy_w3, hy_b3, hy_bias,
              q_norm_g, k_norm_g, lam_q, lam_k, subln_g, w_f, w_h, w_a, w_o,
              w_router, b_router, w_e1, b_e1, w_e2, b_e2):
    xc = ctx
    for l in range(DEPTH):
        p = {
            'w_mod': w_mod[l], 'b_mod': b_mod[l], 'norm1_g': norm1_g[l], 'norm2_g': norm2_g[l],
            'w_in': w_in[l], 'hy_conv_w': hy_conv_w[l], 'hy_conv_b': hy_conv_b[l],
            'hy_w1': hy_w1[l], 'hy_b1': hy_b1[l], 'hy_freq1': hy_freq1[l],
            'hy_w2': hy_w2[l], 'hy_b2': hy_b2[l], 'hy_freq2': hy_freq2[l],
            'hy_w3': hy_w3[l], 'hy_b3': hy_b3[l], 'hy_bias': hy_bias[l],
            'q_norm_g': q_norm_g[l], 'k_norm_g': k_norm_g[l], 'lam_q': lam_q[l], 'lam_k': lam_k[l],
            'subln_g': subln_g[l], 'w_f': w_f[l], 'w_h': w_h[l], 'w_a': w_a[l], 'w_o': w_o[l],
            'w_router': w_router[l], 'b_router': b_router[l],
            'w_e1': w_e1[l], 'b_e1': b_e1[l], 'w_e2': w_e2[l], 'b_e2': b_e2[l],
        }
        x, xc = hybrid_layer(l, x, xc, c, c_ctx, p, l < DEPTH - 1)
    return x
```

```python
import contextlib
import math
import numpy as np
import ml_dtypes
import concourse.bass as bass
import concourse.mybir as mybir
from concourse.bass_utils import run_bass_kernel_spmd

F32 = mybir.dt.float32
BF16 = mybir.dt.bfloat16
AF = mybir.ActivationFunctionType
ALU = mybir.AluOpType
AX = mybir.AxisListType

D = 1024
L = 8192
NCTX = 256
T = L + NCTX
DEPTH = 2
EPS = 1e-6
SUBLN_EPS = 1e-5
OFF_F, OFF_HY, OFF_Q, OFF_K, OFF_V, OFF_G = 0, 256, 1024, 1536, 2048, 2560
NMIX = 1280
NEXP_CORE = 16
PAIRS = [[0, 4], [1, 5], [2, 6], [3, 7]]
HALVES = [[0, 1, 2, 3], [4, 5, 6, 7]]


class Buf:
    __slots__ = ("name", "w", "r", "dsem")

    def __init__(self, name):
        self.name = name
        self.w = []
        self.r = []
        self.dsem = None


class Tile:
    def __init__(self, t, b):
        self.t = t
        self.b = b

    def __getitem__(self, k):
        return self.t[k]


class Op:
    __slots__ = ("id", "eng", "fn", "deps", "isdma", "signal", "wbuf")

    def __init__(self, id, eng, fn, deps, isdma, wbuf=None):
        self.id = id; self.eng = eng; self.fn = fn; self.deps = deps
        self.isdma = isdma; self.signal = isdma; self.wbuf = wbuf


class FW:
    ENG = ("pe", "act", "dve", "pool", "sp")
    NDSEM = 72

    def __init__(self, nc):
        self.nc = nc
        self.es = contextlib.ExitStack()
        self.h = {"pe": nc.tensor, "act": nc.scalar, "dve": nc.vector, "pool": nc.gpsimd, "sp": nc.sync}
        self.esem = {e: self.es.enter_context(nc.semaphore("e_" + e)) for e in self.ENG}
        self.ecnt = {e: 0 for e in self.ENG}
        self.fsem = self.es.enter_context(nc.semaphore("fence"))
        self.fcnt = 0
        self.dsems = [self.es.enter_context(nc.semaphore("d%d" % i)) for i in range(self.NDSEM)]
        self.dissued = [0] * self.NDSEM
        self.dnext = 0
        self.seen = {e: {} for e in self.ENG}
        self.done = {}
        self.pending = []
        self.bufs = []
        self.opmap = {}
        self.nid = 0
        self.ninst = 0

    def buf(self, name):
        b = Buf(name)
        self.bufs.append(b)
        return b

    def sb(self, st, name, shape, dtype=F32):
        self.nid += 1
        name = "%s_%d" % (name, self.nid)
        t = st.enter_context(self.nc.sbuf_tensor(name, shape, dtype))
        return Tile(t, self.buf(name))

    def ps(self, st, name, shape=(128, 512), dtype=F32):
        self.nid += 1
        name = "%s_%d" % (name, self.nid)
        t = st.enter_context(self.nc.psum_tensor(name, list(shape), dtype))
        return Tile(t, self.buf(name))

    def dram(self, name, shape, dtype, kind="Internal"):
        t = self.nc.dram_tensor(name, list(shape), dtype, kind=kind)
        return Tile(t, self.buf(name))

    def op(self, eng, fn, R=(), W=()):
        deps = set()
        for b in R:
            deps.update(b.w)
        for b in W:
            deps.update(b.w)
            deps.update(b.r)
        o = Op(self.nid, eng, fn, deps, False)
        self.nid += 1
        self.pending.append(o)
        for b in R:
            b.r.append(o.id)
        for b in W:
            b.w = [o.id]; b.r = []
        return o

    def dma(self, q, out_ap, in_ap, W, R, part=False, **kw):
        deps = set(R.w)
        for x in W.w:
            if part:
                p = self.opmap.get(x)
                if p is not None and p.isdma and p.wbuf is W:
                    continue
            deps.add(x)
        deps.update(W.r)
        fn = (lambda h, o=out_ap, i=in_ap, kw=kw: h.dma_start(out=o, in_=i, **kw))
        o = Op(self.nid, q, fn, deps, True, wbuf=W)
        self.nid += 1
        self.pending.append(o)
        self.opmap[o.id] = o
        R.r.append(o.id)
        if part:
            W.w = list(W.w) + [o.id]
        else:
            W.w = [o.id]
            W.r = []
        return o

    def _wait(self, eng, key, val):
        s = self.seen[eng]
        if s.get(key, 0) >= val:
            return
        s[key] = val
        if key[0] == "e":
            sem = self.esem[key[1]]
        elif key[0] == "f":
            sem = self.fsem
        else:
            sem = self.dsems[key[1]]
        self.h[eng].wait_ge(sem, val)
        self.ninst += 1

    def flush(self):
        ops = self.pending
        self.pending = []
        ids = {o.id: o for o in ops}
        for o in ops:
            for d in o.deps:
                p = ids.get(d)
                if p is not None and not p.isdma and not (p.eng == "pe" and o.eng == "pe"):
                    p.signal = True
        for o in ops:
            for d in o.deps:
                ev = self.done.get(d)
                if ev is None:
                    continue
                key, val = ev
                if key == ("e", "pe") and o.eng == "pe":
                    continue
                if key[0] == "d":
                    val = max(val, self.dissued[key[1]])
                self._wait(o.eng, key, val)
            inst = o.fn(self.h[o.eng])
            self.ninst += 1
            if o.isdma:
                b = o.wbuf
                if b.dsem is None:
                    b.dsem = self.dnext % self.NDSEM
                    self.dnext += 1
                k = b.dsem
                self.dissued[k] += 16
                inst.then_inc(self.dsems[k], 16)
                self.done[o.id] = (("d", k), self.dissued[k])
            elif o.signal:
                self.ecnt[o.eng] += 1
                inst.then_inc(self.esem[o.eng], 1)
                self.done[o.id] = (("e", o.eng), self.ecnt[o.eng])

    def fence(self):
        last = {}
        for o in self.pending:
            if not o.isdma:
                last[o.eng] = o
        for o in last.values():
            o.signal = True
        self.flush()
        for e in self.ENG:
            if self.ecnt[e] > 0:
                self._wait("sp", ("e", e), self.ecnt[e])
        for k in range(self.NDSEM):
            if self.dissued[k] > 0:
                self._wait("sp", ("d", k), self.dissued[k])
        self.fcnt += 1
        self.h["sp"].sem_inc(self.fsem, 1)
        self.ninst += 1
        for e in self.ENG:
            self._wait(e, ("f",), self.fcnt)
            for e2 in self.ENG:
                self.seen[e][("e", e2)] = self.ecnt[e2]
            for k in range(self.NDSEM):
                self.seen[e][("d", k)] = self.dissued[k]
        for b in self.bufs:
            b.w = []; b.r = []
        self.done = {}
        self.opmap = {}

    def close(self):
        self.es.close()


class Pack:
    def __init__(self):
        self.off = {}
        self.n = 0
        self.items = []

    def add(self, name, width):
        self.off[name] = (self.n, width)
        self.n += width

    def fill(self, arr, name, val):
        o, w = self.off[name]
        val = np.asarray(val, np.float32)
        val = val.reshape(val.shape[0], -1)
        assert val.shape[1] == w, (name, val.shape, w)
        arr[:val.shape[0], o:o + w] = val


def fm(v, nk):
    return np.asarray(v, np.float32).reshape(nk, 128).T


def rep(v):
    v = np.asarray(v, np.float32).reshape(1, -1)
    return np.broadcast_to(v, (128, v.shape[1]))


def make_sm_layout():
    P = Pack()
    P.add("cT", 16)
    P.add("crep", 2 * 8 * 128)
    P.add("deltabc", 128)
    for l in range(DEPTH):
        P.add("bmodT%d" % l, 48)
        P.add("bmodbc%d" % l, 2 * 1024)
        P.add("g1T%d" % l, 8)
        P.add("g2T%d" % l, 8)
        P.add("qg%d" % l, 1)
        P.add("kg%d" % l, 1)
        P.add("subg%d" % l, 1)
        P.add("lamq%d" % l, 128)
        P.add("lamk%d" % l, 128)
        P.add("wrt%d" % l, 8 * 32)
        P.add("brt%d" % l, 32)
        P.add("b1T%d" % l, 16 * 16)
        P.add("b2r%d" % l, 1024)
        P.add("cwbc%d" % l, 3 * 384)
        P.add("cbbc%d" % l, 384)
        P.add("hw1%d" % l, 64)
        P.add("hb1%d" % l, 1)
        P.add("hf1%d" % l, 1)
        P.add("hw2%d" % l, 64)
        P.add("hb2%d" % l, 1)
        P.add("hf2%d" % l, 1)
        P.add("hw3%d" % l, 512)
        P.add("hb3bc%d" % l, 512)
        P.add("hbias%d" % l, 256)
    return P


SM = make_sm_layout()


def make_cst_layout():
    P = Pack()
    P.add("ident", 128)
    P.add("ones", 128)
    P.add("bones", 128)
    P.add("rot", 128)
    P.add("ntl8192", 64)
    P.add("ntl256", 2)
    P.add("c64", 128)
    P.add("s64", 128)
    return P


CST = make_cst_layout()


def rope_tables():
    rows = L // 64
    row = np.repeat(np.arange(rows), 64).astype(np.float32)
    col = np.tile(np.arange(64), rows).astype(np.float32)
    nf = 16
    inv = (10000.0 ** (-np.arange(nf, dtype=np.float32) / nf)).astype(np.float32)
    angr = row[None, :] * inv[:, None]
    angc = col[None, :] * inv[:, None]
    ang64 = np.concatenate([angr, angr, angc, angc], 0)
    cos = np.cos(ang64).astype(np.float32)
    sin = np.sin(ang64).astype(np.float32)
    cos = np.concatenate([cos, np.ones((64, NCTX), np.float32)], 1)
    sin = np.concatenate([sin, np.zeros((64, NCTX), np.float32)], 1)
    return np.concatenate([cos, cos], 0), np.concatenate([sin, sin], 0)


def hy_emb(l):
    t = np.linspace(0.0, 1.0, l, dtype=np.float32)[:, None]
    ang = (np.float32(2.0 * math.pi / l) * np.arange(l, dtype=np.float32))[:, None]
    bands = np.linspace(1e-4, 15, 16, dtype=np.float32)[None, :]
    emb = np.concatenate([t, np.cos(bands * ang), -np.sin(bands * ang)], -1)
    return emb.T.astype(np.float32)


def hy_deltas(s):
    d = np.abs(np.linspace(math.log(1e-2) / 1.5, math.log(1e-2) / 0.3, 256, dtype=np.float32))
    return d[128 * s:128 * s + 128]


def rot_lhsT():
    R = np.zeros((128, 128), np.float32)
    for blk in range(2):
        for base in (0, 32):
            for j in range(16):
                a = blk * 64 + base + j
                b = a + 16
                R[a, b] = -1.0
                R[b, a] = 1.0
    return R.T.copy()


def shared_layout():
    off = {}
    n = 0
    for l in range(DEPTH):
        off["wg%d" % l] = (n, (1024, 3072)); n += 1024 * 3072
        off["wmod%d" % l] = (n, (1024, 6144)); n += 1024 * 6144
        off["wo%d" % l] = (n, (1024, 1024)); n += 1024 * 1024
    off["ropec"] = (n, (128, T)); n += 128 * T
    off["ropes"] = (n, (128, T)); n += 128 * T
    rows = -(-n // (512 * 2048)) * 512
    return off, rows


SHOFF, SHROWS = shared_layout()


def host_prep(inp):
    x = inp["x"]; ctx = inp["ctx"]; c = inp["c"]; c_ctx = inp["c_ctx"]
    blob = np.zeros((SHROWS * 2048,), np.float32)

    def put(name, arr):
        o, shp = SHOFF[name]
        blob[o:o + arr.size] = np.ascontiguousarray(arr, np.float32).reshape(-1)

    for l in range(DEPTH):
        put("wg%d" % l, inp["w_in"][l][:, OFF_G:])
        put("wmod%d" % l, inp["w_mod"][l])
        put("wo%d" % l, inp["w_o"][l])
    rc, rs = rope_tables()
    put("ropec", rc); put("ropes", rs)
    blob = blob.reshape(SHROWS // 512, 4, 128, 2048)

    cst = np.zeros((128, CST.n), np.float32)
    CST.fill(cst, "ident", np.eye(128, dtype=np.float32))
    CST.fill(cst, "ones", np.ones((128, 128), np.float32))
    bo = np.zeros((128, 128), np.float32); bo[:64, :64] = 1; bo[64:, 64:] = 1
    CST.fill(cst, "bones", bo)
    CST.fill(cst, "rot", rot_lhsT())
    pp = np.arange(128)[:, None]
    CST.fill(cst, "ntl8192", -((np.arange(64)[None, :] * 128 + pp) / (L - 1.0)))
    CST.fill(cst, "ntl256", -((np.arange(2)[None, :] * 128 + pp) / (NCTX - 1.0)))
    a64 = np.arange(64)
    c64 = np.cos(2 * np.pi * np.outer(a64, a64) / 64); s64 = np.sin(2 * np.pi * np.outer(a64, a64) / 64)
    z = np.zeros((64, 64))
    CST.fill(cst, "c64", np.block([[c64, z], [z, c64]]))
    CST.fill(cst, "s64", np.block([[s64, z], [z, s64]]))
    embc = np.concatenate([hy_emb(L), hy_emb(NCTX)], 1).astype(np.float32)
    fftc = fft_constants()

    maps = []
    for r in range(8):
        s, b = r // 4, r % 4
        m = {}
        m["xs"] = np.ascontiguousarray(x[b, s * 4096:(s + 1) * 4096])
        m["ctxb"] = np.ascontiguousarray(ctx[b])
        m["wsh"] = np.ascontiguousarray(blob[:, b]).reshape(SHROWS // 4, 2048)
        m["cst"] = cst
        m["embc"] = embc
        m["fftc"] = fftc
        wmix = np.zeros((DEPTH, 1024, NMIX), np.float32)
        wout = np.zeros((DEPTH, 512, 1024), np.float32)
        sm = np.zeros((128, SM.n), np.float32)
        SM.fill(sm, "cT", np.stack([fm(c[b], 8), fm(c_ctx, 8)], -1).reshape(128, 16))
        crep = np.zeros((128, 2, 8, 128), np.float32)
        crep[:, 0] = fm(c[b], 8)[:, :, None]
        crep[:, 1] = fm(c_ctx, 8)[:, :, None]
        SM.fill(sm, "crep", crep.reshape(128, -1))
        SM.fill(sm, "deltabc", rep(hy_deltas(s)))
        for l in range(DEPTH):
            w_in = inp["w_in"][l]
            cols = np.concatenate([
                np.arange(OFF_F + 128 * s, OFF_F + 128 * s + 128),
                np.arange(OFF_HY + 128 * s, OFF_HY + 128 * s + 128),
                np.arange(OFF_HY + 256 + 128 * s, OFF_HY + 256 + 128 * s + 128),
                np.arange(OFF_HY + 512 + 128 * s, OFF_HY + 512 + 128 * s + 128),
                np.arange(OFF_Q + 256 * s, OFF_Q + 256 * s + 256),
                np.arange(OFF_K + 256 * s, OFF_K + 256 * s + 256),
                np.arange(OFF_V + 256 * s, OFF_V + 256 * s + 256)])
            wmix[l] = w_in[:, cols]
            wout[l, 0:128] = inp["w_f"][l][128 * s:128 * s + 128]
            wout[l, 128:256] = inp["w_h"][l][128 * s:128 * s + 128]
            wout[l, 256:512] = inp["w_a"][l][256 * s:256 * s + 256]
            SM.fill(sm, "bmodT%d" % l, fm(inp["b_mod"][l], 48))
            bm = inp["b_mod"][l].reshape(6, 1024)
            SM.fill(sm, "bmodbc%d" % l, rep(np.concatenate([bm[2], bm[5]])))
            SM.fill(sm, "g1T%d" % l, fm(inp["norm1_g"][l], 8))
            SM.fill(sm, "g2T%d" % l, fm(inp["norm2_g"][l], 8))
            SM.fill(sm, "qg%d" % l, np.tile(inp["q_norm_g"][l], 2).reshape(128, 1))
            SM.fill(sm, "kg%d" % l, np.tile(inp["k_norm_g"][l], 2).reshape(128, 1))
            SM.fill(sm, "subg%d" % l, inp["subln_g"][l].reshape(128, 1))
            SM.fill(sm, "lamq%d" % l, rep(inp["lam_q"][l].reshape(-1)))
            SM.fill(sm, "lamk%d" % l, rep(inp["lam_k"][l].reshape(-1)))
            es_ = [16 * s + le for le in range(16)]
            perm = es_ + [e for e in range(32) if e not in es_]
            SM.fill(sm, "wrt%d" % l, inp["w_router"][l][:, perm].reshape(8, 128, 32).transpose(1, 0, 2).reshape(128, 256))
            SM.fill(sm, "brt%d" % l, rep(inp["b_router"][l][perm]))
            b1 = inp["b_e1"][l][es_]
            b1p = np.concatenate([b1[:, 0::2], b1[:, 1::2]], 1)
            SM.fill(sm, "b1T%d" % l, b1p.reshape(16, 16, 128).transpose(2, 0, 1).reshape(128, 256))
            SM.fill(sm, "b2r%d" % l, inp["b_e2"][l][es_])
            cw = inp["hy_conv_w"][l]; cb = inp["hy_conv_b"][l]
            vx = np.concatenate([np.arange(128 * s, 128 * s + 128), np.arange(256 + 128 * s, 256 + 128 * s + 128),
                                 np.arange(512 + 128 * s, 512 + 128 * s + 128)])
            SM.fill(sm, "cwbc%d" % l, rep(cw[:, vx].reshape(-1)))
            SM.fill(sm, "cbbc%d" % l, rep(cb[vx]))
            SM.fill(sm, "hw1%d" % l, inp["hy_w1"][l])
            SM.fill(sm, "hb1%d" % l, inp["hy_b1"][l].reshape(64, 1))
            SM.fill(sm, "hf1%d" % l, inp["hy_freq1"][l].reshape(64, 1))
            SM.fill(sm, "hw2%d" % l, inp["hy_w2"][l])
            SM.fill(sm, "hb2%d" % l, inp["hy_b2"][l].reshape(64, 1))
            SM.fill(sm, "hf2%d" % l, inp["hy_freq2"][l].reshape(64, 1))
            w3 = inp["hy_w3"][l].reshape(64, 2, 2, 256)[:, :, :, 128 * s:128 * s + 128].reshape(64, 512)
            b3 = inp["hy_b3"][l].reshape(2, 2, 256)[:, :, 128 * s:128 * s + 128].reshape(512)
            SM.fill(sm, "hw3%d" % l, w3)
            SM.fill(sm, "hb3bc%d" % l, rep(b3))
            SM.fill(sm, "hbias%d" % l, rep(inp["hy_bias"][l][:, 128 * s:128 * s + 128].reshape(-1)))
        m["wmix"] = wmix
        m["wout"] = wout
        m["sm"] = sm
        wexp = np.zeros((DEPTH, 6144, 2048), np.float32)
        for l in range(DEPTH):
            full = np.zeros((16, 1536, 2048), np.float32)
            for le in range(16):
                e = 16 * s + le
                w1 = inp["w_e1"][l][e]
                full[le, :1024] = np.concatenate([w1[:, 0::2], w1[:, 1::2]], 1)
                full[le, 1024:] = inp["w_e2"][l][e].reshape(512, 2048)
            wexp[l] = full.reshape(48, 4, 128, 2048)[:, b].reshape(6144, 2048)
        m["wexp"] = wexp
        maps.append(m)
    return maps


def dview(tile, off, shape):
    r, c = shape
    return bass.AP(tile.t, off, [[c, r], [1, c]])


class Prog:
    def __init__(self, stop=None, dumps=(), with_exp=True):
        self.stop = stop
        self.dumps = list(dumps)
        self.with_exp = with_exp
        nc = self.nc = bass.Bass("TRN2", target_bir_lowering=False)
        fw = self.fw = FW(nc)
        self.ncc = 0
        self.xs = fw.dram("xs", [4096, 1024], F32, "ExternalInput")
        self.ctxb = fw.dram("ctxb", [NCTX, 1024], F32, "ExternalInput")
        self.wsh = fw.dram("wsh", [SHROWS // 4, 2048], F32, "ExternalInput")
        self.cst_d = fw.dram("cst", [128, CST.n], F32, "ExternalInput")
        self.wmix = fw.dram("wmix", [DEPTH, 1024, NMIX], F32, "ExternalInput")
        self.wout = fw.dram("wout", [DEPTH, 512, 1024], F32, "ExternalInput")
        self.sm_d = fw.dram("sm", [128, SM.n], F32, "ExternalInput")
        self.embc = fw.dram("embc", [33, T], F32, "ExternalInput")
        self.fftc_d = fw.dram("fftc", [128, FFTC.n], F32, "ExternalInput")
        if with_exp:
            self.wexp = fw.dram("wexp", [DEPTH, 6144, 2048], F32, "ExternalInput")
        self.out = fw.dram("out", [L, 1024], F32, "ExternalOutput")
        self.X = fw.dram("X", [T, 1024], F32)
        self.SH = fw.dram("SH", [SHROWS, 2048], F32)
        self.QT = fw.dram("QT", [256, T], BF16)
        self.KT = fw.dram("KT", [256, T], BF16)
        self.V = fw.dram("V", [T, 256], BF16)
        self.GT = fw.dram("GT", [3072, T], BF16)
        self.PFH = fw.dram("PFH", [T + 4, 512], F32)
        self.MODBC = fw.dram("MODBC", [128, 4096], F32)
        self.FM = fw.dram("FM", [T, 2, 128], F32)
        self.HM = fw.dram("HM", [T, 128], F32)
        self.OT = fw.dram("OT", [256, T], BF16)
        self.ARin = fw.dram("ARin", [T, 1024], F32)
        self.dump_out = {}

    def coll_seq(self, items):
        fw = self.fw
        sem = fw.es.enter_context(self.nc.semaphore("cc%d" % self.ncc))
        self.ncc += 1
        fw.fence()
        for (kind, groups, in_ap, out_ap) in items:
            op = ALU.bypass if kind in ("AllGather", "AllToAll") else ALU.add
            fw.h["pool"].collective_compute(kind, op, replica_groups=groups, ins=[in_ap], outs=[out_ap]).then_inc(sem)
        for e in fw.ENG:
            fw.h[e].wait_ge(sem, len(items))

    def ld(self, ps, name, rows=128, q="sp"):
        o, w = SM.off[name]
        t = self.fw.sb(ps, "sm_" + name, [rows, w])
        kw = dict(allow_slow_non_contiguous=True) if w == 1 else {}
        self.fw.dma(q, t[:], self.sm_d.t.ap()[0:rows, o:o + w], t.b, self.sm_d.b, **kw)
        return t

    def cv(self, name):
        o, w = CST.off[name]
        return self.cst[:, o:o + w]

    def dump(self, name, tile):
        if name in self.dumps:
            t = tile.t
            d = self.fw.dram("dump_" + name, list(t.shape), t.dtype, "ExternalOutput")
            self.fw.dma("sp", d.t.ap(), t.ap(), d.b, tile.b)
            self.dump_out[name] = "dump_" + name
            self.fw.fence()

    def bigcopy(self, dst, dap, src, sap, rows, step=256):
        for r0 in range(0, rows, step):
            r1 = min(rows, r0 + step)
            self.fw.dma("sp", dap[r0:r1, :], sap[r0:r1, :], dst.b, src.b, part=True)

    def phase_gather(self):
        fw = self.fw
        xsI = fw.dram("xsI", [4096, 1024], F32)
        wshI = fw.dram("wshI", [SHROWS // 4, 2048], F32)
        self.bigcopy(xsI, xsI.t.ap(), self.xs, self.xs.t.ap(), 4096, 512)
        self.bigcopy(wshI, wshI.t.ap(), self.wsh, self.wsh.t.ap(), SHROWS // 4)
        fw.dma("sp", self.X.t.ap()[L:T, :], self.ctxb.t.ap(), self.X.b, self.ctxb.b, part=True)
        if self.with_exp:
            self.wexpI = fw.dram("wexpI", [DEPTH, 6144, 2048], F32)
            for l in range(DEPTH):
                self.bigcopy(self.wexpI, self.wexpI.t.ap()[l], self.wexp, self.wexp.t.ap()[l], 6144)
        XG = fw.dram("XG", [8, 1024, 1024], F32)
        items = [("AllGather", PAIRS, xsI.t.ap()[c * 512:(c + 1) * 512, :], XG.t.ap()[c]) for c in range(8)]
        items += [("AllGather", HALVES, wshI.t.ap()[c * 128:(c + 1) * 128, :], self.SH.t.ap()[c * 512:(c + 1) * 512, :])
                  for c in range(SHROWS // 512)]
        if self.with_exp:
            self.WEl = [fw.dram("WE%d" % l, [24576, 2048], F32) for l in range(DEPTH)]
            for l in range(DEPTH):
                items += [("AllGather", HALVES, self.wexpI.t.ap()[l, c * 128:(c + 1) * 128, :], self.WEl[l].t.ap()[c * 512:(c + 1) * 512, :])
                          for c in range(48)]
        self.coll_seq(items)
        for c in range(8):
            for r in range(2):
                for q in range(2):
                    fw.dma("sp", self.X.t.ap()[r * 4096 + c * 512 + q * 256: r * 4096 + c * 512 + q * 256 + 256, :],
                           XG.t.ap()[c, r * 512 + q * 256: r * 512 + q * 256 + 256, :], self.X.b, XG.b, part=True)
        fw.fence()

    def phase_mod(self, l):
        fw = self.fw
        with contextlib.ExitStack() as ps:
            cT = self.ld(ps, "cT"); crep = self.ld(ps, "crep")
            bmodT = self.ld(ps, "bmodT%d" % l); bmodbc = self.ld(ps, "bmodbc%d" % l)
            g1T = self.ld(ps, "g1T%d" % l); g2T = self.ld(ps, "g2T%d" % l)
            lamq = self.ld(ps, "lamq%d" % l); lamk = self.ld(ps, "lamk%d" % l)
            for nm, dst in (("qg%d" % l, 0), ("kg%d" % l, 1), ("subg%d" % l, 2)):
                o, w = SM.off[nm]
                fw.dma("sp", self.qks[:, dst:dst + 1], self.sm_d.t.ap()[:, o:o + 1], self.qks.b, self.sm_d.b, part=True, allow_slow_non_contiguous=True)
            sc = fw.sb(ps, "sc", [128, 8, 2])
            screp = fw.sb(ps, "screp", [128, 2, 8, 128])
            mbc = fw.sb(ps, "mbc", [128, 2, 2, 1024])
            pm = fw.ps(ps, "pm")
            pbc = [fw.ps(ps, "pbc%d" % i) for i in range(2)]
            wm = [fw.sb(ps, "wm%d" % i, [128, 8, 512]) for i in range(2)]
            fw.op("act", lambda e: e.activation(out=sc[:].rearrange("p k j -> p (k j)"), in_=cT[:], func=AF.Silu),
                  R=[cT.b], W=[sc.b])
            fw.op("act", lambda e: e.activation(out=screp[:].rearrange("p j k m -> p (j k m)"), in_=crep[:], func=AF.Silu),
                  R=[crep.b], W=[screp.b])
            o, _ = SHOFF["wmod%d" % l]
            wv = dview(self.SH, o, (1024, 6144)).rearrange("(k p) c -> p k c", p=128)
            for cb in range(12):
                w = wm[cb % 2]
                fw.dma("sp", w[:], wv[:, :, cb * 512:(cb + 1) * 512], w.b, self.SH.b)
                for oc in range(4):
                    col = (cb * 4 + oc) * 2
                    for k in range(8):
                        fw.op("pe", lambda e, w=w, oc=oc, k=k, col=col: e.matmul(
                            pm[:, col:col + 2], w[:, k, oc * 128:(oc + 1) * 128], sc[:, k, :], start=(k == 0), stop=(k == 7)),
                            R=[w.b, sc.b], W=[pm.b])
                if cb in (4, 5, 10, 11):
                    which = 0 if cb < 6 else 1
                    half = cb - 4 if cb < 6 else cb - 10
                    for j in range(2):
                        pb = pbc[j]
                        for k in range(8):
                            fw.op("pe", lambda e, w=w, j=j, k=k, pb=pb: e.matmul(
                                pb[:], screp[:, j, k, :], w[:, k, :], start=(k == 0), stop=(k == 7)),
                                R=[w.b, screp.b], W=[pb.b])
                        bsl = bmodbc[:, which * 1024 + half * 512: which * 1024 + half * 512 + 512]
                        fw.op("dve", lambda e, pb=pb, j=j, which=which, half=half, bsl=bsl: e.tensor_tensor(
                            out=mbc[:, j, which, half * 512:(half + 1) * 512], in0=pb[:], in1=bsl, op=ALU.add),
                            R=[pb.b, bmodbc.b], W=[mbc.b])
            fw.dma("sp", self.MODBC.t.ap(), mbc[:].rearrange("p j w c -> p (j w c)"), self.MODBC.b, mbc.b)
            pmv = pm[:, 0:96].rearrange("p (a j) -> p a j", j=2)
            for j in range(2):
                fw.op("dve", lambda e, j=j: e.tensor_tensor(out=self.modT[:, :, j], in0=pmv[:, :, j], in1=bmodT[:], op=ALU.add),
                      R=[pm.b, bmodT.b], W=[self.modT.b])
            for (gT, gs, sh, ishift, iscale) in ((g1T, self.gs1, self.sh1, 0, 1), (g2T, self.gs2, self.sh2, 3, 4)):
                for j in range(2):
                    fw.op("dve", lambda e, j=j, gs=gs, iscale=iscale, gT=gT: e.scalar_tensor_tensor(
                        out=gs[:, :, j], in0=self.modT[:, iscale * 8:(iscale + 1) * 8, j], scalar=1.0, in1=gT[:],
                        op0=ALU.add, op1=ALU.mult), R=[self.modT.b, gT.b], W=[gs.b])
                    fw.op("dve", lambda e, j=j, sh=sh, ishift=ishift: e.tensor_copy(
                        out=sh[:, :, j], in_=self.modT[:, ishift * 8:(ishift + 1) * 8, j]), R=[self.modT.b], W=[sh.b])
            lp = fw.sb(ps, "lp", [128, 2, 64])
            le = fw.sb(ps, "le", [128, 2])
            fw.op("dve", lambda e: e.tensor_tensor(out=lp[:].rearrange("p a b -> p (a b)"), in0=lamq[:], in1=lamk[:], op=ALU.mult),
                  R=[lamq.b, lamk.b], W=[lp.b])
            fw.op("dve", lambda e: e.tensor_reduce(out=le[:], in_=lp[:], axis=AX.X, op=ALU.add), R=[lp.b], W=[le.b])
            fw.op("act", lambda e: e.activation(out=le[:], in_=le[:], func=AF.Exp), R=[le.b], W=[le.b])
            lam_init = 0.8 - 0.6 * math.exp(-0.3 * l)
            fw.op("dve", lambda e: e.tensor_tensor(out=self.lam[:, 0:1], in0=le[:, 0:1], in1=le[:, 1:2], op=ALU.subtract),
                  R=[le.b], W=[self.lam.b])
            fw.op("dve", lambda e: e.tensor_scalar(out=self.lam[:, 0:1], in0=self.lam[:, 0:1], scalar1=lam_init, scalar2=None,
                                                   op0=ALU.add), R=[self.lam.b], W=[self.lam.b])
            fw.op("dve", lambda e: e.tensor_scalar(out=self.lam[:, 1:2], in0=self.lam[:, 0:1], scalar1=-1.0, scalar2=None,
                                                   op0=ALU.mult), R=[self.lam.b], W=[self.lam.b])
            fw.fence()

    def pfh_row(self, t0):
        return t0 + 1 if t0 < L else (t0 - L) + L + 3

    def phase_proj(self, l):
        fw = self.fw
        with contextlib.ExitStack() as ps:
            Wm = fw.sb(ps, "Wm", [128, 8, NMIX], BF16)
            Wg = fw.sb(ps, "Wg", [128, 8, 3072], BF16)
            stg = [fw.sb(ps, "stg%d" % i, [128, 8, 256]) for i in range(2)]
            wmv = self.wmix.t.ap()[l].rearrange("(k p) c -> p k c", p=128)
            og, _ = SHOFF["wg%d" % l]
            wgv = dview(self.SH, og, (1024, 3072)).rearrange("(k p) c -> p k c", p=128)
            n = 0
            for (src, srcb, dst, nb) in ((wmv, self.wmix.b, Wm, NMIX // 256), (wgv, self.SH.b, Wg, 12)):
                for cb in range(nb):
                    s = stg[n % 2]; n += 1
                    fw.dma("sp", s[:], src[:, :, cb * 256:(cb + 1) * 256], s.b, srcb)
                    fw.op("act" if n % 2 else "dve",
                          (lambda e, s=s, dst=dst, cb=cb: e.activation(out=dst[:, :, cb * 256:(cb + 1) * 256], in_=s[:], func=AF.Copy))
                          if n % 2 else
                          (lambda e, s=s, dst=dst, cb=cb: e.tensor_copy(out=dst[:, :, cb * 256:(cb + 1) * 256], in_=s[:])),
                          R=[s.b], W=[dst.b])
            xt = [fw.sb(ps, "xt%d" % i, [128, 1024]) for i in range(2)]
            xn = [fw.sb(ps, "xn%d" % i, [128, 1024]) for i in range(2)]
            junk = fw.sb(ps, "junk", [128, 1024])
            ss = [fw.sb(ps, "ss%d" % i, [128, 2]) for i in range(2)]
            hT = [fw.sb(ps, "hT%d" % i, [128, 8, 512], BF16) for i in range(2)]
            rc = [fw.sb(ps, "rc%d" % i, [128, 512]) for i in range(2)]
            rs_ = [fw.sb(ps, "rs%d" % i, [128, 512]) for i in range(2)]
            sq = fw.sb(ps, "sq", [128, 512]); rq = fw.sb(ps, "rq", [128, 512]); qn = fw.sb(ps, "qn", [128, 512])
            t1 = fw.sb(ps, "t1", [128, 512]); t2 = fw.sb(ps, "t2", [128, 512])
            qo = [fw.sb(ps, "qo%d" % i, [128, 512], BF16) for i in range(2)]
            fhs = [fw.sb(ps, "fhs%d" % i, [128, 512]) for i in range(2)]
            vs = [fw.sb(ps, "vs%d" % i, [128, 256], BF16) for i in range(2)]
            gst = [fw.sb(ps, "gst%d" % i, [128, 4, 512], BF16) for i in range(2)]
            zt = fw.sb(ps, "zt", [4, 512])
            pt = [fw.ps(ps, "pt%d" % i) for i in range(2)]
            pj = [fw.ps(ps, "pj%d" % i) for i in range(3)]
            pn = [fw.ps(ps, "pn%d" % i) for i in range(2)]
            ident = self.cv("ident"); bones = self.cv("bones"); rot = self.cv("rot")
            cb_ = self.cstb
            fw.op("dve", lambda e: e.memset(zt[:], 0.0), W=[zt.b])
            for r in (0, L + 1, L + 2, T + 3):
                fw.dma("sp", self.PFH.t.ap()[r:r + 1, :], zt[0:1, :], self.PFH.b, zt.b, part=True)
            oc_, _ = SHOFF["ropec"]; os_, _ = SHOFF["ropes"]
            rcv = dview(self.SH, oc_, (128, T)); rsv = dview(self.SH, os_, (128, T))
            npj = 0
            ntile = (T + 511) // 512
            for tt in range(ntile):
                t0 = tt * 512
                n_ = min(512, T - t0)
                nsub = n_ // 128
                j = 0 if t0 < L else 1
                h = hT[tt % 2]
                for i in range(nsub):
                    x = xt[i % 2]; y = xn[i % 2]; s2 = ss[i % 2]
                    fw.dma("sp", x[:], self.X.t.ap()[t0 + i * 128:t0 + (i + 1) * 128, :], x.b, self.X.b)
                    fw.op("act", lambda e, x=x, s2=s2: e.activation(out=junk[:], in_=x[:], func=AF.Square, accum_out=s2[:, 0:1]),
                          R=[x.b], W=[junk.b, s2.b])
                    fw.op("act", lambda e, s2=s2: e.activation(out=s2[:, 1:2], in_=s2[:, 0:1], func=AF.Sqrt, scale=1.0 / D, bias=self.epsb[:, 0:1]),
                          R=[s2.b], W=[s2.b])
                    fw.op("dve", lambda e, s2=s2: e.reciprocal(out=s2[:, 1:2], in_=s2[:, 1:2]), R=[s2.b], W=[s2.b])
                    fw.op("dve", lambda e, x=x, y=y, s2=s2: e.tensor_scalar(out=y[:], in0=x[:], scalar1=s2[:, 1:2], scalar2=None, op0=ALU.mult),
                          R=[x.b, s2.b], W=[y.b])
                    for g in range(2):
                        for q in range(4):
                            k = 4 * g + q
                            fw.op("pe", lambda e, y=y, g=g, q=q, k=k: e.transpose(out=pt[g][:, q * 128:(q + 1) * 128], in_=y[:, k * 128:(k + 1) * 128], identity=ident),
                                  R=[y.b, cb_], W=[pt[g].b])
                        for q in range(4):
                            k = 4 * g + q
                            fw.op("dve", lambda e, g=g, q=q, k=k, i=i, h=h, j=j: e.tensor_scalar(
                                out=h[:, k, i * 128:(i + 1) * 128], in0=pt[g][:, q * 128:(q + 1) * 128],
                                scalar1=self.gs1[:, k, j:j + 1], scalar2=self.sh1[:, k, j:j + 1], op0=ALU.mult, op1=ALU.add),
                                R=[pt[g].b, self.gs1.b, self.sh1.b], W=[h.b])
                c_ = rc[tt % 2]; s_ = rs_[tt % 2]
                fw.dma("sp", c_[:, 0:n_], rcv[:, t0:t0 + n_], c_.b, self.SH.b)
                fw.dma("sp", s_[:, 0:n_], rsv[:, t0:t0 + n_], s_.b, self.SH.b)
                for i in range(nsub):
                    p = pj[npj % 3]; npj += 1
                    for k in range(8):
                        fw.op("pe", lambda e, p=p, k=k, i=i, h=h: e.matmul(p[:, 0:512], h[:, k, i * 128:(i + 1) * 128], Wm[:, k, 0:512], start=(k == 0), stop=(k == 7)),
                              R=[h.b, Wm.b], W=[p.b])
                    f = fhs[i % 2]
                    fw.op("act", lambda e, p=p, f=f: e.activation(out=f[:], in_=p[:, 0:512], func=AF.Copy), R=[p.b], W=[f.b])
                    r0 = self.pfh_row(t0 + i * 128)
                    fw.dma("sp", self.PFH.t.ap()[r0:r0 + 128, :], f[:], self.PFH.b, f.b, part=True)
                    p = pj[npj % 3]; npj += 1
                    for k in range(8):
                        fw.op("pe", lambda e, p=p, k=k, i=i, h=h: e.matmul(p[:, 0:256], h[:, k, i * 128:(i + 1) * 128], Wm[:, k, 1024:1280], start=(k == 0), stop=(k == 7)),
                              R=[h.b, Wm.b], W=[p.b])
                    v = vs[i % 2]
                    fw.op("act", lambda e, p=p, v=v: e.activation(out=v[:], in_=p[:, 0:256], func=AF.Copy), R=[p.b], W=[v.b])
                    fw.dma("sp", self.V.t.ap()[t0 + i * 128:t0 + (i + 1) * 128, :], v[:], self.V.b, v.b, part=True)
                for c in range(4):
                    p = pj[npj % 3]; npj += 1
                    c0 = 512 + c * 128
                    for k in range(8):
                        fw.op("pe", lambda e, p=p, k=k, h=h, n_=n_, c0=c0: e.matmul(p[:, 0:n_], Wm[:, k, c0:c0 + 128], h[:, k, 0:n_], start=(k == 0), stop=(k == 7)),
                              R=[h.b, Wm.b], W=[p.b])
                    fw.op("act", lambda e, p=p, n_=n_: e.activation(out=sq[:, 0:n_], in_=p[:, 0:n_], func=AF.Square), R=[p.b], W=[sq.b])
                    pa = pn[0]
                    fw.op("pe", lambda e, pa=pa, n_=n_: e.matmul(pa[:, 0:n_], bones, sq[:, 0:n_], start=True, stop=True), R=[sq.b, cb_], W=[pa.b])
                    fw.op("act", lambda e, pa=pa, n_=n_: e.activation(out=rq[:, 0:n_], in_=pa[:, 0:n_], func=AF.Sqrt, scale=1.0 / 64, bias=self.epsb[:, 0:1]),
                          R=[pa.b], W=[rq.b])
                    fw.op("dve", lambda e, n_=n_: e.reciprocal(out=rq[:, 0:n_], in_=rq[:, 0:n_]), R=[rq.b], W=[rq.b])
                    gi = 0 if c < 2 else 1
                    fw.op("dve", lambda e, p=p, n_=n_, gi=gi: e.scalar_tensor_tensor(out=qn[:, 0:n_], in0=p[:, 0:n_], scalar=self.qks[:, gi:gi + 1], in1=rq[:, 0:n_],
                                                                                   op0=ALU.mult, op1=ALU.mult), R=[p.b, rq.b, self.qks.b], W=[qn.b])
                    pb = pn[1]
                    fw.op("pe", lambda e, pb=pb, n_=n_: e.matmul(pb[:, 0:n_], rot, qn[:, 0:n_], start=True, stop=True), R=[qn.b, cb_], W=[pb.b])
                    fw.op("dve", lambda e, n_=n_, c_=c_: e.tensor_tensor(out=t1[:, 0:n_], in0=qn[:, 0:n_], in1=c_[:, 0:n_], op=ALU.mult), R=[qn.b, c_.b], W=[t1.b])
                    fw.op("dve", lambda e, pb=pb, n_=n_, s_=s_: e.tensor_tensor(out=t2[:, 0:n_], in0=pb[:, 0:n_], in1=s_[:, 0:n_], op=ALU.mult), R=[pb.b, s_.b], W=[t2.b])
                    o_ = qo[c % 2]
                    fw.op("dve", lambda e, n_=n_, o_=o_: e.tensor_tensor(out=o_[:, 0:n_], in0=t1[:, 0:n_], in1=t2[:, 0:n_], op=ALU.add), R=[t1.b, t2.b], W=[o_.b])
                    dstT = self.QT if c < 2 else self.KT
                    cc = c % 2
                    fw.dma("sp", dstT.t.ap()[cc * 128:(cc + 1) * 128, t0:t0 + n_], o_[:, 0:n_], dstT.b, o_.b, part=True)
                for c4 in range(6):
                    g_ = gst[c4 % 2]
                    for c in range(4):
                        cg = c4 * 4 + c
                        p = pj[npj % 3]; npj += 1
                        for k in range(8):
                            fw.op("pe", lambda e, p=p, k=k, h=h, n_=n_, cg=cg: e.matmul(p[:, 0:n_], Wg[:, k, cg * 128:(cg + 1) * 128], h[:, k, 0:n_], start=(k == 0), stop=(k == 7)),
                                  R=[h.b, Wg.b], W=[p.b])
                        fw.op("act", lambda e, p=p, n_=n_, g_=g_, c=c: e.activation(out=g_[:, c, 0:n_], in_=p[:, 0:n_], func=AF.Sigmoid), R=[p.b], W=[g_.b])
                    fw.dma("sp", self.GT.t.ap()[c4 * 512:(c4 + 1) * 512, t0:t0 + n_].rearrange("(c p) t -> p c t", p=128), g_[:, :, 0:n_], self.GT.b, g_.b, part=True)
            fw.fence()

    def build(self):
        fw = self.fw
        with contextlib.ExitStack() as st:
            self.cst = fw.sb(st, "cstt", [128, CST.n]); self.cstb = self.cst.b
            self.modT = fw.sb(st, "modT", [128, 48, 2])
            self.gs1 = fw.sb(st, "gs1", [128, 8, 2]); self.sh1 = fw.sb(st, "sh1", [128, 8, 2])
            self.gs2 = fw.sb(st, "gs2", [128, 8, 2]); self.sh2 = fw.sb(st, "sh2", [128, 8, 2])
            self.lam = fw.sb(st, "lam", [128, 2])
            self.qks = fw.sb(st, "qks", [128, 4])
            self.epsb = fw.sb(st, "epsb", [128, 2])
            fw.dma("sp", self.cst[:], self.cst_d.t.ap(), self.cst.b, self.cst_d.b)
            fw.op("dve", lambda e: e.memset(self.epsb[:, 0:1], EPS), W=[self.epsb.b])
            fw.op("dve", lambda e: e.memset(self.epsb[:, 1:2], SUBLN_EPS), W=[self.epsb.b])
            self.phase_gather()
            for l in range(DEPTH if self.stop != ("gather", 0) else 0):
                self.phase_mod(l)
                if self.stop == ("mod", l):
                    break
                self.phase_proj(l)
                if self.stop == ("proj", l):
                    break
                self.phase_seqmix(l, "lat")
                if l == 0:
                    self.phase_seqmix(l, "ctx")
                if self.stop == ("seq", l):
                    break
                self.phase_attn(l)
                if self.stop == ("attn", l):
                    break
                self.phase_outproj(l)
                if self.stop == ("outp", l):
                    break
                if self.with_exp:
                    self.phase_moe(l)
            for nm, tl in (("QT", self.QT), ("KT", self.KT), ("V", self.V), ("PFH", self.PFH), ("GT", self.GT), ("X", self.X), ("MODBC", self.MODBC), ("FM", self.FM), ("HM", self.HM), ("OT", self.OT)):
                self.dump(nm, tl)
            self.bigcopy(self.out, self.out.t.ap(), self.X, self.X.t.ap(), L, 512)
            fw.fence()
        fw.close()
        return self.nc


FFT_CFG = {"FA": (64, 64), "HB": (128, 64), "FC": (2, 2), "HD": (4, 2)}


def make_fft_layout():
    P = Pack()
    for nm in ("w128r", "w128i", "w128n"):
        P.add(nm, 128)
    for cfg, (n1, n1in) in FFT_CFG.items():
        for nm in ("w1r", "w1i", "w1n"):
            P.add(cfg + nm, n1)
        P.add(cfg + "tr", 128); P.add(cfg + "ti", 128); P.add(cfg + "tn", 128)
        if cfg in ("HB", "HD"):
            for nm in ("v1r", "v1i", "v1n"):
                P.add(cfg + nm, n1in)
    return P


FFTC = make_fft_layout()


def fft_constants():
    c = np.zeros((128, FFTC.n), np.float64)

    def put(name, val):
        o, w = FFTC.off[name]
        c[:val.shape[0], o:o + w] = val

    a = np.arange(128)
    w128 = np.exp(-2j * np.pi * np.outer(a, a) / 128)
    put("w128r", w128.real); put("w128i", w128.imag); put("w128n", -w128.imag)
    for cfg, (n1, n1in) in FFT_CFG.items():
        N = n1 * 128
        b = np.arange(n1)
        w1 = np.exp(-2j * np.pi * np.outer(b, b) / n1)
        put(cfg + "w1r", w1.real); put(cfg + "w1i", w1.imag); put(cfg + "w1n", -w1.imag)
        tw = np.exp(-2j * np.pi * np.outer(b, a) / N)
        put(cfg + "tr", tw.real); put(cfg + "ti", tw.imag); put(cfg + "tn", -tw.imag)
        if cfg in ("HB", "HD"):
            v1 = np.exp(+2j * np.pi * np.outer(b, np.arange(n1in)) / n1) / N
            put(cfg + "v1r", v1.real); put(cfg + "v1i", v1.imag); put(cfg + "v1n", -v1.imag)
    return c.astype(np.float32)


CG = 32
NCOL = 128 * CG


def _fc(self, name):
    o, w = FFTC.off[name]
    return self.fftc[:, o:o + w]


def _mm_blocks(self, outs, terms, M, K, ncols):
    fw = self.fw
    nb = 0
    for c0 in range(0, ncols, 512):
        cw = min(512, ncols - c0)
        for oi, (ot, tl) in enumerate(zip(outs, terms)):
            p = self.fps[self.nfps % len(self.fps)]; self.nfps += 1
            for ti, (lh, rt) in enumerate(tl):
                fw.op("pe", lambda e, p=p, lh=lh, rt=rt, c0=c0, cw=cw, ti=ti, n=len(tl): e.matmul(
                    p[0:M, 0:cw], lh, rt[0:K, c0:c0 + cw], start=(ti == 0), stop=(ti == n - 1)),
                    R=[rt.b, self.fftc.b], W=[p.b])
            if nb % 2 == 0:
                fw.op("act", lambda e, p=p, ot=ot, c0=c0, cw=cw: e.activation(out=ot[0:M, c0:c0 + cw], in_=p[0:M, 0:cw], func=AF.Copy),
                      R=[p.b], W=[ot.b])
            else:
                fw.op("dve", lambda e, p=p, ot=ot, c0=c0, cw=cw: e.tensor_copy(out=ot[0:M, c0:c0 + cw], in_=p[0:M, 0:cw]),
                      R=[p.b], W=[ot.b])
            nb += 1


def _cmul(self, outr, outi, ar, ai, br, bi, P, ncols, t1, t2, bshape=None):
    fw = self.fw

    def A(t):
        return t[0:P, 0:ncols]

    def Bv(t):
        return t if bshape else t[0:P, 0:ncols]

    def V(t):
        return t[0:P, 0:ncols].rearrange("p (a c) -> p a c", c=bshape) if bshape else t[0:P, 0:ncols]
    rb = [] if bshape else None
    fw.op("dve", lambda e: e.tensor_tensor(out=V(t1), in0=V(ar), in1=Bv(br), op=ALU.mult), R=[ar.b, self.fftc.b] + ([br.b] if not bshape else []), W=[t1.b])
    fw.op("dve", lambda e: e.tensor_tensor(out=V(t2), in0=V(ai), in1=Bv(bi), op=ALU.mult), R=[ai.b, self.fftc.b] + ([bi.b] if not bshape else []), W=[t2.b])
    fw.op("dve", lambda e: e.tensor_tensor(out=A(t1), in0=A(t1), in1=A(t2), op=ALU.subtract), R=[t1.b, t2.b], W=[t1.b])
    fw.op("dve", lambda e: e.tensor_tensor(out=V(t2), in0=V(ar), in1=Bv(bi), op=ALU.mult), R=[ar.b, self.fftc.b] + ([bi.b] if not bshape else []), W=[t2.b])
    fw.op("dve", lambda e: e.tensor_tensor(out=V(outi), in0=V(ai), in1=Bv(br), op=ALU.mult), R=[ai.b, self.fftc.b] + ([br.b] if not bshape else []), W=[outi.b])
    fw.op("dve", lambda e: e.tensor_tensor(out=A(outi), in0=A(outi), in1=A(t2), op=ALU.add), R=[outi.b, t2.b], W=[outi.b])
    fw.op("dve", lambda e: e.tensor_copy(out=A(outr), in_=A(t1)), R=[t1.b], W=[outr.b])


def _dtrans(self, src, P1, dst):
    fw = self.fw
    sc = self.tscr[self.ntscr % len(self.tscr)]; self.ntscr += 1
    fw.dma("sp", sc.t.ap()[0:P1, :], src[0:P1, 0:NCOL], sc.b, src.b)
    v = sc.t.ap()[0:P1, :].rearrange("a (j c) -> j a c", c=CG)
    fw.dma("sp", dst[:, 0:P1 * CG].rearrange("p (a c) -> p a c", c=CG), v, dst.b, sc.b)


def _fft_fwd(self, cfg, x, Xr, Xi, W):
    n1, n1in = FFT_CFG[cfg]
    a_r, a_i, t1, t2 = W[0], W[1], W[2], W[3]
    fc = self._fc
    pre = "HB" if cfg == "HB" else cfg
    w1r = fc(cfg + "w1r")[0:n1in, 0:n1]; w1i = fc(cfg + "w1i")[0:n1in, 0:n1]
    self._mm_blocks([a_r, a_i], [[(w1r, x)], [(w1i, x)]], n1, n1in, NCOL)
    tr = fc(cfg + "tr")[0:n1, :].unsqueeze(2).broadcast_to([n1, 128, CG])
    ti = fc(cfg + "ti")[0:n1, :].unsqueeze(2).broadcast_to([n1, 128, CG])
    self._cmul(a_r, a_i, a_r, a_i, tr, ti, n1, NCOL, t1, t2, bshape=CG)
    self._dtrans(a_r, n1, t1)
    self._dtrans(a_i, n1, t2)
    wr = fc("w128r"); wi = fc("w128i"); wn = fc("w128n")
    self._mm_blocks([Xr, Xi], [[(wr, t1), (wn, t2)], [(wi, t1), (wr, t2)]], 128, 128, n1 * CG)


def _fft_inv(self, cfg, Yr, Yi, y, W):
    n1, n1in = FFT_CFG[cfg]
    e_r, e_i, t1, t2 = W[0], W[1], W[2], W[3]
    fc = self._fc
    wr = fc("w128r"); wi = fc("w128i"); wn = fc("w128n")
    self._mm_blocks([e_r, e_i], [[(wr, Yr), (wi, Yi)], [(wn, Yr), (wr, Yi)]], 128, 128, n1 * CG)
    self._dtrans_back(e_r, n1, t1)
    self._dtrans_back(e_i, n1, t2)
    tr = fc(cfg + "tr")[0:n1, :].unsqueeze(2).broadcast_to([n1, 128, CG])
    tn = fc(cfg + "tn")[0:n1, :].unsqueeze(2).broadcast_to([n1, 128, CG])
    self._cmul(t1, t2, t1, t2, tr, tn, n1, NCOL, e_r, e_i, bshape=CG)
    v1r = fc(cfg + "v1r")[0:n1, 0:n1in]; v1n = fc(cfg + "v1n")[0:n1, 0:n1in]
    self._mm_blocks([y], [[(v1r, t1), (v1n, t2)]], n1in, n1, NCOL)


def _dtrans_back(self, src, P1, dst):
    fw = self.fw
    sc = self.tscr[self.ntscr % len(self.tscr)]; self.ntscr += 1
    v = sc.t.ap()[0:P1, :].rearrange("a (j c) -> j a c", c=CG)
    fw.dma("sp", v, src[:, 0:P1 * CG].rearrange("p (a c) -> p a c", c=CG), sc.b, src.b)
    fw.dma("sp", dst[0:P1, 0:NCOL], sc.t.ap()[0:P1, :], dst.b, sc.b)


for _f in (_fc, _mm_blocks, _cmul, _dtrans, _dtrans_back, _fft_fwd, _fft_inv):
    setattr(Prog, _f.__name__, _f)

MAGIC = 12582912.0
TWO_PI = 2.0 * math.pi


def _rr_sin(self, out, arg, tmp, P, n):
    fw = self.fw
    fw.op("dve", lambda e: e.tensor_scalar(out=tmp[0:P, 0:n], in0=arg[0:P, 0:n], scalar1=1.0 / TWO_PI, scalar2=MAGIC, op0=ALU.mult, op1=ALU.add),
          R=[arg.b], W=[tmp.b])
    fw.op("dve", lambda e: e.tensor_scalar(out=tmp[0:P, 0:n], in0=tmp[0:P, 0:n], scalar1=-MAGIC, scalar2=None, op0=ALU.add), R=[tmp.b], W=[tmp.b])
    fw.op("dve", lambda e: e.scalar_tensor_tensor(out=tmp[0:P, 0:n], in0=tmp[0:P, 0:n], scalar=-TWO_PI, in1=arg[0:P, 0:n], op0=ALU.mult, op1=ALU.add),
          R=[tmp.b, arg.b], W=[tmp.b])
    fw.op("dve", lambda e: e.tensor_scalar(out=tmp[0:P, 0:n], in0=tmp[0:P, 0:n], scalar1=3.14159, scalar2=-3.14159, op0=ALU.min, op1=ALU.max),
          R=[tmp.b], W=[tmp.b])
    fw.op("act", lambda e: e.activation(out=out[0:P, 0:n], in_=tmp[0:P, 0:n], func=AF.Sin), R=[tmp.b], W=[out.b])


def _gen_filter(self, l, ps, Lseq, ecol0, ntl, HF, sml, rn):
    fw = self.fw
    hw1, hb1, hf1, hw2, hb2, hf2, hw3, hb3, dbc = sml
    et = fw.sb(ps, "f_e", [33, 512]); arg = fw.sb(ps, "f_arg", [64, 512]); tmp = fw.sb(ps, "f_tmp", [64, 512])
    z1 = fw.sb(ps, "f_z1", [64, 512]); z2 = fw.sb(ps, "f_z2", [64, 512])
    hh = [fw.sb(ps, "f_h%d" % i, [128, 512]) for i in range(2)]
    ab = fw.sb(ps, "f_ab", [128, 512]); dec = fw.sb(ps, "f_dec", [128, 128])
    nrm = fw.sb(ps, "f_nrm", [128, 512])
    p1 = self.fps[0]; p2 = self.fps[1]; p3 = self.fps[2]; pN = self.fps[3]
    ones = self.cv("ones")
    nsub_tot = Lseq // 128
    si = 0
    for t0 in range(0, Lseq, 512):
        n = min(512, Lseq - t0)
        fw.dma("sp", et[:, 0:n], self.embc.t.ap()[:, ecol0 + t0:ecol0 + t0 + n], et.b, self.embc.b)
        fw.op("pe", lambda e, n=n: e.matmul(p1[0:64, 0:n], hw1[0:33, :], et[:, 0:n], start=True, stop=True), R=[et.b, hw1.b], W=[p1.b])
        fw.op("dve", lambda e, n=n: e.tensor_scalar(out=arg[:, 0:n], in0=p1[0:64, 0:n], scalar1=hb1[0:64, 0:1], scalar2=hf1[0:64, 0:1], op0=ALU.add, op1=ALU.mult),
              R=[p1.b, hb1.b, hf1.b], W=[arg.b])
        self._rr_sin(z1, arg, tmp, 64, n)
        fw.op("pe", lambda e, n=n: e.matmul(p2[0:64, 0:n], hw2[0:64, :], z1[:, 0:n], start=True, stop=True), R=[z1.b, hw2.b], W=[p2.b])
        fw.op("dve", lambda e, n=n: e.tensor_scalar(out=arg[:, 0:n], in0=p2[0:64, 0:n], scalar1=hb2[0:64, 0:1], scalar2=hf2[0:64, 0:1], op0=ALU.add, op1=ALU.mult),
              R=[p2.b, hb2.b, hf2.b], W=[arg.b])
        self._rr_sin(z2, arg, tmp, 64, n)
        for i in range(n // 128):
            h = hh[si % 2]
            fw.op("pe", lambda e, i=i: e.matmul(p3[:, :], z2[:, i * 128:(i + 1) * 128], hw3[0:64, :], start=True, stop=True), R=[z2.b, hw3.b], W=[p3.b])
            fw.op("dve", lambda e, h=h: e.tensor_tensor(out=h[:], in0=p3[:], in1=hb3[:], op=ALU.add), R=[p3.b, hb3.b], W=[h.b])
            fw.op("act", lambda e, si=si: e.activation(out=dec[:], in_=dbc[:], func=AF.Exp, scale=ntl[:, si:si + 1]), R=[dbc.b, self.cstb], W=[dec.b])
            fw.op("dve", lambda e, h=h: e.tensor_tensor(out=h[:].rearrange("p (g c) -> p g c", c=128), in0=h[:].rearrange("p (g c) -> p g c", c=128),
                                                        in1=dec[:].unsqueeze(1).broadcast_to([128, 4, 128]), op=ALU.mult), R=[h.b, dec.b], W=[h.b])
            if si == 0:
                for c0 in (128, 384):
                    fw.op("dve", lambda e, h=h, c0=c0: e.memset(h[0:1, c0:c0 + 128], 0.0), W=[h.b])
            fw.op("act", lambda e, h=h: e.activation(out=ab[:], in_=h[:], func=AF.Abs), R=[h.b], W=[ab.b])
            fw.op("pe", lambda e, si=si: e.matmul(pN[:], ones, ab[:], start=(si == 0), stop=(si == nsub_tot - 1)), R=[ab.b, self.cstb], W=[pN.b])
            fw.dma("sp", HF.t.ap()[t0 + i * 128:t0 + (i + 1) * 128, :], h[:], HF.b, h.b, part=True)
            si += 1
    fw.op("dve", lambda e: e.tensor_copy(out=nrm[:], in_=pN[:]), R=[pN.b], W=[nrm.b])
    nv = nrm[:].rearrange("p (o d c) -> p o d c", o=2, d=2)
    fw.op("dve", lambda e: e.tensor_tensor(out=rn[:], in0=nv[:, :, 0, :], in1=nv[:, :, 1, :], op=ALU.add), R=[nrm.b], W=[rn.b])
    fw.op("dve", lambda e: e.reciprocal(out=rn[:], in_=rn[:]), R=[rn.b], W=[rn.b])
    return rn


def _seq_cfg(self, which):
    if which == "lat":
        return "FA", "HB", L, 1, 0, 0, "ntl8192"
    return "FC", "HD", NCTX, L + 3, L, L, "ntl256"


def phase_seqmix(self, l, which):
    fw = self.fw
    fcfg, hcfg, Lseq, prow, tbase, ecol0, ntlname = self._seq_cfg(which)
    n1f, n1fin = FFT_CFG[fcfg]
    n1h, n1hin = FFT_CFG[hcfg]
    with contextlib.ExitStack() as ps:
        self.fftc = fw.sb(ps, "fftc_t", [128, FFTC.n])
        fw.dma("sp", self.fftc[:], self.fftc_d.t.ap(), self.fftc.b, self.fftc_d.b)
        self.fps = [fw.ps(ps, "fps%d" % i) for i in range(6)]
        self.nfps = 0
        self.tscr = [fw.dram("tscr%d_%d_%s" % (i, l, which), [128, NCOL], F32) for i in range(2)]
        self.ntscr = 0
        W = [fw.sb(ps, "fw%d" % i, [128, NCOL]) for i in range(4)]
        Xr = fw.sb(ps, "fXr", [128, NCOL]); Xi = fw.sb(ps, "fXi", [128, NCOL])
        zv = fw.sb(ps, "fzv", [64, NCOL]); zx = fw.sb(ps, "fzx", [64, NCOL])
        halo = fw.sb(ps, "fhalo", [64, 130, CG])
        for cg in range(4):
            src = self.PFH.t.ap()[prow:prow + Lseq, cg * CG:(cg + 1) * CG].rearrange("(a j) c -> a j c", j=128)
            fw.dma("sp", zv[0:n1fin, :].rearrange("p (j c) -> p j c", c=CG), src, zv.b, self.PFH.b)
            self._fft_fwd(fcfg, zv, Xr, Xi, W)
            for ri, X_ in ((0, Xr), (1, Xi)):
                dst = self.FM.t.ap()[tbase:tbase + Lseq, ri, cg * CG:(cg + 1) * CG].rearrange("(p a) c -> p a c", a=n1f)
                fw.dma("sp", dst, X_[:, 0:n1f * CG].rearrange("p (a c) -> p a c", c=CG), self.FM.b, X_.b, part=True)
        fw.fence()
        rn = fw.sb(ps, "f_rn", [128, 2, 128])
        with contextlib.ExitStack() as ps2:
            sml = [self.ld(ps2, "hw1%d" % l, 33), self.ld(ps2, "hb1%d" % l, 64), self.ld(ps2, "hf1%d" % l, 64),
                   self.ld(ps2, "hw2%d" % l, 64), self.ld(ps2, "hb2%d" % l, 64), self.ld(ps2, "hf2%d" % l, 64),
                   self.ld(ps2, "hw3%d" % l, 64), self.ld(ps2, "hb3bc%d" % l), self.ld(ps2, "deltabc")]
            o_, w_ = CST.off[ntlname]
            ntl = self.cst[:, o_:o_ + w_]
            HF = fw.dram("HF_%d_%s" % (l, which), [Lseq, 512], F32)
            self._gen_filter(l, ps2, Lseq, ecol0, ntl, HF, sml, rn)
            fw.fence()
        hbias = self.ld(ps, "hbias%d" % l)
        HS = fw.dram("HS_%d_%s" % (l, which), [2, 4, 2, 128, n1h * CG], F32)
        for o in range(2):
            for cg in range(4):
                nc_ = n1h * CG
                for d in range(2):
                    src = HF.t.ap()[:, o * 256 + d * 128 + cg * CG: o * 256 + d * 128 + (cg + 1) * CG].rearrange("(a j) c -> a j c", j=128)
                    fw.dma("sp", zv[0:n1hin, :].rearrange("p (j c) -> p j c", c=CG), src, zv.b, HF.b)
                    self._fft_fwd(hcfg, zv, Xr, Xi, W)
                    if d == 0:
                        fw.dma("sp", HS.t.ap()[o, cg, 0], Xr[:, 0:nc_], HS.b, Xr.b)
                        fw.dma("sp", HS.t.ap()[o, cg, 1], Xi[:, 0:nc_], HS.b, Xi.b)
                fw.dma("sp", W[0][:, 0:nc_], HS.t.ap()[o, cg, 0], W[0].b, HS.b)
                fw.dma("sp", W[1][:, 0:nc_], HS.t.ap()[o, cg, 1], W[1].b, HS.b)
                rnb = rn[:, o, cg * CG:(cg + 1) * CG].unsqueeze(1).broadcast_to([128, n1h, CG])
                bb = hbias[:, o * 128 + cg * CG: o * 128 + (cg + 1) * CG].unsqueeze(1).broadcast_to([128, n1h, CG])

                def v3(t, nc_=nc_):
                    return t[:, 0:nc_].rearrange("p (a c) -> p a c", c=CG)
                fw.op("dve", lambda e, v3=v3: e.tensor_tensor(out=v3(W[0]), in0=v3(W[0]), in1=v3(Xr), op=ALU.add), R=[Xr.b, W[0].b], W=[W[0].b])
                fw.op("dve", lambda e, v3=v3: e.tensor_tensor(out=v3(W[1]), in0=v3(W[1]), in1=v3(Xi), op=ALU.subtract), R=[Xi.b, W[1].b], W=[W[1].b])
                fw.op("dve", lambda e, rnb=rnb, v3=v3: e.tensor_tensor(out=v3(W[0]), in0=v3(W[0]), in1=rnb, op=ALU.mult), R=[W[0].b, rn.b], W=[W[0].b])
                fw.op("dve", lambda e, rnb=rnb, v3=v3: e.tensor_tensor(out=v3(W[1]), in0=v3(W[1]), in1=rnb, op=ALU.mult), R=[W[1].b, rn.b], W=[W[1].b])
                fw.op("dve", lambda e, bb=bb, v3=v3: e.tensor_tensor(out=v3(W[0]), in0=v3(W[0]), in1=bb, op=ALU.add), R=[W[0].b, hbias.b], W=[W[0].b])
                fw.dma("sp", HS.t.ap()[o, cg, 0], W[0][:, 0:nc_], HS.b, W[0].b)
                fw.dma("sp", HS.t.ap()[o, cg, 1], W[1][:, 0:nc_], HS.b, W[1].b)
        fw.fence()
        cw = self.ld(ps, "cwbc%d" % l); cbv = self.ld(ps, "cbbc%d" % l)

        def shortconv(dst, wi, cg):
            col0 = 128 + wi * 128 + cg * CG
            base = self.PFH.t.ap()[prow - 1:prow - 1 + Lseq + 2, col0:col0 + CG]
            src = bass.AP(self.PFH.t, base.offset, [[128 * 512, n1hin], [512, 130], [1, CG]])
            fw.dma("sp", halo[0:n1hin, :, :], src, halo.b, self.PFH.b)
            d3 = dst[0:n1hin, :].rearrange("p (j c) -> p j c", c=CG)
            for j in range(3):
                wv = cw[0:n1hin, j * 384 + wi * 128 + cg * CG: j * 384 + wi * 128 + (cg + 1) * CG].unsqueeze(1).broadcast_to([n1hin, 128, CG])
                if j == 0:
                    fw.op("dve", lambda e, wv=wv: e.tensor_tensor(out=d3, in0=halo[0:n1hin, 0:128, :], in1=wv, op=ALU.mult), R=[halo.b, cw.b], W=[dst.b])
                else:
                    fw.op("dve", lambda e, wv=wv, j=j: e.tensor_tensor(out=W[3][0:n1hin, :].rearrange("p (j c) -> p j c", c=CG), in0=halo[0:n1hin, j:j + 128, :], in1=wv, op=ALU.mult),
                          R=[halo.b, cw.b], W=[W[3].b])
                    fw.op("dve", lambda e: e.tensor_tensor(out=dst[0:n1hin, :], in0=dst[0:n1hin, :], in1=W[3][0:n1hin, :], op=ALU.add), R=[dst.b, W[3].b], W=[dst.b])
            bv = cbv[0:n1hin, wi * 128 + cg * CG: wi * 128 + (cg + 1) * CG].unsqueeze(1).broadcast_to([n1hin, 128, CG])
            fw.op("dve", lambda e, bv=bv: e.tensor_tensor(out=d3, in0=d3, in1=bv, op=ALU.add), R=[dst.b, cbv.b], W=[dst.b])

        for cg in range(4):
            shortconv(zv, 0, cg)
            shortconv(zx, 1, cg)
            for o in range(2):
                src_t = zv if o == 0 else zx
                self._fft_fwd(hcfg, src_t, Xr, Xi, W)
                nc_ = n1h * CG
                fw.dma("sp", W[0][:, 0:nc_], HS.t.ap()[o, cg, 0], W[0].b, HS.b)
                fw.dma("sp", W[1][:, 0:nc_], HS.t.ap()[o, cg, 1], W[1].b, HS.b)
                self._cmul(Xr, Xi, Xr, Xi, W[0], W[1], 128, nc_, W[2], W[3])
                self._fft_inv(hcfg, Xr, Xi, zv, W)
                if o == 0:
                    fw.op("dve", lambda e: e.tensor_tensor(out=zx[0:n1hin, :], in0=zx[0:n1hin, :], in1=zv[0:n1hin, :], op=ALU.mult), R=[zx.b, zv.b], W=[zx.b])
                else:
                    shortconv(W[0], 2, cg)
                    fw.op("dve", lambda e: e.tensor_tensor(out=zv[0:n1hin, :], in0=zv[0:n1hin, :], in1=W[0][0:n1hin, :], op=ALU.mult), R=[zv.b, W[0].b], W=[zv.b])
                    dst = self.HM.t.ap()[tbase:tbase + Lseq, cg * CG:(cg + 1) * CG].rearrange("(a j) c -> a j c", j=128)
                    fw.dma("sp", dst, zv[0:n1hin, :].rearrange("p (j c) -> p j c", c=CG), self.HM.b, zv.b, part=True)
        fw.fence()


def halo_flat(halo):
    return Tile(halo.t, halo.b)


for _f in (_rr_sin, _gen_filter, _seq_cfg, phase_seqmix):
    setattr(Prog, _f.__name__, _f)


def phase_attn(self, l):
    fw = self.fw
    lam_init = 0.8 - 0.6 * math.exp(-0.3 * l)
    with contextlib.ExitStack() as ps:
        KTs = [fw.sb(ps, "aKT%d" % h, [128, T], BF16) for h in range(2)]
        Vs = fw.sb(ps, "aV", [128, T // 128, 256], BF16)
        onesb = fw.sb(ps, "aones", [128, 128], BF16)
        Qs = [fw.sb(ps, "aQ%d" % i, [128, 512], BF16) for i in range(2)]
        pts = [fw.sb(ps, "apt%d" % i, [128, 512], BF16) for i in range(4)]
        r0 = fw.sb(ps, "ar0", [128, 512]); r1 = fw.sb(ps, "ar1", [128, 512])
        a0 = fw.sb(ps, "aa0", [128, 512]); a1 = fw.sb(ps, "aa1", [128, 512])
        sq = fw.sb(ps, "asq", [128, 512])
        ob = [fw.sb(ps, "aob%d" % i, [128, 512], BF16) for i in range(2)]
        pss = [fw.ps(ps, "aps%d" % i) for i in range(3)]
        po = [fw.ps(ps, "apo%d" % i) for i in range(2)]
        pz = [fw.ps(ps, "apz%d" % i) for i in range(2)]
        ones = self.cv("ones")
        fw.op("dve", lambda e: e.tensor_copy(out=onesb[:], in_=ones), R=[self.cstb], W=[onesb.b])
        for h in range(2):
            fw.dma("sp", KTs[h][:], self.KT.t.ap()[h * 128:(h + 1) * 128, :], KTs[h].b, self.KT.b)
        fw.dma("sp", Vs[:], self.V.t.ap().rearrange("(ch p) c -> p ch c", p=128), Vs.b, self.V.b)
        qtiles = [(q0, 512, list(range(T // 128))) for q0 in range(0, L, 512)]
        if l == 0:
            qtiles.append((L, NCTX, [L // 128, L // 128 + 1]))
        nq = 0; npt = 0; nps = 0
        for (q0, n_, kcs) in qtiles:
            for h in range(2):
                Q = Qs[nq % 2]; nq += 1
                fw.dma("sp", Q[:, 0:n_], self.QT.t.ap()[h * 128:(h + 1) * 128, q0:q0 + n_], Q.b, self.QT.b)
                LOOK = 2
                for m in range(2):
                    nk = len(kcs)
                    ptq = {}
                    for kk in range(nk + LOOK):
                        if kk < nk:
                            kc = kcs[kk]
                            s_ = pss[nps % 3]; nps += 1
                            fw.op("pe", lambda e, s_=s_, h=h, m=m, kc=kc, Q=Q, n_=n_: e.matmul(
                                s_[:, 0:n_], KTs[h][m * 64:(m + 1) * 64, kc * 128:(kc + 1) * 128], Q[m * 64:(m + 1) * 64, 0:n_], start=True, stop=True),
                                R=[KTs[h].b, Q.b], W=[s_.b])
                            pt = pts[npt % 4]; npt += 1
                            fw.op("act", lambda e, s_=s_, pt=pt, n_=n_: e.activation(out=pt[:, 0:n_], in_=s_[:, 0:n_], func=AF.Exp, scale=0.125),
                                  R=[s_.b], W=[pt.b])
                            ptq[kk] = pt
                        ki = kk - LOOK
                        if ki >= 0:
                            kc = kcs[ki]; pt = ptq.pop(ki)
                            fw.op("pe", lambda e, pt=pt, h=h, m=m, kc=kc, ki=ki, n_=n_, nk=nk: e.matmul(
                                po[m][:, 0:n_], Vs[:, kc, h * 128:(h + 1) * 128], pt[:, 0:n_], start=(ki == 0), stop=(ki == nk - 1)),
                                R=[Vs.b, pt.b], W=[po[m].b])
                            fw.op("pe", lambda e, pt=pt, m=m, ki=ki, n_=n_, nk=nk: e.matmul(
                                pz[m][:, 0:n_], onesb[:], pt[:, 0:n_], start=(ki == 0), stop=(ki == nk - 1)),
                                R=[onesb.b, pt.b], W=[pz[m].b])
                fw.op("dve", lambda e, n_=n_: e.reciprocal(out=r0[:, 0:n_], in_=pz[0][:, 0:n_]), R=[pz[0].b], W=[r0.b])
                fw.op("dve", lambda e, n_=n_: e.reciprocal(out=r1[:, 0:n_], in_=pz[1][:, 0:n_]), R=[pz[1].b], W=[r1.b])
                fw.op("dve", lambda e, n_=n_: e.tensor_tensor(out=a0[:, 0:n_], in0=po[0][:, 0:n_], in1=r0[:, 0:n_], op=ALU.mult), R=[po[0].b, r0.b], W=[a0.b])
                fw.op("dve", lambda e, n_=n_: e.tensor_tensor(out=a1[:, 0:n_], in0=po[1][:, 0:n_], in1=r1[:, 0:n_], op=ALU.mult), R=[po[1].b, r1.b], W=[a1.b])
                fw.op("dve", lambda e, n_=n_: e.scalar_tensor_tensor(out=a0[:, 0:n_], in0=a1[:, 0:n_], scalar=self.lam[:, 1:2], in1=a0[:, 0:n_], op0=ALU.mult, op1=ALU.add),
                      R=[a0.b, a1.b, self.lam.b], W=[a0.b])
                fw.op("act", lambda e, n_=n_: e.activation(out=sq[:, 0:n_], in_=a0[:, 0:n_], func=AF.Square), R=[a0.b], W=[sq.b])
                s_ = pss[nps % 3]; nps += 1
                fw.op("pe", lambda e, s_=s_, n_=n_: e.matmul(s_[:, 0:n_], ones, sq[:, 0:n_], start=True, stop=True), R=[sq.b, self.cstb], W=[s_.b])
                fw.op("act", lambda e, s_=s_, n_=n_: e.activation(out=r0[:, 0:n_], in_=s_[:, 0:n_], func=AF.Sqrt, scale=1.0 / 128, bias=self.epsb[:, 1:2]), R=[s_.b], W=[r0.b])
                fw.op("dve", lambda e, n_=n_: e.reciprocal(out=r0[:, 0:n_], in_=r0[:, 0:n_]), R=[r0.b], W=[r0.b])
                fw.op("dve", lambda e, n_=n_: e.scalar_tensor_tensor(out=a0[:, 0:n_], in0=a0[:, 0:n_], scalar=self.qks[:, 2:3], in1=r0[:, 0:n_], op0=ALU.mult, op1=ALU.mult),
                      R=[a0.b, r0.b, self.qks.b], W=[a0.b])
                o_ = ob[nq % 2]
                fw.op("dve", lambda e, n_=n_, o_=o_: e.tensor_scalar(out=o_[:, 0:n_], in0=a0[:, 0:n_], scalar1=(1.0 - lam_init), scalar2=None, op0=ALU.mult), R=[a0.b], W=[o_.b])
                fw.dma("sp", self.OT.t.ap()[h * 128:(h + 1) * 128, q0:q0 + n_], o_[:, 0:n_], self.OT.b, o_.b, part=True)
        fw.fence()


def allreduce_to_X(self, nrows):
    items = []
    for r0 in range(0, nrows, 1024):
        r1 = min(nrows, r0 + 1024)
        items.append(("AllReduce", PAIRS, self.ARin.t.ap()[r0:r1, :], self.X.t.ap()[r0:r1, :]))
    self.coll_seq(items)
    self.fw.fence()


def phase_outproj(self, l):
    fw = self.fw
    ntok = T if l == 0 else L
    with contextlib.ExitStack() as ps:
        Wfc = [fw.sb(ps, "oWc%d" % i, [128, 1024], BF16) for i in range(2)]
        Wfs = [fw.sb(ps, "oWs%d" % i, [128, 1024], BF16) for i in range(2)]
        Wh = fw.sb(ps, "oWh", [128, 1024], BF16)
        Wa = fw.sb(ps, "oWa", [128, 2, 1024], BF16)
        Wo = fw.sb(ps, "oWo", [128, 8, 1024], BF16)
        stg = fw.sb(ps, "ostg", [128, 8, 1024])
        wf32 = fw.sb(ps, "owf", [128, 1024])
        pp = [fw.ps(ps, "opp%d" % i) for i in range(7)]
        npp = 0
        c64 = self.cv("c64"); s64 = self.cv("s64"); ident = self.cv("ident")
        wov = self.wout.t.ap()[l]
        fw.dma("sp", wf32[:], wov[0:128, :], wf32.b, self.wout.b)
        for (cm, dsts) in ((c64, Wfc), (s64, Wfs)):
            for half in range(2):
                p = pp[npp % 7]; npp += 1
                fw.op("pe", lambda e, p=p, cm=cm, half=half: e.matmul(p[:], cm, wf32[:, half * 512:(half + 1) * 512], start=True, stop=True), R=[wf32.b, self.cstb], W=[p.b])
                for i, Ls in enumerate((L, NCTX)):
                    fw.op("act", lambda e, p=p, d=dsts[i], half=half, Ls=Ls: e.activation(out=d[:, half * 512:(half + 1) * 512], in_=p[:], func=AF.Copy, scale=1.0 / math.sqrt(64.0 * Ls)),
                          R=[p.b], W=[dsts[i].b])
        fw.dma("sp", stg[:, 0, :], wov[128:256, :], stg.b, self.wout.b)
        fw.op("dve", lambda e: e.tensor_copy(out=Wh[:], in_=stg[:, 0, :]), R=[stg.b], W=[Wh.b])
        fw.dma("sp", stg[:, 0:2, :], wov[256:512, :].rearrange("(k p) c -> p k c", p=128), stg.b, self.wout.b)
        fw.op("dve", lambda e: e.tensor_copy(out=Wa[:], in_=stg[:, 0:2, :]), R=[stg.b], W=[Wa.b])
        oo, _ = SHOFF["wo%d" % l]
        fw.dma("sp", stg[:], dview(self.SH, oo, (1024, 1024)).rearrange("(k p) c -> p k c", p=128), stg.b, self.SH.b)
        fw.op("act", lambda e: e.activation(out=Wo[:], in_=stg[:], func=AF.Copy), R=[stg.b], W=[Wo.b])
        fm = fw.sb(ps, "ofm", [128, 4, 2, 128]); hm = fw.sb(ps, "ohm", [128, 4, 128])
        FrT = fw.sb(ps, "oFr", [128, 512], BF16); FiT = fw.sb(ps, "oFi", [128, 512], BF16); HT = fw.sb(ps, "oHT", [128, 512], BF16)
        OTt = fw.sb(ps, "oOT", [128, 2, 512], BF16)
        G = fw.sb(ps, "oG", [128, 24, 512], BF16)
        xt = fw.sb(ps, "oxt", [128, 4, 1024])
        mb = fw.sb(ps, "omb", [128, 1024])
        m1 = fw.sb(ps, "om1", [128, 512]); m2 = fw.sb(ps, "om2", [128, 512])
        M = fw.sb(ps, "oM", [128, 8, 512], BF16)
        tz = fw.sb(ps, "otz", [128, 512]); ar = [fw.sb(ps, "oar%d" % i, [128, 512]) for i in range(2)]
        lastj = -1
        nar = 0
        for t0 in range(0, ntok, 512):
            n_ = min(512, ntok - t0); nsub = n_ // 128
            j = 0 if t0 < L else 1
            if j != lastj:
                fw.dma("sp", mb[:], self.MODBC.t.ap()[:, j * 2048:j * 2048 + 1024], mb.b, self.MODBC.b)
                lastj = j
            fw.dma("sp", fm[:, 0:nsub], self.FM.t.ap()[t0:t0 + n_].rearrange("(s p) r c -> p s r c", p=128), fm.b, self.FM.b)
            fw.dma("sp", hm[:, 0:nsub], self.HM.t.ap()[t0:t0 + n_].rearrange("(s p) c -> p s c", p=128), hm.b, self.HM.b)
            fw.dma("sp", OTt[:, :, 0:n_], self.OT.t.ap()[:, t0:t0 + n_].rearrange("(k p) t -> p k t", p=128), OTt.b, self.OT.b)
            fw.dma("sp", G[:, :, 0:n_], self.GT.t.ap()[:, t0:t0 + n_].rearrange("(k p) t -> p k t", p=128), G.b, self.GT.b)
            fw.dma("sp", xt[:, 0:nsub], self.X.t.ap()[t0:t0 + n_].rearrange("(s p) c -> p s c", p=128), xt.b, self.X.b)
            for (srcf, dstT) in ((lambda s_: fm[:, s_, 0, :], FrT), (lambda s_: fm[:, s_, 1, :], FiT), (lambda s_: hm[:, s_, :], HT)):
                p = pp[npp % 7]; npp += 1
                srcb = fm.b if dstT is not HT else hm.b
                for s_ in range(nsub):
                    fw.op("pe", lambda e, p=p, s_=s_, srcf=srcf: e.transpose(out=p[:, s_ * 128:(s_ + 1) * 128], in_=srcf(s_), identity=ident), R=[srcb, self.cstb], W=[p.b])
                fw.op("act", lambda e, p=p, dstT=dstT, n_=n_: e.activation(out=dstT[:, 0:n_], in_=p[:, 0:n_], func=AF.Copy), R=[p.b], W=[dstT.b])
            for fc in range(8):
                fsl = slice(fc * 128, (fc + 1) * 128)
                pf = pp[npp % 7]; npp += 1
                fw.op("pe", lambda e, pf=pf, fsl=fsl, n_=n_, j=j: e.matmul(pf[:, 0:n_], Wfc[j][:, fsl], FrT[:, 0:n_], start=True, stop=False), R=[Wfc[j].b, FrT.b], W=[pf.b])
                fw.op("pe", lambda e, pf=pf, fsl=fsl, n_=n_, j=j: e.matmul(pf[:, 0:n_], Wfs[j][:, fsl], FiT[:, 0:n_], start=False, stop=True), R=[Wfs[j].b, FiT.b], W=[pf.b])
                ph = pp[npp % 7]; npp += 1
                fw.op("pe", lambda e, ph=ph, fsl=fsl, n_=n_: e.matmul(ph[:, 0:n_], Wh[:, fsl], HT[:, 0:n_], start=True, stop=True), R=[Wh.b, HT.b], W=[ph.b])
                pa = pp[npp % 7]; npp += 1
                for k in range(2):
                    fw.op("pe", lambda e, pa=pa, fsl=fsl, n_=n_, k=k: e.matmul(pa[:, 0:n_], Wa[:, k, fsl], OTt[:, k, 0:n_], start=(k == 0), stop=(k == 1)), R=[Wa.b, OTt.b], W=[pa.b])
                fw.op("dve", lambda e, pf=pf, fc=fc, n_=n_: e.tensor_tensor(out=m1[:, 0:n_], in0=pf[:, 0:n_], in1=G[:, fc, 0:n_], op=ALU.mult), R=[pf.b, G.b], W=[m1.b])
                fw.op("dve", lambda e, ph=ph, fc=fc, n_=n_: e.tensor_tensor(out=m2[:, 0:n_], in0=ph[:, 0:n_], in1=G[:, 8 + fc, 0:n_], op=ALU.mult), R=[ph.b, G.b], W=[m2.b])
                fw.op("dve", lambda e, n_=n_: e.tensor_tensor(out=m1[:, 0:n_], in0=m1[:, 0:n_], in1=m2[:, 0:n_], op=ALU.add), R=[m1.b, m2.b], W=[m1.b])
                fw.op("dve", lambda e, pa=pa, fc=fc, n_=n_: e.tensor_tensor(out=m2[:, 0:n_], in0=pa[:, 0:n_], in1=G[:, 16 + fc, 0:n_], op=ALU.mult), R=[pa.b, G.b], W=[m2.b])
                fw.op("dve", lambda e, fc=fc, n_=n_: e.tensor_tensor(out=M[:, fc, 0:n_], in0=m1[:, 0:n_], in1=m2[:, 0:n_], op=ALU.add), R=[m1.b, m2.b], W=[M.b])
            for s_ in range(nsub):
                for half in range(2):
                    p = pp[npp % 7]; npp += 1
                    for k in range(8):
                        fw.op("pe", lambda e, p=p, k=k, s_=s_, half=half: e.matmul(p[:], M[:, k, s_ * 128:(s_ + 1) * 128], Wo[:, k, half * 512:(half + 1) * 512], start=(k == 0), stop=(k == 7)),
                              R=[M.b, Wo.b], W=[p.b])
                    fw.op("dve", lambda e, p=p, half=half: e.tensor_tensor(out=tz[:], in0=p[:], in1=mb[:, half * 512:(half + 1) * 512], op=ALU.mult), R=[p.b, mb.b], W=[tz.b])
                    a_ = ar[nar % 2]; nar += 1
                    fw.op("dve", lambda e, a_=a_, s_=s_, half=half: e.scalar_tensor_tensor(out=a_[:], in0=xt[:, s_, half * 512:(half + 1) * 512], scalar=0.5, in1=tz[:], op0=ALU.mult, op1=ALU.add),
                          R=[xt.b, tz.b], W=[a_.b])
                    fw.dma("sp", self.ARin.t.ap()[t0 + s_ * 128:t0 + (s_ + 1) * 128, half * 512:(half + 1) * 512], a_[:], self.ARin.b, a_.b, part=True)
        fw.fence()
    self.allreduce_to_X(ntok)


for _f in (phase_attn, allreduce_to_X, phase_outproj):
    setattr(Prog, _f.__name__, _f)


def phase_moe(self, l):
    fw = self.fw
    ntok = T if l == 0 else L
    WEB1 = fw.dram("WEB1_%d" % l, [16, 1024, 2048], BF16)
    WEB2 = fw.dram("WEB2_%d" % l, [16, 1024, 1024], BF16)
    with contextlib.ExitStack() as ps:
        st_ = [fw.sb(ps, "mst%d" % i, [128, 8, 512]) for i in range(2)]
        sb_ = [fw.sb(ps, "msb%d" % i, [128, 8, 512], BF16) for i in range(2)]
        n = 0
        for le in range(16):
            WE = self.WEl[l]
            w1v = WE.t.ap()[le * 1536:le * 1536 + 1024, :].rearrange("(k p) c -> p k c", p=128)
            w2v = bass.AP(WE.t, WE.t.ap()[le * 1536 + 1024:le * 1536 + 1536, :].offset, [[1024, 1024], [1, 1024]]).rearrange("(k p) c -> p k c", p=128)
            for (src, dst, ncb) in ((w1v, WEB1, 4), (w2v, WEB2, 2)):
                for cb in range(ncb):
                    a = st_[n % 2]; b_ = sb_[n % 2]
                    fw.dma("sp", a[:], src[:, :, cb * 512:(cb + 1) * 512], a.b, WE.b)
                    eng = ("act", "dve")[n % 2]
                    if eng == "act":
                        fw.op("act", lambda e, a=a, b_=b_: e.activation(out=b_[:], in_=a[:], func=AF.Copy), R=[a.b], W=[b_.b])
                    else:
                        fw.op(eng, lambda e, a=a, b_=b_: e.tensor_copy(out=b_[:], in_=a[:]), R=[a.b], W=[b_.b])
                    fw.dma("sp", dst.t.ap()[le].rearrange("(k p) c -> p k c", p=128)[:, :, cb * 512:(cb + 1) * 512], b_[:], dst.b, b_.b, part=True)
                    n += 1
        fw.fence()
    with contextlib.ExitStack() as ps:
        W1 = [fw.sb(ps, "mW1_%d" % i, [128, 8, 2048], BF16) for i in range(2)]
        W2 = [fw.sb(ps, "mW2_%d" % i, [128, 8, 1024], BF16) for i in range(2)]
        wrt = self.ld(ps, "wrt%d" % l); brt = self.ld(ps, "brt%d" % l)
        b1T = self.ld(ps, "b1T%d" % l); b2r = self.ld(ps, "b2r%d" % l, 16)
        xt = fw.sb(ps, "mxt", [128, 4, 1024]); xn = fw.sb(ps, "mxn", [128, 1024]); junk = fw.sb(ps, "mjunk", [128, 1024])
        ss = fw.sb(ps, "mss", [128, 2])
        h2T = fw.sb(ps, "mh2T", [128, 8, 512], BF16)
        h2f = fw.sb(ps, "mh2f", [128, 8, 128])
        lg = fw.sb(ps, "mlg", [128, 32]); m8 = fw.sb(ps, "mm8", [128, 8]); msk = fw.sb(ps, "mmsk", [128, 32])
        nb = fw.sb(ps, "mnb", [128, 2])
        Gt = fw.sb(ps, "mG", [128, 4, 32])
        gT = fw.sb(ps, "mgT", [16, 128])
        Y = fw.sb(ps, "mY", [128, 4, 1024])
        A = fw.sb(ps, "mA", [128, 8, 512], BF16)
        g2 = [fw.sb(ps, "mg%d" % i, [128, 512]) for i in range(2)]; sg2 = [fw.sb(ps, "msg%d" % i, [128, 512]) for i in range(2)]; ln2 = [fw.sb(ps, "mln%d" % i, [128, 512]) for i in range(2)]
        mb = fw.sb(ps, "mmb", [128, 1024])
        tz = fw.sb(ps, "mtz", [128, 1024])
        pp = [fw.ps(ps, "mpp%d" % i) for i in range(8)]
        npp = 0
        ident = self.cv("ident")
        lastj = -1
        nw = 0
        for t0 in range(0, ntok, 512):
            n_ = min(512, ntok - t0); nsub = n_ // 128
            j = 0 if t0 < L else 1
            if j != lastj:
                fw.dma("sp", mb[:], self.MODBC.t.ap()[:, j * 2048 + 1024:j * 2048 + 2048], mb.b, self.MODBC.b)
                lastj = j
            fw.dma("sp", xt[:, 0:nsub], self.X.t.ap()[t0:t0 + n_].rearrange("(s p) c -> p s c", p=128), xt.b, self.X.b)
            for s_ in range(nsub):
                fw.op("act", lambda e, s_=s_: e.activation(out=junk[:], in_=xt[:, s_, :], func=AF.Square, accum_out=ss[:, 0:1]), R=[xt.b], W=[junk.b, ss.b])
                fw.op("act", lambda e: e.activation(out=ss[:, 1:2], in_=ss[:, 0:1], func=AF.Sqrt, scale=1.0 / D, bias=self.epsb[:, 0:1]), R=[ss.b], W=[ss.b])
                fw.op("dve", lambda e: e.reciprocal(out=ss[:, 1:2], in_=ss[:, 1:2]), R=[ss.b], W=[ss.b])
                fw.op("dve", lambda e, s_=s_: e.tensor_scalar(out=xn[:], in0=xt[:, s_, :], scalar1=ss[:, 1:2], scalar2=None, op0=ALU.mult), R=[xt.b, ss.b], W=[xn.b])
                for g in range(2):
                    p = pp[npp % 8]; npp += 1
                    for q in range(4):
                        k = 4 * g + q
                        fw.op("pe", lambda e, p=p, q=q, k=k: e.transpose(out=p[:, q * 128:(q + 1) * 128], in_=xn[:, k * 128:(k + 1) * 128], identity=ident), R=[xn.b, self.cstb], W=[p.b])
                    for q in range(4):
                        k = 4 * g + q
                        fw.op("dve", lambda e, p=p, q=q, k=k, j=j: e.tensor_scalar(out=h2f[:, k, :], in0=p[:, q * 128:(q + 1) * 128], scalar1=self.gs2[:, k, j:j + 1], scalar2=self.sh2[:, k, j:j + 1],
                                                                                  op0=ALU.mult, op1=ALU.add), R=[p.b, self.gs2.b, self.sh2.b], W=[h2f.b])
                fw.op("act", lambda e, s_=s_: e.activation(out=h2T[:, :, s_ * 128:(s_ + 1) * 128], in_=h2f[:], func=AF.Copy), R=[h2f.b], W=[h2T.b])
                p = pp[npp % 8]; npp += 1
                for k in range(8):
                    fw.op("pe", lambda e, p=p, k=k: e.matmul(p[:, 0:32], h2f[:, k, :], wrt[:, k * 32:(k + 1) * 32], start=(k == 0), stop=(k == 7)), R=[h2f.b, wrt.b], W=[p.b])
                fw.op("dve", lambda e, p=p: e.tensor_tensor(out=lg[:], in0=p[:, 0:32], in1=brt[:], op=ALU.add), R=[p.b, brt.b], W=[lg.b])
                fw.op("dve", lambda e: e.max(out=m8[:], in_=lg[:]), R=[lg.b], W=[m8.b])
                fw.op("dve", lambda e: e.tensor_scalar(out=msk[:], in0=lg[:], scalar1=m8[:, 3:4], scalar2=None, op0=ALU.is_ge), R=[lg.b, m8.b], W=[msk.b])
                fw.op("dve", lambda e: e.tensor_scalar(out=nb[:, 0:1], in0=m8[:, 0:1], scalar1=-1.0, scalar2=None, op0=ALU.mult), R=[m8.b], W=[nb.b])
                fw.op("act", lambda e: e.activation(out=lg[:], in_=lg[:], func=AF.Exp, bias=nb[:, 0:1]), R=[lg.b, nb.b], W=[lg.b])
                fw.op("dve", lambda e: e.tensor_tensor(out=lg[:], in0=lg[:], in1=msk[:], op=ALU.mult), R=[lg.b, msk.b], W=[lg.b])
                fw.op("dve", lambda e: e.tensor_reduce(out=nb[:, 1:2], in_=lg[:], axis=AX.X, op=ALU.add), R=[lg.b], W=[nb.b])
                fw.op("dve", lambda e: e.reciprocal(out=nb[:, 1:2], in_=nb[:, 1:2]), R=[nb.b], W=[nb.b])
                fw.op("dve", lambda e, s_=s_: e.tensor_scalar(out=Gt[:, s_, :], in0=lg[:], scalar1=nb[:, 1:2], scalar2=None, op0=ALU.mult), R=[lg.b, nb.b], W=[Gt.b])
                p = pp[npp % 8]; npp += 1
                fw.op("pe", lambda e, p=p, s_=s_: e.transpose(out=p[0:16, 0:128], in_=Gt[:, s_, 0:16], identity=ident), R=[Gt.b, self.cstb], W=[p.b])
                fw.op("dve", lambda e, p=p: e.tensor_copy(out=gT[:], in_=p[0:16, 0:128]), R=[p.b], W=[gT.b])
                for half in range(2):
                    p = pp[npp % 8]; npp += 1
                    fw.op("pe", lambda e, p=p, half=half: e.matmul(p[:], gT[:], b2r[0:16, half * 512:(half + 1) * 512], start=True, stop=True), R=[gT.b, b2r.b], W=[p.b])
                    fw.op("act", lambda e, p=p, half=half, s_=s_: e.activation(out=Y[:, s_, half * 512:(half + 1) * 512], in_=p[:], func=AF.Copy), R=[p.b], W=[Y.b])
            for le in range(16):
                w1 = W1[nw % 2]; w2 = W2[nw % 2]; nw += 1
                for cb in range(4):
                    fw.dma("sp", w1[:, :, cb * 512:(cb + 1) * 512], WEB1.t.ap()[le].rearrange("(k p) c -> p k c", p=128)[:, :, cb * 512:(cb + 1) * 512], w1.b, WEB1.b, part=(cb > 0))
                for cb in range(2):
                    fw.dma("sp", w2[:, :, cb * 512:(cb + 1) * 512], WEB2.t.ap()[le].rearrange("(k p) c -> p k c", p=128)[:, :, cb * 512:(cb + 1) * 512], w2.b, WEB2.b, part=(cb > 0))
                for jc in range(8):
                    pg = pp[npp % 8]; npp += 1
                    pl = pp[npp % 8]; npp += 1
                    g_ = g2[jc % 2]; sg = sg2[jc % 2]; ln = ln2[jc % 2]
                    for k in range(8):
                        fw.op("pe", lambda e, pg=pg, k=k, jc=jc, w1=w1, n_=n_: e.matmul(pg[:, 0:n_], w1[:, k, jc * 128:(jc + 1) * 128], h2T[:, k, 0:n_], start=(k == 0), stop=(k == 7)), R=[w1.b, h2T.b], W=[pg.b])
                    for k in range(8):
                        fw.op("pe", lambda e, pl=pl, k=k, jc=jc, w1=w1, n_=n_: e.matmul(pl[:, 0:n_], w1[:, k, 1024 + jc * 128:1024 + (jc + 1) * 128], h2T[:, k, 0:n_], start=(k == 0), stop=(k == 7)), R=[w1.b, h2T.b], W=[pl.b])
                    bg = b1T[:, le * 16 + jc:le * 16 + jc + 1]; bl = b1T[:, le * 16 + 8 + jc:le * 16 + 8 + jc + 1]
                    fw.op("dve", lambda e, pg=pg, bg=bg, n_=n_, g_=g_: e.tensor_scalar(out=g_[:, 0:n_], in0=pg[:, 0:n_], scalar1=bg, scalar2=7.0, op0=ALU.add, op1=ALU.min), R=[pg.b, b1T.b], W=[g_.b])
                    fw.op("act", lambda e, n_=n_, g_=g_, sg=sg: e.activation(out=sg[:, 0:n_], in_=g_[:, 0:n_], func=AF.Sigmoid, scale=1.702), R=[g_.b], W=[sg.b])
                    fw.op("act", lambda e, pl=pl, bl=bl, n_=n_, ln=ln: e.activation(out=ln[:, 0:n_], in_=pl[:, 0:n_], func=AF.Identity, bias=bl), R=[pl.b, b1T.b], W=[ln.b])
                    fw.op("dve", lambda e, n_=n_, ln=ln: e.tensor_scalar(out=ln[:, 0:n_], in0=ln[:, 0:n_], scalar1=7.0, scalar2=-7.0, op0=ALU.min, op1=ALU.max), R=[ln.b], W=[ln.b])
                    fw.op("dve", lambda e, n_=n_, g_=g_, sg=sg: e.tensor_tensor(out=g_[:, 0:n_], in0=g_[:, 0:n_], in1=sg[:, 0:n_], op=ALU.mult), R=[g_.b, sg.b], W=[g_.b])
                    fw.op("dve", lambda e, jc=jc, n_=n_, g_=g_, ln=ln: e.scalar_tensor_tensor(out=A[:, jc, 0:n_], in0=ln[:, 0:n_], scalar=1.0, in1=g_[:, 0:n_], op0=ALU.add, op1=ALU.mult), R=[g_.b, ln.b], W=[A.b])
                for s_ in range(nsub):
                    for half in range(2):
                        p = pp[npp % 8]; npp += 1
                        for k in range(8):
                            fw.op("pe", lambda e, p=p, k=k, s_=s_, half=half, w2=w2: e.matmul(p[:], A[:, k, s_ * 128:(s_ + 1) * 128], w2[:, k, half * 512:(half + 1) * 512], start=(k == 0), stop=(k == 7)),
                                  R=[A.b, w2.b], W=[p.b])
                        fw.op("dve", lambda e, p=p, s_=s_, half=half, le=le: e.scalar_tensor_tensor(out=Y[:, s_, half * 512:(half + 1) * 512], in0=p[:], scalar=Gt[:, s_, le:le + 1],
                                                                                                   in1=Y[:, s_, half * 512:(half + 1) * 512], op0=ALU.mult, op1=ALU.add), R=[p.b, Gt.b, Y.b], W=[Y.b])
            for s_ in range(nsub):
                fw.op("dve", lambda e, s_=s_: e.tensor_tensor(out=tz[:], in0=Y[:, s_, :], in1=mb[:], op=ALU.mult), R=[Y.b, mb.b], W=[tz.b])
                fw.op("dve", lambda e, s_=s_: e.scalar_tensor_tensor(out=tz[:], in0=xt[:, s_, :], scalar=0.5, in1=tz[:], op0=ALU.mult, op1=ALU.add), R=[xt.b, tz.b], W=[tz.b])
                fw.dma("sp", self.ARin.t.ap()[t0 + s_ * 128:t0 + (s_ + 1) * 128, :], tz[:], self.ARin.b, tz.b, part=True)
        fw.fence()
    self.allreduce_to_X(ntok)


setattr(Prog, "phase_moe", phase_moe)


_CACHE = {}


def kernel(**inputs):
    inp = {k: np.asarray(v) for k, v in inputs.items()}
    maps = host_prep(inp)
    P = Prog(stop=None, dumps=(), with_exp=True)
    nc = P.build()
    res = run_bass_kernel_spmd(nc, maps, core_ids=list(range(8)))
    out = np.stack([np.asarray(res.results[b]["out"]) for b in range(4)], 0)
    return out.astype(np.float32)
```

```python
import contextlib
import math
import numpy as np
import ml_dtypes
import concourse.bass as bass
import concourse.mybir as mybir
from concourse.bass_utils import run_bass_kernel_spmd

F32 = mybir.dt.float32
BF16 = mybir.dt.bfloat16
AF = mybir.ActivationFunctionType
ALU = mybir.AluOpType
AX = mybir.AxisListType

D = 1024
L = 8192
NCTX = 256
T = L + NCTX
DEPTH = 2
EPS = 1e-6
SUBLN_EPS = 1e-5
OFF_F, OFF_HY, OFF_Q, OFF_K, OFF_V, OFF_G = 0, 256, 1024, 1536, 2048, 2560
NMIX = 1280
NEXP_CORE = 16
PAIRS = [[0, 4], [1, 5], [2, 6], [3, 7]]
HALVES = [[0, 1, 2, 3], [4, 5, 6, 7]]


class Buf:
    __slots__ = ("name", "w", "r", "dsem")

    def __init__(self, name):
        self.name = name
        self.w = []
        self.r = []
        self.dsem = None


class Tile:
    def __init__(self, t, b):
        self.t = t
        self.b = b

    def __getitem__(self, k):
        return self.t[k]


class Op:
    __slots__ = ("id", "eng", "fn", "deps", "isdma", "signal", "wbuf")

    def __init__(self, id, eng, fn, deps, isdma, wbuf=None):
        self.id = id; self.eng = eng; self.fn = fn; self.deps = deps
        self.isdma = isdma; self.signal = isdma; self.wbuf = wbuf


class FW:
    ENG = ("pe", "act", "dve", "pool", "sp")
    NDSEM = 72

    def __init__(self, nc):
        self.nc = nc
        self.es = contextlib.ExitStack()
        self.h = {"pe": nc.tensor, "act": nc.scalar, "dve": nc.vector, "pool": nc.gpsimd, "sp": nc.sync}
        self.esem = {e: self.es.enter_context(nc.semaphore("e_" + e)) for e in self.ENG}
        self.ecnt = {e: 0 for e in self.ENG}
        self.fsem = self.es.enter_context(nc.semaphore("fence"))
        self.fcnt = 0
        self.dsems = [self.es.enter_context(nc.semaphore("d%d" % i)) for i in range(self.NDSEM)]
        self.dissued = [0] * self.NDSEM
        self.dnext = 0
        self.seen = {e: {} for e in self.ENG}
        self.done = {}
        self.pending = []
        self.bufs = []
        self.opmap = {}
        self.nid = 0
        self.ninst = 0

    def buf(self, name):
        b = Buf(name)
        self.bufs.append(b)
        return b

    def sb(self, st, name, shape, dtype=F32):
        self.nid += 1
        name = "%s_%d" % (name, self.nid)
        t = st.enter_context(self.nc.sbuf_tensor(name, shape, dtype))
        return Tile(t, self.buf(name))

    def ps(self, st, name, shape=(128, 512), dtype=F32):
        self.nid += 1
        name = "%s_%d" % (name, self.nid)
        t = st.enter_context(self.nc.psum_tensor(name, list(shape), dtype))
        return Tile(t, self.buf(name))

    def dram(self, name, shape, dtype, kind="Internal"):
        t = self.nc.dram_tensor(name, list(shape), dtype, kind=kind)
        return Tile(t, self.buf(name))

    def op(self, eng, fn, R=(), W=()):
        deps = set()
        for b in R:
            deps.update(b.w)
        for b in W:
            deps.update(b.w)
            deps.update(b.r)
        o = Op(self.nid, eng, fn, deps, False)
        self.nid += 1
        self.pending.append(o)
        for b in R:
            b.r.append(o.id)
        for b in W:
            b.w = [o.id]; b.r = []
        return o

    def dma(self, q, out_ap, in_ap, W, R, part=False, **kw):
        deps = set(R.w)
        for x in W.w:
            if part:
                p = self.opmap.get(x)
                if p is not None and p.isdma and p.wbuf is W:
                    continue
            deps.add(x)
        deps.update(W.r)
        fn = (lambda h, o=out_ap, i=in_ap, kw=kw: h.dma_start(out=o, in_=i, **kw))
        o = Op(self.nid, q, fn, deps, True, wbuf=W)
        self.nid += 1
        self.pending.append(o)
        self.opmap[o.id] = o
        R.r.append(o.id)
        if part:
            W.w = list(W.w) + [o.id]
        else:
            W.w = [o.id]
            W.r = []
        return o

    def _wait(self, eng, key, val):
        s = self.seen[eng]
        if s.get(key, 0) >= val:
            return
        s[key] = val
        if key[0] == "e":
            sem = self.esem[key[1]]
        elif key[0] == "f":
            sem = self.fsem
        else:
            sem = self.dsems[key[1]]
        self.h[eng].wait_ge(sem, val)
        self.ninst += 1

    def flush(self):
        ops = self.pending
        self.pending = []
        ids = {o.id: o for o in ops}
        for o in ops:
            for d in o.deps:
                p = ids.get(d)
                if p is not None and not p.isdma and not (p.eng == "pe" and o.eng == "pe"):
                    p.signal = True
        for o in ops:
            for d in o.deps:
                ev = self.done.get(d)
                if ev is None:
                    continue
                key, val = ev
                if key == ("e", "pe") and o.eng == "pe":
                    continue
                if key[0] == "d":
                    val = max(val, self.dissued[key[1]])
                self._wait(o.eng, key, val)
            inst = o.fn(self.h[o.eng])
            self.ninst += 1
            if o.isdma:
                b = o.wbuf
                if b.dsem is None:
                    b.dsem = self.dnext % self.NDSEM
                    self.dnext += 1
                k = b.dsem
                self.dissued[k] += 16
                inst.then_inc(self.dsems[k], 16)
                self.done[o.id] = (("d", k), self.dissued[k])
            elif o.signal:
                self.ecnt[o.eng] += 1
                inst.then_inc(self.esem[o.eng], 1)
                self.done[o.id] = (("e", o.eng), self.ecnt[o.eng])

    def fence(self):
        last = {}
        for o in self.pending:
            if not o.isdma:
                last[o.eng] = o
        for o in last.values():
            o.signal = True
        self.flush()
        for e in self.ENG:
            if self.ecnt[e] > 0:
                self._wait("sp", ("e", e), self.ecnt[e])
        for k in range(self.NDSEM):
            if self.dissued[k] > 0:
                self._wait("sp", ("d", k), self.dissued[k])
        self.fcnt += 1
        self.h["sp"].sem_inc(self.fsem, 1)
        self.ninst += 1
        for e in self.ENG:
            self._wait(e, ("f",), self.fcnt)
            for e2 in self.ENG:
                self.seen[e][("e", e2)] = self.ecnt[e2]
            for k in range(self.NDSEM):
                self.seen[e][("d", k)] = self.dissued[k]
        for b in self.bufs:
            b.w = []; b.r = []
        self.done = {}
        self.opmap = {}

    def close(self):
        self.es.close()


class Pack:
    def __init__(self):
        self.off = {}
        self.n = 0
        self.items = []

    def add(self, name, width):
        self.off[name] = (self.n, width)
        self.n += width

    def fill(self, arr, name, val):
        o, w = self.off[name]
        val = np.asarray(val, np.float32)
        val = val.reshape(val.shape[0], -1)
        assert val.shape[1] == w, (name, val.shape, w)
        arr[:val.shape[0], o:o + w] = val


def fm(v, nk):
    return np.asarray(v, np.float32).reshape(nk, 128).T


def rep(v):
    v = np.asarray(v, np.float32).reshape(1, -1)
    return np.broadcast_to(v, (128, v.shape[1]))


def make_sm_layout():
    P = Pack()
    P.add("cT", 16)
    P.add("crep", 2 * 8 * 128)
    P.add("deltabc", 128)
    for l in range(DEPTH):
        P.add("bmodT%d" % l, 48)
        P.add("bmodbc%d" % l, 2 * 1024)
        P.add("g1T%d" % l, 8)
        P.add("g2T%d" % l, 8)
        P.add("qg%d" % l, 1)
        P.add("kg%d" % l, 1)
        P.add("subg%d" % l, 1)
        P.add("lamq%d" % l, 128)
        P.add("lamk%d" % l, 128)
        P.add("wrt%d" % l, 8 * 32)
        P.add("brt%d" % l, 32)
        P.add("b1T%d" % l, 16 * 16)
        P.add("b2r%d" % l, 1024)
        P.add("cwbc%d" % l, 3 * 384)
        P.add("cbbc%d" % l, 384)
        P.add("hw1%d" % l, 64)
        P.add("hb1%d" % l, 1)
        P.add("hf1%d" % l, 1)
        P.add("hw2%d" % l, 64)
        P.add("hb2%d" % l, 1)
        P.add("hf2%d" % l, 1)
        P.add("hw3%d" % l, 512)
        P.add("hb3bc%d" % l, 512)
        P.add("hbias%d" % l, 256)
    return P


SM = make_sm_layout()


def make_cst_layout():
    P = Pack()
    P.add("ident", 128)
    P.add("ones", 128)
    P.add("bones", 128)
    P.add("rot", 128)
    P.add("ntl8192", 64)
    P.add("ntl256", 2)
    P.add("c64", 128)
    P.add("s64", 128)
    return P


CST = make_cst_layout()


def rope_tables():
    rows = L // 64
    row = np.repeat(np.arange(rows), 64).astype(np.float32)
    col = np.tile(np.arange(64), rows).astype(np.float32)
    nf = 16
    inv = (10000.0 ** (-np.arange(nf, dtype=np.float32) / nf)).astype(np.float32)
    angr = row[None, :] * inv[:, None]
    angc = col[None, :] * inv[:, None]
    ang64 = np.concatenate([angr, angr, angc, angc], 0)
    cos = np.cos(ang64).astype(np.float32)
    sin = np.sin(ang64).astype(np.float32)
    cos = np.concatenate([cos, np.ones((64, NCTX), np.float32)], 1)
    sin = np.concatenate([sin, np.zeros((64, NCTX), np.float32)], 1)
    return np.concatenate([cos, cos], 0), np.concatenate([sin, sin], 0)


def hy_emb(l):
    t = np.linspace(0.0, 1.0, l, dtype=np.float32)[:, None]
    ang = (np.float32(2.0 * math.pi / l) * np.arange(l, dtype=np.float32))[:, None]
    bands = np.linspace(1e-4, 15, 16, dtype=np.float32)[None, :]
    emb = np.concatenate([t, np.cos(bands * ang), -np.sin(bands * ang)], -1)
    return emb.T.astype(np.float32)


def hy_deltas(s):
    d = np.abs(np.linspace(math.log(1e-2) / 1.5, math.log(1e-2) / 0.3, 256, dtype=np.float32))
    return d[128 * s:128 * s + 128]


def rot_lhsT():
    R = np.zeros((128, 128), np.float32)
    for blk in range(2):
        for base in (0, 32):
            for j in range(16):
                a = blk * 64 + base + j
                b = a + 16
                R[a, b] = -1.0
                R[b, a] = 1.0
    return R.T.copy()


def shared_layout():
    off = {}
    n = 0
    for l in range(DEPTH):
        off["wg%d" % l] = (n, (1024, 3072)); n += 1024 * 3072
        off["wmod%d" % l] = (n, (1024, 6144)); n += 1024 * 6144
        off["wo%d" % l] = (n, (1024, 1024)); n += 1024 * 1024
    off["ropec"] = (n, (128, T)); n += 128 * T
    off["ropes"] = (n, (128, T)); n += 128 * T
    rows = -(-n // (512 * 2048)) * 512
    return off, rows


SHOFF, SHROWS = shared_layout()


def host_prep(inp):
    x = inp["x"]; ctx = inp["ctx"]; c = inp["c"]; c_ctx = inp["c_ctx"]
    blob = np.zeros((SHROWS * 2048,), np.float32)

    def put(name, arr):
        o, shp = SHOFF[name]
        blob[o:o + arr.size] = np.ascontiguousarray(arr, np.float32).reshape(-1)

    for l in range(DEPTH):
        put("wg%d" % l, inp["w_in"][l][:, OFF_G:])
        put("wmod%d" % l, inp["w_mod"][l])
        put("wo%d" % l, inp["w_o"][l])
    rc, rs = rope_tables()
    put("ropec", rc); put("ropes", rs)
    blob = blob.reshape(SHROWS // 512, 4, 128, 2048)

    cst = np.zeros((128, CST.n), np.float32)
    CST.fill(cst, "ident", np.eye(128, dtype=np.float32))
    CST.fill(cst, "ones", np.ones((128, 128), np.float32))
    bo = np.zeros((128, 128), np.float32); bo[:64, :64] = 1; bo[64:, 64:] = 1
    CST.fill(cst, "bones", bo)
    CST.fill(cst, "rot", rot_lhsT())
    pp = np.arange(128)[:, None]
    CST.fill(cst, "ntl8192", -((np.arange(64)[None, :] * 128 + pp) / (L - 1.0)))
    CST.fill(cst, "ntl256", -((np.arange(2)[None, :] * 128 + pp) / (NCTX - 1.0)))
    a64 = np.arange(64)
    c64 = np.cos(2 * np.pi * np.outer(a64, a64) / 64); s64 = np.sin(2 * np.pi * np.outer(a64, a64) / 64)
    z = np.zeros((64, 64))
    CST.fill(cst, "c64", np.block([[c64, z], [z, c64]]))
    CST.fill(cst, "s64", np.block([[s64, z], [z, s64]]))
    embc = np.concatenate([hy_emb(L), hy_emb(NCTX)], 1).astype(np.float32)
    fftc = fft_constants()

    maps = []
    for r in range(8):
        s, b = r // 4, r % 4
        m = {}
        m["xs"] = np.ascontiguousarray(x[b, s * 4096:(s + 1) * 4096])
        m["ctxb"] = np.ascontiguousarray(ctx[b])
        m["wsh"] = np.ascontiguousarray(blob[:, b]).reshape(SHROWS // 4, 2048)
        m["cst"] = cst
        m["embc"] = embc
        m["fftc"] = fftc
        wmix = np.zeros((DEPTH, 1024, NMIX), np.float32)
        wout = np.zeros((DEPTH, 512, 1024), np.float32)
        sm = np.zeros((128, SM.n), np.float32)
        SM.fill(sm, "cT", np.stack([fm(c[b], 8), fm(c_ctx, 8)], -1).reshape(128, 16))
        crep = np.zeros((128, 2, 8, 128), np.float32)
        crep[:, 0] = fm(c[b], 8)[:, :, None]
        crep[:, 1] = fm(c_ctx, 8)[:, :, None]
        SM.fill(sm, "crep", crep.reshape(128, -1))
        SM.fill(sm, "deltabc", rep(hy_deltas(s)))
        for l in range(DEPTH):
            w_in = inp["w_in"][l]
            cols = np.concatenate([
                np.arange(OFF_F + 128 * s, OFF_F + 128 * s + 128),
                np.arange(OFF_HY + 128 * s, OFF_HY + 128 * s + 128),
                np.arange(OFF_HY + 256 + 128 * s, OFF_HY + 256 + 128 * s + 128),
                np.arange(OFF_HY + 512 + 128 * s, OFF_HY + 512 + 128 * s + 128),
                np.arange(OFF_Q + 256 * s, OFF_Q + 256 * s + 256),
                np.arange(OFF_K + 256 * s, OFF_K + 256 * s + 256),
                np.arange(OFF_V + 256 * s, OFF_V + 256 * s + 256)])
            wmix[l] = w_in[:, cols]
            wout[l, 0:128] = inp["w_f"][l][128 * s:128 * s + 128]
            wout[l, 128:256] = inp["w_h"][l][128 * s:128 * s + 128]
            wout[l, 256:512] = inp["w_a"][l][256 * s:256 * s + 256]
            SM.fill(sm, "bmodT%d" % l, fm(inp["b_mod"][l], 48))
            bm = inp["b_mod"][l].reshape(6, 1024)
            SM.fill(sm, "bmodbc%d" % l, rep(np.concatenate([bm[2], bm[5]])))
            SM.fill(sm, "g1T%d" % l, fm(inp["norm1_g"][l], 8))
            SM.fill(sm, "g2T%d" % l, fm(inp["norm2_g"][l], 8))
            SM.fill(sm, "qg%d" % l, np.tile(inp["q_norm_g"][l], 2).reshape(128, 1))
            SM.fill(sm, "kg%d" % l, np.tile(inp["k_norm_g"][l], 2).reshape(128, 1))
            SM.fill(sm, "subg%d" % l, inp["subln_g"][l].reshape(128, 1))
            SM.fill(sm, "lamq%d" % l, rep(inp["lam_q"][l].reshape(-1)))
            SM.fill(sm, "lamk%d" % l, rep(inp["lam_k"][l].reshape(-1)))
            es_ = [16 * s + le for le in range(16)]
            perm = es_ + [e for e in range(32) if e not in es_]
            SM.fill(sm, "wrt%d" % l, inp["w_router"][l][:, perm].reshape(8, 128, 32).transpose(1, 0, 2).reshape(128, 256))
            SM.fill(sm, "brt%d" % l, rep(inp["b_router"][l][perm]))
            b1 = inp["b_e1"][l][es_]
            b1p = np.concatenate([b1[:, 0::2], b1[:, 1::2]], 1)
            SM.fill(sm, "b1T%d" % l, b1p.reshape(16, 16, 128).transpose(2, 0, 1).reshape(128, 256))
            SM.fill(sm, "b2r%d" % l, inp["b_e2"][l][es_])
            cw = inp["hy_conv_w"][l]; cb = inp["hy_conv_b"][l]
            vx = np.concatenate([np.arange(128 * s, 128 * s + 128), np.arange(256 + 128 * s, 256 + 128 * s + 128),
                                 np.arange(512 + 128 * s, 512 + 128 * s + 128)])
            SM.fill(sm, "cwbc%d" % l, rep(cw[:, vx].reshape(-1)))
            SM.fill(sm, "cbbc%d" % l, rep(cb[vx]))
            SM.fill(sm, "hw1%d" % l, inp["hy_w1"][l])
            SM.fill(sm, "hb1%d" % l, inp["hy_b1"][l].reshape(64, 1))
            SM.fill(sm, "hf1%d" % l, inp["hy_freq1"][l].reshape(64, 1))
            SM.fill(sm, "hw2%d" % l, inp["hy_w2"][l])
            SM.fill(sm, "hb2%d" % l, inp["hy_b2"][l].reshape(64, 1))
            SM.fill(sm, "hf2%d" % l, inp["hy_freq2"][l].reshape(64, 1))
            w3 = inp["hy_w3"][l].reshape(64, 2, 2, 256)[:, :, :, 128 * s:128 * s + 128].reshape(64, 512)
            b3 = inp["hy_b3"][l].reshape(2, 2, 256)[:, :, 128 * s:128 * s + 128].reshape(512)
            SM.fill(sm, "hw3%d" % l, w3)
            SM.fill(sm, "hb3bc%d" % l, rep(b3))
            SM.fill(sm, "hbias%d" % l, rep(inp["hy_bias"][l][:, 128 * s:128 * s + 128].reshape(-1)))
        m["wmix"] = wmix
        m["wout"] = wout
        m["sm"] = sm
        wexp = np.zeros((DEPTH, 6144, 2048), np.float32)
        for l in range(DEPTH):
            full = np.zeros((16, 1536, 2048), np.float32)
            for le in range(16):
                e = 16 * s + le
                w1 = inp["w_e1"][l][e]
                full[le, :1024] = np.concatenate([w1[:, 0::2], w1[:, 1::2]], 1)
                full[le, 1024:] = inp["w_e2"][l][e].reshape(512, 2048)
            wexp[l] = full.reshape(48, 4, 128, 2048)[:, b].reshape(6144, 2048)
        m["wexp"] = wexp
        maps.append(m)
    return maps


def dview(tile, off, shape):
    r, c = shape
    return bass.AP(tile.t, off, [[c, r], [1, c]])


class Prog:
    def __init__(self, stop=None, dumps=(), with_exp=True):
        self.stop = stop
        self.dumps = list(dumps)
        self.with_exp = with_exp
        nc = self.nc = bass.Bass("TRN2", target_bir_lowering=False)
        fw = self.fw = FW(nc)
        self.ncc = 0
        self.xs = fw.dram("xs", [4096, 1024], F32, "ExternalInput")
        self.ctxb = fw.dram("ctxb", [NCTX, 1024], F32, "ExternalInput")
        self.wsh = fw.dram("wsh", [SHROWS // 4, 2048], F32, "ExternalInput")
        self.cst_d = fw.dram("cst", [128, CST.n], F32, "ExternalInput")
        self.wmix = fw.dram("wmix", [DEPTH, 1024, NMIX], F32, "ExternalInput")
        self.wout = fw.dram("wout", [DEPTH, 512, 1024], F32, "ExternalInput")
        self.sm_d = fw.dram("sm", [128, SM.n], F32, "ExternalInput")
        self.embc = fw.dram("embc", [33, T], F32, "ExternalInput")
        self.fftc_d = fw.dram("fftc", [128, FFTC.n], F32, "ExternalInput")
        if with_exp:
            self.wexp = fw.dram("wexp", [DEPTH, 6144, 2048], F32, "ExternalInput")
        self.out = fw.dram("out", [L, 1024], F32, "ExternalOutput")
        self.X = fw.dram("X", [T, 1024], F32)
        self.SH = fw.dram("SH", [SHROWS, 2048], F32)
        self.QT = fw.dram("QT", [256, T], BF16)
        self.KT = fw.dram("KT", [256, T], BF16)
        self.V = fw.dram("V", [T, 256], BF16)
        self.GT = fw.dram("GT", [3072, T], BF16)
        self.PFH = fw.dram("PFH", [T + 4, 512], F32)
        self.MODBC = fw.dram("MODBC", [128, 4096], F32)
        self.FM = fw.dram("FM", [T, 2, 128], F32)
        self.HM = fw.dram("HM", [T, 128], F32)
        self.OT = fw.dram("OT", [256, T], BF16)
        self.ARin = fw.dram("ARin", [T, 1024], F32)
        self.dump_out = {}

    def coll_seq(self, items):
        fw = self.fw
        sem = fw.es.enter_context(self.nc.semaphore("cc%d" % self.ncc))
        self.ncc += 1
        fw.fence()
        for (kind, groups, in_ap, out_ap) in items:
            op = ALU.bypass if kind in ("AllGather", "AllToAll") else ALU.add
            fw.h["pool"].collective_compute(kind, op, replica_groups=groups, ins=[in_ap], outs=[out_ap]).then_inc(sem)
        for e in fw.ENG:
            fw.h[e].wait_ge(sem, len(items))

    def ld(self, ps, name, rows=128, q="sp"):
        o, w = SM.off[name]
        t = self.fw.sb(ps, "sm_" + name, [rows, w])
        kw = dict(allow_slow_non_contiguous=True) if w == 1 else {}
        self.fw.dma(q, t[:], self.sm_d.t.ap()[0:rows, o:o + w], t.b, self.sm_d.b, **kw)
        return t

    def cv(self, name):
        o, w = CST.off[name]
        return self.cst[:, o:o + w]

    def dump(self, name, tile):
        if name in self.dumps:
            t = tile.t
            d = self.fw.dram("dump_" + name, list(t.shape), t.dtype, "ExternalOutput")
            self.fw.dma("sp", d.t.ap(), t.ap(), d.b, tile.b)
            self.dump_out[name] = "dump_" + name
            self.fw.fence()

    def bigcopy(self, dst, dap, src, sap, rows, step=256):
        for r0 in range(0, rows, step):
            r1 = min(rows, r0 + step)
            self.fw.dma("sp", dap[r0:r1, :], sap[r0:r1, :], dst.b, src.b, part=True)

    def phase_gather(self):
        fw = self.fw
        xsI = fw.dram("xsI", [4096, 1024], F32)
        wshI = fw.dram("wshI", [SHROWS // 4, 2048], F32)
        self.bigcopy(xsI, xsI.t.ap(), self.xs, self.xs.t.ap(), 4096, 512)
        self.bigcopy(wshI, wshI.t.ap(), self.wsh, self.wsh.t.ap(), SHROWS // 4)
        fw.dma("sp", self.X.t.ap()[L:T, :], self.ctxb.t.ap(), self.X.b, self.ctxb.b, part=True)
        if self.with_exp:
            self.wexpI = fw.dram("wexpI", [DEPTH, 6144, 2048], F32)
            for l in range(DEPTH):
                self.bigcopy(self.wexpI, self.wexpI.t.ap()[l], self.wexp, self.wexp.t.ap()[l], 6144)
        XG = fw.dram("XG", [8, 1024, 1024], F32)
        items = [("AllGather", PAIRS, xsI.t.ap()[c * 512:(c + 1) * 512, :], XG.t.ap()[c]) for c in range(8)]
        items += [("AllGather", HALVES, wshI.t.ap()[c * 128:(c + 1) * 128, :], self.SH.t.ap()[c * 512:(c + 1) * 512, :])
                  for c in range(SHROWS // 512)]
        self.coll_seq(items)
        if self.with_exp:
            self.WEl = [fw.dram("WE%d" % l, [24576, 2048], F32) for l in range(DEPTH)]
            self.expsem = fw.es.enter_context(self.nc.semaphore("expsem"))
            for l in range(DEPTH):
                for c in range(48):
                    fw.h["pool"].collective_compute("AllGather", ALU.bypass, replica_groups=HALVES,
                                                    ins=[self.wexpI.t.ap()[l, c * 128:(c + 1) * 128, :]],
                                                    outs=[self.WEl[l].t.ap()[c * 512:(c + 1) * 512, :]]).then_inc(self.expsem)
        for c in range(8):
            for r in range(2):
                for q in range(2):
                    fw.dma("sp", self.X.t.ap()[r * 4096 + c * 512 + q * 256: r * 4096 + c * 512 + q * 256 + 256, :],
                           XG.t.ap()[c, r * 512 + q * 256: r * 512 + q * 256 + 256, :], self.X.b, XG.b, part=True)
        fw.fence()

    def phase_mod(self, l):
        fw = self.fw
        with contextlib.ExitStack() as ps:
            cT = self.ld(ps, "cT"); crep = self.ld(ps, "crep")
            bmodT = self.ld(ps, "bmodT%d" % l); bmodbc = self.ld(ps, "bmodbc%d" % l)
            g1T = self.ld(ps, "g1T%d" % l); g2T = self.ld(ps, "g2T%d" % l)
            lamq = self.ld(ps, "lamq%d" % l); lamk = self.ld(ps, "lamk%d" % l)
            for nm, dst in (("qg%d" % l, 0), ("kg%d" % l, 1), ("subg%d" % l, 2)):
                o, w = SM.off[nm]
                fw.dma("sp", self.qks[:, dst:dst + 1], self.sm_d.t.ap()[:, o:o + 1], self.qks.b, self.sm_d.b, part=True, allow_slow_non_contiguous=True)
            sc = fw.sb(ps, "sc", [128, 8, 2])
            screp = fw.sb(ps, "screp", [128, 2, 8, 128])
            mbc = fw.sb(ps, "mbc", [128, 2, 2, 1024])
            pm = fw.ps(ps, "pm")
            pbc = [fw.ps(ps, "pbc%d" % i) for i in range(2)]
            wm = [fw.sb(ps, "wm%d" % i, [128, 8, 512]) for i in range(2)]
            fw.op("act", lambda e: e.activation(out=sc[:].rearrange("p k j -> p (k j)"), in_=cT[:], func=AF.Silu),
                  R=[cT.b], W=[sc.b])
            fw.op("act", lambda e: e.activation(out=screp[:].rearrange("p j k m -> p (j k m)"), in_=crep[:], func=AF.Silu),
                  R=[crep.b], W=[screp.b])
            o, _ = SHOFF["wmod%d" % l]
            wv = dview(self.SH, o, (1024, 6144)).rearrange("(k p) c -> p k c", p=128)
            for cb in range(12):
                w = wm[cb % 2]
                fw.dma("sp", w[:], wv[:, :, cb * 512:(cb + 1) * 512], w.b, self.SH.b)
                for oc in range(4):
                    col = (cb * 4 + oc) * 2
                    for k in range(8):
                        fw.op("pe", lambda e, w=w, oc=oc, k=k, col=col: e.matmul(
                            pm[:, col:col + 2], w[:, k, oc * 128:(oc + 1) * 128], sc[:, k, :], start=(k == 0), stop=(k == 7)),
                            R=[w.b, sc.b], W=[pm.b])
                if cb in (4, 5, 10, 11):
                    which = 0 if cb < 6 else 1
                    half = cb - 4 if cb < 6 else cb - 10
                    for j in range(2):
                        pb = pbc[j]
                        for k in range(8):
                            fw.op("pe", lambda e, w=w, j=j, k=k, pb=pb: e.matmul(
                                pb[:], screp[:, j, k, :], w[:, k, :], start=(k == 0), stop=(k == 7)),
                                R=[w.b, screp.b], W=[pb.b])
                        bsl = bmodbc[:, which * 1024 + half * 512: which * 1024 + half * 512 + 512]
                        fw.op("dve", lambda e, pb=pb, j=j, which=which, half=half, bsl=bsl: e.tensor_tensor(
                            out=mbc[:, j, which, half * 512:(half + 1) * 512], in0=pb[:], in1=bsl, op=ALU.add),
                            R=[pb.b, bmodbc.b], W=[mbc.b])
            fw.dma("sp", self.MODBC.t.ap(), mbc[:].rearrange("p j w c -> p (j w c)"), self.MODBC.b, mbc.b)
            pmv = pm[:, 0:96].rearrange("p (a j) -> p a j", j=2)
            for j in range(2):
                fw.op("dve", lambda e, j=j: e.tensor_tensor(out=self.modT[:, :, j], in0=pmv[:, :, j], in1=bmodT[:], op=ALU.add),
                      R=[pm.b, bmodT.b], W=[self.modT.b])
            for (gT, gs, sh, ishift, iscale) in ((g1T, self.gs1, self.sh1, 0, 1), (g2T, self.gs2, self.sh2, 3, 4)):
                for j in range(2):
                    fw.op("dve", lambda e, j=j, gs=gs, iscale=iscale, gT=gT: e.scalar_tensor_tensor(
                        out=gs[:, :, j], in0=self.modT[:, iscale * 8:(iscale + 1) * 8, j], scalar=1.0, in1=gT[:],
                        op0=ALU.add, op1=ALU.mult), R=[self.modT.b, gT.b], W=[gs.b])
                    fw.op("dve", lambda e, j=j, sh=sh, ishift=ishift: e.tensor_copy(
                        out=sh[:, :, j], in_=self.modT[:, ishift * 8:(ishift + 1) * 8, j]), R=[self.modT.b], W=[sh.b])
            lp = fw.sb(ps, "lp", [128, 2, 64])
            le = fw.sb(ps, "le", [128, 2])
            fw.op("dve", lambda e: e.tensor_tensor(out=lp[:].rearrange("p a b -> p (a b)"), in0=lamq[:], in1=lamk[:], op=ALU.mult),
                  R=[lamq.b, lamk.b], W=[lp.b])
            fw.op("dve", lambda e: e.tensor_reduce(out=le[:], in_=lp[:], axis=AX.X, op=ALU.add), R=[lp.b], W=[le.b])
            fw.op("act", lambda e: e.activation(out=le[:], in_=le[:], func=AF.Exp), R=[le.b], W=[le.b])
            lam_init = 0.8 - 0.6 * math.exp(-0.3 * l)
            fw.op("dve", lambda e: e.tensor_tensor(out=self.lam[:, 0:1], in0=le[:, 0:1], in1=le[:, 1:2], op=ALU.subtract),
                  R=[le.b], W=[self.lam.b])
            fw.op("dve", lambda e: e.tensor_scalar(out=self.lam[:, 0:1], in0=self.lam[:, 0:1], scalar1=lam_init, scalar2=None,
                                                   op0=ALU.add), R=[self.lam.b], W=[self.lam.b])
            fw.op("dve", lambda e: e.tensor_scalar(out=self.lam[:, 1:2], in0=self.lam[:, 0:1], scalar1=-1.0, scalar2=None,
                                                   op0=ALU.mult), R=[self.lam.b], W=[self.lam.b])
            fw.fence()

    def pfh_row(self, t0):
        return t0 + 1 if t0 < L else (t0 - L) + L + 3

    def phase_proj(self, l):
        fw = self.fw
        with contextlib.ExitStack() as ps:
            Wm = fw.sb(ps, "Wm", [128, 8, NMIX], BF16)
            Wg = fw.sb(ps, "Wg", [128, 8, 3072], BF16)
            stg = [fw.sb(ps, "stg%d" % i, [128, 8, 256]) for i in range(2)]
            wmv = self.wmix.t.ap()[l].rearrange("(k p) c -> p k c", p=128)
            og, _ = SHOFF["wg%d" % l]
            wgv = dview(self.SH, og, (1024, 3072)).rearrange("(k p) c -> p k c", p=128)
            n = 0
            for (src, srcb, dst, nb) in ((wmv, self.wmix.b, Wm, NMIX // 256), (wgv, self.SH.b, Wg, 12)):
                for cb in range(nb):
                    s = stg[n % 2]; n += 1
                    fw.dma("sp", s[:], src[:, :, cb * 256:(cb + 1) * 256], s.b, srcb)
                    fw.op("act" if n % 2 else "dve",
                          (lambda e, s=s, dst=dst, cb=cb: e.activation(out=dst[:, :, cb * 256:(cb + 1) * 256], in_=s[:], func=AF.Copy))
                          if n % 2 else
                          (lambda e, s=s, dst=dst, cb=cb: e.tensor_copy(out=dst[:, :, cb * 256:(cb + 1) * 256], in_=s[:])),
                          R=[s.b], W=[dst.b])
            xt = [fw.sb(ps, "xt%d" % i, [128, 1024]) for i in range(2)]
            xn = [fw.sb(ps, "xn%d" % i, [128, 1024]) for i in range(2)]
            junk = fw.sb(ps, "junk", [128, 1024])
            ss = [fw.sb(ps, "ss%d" % i, [128, 2]) for i in range(2)]
            hT = [fw.sb(ps, "hT%d" % i, [128, 8, 512], BF16) for i in range(2)]
            rc = [fw.sb(ps, "rc%d" % i, [128, 512]) for i in range(2)]
            rs_ = [fw.sb(ps, "rs%d" % i, [128, 512]) for i in range(2)]
            sq = fw.sb(ps, "sq", [128, 512]); rq = fw.sb(ps, "rq", [128, 512]); qn = fw.sb(ps, "qn", [128, 512])
            t1 = fw.sb(ps, "t1", [128, 512]); t2 = fw.sb(ps, "t2", [128, 512])
            qo = [fw.sb(ps, "qo%d" % i, [128, 512], BF16) for i in range(2)]
            fhs = [fw.sb(ps, "fhs%d" % i, [128, 512]) for i in range(2)]
            vs = [fw.sb(ps, "vs%d" % i, [128, 256], BF16) for i in range(2)]
            gst = [fw.sb(ps, "gst%d" % i, [128, 4, 512], BF16) for i in range(2)]
            zt = fw.sb(ps, "zt", [4, 512])
            pt = [fw.ps(ps, "pt%d" % i) for i in range(2)]
            pj = [fw.ps(ps, "pj%d" % i) for i in range(3)]
            pn = [fw.ps(ps, "pn%d" % i) for i in range(2)]
            ident = self.cv("ident"); bones = self.cv("bones"); rot = self.cv("rot")
            cb_ = self.cstb
            fw.op("dve", lambda e: e.memset(zt[:], 0.0), W=[zt.b])
            for r in (0, L + 1, L + 2, T + 3):
                fw.dma("sp", self.PFH.t.ap()[r:r + 1, :], zt[0:1, :], self.PFH.b, zt.b, part=True)
            oc_, _ = SHOFF["ropec"]; os_, _ = SHOFF["ropes"]
            rcv = dview(self.SH, oc_, (128, T)); rsv = dview(self.SH, os_, (128, T))
            npj = 0
            ntile = (T + 511) // 512
            for tt in range(ntile):
                t0 = tt * 512
                n_ = min(512, T - t0)
                nsub = n_ // 128
                j = 0 if t0 < L else 1
                h = hT[tt % 2]
                for i in range(nsub):
                    x = xt[i % 2]; y = xn[i % 2]; s2 = ss[i % 2]
                    fw.dma("sp", x[:], self.X.t.ap()[t0 + i * 128:t0 + (i + 1) * 128, :], x.b, self.X.b)
                    fw.op("act", lambda e, x=x, s2=s2: e.activation(out=junk[:], in_=x[:], func=AF.Square, accum_out=s2[:, 0:1]),
                          R=[x.b], W=[junk.b, s2.b])
                    fw.op("act", lambda e, s2=s2: e.activation(out=s2[:, 1:2], in_=s2[:, 0:1], func=AF.Sqrt, scale=1.0 / D, bias=self.epsb[:, 0:1]),
                          R=[s2.b], W=[s2.b])
                    fw.op("dve", lambda e, s2=s2: e.reciprocal(out=s2[:, 1:2], in_=s2[:, 1:2]), R=[s2.b], W=[s2.b])
                    fw.op("dve", lambda e, x=x, y=y, s2=s2: e.tensor_scalar(out=y[:], in0=x[:], scalar1=s2[:, 1:2], scalar2=None, op0=ALU.mult),
                          R=[x.b, s2.b], W=[y.b])
                    for g in range(2):
                        for q in range(4):
                            k = 4 * g + q
                            fw.op("pe", lambda e, y=y, g=g, q=q, k=k: e.transpose(out=pt[g][:, q * 128:(q + 1) * 128], in_=y[:, k * 128:(k + 1) * 128], identity=ident),
                                  R=[y.b, cb_], W=[pt[g].b])
                        for q in range(4):
                            k = 4 * g + q
                            fw.op("dve", lambda e, g=g, q=q, k=k, i=i, h=h, j=j: e.tensor_scalar(
                                out=h[:, k, i * 128:(i + 1) * 128], in0=pt[g][:, q * 128:(q + 1) * 128],
                                scalar1=self.gs1[:, k, j:j + 1], scalar2=self.sh1[:, k, j:j + 1], op0=ALU.mult, op1=ALU.add),
                                R=[pt[g].b, self.gs1.b, self.sh1.b], W=[h.b])
                c_ = rc[tt % 2]; s_ = rs_[tt % 2]
                fw.dma("sp", c_[:, 0:n_], rcv[:, t0:t0 + n_], c_.b, self.SH.b)
                fw.dma("sp", s_[:, 0:n_], rsv[:, t0:t0 + n_], s_.b, self.SH.b)
                for i in range(nsub):
                    p = pj[npj % 3]; npj += 1
                    for k in range(8):
                        fw.op("pe", lambda e, p=p, k=k, i=i, h=h: e.matmul(p[:, 0:512], h[:, k, i * 128:(i + 1) * 128], Wm[:, k, 0:512], start=(k == 0), stop=(k == 7)),
                              R=[h.b, Wm.b], W=[p.b])
                    f = fhs[i % 2]
                    fw.op("act", lambda e, p=p, f=f: e.activation(out=f[:], in_=p[:, 0:512], func=AF.Copy), R=[p.b], W=[f.b])
                    r0 = self.pfh_row(t0 + i * 128)
                    fw.dma("sp", self.PFH.t.ap()[r0:r0 + 128, :], f[:], self.PFH.b, f.b, part=True)
                    p = pj[npj % 3]; npj += 1
                    for k in range(8):
                        fw.op("pe", lambda e, p=p, k=k, i=i, h=h: e.matmul(p[:, 0:256], h[:, k, i * 128:(i + 1) * 128], Wm[:, k, 1024:1280], start=(k == 0), stop=(k == 7)),
                              R=[h.b, Wm.b], W=[p.b])
                    v = vs[i % 2]
                    fw.op("act", lambda e, p=p, v=v: e.activation(out=v[:], in_=p[:, 0:256], func=AF.Copy), R=[p.b], W=[v.b])
                    fw.dma("sp", self.V.t.ap()[t0 + i * 128:t0 + (i + 1) * 128, :], v[:], self.V.b, v.b, part=True)
                for c in range(4):
                    p = pj[npj % 3]; npj += 1
                    c0 = 512 + c * 128
                    for k in range(8):
                        fw.op("pe", lambda e, p=p, k=k, h=h, n_=n_, c0=c0: e.matmul(p[:, 0:n_], Wm[:, k, c0:c0 + 128], h[:, k, 0:n_], start=(k == 0), stop=(k == 7)),
                              R=[h.b, Wm.b], W=[p.b])
                    fw.op("act", lambda e, p=p, n_=n_: e.activation(out=sq[:, 0:n_], in_=p[:, 0:n_], func=AF.Square), R=[p.b], W=[sq.b])
                    pa = pn[0]
                    fw.op("pe", lambda e, pa=pa, n_=n_: e.matmul(pa[:, 0:n_], bones, sq[:, 0:n_], start=True, stop=True), R=[sq.b, cb_], W=[pa.b])
                    fw.op("act", lambda e, pa=pa, n_=n_: e.activation(out=rq[:, 0:n_], in_=pa[:, 0:n_], func=AF.Sqrt, scale=1.0 / 64, bias=self.epsb[:, 0:1]),
                          R=[pa.b], W=[rq.b])
                    fw.op("dve", lambda e, n_=n_: e.reciprocal(out=rq[:, 0:n_], in_=rq[:, 0:n_]), R=[rq.b], W=[rq.b])
                    gi = 0 if c < 2 else 1
                    fw.op("dve", lambda e, p=p, n_=n_, gi=gi: e.scalar_tensor_tensor(out=qn[:, 0:n_], in0=p[:, 0:n_], scalar=self.qks[:, gi:gi + 1], in1=rq[:, 0:n_],
                                                                                   op0=ALU.mult, op1=ALU.mult), R=[p.b, rq.b, self.qks.b], W=[qn.b])
                    pb = pn[1]
                    fw.op("pe", lambda e, pb=pb, n_=n_: e.matmul(pb[:, 0:n_], rot, qn[:, 0:n_], start=True, stop=True), R=[qn.b, cb_], W=[pb.b])
                    fw.op("dve", lambda e, n_=n_, c_=c_: e.tensor_tensor(out=t1[:, 0:n_], in0=qn[:, 0:n_], in1=c_[:, 0:n_], op=ALU.mult), R=[qn.b, c_.b], W=[t1.b])
                    fw.op("dve", lambda e, pb=pb, n_=n_, s_=s_: e.tensor_tensor(out=t2[:, 0:n_], in0=pb[:, 0:n_], in1=s_[:, 0:n_], op=ALU.mult), R=[pb.b, s_.b], W=[t2.b])
                    o_ = qo[c % 2]
                    fw.op("dve", lambda e, n_=n_, o_=o_: e.tensor_tensor(out=o_[:, 0:n_], in0=t1[:, 0:n_], in1=t2[:, 0:n_], op=ALU.add), R=[t1.b, t2.b], W=[o_.b])
                    dstT = self.QT if c < 2 else self.KT
                    cc = c % 2
                    fw.dma("sp", dstT.t.ap()[cc * 128:(cc + 1) * 128, t0:t0 + n_], o_[:, 0:n_], dstT.b, o_.b, part=True)
                for c4 in range(6):
                    g_ = gst[c4 % 2]
                    for c in range(4):
                        cg = c4 * 4 + c
                        p = pj[npj % 3]; npj += 1
                        for k in range(8):
                            fw.op("pe", lambda e, p=p, k=k, h=h, n_=n_, cg=cg: e.matmul(p[:, 0:n_], Wg[:, k, cg * 128:(cg + 1) * 128], h[:, k, 0:n_], start=(k == 0), stop=(k == 7)),
                                  R=[h.b, Wg.b], W=[p.b])
                        fw.op("act", lambda e, p=p, n_=n_, g_=g_, c=c: e.activation(out=g_[:, c, 0:n_], in_=p[:, 0:n_], func=AF.Sigmoid), R=[p.b], W=[g_.b])
                    fw.dma("sp", self.GT.t.ap()[c4 * 512:(c4 + 1) * 512, t0:t0 + n_].rearrange("(c p) t -> p c t", p=128), g_[:, :, 0:n_], self.GT.b, g_.b, part=True)
            fw.fence()

    def build(self):
        fw = self.fw
        with contextlib.ExitStack() as st:
            self.cst = fw.sb(st, "cstt", [128, CST.n]); self.cstb = self.cst.b
            self.modT = fw.sb(st, "modT", [128, 48, 2])
            self.gs1 = fw.sb(st, "gs1", [128, 8, 2]); self.sh1 = fw.sb(st, "sh1", [128, 8, 2])
            self.gs2 = fw.sb(st, "gs2", [128, 8, 2]); self.sh2 = fw.sb(st, "sh2", [128, 8, 2])
            self.lam = fw.sb(st, "lam", [128, 2])
            self.qks = fw.sb(st, "qks", [128, 4])
            self.epsb = fw.sb(st, "epsb", [128, 2])
            fw.dma("sp", self.cst[:], self.cst_d.t.ap(), self.cst.b, self.cst_d.b)
            fw.op("dve", lambda e: e.memset(self.epsb[:, 0:1], EPS), W=[self.epsb.b])
            fw.op("dve", lambda e: e.memset(self.epsb[:, 1:2], SUBLN_EPS), W=[self.epsb.b])
            self.phase_gather()
            for l in range(DEPTH if self.stop != ("gather", 0) else 0):
                self.phase_mod(l)
                if self.stop == ("mod", l):
                    break
                self.phase_proj(l)
                if self.stop == ("proj", l):
                    break
                self.phase_seqmix(l, "lat")
                if l == 0:
                    self.phase_seqmix(l, "ctx")
                if self.stop == ("seq", l):
                    break
                self.phase_attn(l)
                if self.stop == ("attn", l):
                    break
                self.phase_outproj(l)
                if self.stop == ("outp", l):
                    break
                if self.with_exp:
                    self.phase_moe(l)
            for nm, tl in (("QT", self.QT), ("KT", self.KT), ("V", self.V), ("PFH", self.PFH), ("GT", self.GT), ("X", self.X), ("MODBC", self.MODBC), ("FM", self.FM), ("HM", self.HM), ("OT", self.OT)):
                self.dump(nm, tl)
            self.bigcopy(self.out, self.out.t.ap(), self.X, self.X.t.ap(), L, 512)
            fw.fence()
        fw.close()
        return self.nc


FFT_CFG = {"FA": (64, 64), "HB": (128, 64), "FC": (2, 2), "HD": (4, 2)}


def make_fft_layout():
    P = Pack()
    for nm in ("w128r", "w128i", "w128n"):
        P.add(nm, 128)
    for cfg, (n1, n1in) in FFT_CFG.items():
        for nm in ("w1r", "w1i", "w1n"):
            P.add(cfg + nm, n1)
        P.add(cfg + "tr", 128); P.add(cfg + "ti", 128); P.add(cfg + "tn", 128)
        if cfg in ("HB", "HD"):
            for nm in ("v1r", "v1i", "v1n"):
                P.add(cfg + nm, n1in)
    return P


FFTC = make_fft_layout()


def fft_constants():
    c = np.zeros((128, FFTC.n), np.float64)

    def put(name, val):
        o, w = FFTC.off[name]
        c[:val.shape[0], o:o + w] = val

    a = np.arange(128)
    w128 = np.exp(-2j * np.pi * np.outer(a, a) / 128)
    put("w128r", w128.real); put("w128i", w128.imag); put("w128n", -w128.imag)
    for cfg, (n1, n1in) in FFT_CFG.items():
        N = n1 * 128
        b = np.arange(n1)
        w1 = np.exp(-2j * np.pi * np.outer(b, b) / n1)
        put(cfg + "w1r", w1.real); put(cfg + "w1i", w1.imag); put(cfg + "w1n", -w1.imag)
        tw = np.exp(-2j * np.pi * np.outer(b, a) / N)
        put(cfg + "tr", tw.real); put(cfg + "ti", tw.imag); put(cfg + "tn", -tw.imag)
        if cfg in ("HB", "HD"):
            v1 = np.exp(+2j * np.pi * np.outer(b, np.arange(n1in)) / n1) / N
            put(cfg + "v1r", v1.real); put(cfg + "v1i", v1.imag); put(cfg + "v1n", -v1.imag)
    return c.astype(np.float32)


CG = 32
NCOL = 128 * CG


def _fc(self, name):
    o, w = FFTC.off[name]
    return self.fftc[:, o:o + w]


def _mm_blocks(self, outs, terms, M, K, ncols):
    fw = self.fw
    nb = 0
    for c0 in range(0, ncols, 512):
        cw = min(512, ncols - c0)
        for oi, (ot, tl) in enumerate(zip(outs, terms)):
            p = self.fps[self.nfps % len(self.fps)]; self.nfps += 1
            for ti, (lh, rt) in enumerate(tl):
                fw.op("pe", lambda e, p=p, lh=lh, rt=rt, c0=c0, cw=cw, ti=ti, n=len(tl): e.matmul(
                    p[0:M, 0:cw], lh, rt[0:K, c0:c0 + cw], start=(ti == 0), stop=(ti == n - 1)),
                    R=[rt.b, self.fftc.b], W=[p.b])
            if nb % 2 == 0:
                fw.op("act", lambda e, p=p, ot=ot, c0=c0, cw=cw: e.activation(out=ot[0:M, c0:c0 + cw], in_=p[0:M, 0:cw], func=AF.Copy),
                      R=[p.b], W=[ot.b])
            else:
                fw.op("dve", lambda e, p=p, ot=ot, c0=c0, cw=cw: e.tensor_copy(out=ot[0:M, c0:c0 + cw], in_=p[0:M, 0:cw]),
                      R=[p.b], W=[ot.b])
            nb += 1


def _cmul(self, outr, outi, ar, ai, br, bi, P, ncols, t1, t2, bshape=None):
    fw = self.fw

    def A(t):
        return t[0:P, 0:ncols]

    def Bv(t):
        return t if bshape else t[0:P, 0:ncols]

    def V(t):
        return t[0:P, 0:ncols].rearrange("p (a c) -> p a c", c=bshape) if bshape else t[0:P, 0:ncols]
    rb = [] if bshape else None
    fw.op("dve", lambda e: e.tensor_tensor(out=V(t1), in0=V(ar), in1=Bv(br), op=ALU.mult), R=[ar.b, self.fftc.b] + ([br.b] if not bshape else []), W=[t1.b])
    fw.op("dve", lambda e: e.tensor_tensor(out=V(t2), in0=V(ai), in1=Bv(bi), op=ALU.mult), R=[ai.b, self.fftc.b] + ([bi.b] if not bshape else []), W=[t2.b])
    fw.op("dve", lambda e: e.tensor_tensor(out=A(t1), in0=A(t1), in1=A(t2), op=ALU.subtract), R=[t1.b, t2.b], W=[t1.b])
    fw.op("dve", lambda e: e.tensor_tensor(out=V(t2), in0=V(ar), in1=Bv(bi), op=ALU.mult), R=[ar.b, self.fftc.b] + ([bi.b] if not bshape else []), W=[t2.b])
    fw.op("dve", lambda e: e.tensor_tensor(out=V(outi), in0=V(ai), in1=Bv(br), op=ALU.mult), R=[ai.b, self.fftc.b] + ([br.b] if not bshape else []), W=[outi.b])
    fw.op("dve", lambda e: e.tensor_tensor(out=A(outi), in0=A(outi), in1=A(t2), op=ALU.add), R=[outi.b, t2.b], W=[outi.b])
    fw.op("dve", lambda e: e.tensor_copy(out=A(outr), in_=A(t1)), R=[t1.b], W=[outr.b])


def _dtrans(self, src, P1, dst):
    fw = self.fw
    sc = self.tscr[self.ntscr % len(self.tscr)]; self.ntscr += 1
    fw.dma("sp", sc.t.ap()[0:P1, :], src[0:P1, 0:NCOL], sc.b, src.b)
    v = sc.t.ap()[0:P1, :].rearrange("a (j c) -> j a c", c=CG)
    fw.dma("sp", dst[:, 0:P1 * CG].rearrange("p (a c) -> p a c", c=CG), v, dst.b, sc.b)


def _fft_fwd(self, cfg, x, Xr, Xi, W):
    n1, n1in = FFT_CFG[cfg]
    a_r, a_i, t1, t2 = W[0], W[1], W[2], W[3]
    fc = self._fc
    pre = "HB" if cfg == "HB" else cfg
    w1r = fc(cfg + "w1r")[0:n1in, 0:n1]; w1i = fc(cfg + "w1i")[0:n1in, 0:n1]
    self._mm_blocks([a_r, a_i], [[(w1r, x)], [(w1i, x)]], n1, n1in, NCOL)
    tr = fc(cfg + "tr")[0:n1, :].unsqueeze(2).broadcast_to([n1, 128, CG])
    ti = fc(cfg + "ti")[0:n1, :].unsqueeze(2).broadcast_to([n1, 128, CG])
    self._cmul(a_r, a_i, a_r, a_i, tr, ti, n1, NCOL, t1, t2, bshape=CG)
    self._dtrans(a_r, n1, t1)
    self._dtrans(a_i, n1, t2)
    wr = fc("w128r"); wi = fc("w128i"); wn = fc("w128n")
    self._mm_blocks([Xr, Xi], [[(wr, t1), (wn, t2)], [(wi, t1), (wr, t2)]], 128, 128, n1 * CG)


def _fft_inv(self, cfg, Yr, Yi, y, W):
    n1, n1in = FFT_CFG[cfg]
    e_r, e_i, t1, t2 = W[0], W[1], W[2], W[3]
    fc = self._fc
    wr = fc("w128r"); wi = fc("w128i"); wn = fc("w128n")
    self._mm_blocks([e_r, e_i], [[(wr, Yr), (wi, Yi)], [(wn, Yr), (wr, Yi)]], 128, 128, n1 * CG)
    self._dtrans_back(e_r, n1, t1)
    self._dtrans_back(e_i, n1, t2)
    tr = fc(cfg + "tr")[0:n1, :].unsqueeze(2).broadcast_to([n1, 128, CG])
    tn = fc(cfg + "tn")[0:n1, :].unsqueeze(2).broadcast_to([n1, 128, CG])
    self._cmul(t1, t2, t1, t2, tr, tn, n1, NCOL, e_r, e_i, bshape=CG)
    v1r = fc(cfg + "v1r")[0:n1, 0:n1in]; v1n = fc(cfg + "v1n")[0:n1, 0:n1in]
    self._mm_blocks([y], [[(v1r, t1), (v1n, t2)]], n1in, n1, NCOL)


def _dtrans_back(self, src, P1, dst):
    fw = self.fw
    sc = self.tscr[self.ntscr % len(self.tscr)]; self.ntscr += 1
    v = sc.t.ap()[0:P1, :].rearrange("a (j c) -> j a c", c=CG)
    fw.dma("sp", v, src[:, 0:P1 * CG].rearrange("p (a c) -> p a c", c=CG), sc.b, src.b)
    fw.dma("sp", dst[0:P1, 0:NCOL], sc.t.ap()[0:P1, :], dst.b, sc.b)


for _f in (_fc, _mm_blocks, _cmul, _dtrans, _dtrans_back, _fft_fwd, _fft_inv):
    setattr(Prog, _f.__name__, _f)

MAGIC = 12582912.0
TWO_PI = 2.0 * math.pi


def _rr_sin(self, out, arg, tmp, P, n):
    fw = self.fw
    fw.op("dve", lambda e: e.tensor_scalar(out=tmp[0:P, 0:n], in0=arg[0:P, 0:n], scalar1=1.0 / TWO_PI, scalar2=MAGIC, op0=ALU.mult, op1=ALU.add),
          R=[arg.b], W=[tmp.b])
    fw.op("dve", lambda e: e.tensor_scalar(out=tmp[0:P, 0:n], in0=tmp[0:P, 0:n], scalar1=-MAGIC, scalar2=None, op0=ALU.add), R=[tmp.b], W=[tmp.b])
    fw.op("dve", lambda e: e.scalar_tensor_tensor(out=tmp[0:P, 0:n], in0=tmp[0:P, 0:n], scalar=-TWO_PI, in1=arg[0:P, 0:n], op0=ALU.mult, op1=ALU.add),
          R=[tmp.b, arg.b], W=[tmp.b])
    fw.op("dve", lambda e: e.tensor_scalar(out=tmp[0:P, 0:n], in0=tmp[0:P, 0:n], scalar1=3.14159, scalar2=-3.14159, op0=ALU.min, op1=ALU.max),
          R=[tmp.b], W=[tmp.b])
    fw.op("act", lambda e: e.activation(out=out[0:P, 0:n], in_=tmp[0:P, 0:n], func=AF.Sin), R=[tmp.b], W=[out.b])


def _gen_filter(self, l, ps, Lseq, ecol0, ntl, HF, sml, rn):
    fw = self.fw
    hw1, hb1, hf1, hw2, hb2, hf2, hw3, hb3, dbc = sml
    et = fw.sb(ps, "f_e", [33, 512]); arg = fw.sb(ps, "f_arg", [64, 512]); tmp = fw.sb(ps, "f_tmp", [64, 512])
    z1 = fw.sb(ps, "f_z1", [64, 512]); z2 = fw.sb(ps, "f_z2", [64, 512])
    hh = [fw.sb(ps, "f_h%d" % i, [128, 512]) for i in range(2)]
    ab = fw.sb(ps, "f_ab", [128, 512]); dec = fw.sb(ps, "f_dec", [128, 128])
    nrm = fw.sb(ps, "f_nrm", [128, 512])
    p1 = self.fps[0]; p2 = self.fps[1]; p3 = self.fps[2]; pN = self.fps[3]
    ones = self.cv("ones")
    nsub_tot = Lseq // 128
    si = 0
    for t0 in range(0, Lseq, 512):
        n = min(512, Lseq - t0)
        fw.dma("sp", et[:, 0:n], self.embc.t.ap()[:, ecol0 + t0:ecol0 + t0 + n], et.b, self.embc.b)
        fw.op("pe", lambda e, n=n: e.matmul(p1[0:64, 0:n], hw1[0:33, :], et[:, 0:n], start=True, stop=True), R=[et.b, hw1.b], W=[p1.b])
        fw.op("dve", lambda e, n=n: e.tensor_scalar(out=arg[:, 0:n], in0=p1[0:64, 0:n], scalar1=hb1[0:64, 0:1], scalar2=hf1[0:64, 0:1], op0=ALU.add, op1=ALU.mult),
              R=[p1.b, hb1.b, hf1.b], W=[arg.b])
        self._rr_sin(z1, arg, tmp, 64, n)
        fw.op("pe", lambda e, n=n: e.matmul(p2[0:64, 0:n], hw2[0:64, :], z1[:, 0:n], start=True, stop=True), R=[z1.b, hw2.b], W=[p2.b])
        fw.op("dve", lambda e, n=n: e.tensor_scalar(out=arg[:, 0:n], in0=p2[0:64, 0:n], scalar1=hb2[0:64, 0:1], scalar2=hf2[0:64, 0:1], op0=ALU.add, op1=ALU.mult),
              R=[p2.b, hb2.b, hf2.b], W=[arg.b])
        self._rr_sin(z2, arg, tmp, 64, n)
        for i in range(n // 128):
            h = hh[si % 2]
            fw.op("pe", lambda e, i=i: e.matmul(p3[:, :], z2[:, i * 128:(i + 1) * 128], hw3[0:64, :], start=True, stop=True), R=[z2.b, hw3.b], W=[p3.b])
            fw.op("dve", lambda e, h=h: e.tensor_tensor(out=h[:], in0=p3[:], in1=hb3[:], op=ALU.add), R=[p3.b, hb3.b], W=[h.b])
            fw.op("act", lambda e, si=si: e.activation(out=dec[:], in_=dbc[:], func=AF.Exp, scale=ntl[:, si:si + 1]), R=[dbc.b, self.cstb], W=[dec.b])
            fw.op("dve", lambda e, h=h: e.tensor_tensor(out=h[:].rearrange("p (g c) -> p g c", c=128), in0=h[:].rearrange("p (g c) -> p g c", c=128),
                                                        in1=dec[:].unsqueeze(1).broadcast_to([128, 4, 128]), op=ALU.mult), R=[h.b, dec.b], W=[h.b])
            if si == 0:
                for c0 in (128, 384):
                    fw.op("dve", lambda e, h=h, c0=c0: e.memset(h[0:1, c0:c0 + 128], 0.0), W=[h.b])
            fw.op("act", lambda e, h=h: e.activation(out=ab[:], in_=h[:], func=AF.Abs), R=[h.b], W=[ab.b])
            fw.op("pe", lambda e, si=si: e.matmul(pN[:], ones, ab[:], start=(si == 0), stop=(si == nsub_tot - 1)), R=[ab.b, self.cstb], W=[pN.b])
            fw.dma("sp", HF.t.ap()[t0 + i * 128:t0 + (i + 1) * 128, :], h[:], HF.b, h.b, part=True)
            si += 1
    fw.op("dve", lambda e: e.tensor_copy(out=nrm[:], in_=pN[:]), R=[pN.b], W=[nrm.b])
    nv = nrm[:].rearrange("p (o d c) -> p o d c", o=2, d=2)
    fw.op("dve", lambda e: e.tensor_tensor(out=rn[:], in0=nv[:, :, 0, :], in1=nv[:, :, 1, :], op=ALU.add), R=[nrm.b], W=[rn.b])
    fw.op("dve", lambda e: e.reciprocal(out=rn[:], in_=rn[:]), R=[rn.b], W=[rn.b])
    return rn


def _seq_cfg(self, which):
    if which == "lat":
        return "FA", "HB", L, 1, 0, 0, "ntl8192"
    return "FC", "HD", NCTX, L + 3, L, L, "ntl256"


def phase_seqmix(self, l, which):
    fw = self.fw
    fcfg, hcfg, Lseq, prow, tbase, ecol0, ntlname = self._seq_cfg(which)
    n1f, n1fin = FFT_CFG[fcfg]
    n1h, n1hin = FFT_CFG[hcfg]
    with contextlib.ExitStack() as ps:
        self.fftc = fw.sb(ps, "fftc_t", [128, FFTC.n])
        fw.dma("sp", self.fftc[:], self.fftc_d.t.ap(), self.fftc.b, self.fftc_d.b)
        self.fps = [fw.ps(ps, "fps%d" % i) for i in range(6)]
        self.nfps = 0
        self.tscr = [fw.dram("tscr%d_%d_%s" % (i, l, which), [128, NCOL], F32) for i in range(2)]
        self.ntscr = 0
        W = [fw.sb(ps, "fw%d" % i, [128, NCOL]) for i in range(4)]
        Xr = fw.sb(ps, "fXr", [128, NCOL]); Xi = fw.sb(ps, "fXi", [128, NCOL])
        zv = fw.sb(ps, "fzv", [64, NCOL]); zx = fw.sb(ps, "fzx", [64, NCOL])
        halo = fw.sb(ps, "fhalo", [64, 130, CG])
        for cg in range(4):
            src = self.PFH.t.ap()[prow:prow + Lseq, cg * CG:(cg + 1) * CG].rearrange("(a j) c -> a j c", j=128)
            fw.dma("sp", zv[0:n1fin, :].rearrange("p (j c) -> p j c", c=CG), src, zv.b, self.PFH.b)
            self._fft_fwd(fcfg, zv, Xr, Xi, W)
            for ri, X_ in ((0, Xr), (1, Xi)):
                dst = self.FM.t.ap()[tbase:tbase + Lseq, ri, cg * CG:(cg + 1) * CG].rearrange("(p a) c -> p a c", a=n1f)
                fw.dma("sp", dst, X_[:, 0:n1f * CG].rearrange("p (a c) -> p a c", c=CG), self.FM.b, X_.b, part=True)
        rn = fw.sb(ps, "f_rn", [128, 2, 128])
        with contextlib.ExitStack() as ps2:
            sml = [self.ld(ps2, "hw1%d" % l, 33), self.ld(ps2, "hb1%d" % l, 64), self.ld(ps2, "hf1%d" % l, 64),
                   self.ld(ps2, "hw2%d" % l, 64), self.ld(ps2, "hb2%d" % l, 64), self.ld(ps2, "hf2%d" % l, 64),
                   self.ld(ps2, "hw3%d" % l, 64), self.ld(ps2, "hb3bc%d" % l), self.ld(ps2, "deltabc")]
            o_, w_ = CST.off[ntlname]
            ntl = self.cst[:, o_:o_ + w_]
            HF = fw.dram("HF_%d_%s" % (l, which), [Lseq, 512], F32)
            self._gen_filter(l, ps2, Lseq, ecol0, ntl, HF, sml, rn)
            fw.fence()
        hbias = self.ld(ps, "hbias%d" % l)
        HS = fw.dram("HS_%d_%s" % (l, which), [2, 4, 2, 128, n1h * CG], F32)
        for o in range(2):
            for cg in range(4):
                nc_ = n1h * CG
                for d in range(2):
                    src = HF.t.ap()[:, o * 256 + d * 128 + cg * CG: o * 256 + d * 128 + (cg + 1) * CG].rearrange("(a j) c -> a j c", j=128)
                    fw.dma("sp", zv[0:n1hin, :].rearrange("p (j c) -> p j c", c=CG), src, zv.b, HF.b)
                    self._fft_fwd(hcfg, zv, Xr, Xi, W)
                    if d == 0:
                        fw.dma("sp", HS.t.ap()[o, cg, 0], Xr[:, 0:nc_], HS.b, Xr.b)
                        fw.dma("sp", HS.t.ap()[o, cg, 1], Xi[:, 0:nc_], HS.b, Xi.b)
                fw.dma("sp", W[0][:, 0:nc_], HS.t.ap()[o, cg, 0], W[0].b, HS.b)
                fw.dma("sp", W[1][:, 0:nc_], HS.t.ap()[o, cg, 1], W[1].b, HS.b)
                rnb = rn[:, o, cg * CG:(cg + 1) * CG].unsqueeze(1).broadcast_to([128, n1h, CG])
                bb = hbias[:, o * 128 + cg * CG: o * 128 + (cg + 1) * CG].unsqueeze(1).broadcast_to([128, n1h, CG])

                def v3(t, nc_=nc_):
                    return t[:, 0:nc_].rearrange("p (a c) -> p a c", c=CG)
                fw.op("dve", lambda e, v3=v3: e.tensor_tensor(out=v3(W[0]), in0=v3(W[0]), in1=v3(Xr), op=ALU.add), R=[Xr.b, W[0].b], W=[W[0].b])
                fw.op("dve", lambda e, v3=v3: e.tensor_tensor(out=v3(W[1]), in0=v3(W[1]), in1=v3(Xi), op=ALU.subtract), R=[Xi.b, W[1].b], W=[W[1].b])
                fw.op("dve", lambda e, rnb=rnb, v3=v3: e.tensor_tensor(out=v3(W[0]), in0=v3(W[0]), in1=rnb, op=ALU.mult), R=[W[0].b, rn.b], W=[W[0].b])
                fw.op("dve", lambda e, rnb=rnb, v3=v3: e.tensor_tensor(out=v3(W[1]), in0=v3(W[1]), in1=rnb, op=ALU.mult), R=[W[1].b, rn.b], W=[W[1].b])
                fw.op("dve", lambda e, bb=bb, v3=v3: e.tensor_tensor(out=v3(W[0]), in0=v3(W[0]), in1=bb, op=ALU.add), R=[W[0].b, hbias.b], W=[W[0].b])
                fw.dma("sp", HS.t.ap()[o, cg, 0], W[0][:, 0:nc_], HS.b, W[0].b)
                fw.dma("sp", HS.t.ap()[o, cg, 1], W[1][:, 0:nc_], HS.b, W[1].b)
        cw = self.ld(ps, "cwbc%d" % l); cbv = self.ld(ps, "cbbc%d" % l)

        def shortconv(dst, wi, cg):
            col0 = 128 + wi * 128 + cg * CG
            base = self.PFH.t.ap()[prow - 1:prow - 1 + Lseq + 2, col0:col0 + CG]
            src = bass.AP(self.PFH.t, base.offset, [[128 * 512, n1hin], [512, 130], [1, CG]])
            fw.dma("sp", halo[0:n1hin, :, :], src, halo.b, self.PFH.b)
            d3 = dst[0:n1hin, :].rearrange("p (j c) -> p j c", c=CG)
            for j in range(3):
                wv = cw[0:n1hin, j * 384 + wi * 128 + cg * CG: j * 384 + wi * 128 + (cg + 1) * CG].unsqueeze(1).broadcast_to([n1hin, 128, CG])
                if j == 0:
                    fw.op("dve", lambda e, wv=wv: e.tensor_tensor(out=d3, in0=halo[0:n1hin, 0:128, :], in1=wv, op=ALU.mult), R=[halo.b, cw.b], W=[dst.b])
                else:
                    fw.op("dve", lambda e, wv=wv, j=j: e.tensor_tensor(out=W[3][0:n1hin, :].rearrange("p (j c) -> p j c", c=CG), in0=halo[0:n1hin, j:j + 128, :], in1=wv, op=ALU.mult),
                          R=[halo.b, cw.b], W=[W[3].b])
                    fw.op("dve", lambda e: e.tensor_tensor(out=dst[0:n1hin, :], in0=dst[0:n1hin, :], in1=W[3][0:n1hin, :], op=ALU.add), R=[dst.b, W[3].b], W=[dst.b])
            bv = cbv[0:n1hin, wi * 128 + cg * CG: wi * 128 + (cg + 1) * CG].unsqueeze(1).broadcast_to([n1hin, 128, CG])
            fw.op("dve", lambda e, bv=bv: e.tensor_tensor(out=d3, in0=d3, in1=bv, op=ALU.add), R=[dst.b, cbv.b], W=[dst.b])

        for cg in range(4):
            shortconv(zv, 0, cg)
            shortconv(zx, 1, cg)
            for o in range(2):
                src_t = zv if o == 0 else zx
                self._fft_fwd(hcfg, src_t, Xr, Xi, W)
                nc_ = n1h * CG
                fw.dma("sp", W[0][:, 0:nc_], HS.t.ap()[o, cg, 0], W[0].b, HS.b)
                fw.dma("sp", W[1][:, 0:nc_], HS.t.ap()[o, cg, 1], W[1].b, HS.b)
                self._cmul(Xr, Xi, Xr, Xi, W[0], W[1], 128, nc_, W[2], W[3])
                self._fft_inv(hcfg, Xr, Xi, zv, W)
                if o == 0:
                    fw.op("dve", lambda e: e.tensor_tensor(out=zx[0:n1hin, :], in0=zx[0:n1hin, :], in1=zv[0:n1hin, :], op=ALU.mult), R=[zx.b, zv.b], W=[zx.b])
                else:
                    shortconv(W[0], 2, cg)
                    fw.op("dve", lambda e: e.tensor_tensor(out=zv[0:n1hin, :], in0=zv[0:n1hin, :], in1=W[0][0:n1hin, :], op=ALU.mult), R=[zv.b, W[0].b], W=[zv.b])
                    dst = self.HM.t.ap()[tbase:tbase + Lseq, cg * CG:(cg + 1) * CG].rearrange("(a j) c -> a j c", j=128)
                    fw.dma("sp", dst, zv[0:n1hin, :].rearrange("p (j c) -> p j c", c=CG), self.HM.b, zv.b, part=True)
        fw.fence()


def halo_flat(halo):
    return Tile(halo.t, halo.b)


for _f in (_rr_sin, _gen_filter, _seq_cfg, phase_seqmix):
    setattr(Prog, _f.__name__, _f)


def phase_attn(self, l):
    fw = self.fw
    lam_init = 0.8 - 0.6 * math.exp(-0.3 * l)
    with contextlib.ExitStack() as ps:
        KTs = [fw.sb(ps, "aKT%d" % h, [128, T], BF16) for h in range(2)]
        Vs = fw.sb(ps, "aV", [128, T // 128, 256], BF16)
        onesb = fw.sb(ps, "aones", [128, 128], BF16)
        Qs = [fw.sb(ps, "aQ%d" % i, [128, 512], BF16) for i in range(2)]
        pts = [fw.sb(ps, "apt%d" % i, [128, 512], BF16) for i in range(4)]
        r0 = fw.sb(ps, "ar0", [128, 512]); r1 = fw.sb(ps, "ar1", [128, 512])
        a0 = fw.sb(ps, "aa0", [128, 512]); a1 = fw.sb(ps, "aa1", [128, 512])
        sq = fw.sb(ps, "asq", [128, 512])
        ob = [fw.sb(ps, "aob%d" % i, [128, 512], BF16) for i in range(2)]
        pss = [fw.ps(ps, "aps%d" % i) for i in range(3)]
        po = [fw.ps(ps, "apo%d" % i) for i in range(2)]
        pz = [fw.ps(ps, "apz%d" % i) for i in range(2)]
        ones = self.cv("ones")
        fw.op("dve", lambda e: e.tensor_copy(out=onesb[:], in_=ones), R=[self.cstb], W=[onesb.b])
        for h in range(2):
            fw.dma("sp", KTs[h][:], self.KT.t.ap()[h * 128:(h + 1) * 128, :], KTs[h].b, self.KT.b)
        fw.dma("sp", Vs[:], self.V.t.ap().rearrange("(ch p) c -> p ch c", p=128), Vs.b, self.V.b)
        qtiles = [(q0, 512, list(range(T // 128))) for q0 in range(0, L, 512)]
        if l == 0:
            qtiles.append((L, NCTX, [L // 128, L // 128 + 1]))
        nq = 0; npt = 0; nps = 0
        for (q0, n_, kcs) in qtiles:
            for h in range(2):
                Q = Qs[nq % 2]; nq += 1
                fw.dma("sp", Q[:, 0:n_], self.QT.t.ap()[h * 128:(h + 1) * 128, q0:q0 + n_], Q.b, self.QT.b)
                LOOK = 2
                for m in range(2):
                    nk = len(kcs)
                    ptq = {}
                    for kk in range(nk + LOOK):
                        if kk < nk:
                            kc = kcs[kk]
                            s_ = pss[nps % 3]; nps += 1
                            fw.op("pe", lambda e, s_=s_, h=h, m=m, kc=kc, Q=Q, n_=n_: e.matmul(
                                s_[:, 0:n_], KTs[h][m * 64:(m + 1) * 64, kc * 128:(kc + 1) * 128], Q[m * 64:(m + 1) * 64, 0:n_], start=True, stop=True),
                                R=[KTs[h].b, Q.b], W=[s_.b])
                            pt = pts[npt % 4]; npt += 1
                            fw.op("act", lambda e, s_=s_, pt=pt, n_=n_: e.activation(out=pt[:, 0:n_], in_=s_[:, 0:n_], func=AF.Exp, scale=0.125),
                                  R=[s_.b], W=[pt.b])
                            ptq[kk] = pt
                        ki = kk - LOOK
                        if ki >= 0:
                            kc = kcs[ki]; pt = ptq.pop(ki)
                            fw.op("pe", lambda e, pt=pt, h=h, m=m, kc=kc, ki=ki, n_=n_, nk=nk: e.matmul(
                                po[m][:, 0:n_], Vs[:, kc, h * 128:(h + 1) * 128], pt[:, 0:n_], start=(ki == 0), stop=(ki == nk - 1)),
                                R=[Vs.b, pt.b], W=[po[m].b])
                            fw.op("pe", lambda e, pt=pt, m=m, ki=ki, n_=n_, nk=nk: e.matmul(
                                pz[m][:, 0:n_], onesb[:], pt[:, 0:n_], start=(ki == 0), stop=(ki == nk - 1)),
                                R=[onesb.b, pt.b], W=[pz[m].b])
                fw.op("dve", lambda e, n_=n_: e.reciprocal(out=r0[:, 0:n_], in_=pz[0][:, 0:n_]), R=[pz[0].b], W=[r0.b])
                fw.op("dve", lambda e, n_=n_: e.reciprocal(out=r1[:, 0:n_], in_=pz[1][:, 0:n_]), R=[pz[1].b], W=[r1.b])
                fw.op("dve", lambda e, n_=n_: e.tensor_tensor(out=a0[:, 0:n_], in0=po[0][:, 0:n_], in1=r0[:, 0:n_], op=ALU.mult), R=[po[0].b, r0.b], W=[a0.b])
                fw.op("dve", lambda e, n_=n_: e.tensor_tensor(out=a1[:, 0:n_], in0=po[1][:, 0:n_], in1=r1[:, 0:n_], op=ALU.mult), R=[po[1].b, r1.b], W=[a1.b])
                fw.op("dve", lambda e, n_=n_: e.scalar_tensor_tensor(out=a0[:, 0:n_], in0=a1[:, 0:n_], scalar=self.lam[:, 1:2], in1=a0[:, 0:n_], op0=ALU.mult, op1=ALU.add),
                      R=[a0.b, a1.b, self.lam.b], W=[a0.b])
                fw.op("act", lambda e, n_=n_: e.activation(out=sq[:, 0:n_], in_=a0[:, 0:n_], func=AF.Square), R=[a0.b], W=[sq.b])
                s_ = pss[nps % 3]; nps += 1
                fw.op("pe", lambda e, s_=s_, n_=n_: e.matmul(s_[:, 0:n_], ones, sq[:, 0:n_], start=True, stop=True), R=[sq.b, self.cstb], W=[s_.b])
                fw.op("act", lambda e, s_=s_, n_=n_: e.activation(out=r0[:, 0:n_], in_=s_[:, 0:n_], func=AF.Sqrt, scale=1.0 / 128, bias=self.epsb[:, 1:2]), R=[s_.b], W=[r0.b])
                fw.op("dve", lambda e, n_=n_: e.reciprocal(out=r0[:, 0:n_], in_=r0[:, 0:n_]), R=[r0.b], W=[r0.b])
                fw.op("dve", lambda e, n_=n_: e.scalar_tensor_tensor(out=a0[:, 0:n_], in0=a0[:, 0:n_], scalar=self.qks[:, 2:3], in1=r0[:, 0:n_], op0=ALU.mult, op1=ALU.mult),
                      R=[a0.b, r0.b, self.qks.b], W=[a0.b])
                o_ = ob[nq % 2]
                fw.op("dve", lambda e, n_=n_, o_=o_: e.tensor_scalar(out=o_[:, 0:n_], in0=a0[:, 0:n_], scalar1=(1.0 - lam_init), scalar2=None, op0=ALU.mult), R=[a0.b], W=[o_.b])
                fw.dma("sp", self.OT.t.ap()[h * 128:(h + 1) * 128, q0:q0 + n_], o_[:, 0:n_], self.OT.b, o_.b, part=True)
        fw.fence()


def allreduce_to_X(self, nrows):
    items = []
    for r0 in range(0, nrows, 1024):
        r1 = min(nrows, r0 + 1024)
        items.append(("AllReduce", PAIRS, self.ARin.t.ap()[r0:r1, :], self.X.t.ap()[r0:r1, :]))
    self.coll_seq(items)
    self.fw.fence()


def phase_outproj(self, l):
    fw = self.fw
    ntok = T if l == 0 else L
    with contextlib.ExitStack() as ps:
        Wfc = [fw.sb(ps, "oWc%d" % i, [128, 1024], BF16) for i in range(2)]
        Wfs = [fw.sb(ps, "oWs%d" % i, [128, 1024], BF16) for i in range(2)]
        Wh = fw.sb(ps, "oWh", [128, 1024], BF16)
        Wa = fw.sb(ps, "oWa", [128, 2, 1024], BF16)
        Wo = fw.sb(ps, "oWo", [128, 8, 1024], BF16)
        stg = fw.sb(ps, "ostg", [128, 8, 1024])
        wf32 = fw.sb(ps, "owf", [128, 1024])
        pp = [fw.ps(ps, "opp%d" % i) for i in range(7)]
        npp = 0
        c64 = self.cv("c64"); s64 = self.cv("s64"); ident = self.cv("ident")
        wov = self.wout.t.ap()[l]
        fw.dma("sp", wf32[:], wov[0:128, :], wf32.b, self.wout.b)
        for (cm, dsts) in ((c64, Wfc), (s64, Wfs)):
            for half in range(2):
                p = pp[npp % 7]; npp += 1
                fw.op("pe", lambda e, p=p, cm=cm, half=half: e.matmul(p[:], cm, wf32[:, half * 512:(half + 1) * 512], start=True, stop=True), R=[wf32.b, self.cstb], W=[p.b])
                for i, Ls in enumerate((L, NCTX)):
                    fw.op("act", lambda e, p=p, d=dsts[i], half=half, Ls=Ls: e.activation(out=d[:, half * 512:(half + 1) * 512], in_=p[:], func=AF.Copy, scale=1.0 / math.sqrt(64.0 * Ls)),
                          R=[p.b], W=[dsts[i].b])
        fw.dma("sp", stg[:, 0, :], wov[128:256, :], stg.b, self.wout.b)
        fw.op("dve", lambda e: e.tensor_copy(out=Wh[:], in_=stg[:, 0, :]), R=[stg.b], W=[Wh.b])
        fw.dma("sp", stg[:, 0:2, :], wov[256:512, :].rearrange("(k p) c -> p k c", p=128), stg.b, self.wout.b)
        fw.op("dve", lambda e: e.tensor_copy(out=Wa[:], in_=stg[:, 0:2, :]), R=[stg.b], W=[Wa.b])
        oo, _ = SHOFF["wo%d" % l]
        fw.dma("sp", stg[:], dview(self.SH, oo, (1024, 1024)).rearrange("(k p) c -> p k c", p=128), stg.b, self.SH.b)
        fw.op("act", lambda e: e.activation(out=Wo[:], in_=stg[:], func=AF.Copy), R=[stg.b], W=[Wo.b])
        fm = fw.sb(ps, "ofm", [128, 4, 2, 128]); hm = fw.sb(ps, "ohm", [128, 4, 128])
        FrT = fw.sb(ps, "oFr", [128, 512], BF16); FiT = fw.sb(ps, "oFi", [128, 512], BF16); HT = fw.sb(ps, "oHT", [128, 512], BF16)
        OTt = fw.sb(ps, "oOT", [128, 2, 512], BF16)
        G = fw.sb(ps, "oG", [128, 24, 512], BF16)
        xt = fw.sb(ps, "oxt", [128, 4, 1024])
        mb = fw.sb(ps, "omb", [128, 1024])
        m1 = fw.sb(ps, "om1", [128, 512]); m2 = fw.sb(ps, "om2", [128, 512])
        M = fw.sb(ps, "oM", [128, 8, 512], BF16)
        tz = fw.sb(ps, "otz", [128, 512]); ar = [fw.sb(ps, "oar%d" % i, [128, 512]) for i in range(2)]
        lastj = -1
        nar = 0
        for t0 in range(0, ntok, 512):
            n_ = min(512, ntok - t0); nsub = n_ // 128
            j = 0 if t0 < L else 1
            if j != lastj:
                fw.dma("sp", mb[:], self.MODBC.t.ap()[:, j * 2048:j * 2048 + 1024], mb.b, self.MODBC.b)
                lastj = j
            fw.dma("sp", fm[:, 0:nsub], self.FM.t.ap()[t0:t0 + n_].rearrange("(s p) r c -> p s r c", p=128), fm.b, self.FM.b)
            fw.dma("sp", hm[:, 0:nsub], self.HM.t.ap()[t0:t0 + n_].rearrange("(s p) c -> p s c", p=128), hm.b, self.HM.b)
            fw.dma("sp", OTt[:, :, 0:n_], self.OT.t.ap()[:, t0:t0 + n_].rearrange("(k p) t -> p k t", p=128), OTt.b, self.OT.b)
            fw.dma("sp", G[:, :, 0:n_], self.GT.t.ap()[:, t0:t0 + n_].rearrange("(k p) t -> p k t", p=128), G.b, self.GT.b)
            fw.dma("sp", xt[:, 0:nsub], self.X.t.ap()[t0:t0 + n_].rearrange("(s p) c -> p s c", p=128), xt.b, self.X.b)
            for (srcf, dstT) in ((lambda s_: fm[:, s_, 0, :], FrT), (lambda s_: fm[:, s_, 1, :], FiT), (lambda s_: hm[:, s_, :], HT)):
                p = pp[npp % 7]; npp += 1
                srcb = fm.b if dstT is not HT else hm.b
                for s_ in range(nsub):
                    fw.op("pe", lambda e, p=p, s_=s_, srcf=srcf: e.transpose(out=p[:, s_ * 128:(s_ + 1) * 128], in_=srcf(s_), identity=ident), R=[srcb, self.cstb], W=[p.b])
                fw.op("act", lambda e, p=p, dstT=dstT, n_=n_: e.activation(out=dstT[:, 0:n_], in_=p[:, 0:n_], func=AF.Copy), R=[p.b], W=[dstT.b])
            for fc in range(8):
                fsl = slice(fc * 128, (fc + 1) * 128)
                pf = pp[npp % 7]; npp += 1
                fw.op("pe", lambda e, pf=pf, fsl=fsl, n_=n_, j=j: e.matmul(pf[:, 0:n_], Wfc[j][:, fsl], FrT[:, 0:n_], start=True, stop=False), R=[Wfc[j].b, FrT.b], W=[pf.b])
                fw.op("pe", lambda e, pf=pf, fsl=fsl, n_=n_, j=j: e.matmul(pf[:, 0:n_], Wfs[j][:, fsl], FiT[:, 0:n_], start=False, stop=True), R=[Wfs[j].b, FiT.b], W=[pf.b])
                ph = pp[npp % 7]; npp += 1
                fw.op("pe", lambda e, ph=ph, fsl=fsl, n_=n_: e.matmul(ph[:, 0:n_], Wh[:, fsl], HT[:, 0:n_], start=True, stop=True), R=[Wh.b, HT.b], W=[ph.b])
                pa = pp[npp % 7]; npp += 1
                for k in range(2):
                    fw.op("pe", lambda e, pa=pa, fsl=fsl, n_=n_, k=k: e.matmul(pa[:, 0:n_], Wa[:, k, fsl], OTt[:, k, 0:n_], start=(k == 0), stop=(k == 1)), R=[Wa.b, OTt.b], W=[pa.b])
                fw.op("dve", lambda e, pf=pf, fc=fc, n_=n_: e.tensor_tensor(out=m1[:, 0:n_], in0=pf[:, 0:n_], in1=G[:, fc, 0:n_], op=ALU.mult), R=[pf.b, G.b], W=[m1.b])
                fw.op("dve", lambda e, ph=ph, fc=fc, n_=n_: e.tensor_tensor(out=m2[:, 0:n_], in0=ph[:, 0:n_], in1=G[:, 8 + fc, 0:n_], op=ALU.mult), R=[ph.b, G.b], W=[m2.b])
                fw.op("dve", lambda e, n_=n_: e.tensor_tensor(out=m1[:, 0:n_], in0=m1[:, 0:n_], in1=m2[:, 0:n_], op=ALU.add), R=[m1.b, m2.b], W=[m1.b])
                fw.op("dve", lambda e, pa=pa, fc=fc, n_=n_: e.tensor_tensor(out=m2[:, 0:n_], in0=pa[:, 0:n_], in1=G[:, 16 + fc, 0:n_], op=ALU.mult), R=[pa.b, G.b], W=[m2.b])
                fw.op("dve", lambda e, fc=fc, n_=n_: e.tensor_tensor(out=M[:, fc, 0:n_], in0=m1[:, 0:n_], in1=m2[:, 0:n_], op=ALU.add), R=[m1.b, m2.b], W=[M.b])
            for s_ in range(nsub):
                for half in range(2):
                    p = pp[npp % 7]; npp += 1
                    for k in range(8):
                        fw.op("pe", lambda e, p=p, k=k, s_=s_, half=half: e.matmul(p[:], M[:, k, s_ * 128:(s_ + 1) * 128], Wo[:, k, half * 512:(half + 1) * 512], start=(k == 0), stop=(k == 7)),
                              R=[M.b, Wo.b], W=[p.b])
                    fw.op("dve", lambda e, p=p, half=half: e.tensor_tensor(out=tz[:], in0=p[:], in1=mb[:, half * 512:(half + 1) * 512], op=ALU.mult), R=[p.b, mb.b], W=[tz.b])
                    a_ = ar[nar % 2]; nar += 1
                    fw.op("dve", lambda e, a_=a_, s_=s_, half=half: e.scalar_tensor_tensor(out=a_[:], in0=xt[:, s_, half * 512:(half + 1) * 512], scalar=0.5, in1=tz[:], op0=ALU.mult, op1=ALU.add),
                          R=[xt.b, tz.b], W=[a_.b])
                    fw.dma("sp", self.ARin.t.ap()[t0 + s_ * 128:t0 + (s_ + 1) * 128, half * 512:(half + 1) * 512], a_[:], self.ARin.b, a_.b, part=True)
        fw.fence()
    self.allreduce_to_X(ntok)


for _f in (phase_attn, allreduce_to_X, phase_outproj):
    setattr(Prog, _f.__name__, _f)


def phase_moe(self, l):
    fw = self.fw
    ntok = T if l == 0 else L
    WEB1 = fw.dram("WEB1_%d" % l, [16, 1024, 2048], BF16)
    WEB2 = fw.dram("WEB2_%d" % l, [16, 1024, 1024], BF16)
    fw.fence()
    for e_ in fw.ENG:
        fw.h[e_].wait_ge(self.expsem, 48 * (l + 1))
    with contextlib.ExitStack() as ps:
        st_ = [fw.sb(ps, "mst%d" % i, [128, 8, 512]) for i in range(2)]
        sb_ = [fw.sb(ps, "msb%d" % i, [128, 8, 512], BF16) for i in range(2)]
        n = 0
        for le in range(16):
            WE = self.WEl[l]
            w1v = WE.t.ap()[le * 1536:le * 1536 + 1024, :].rearrange("(k p) c -> p k c", p=128)
            w2v = bass.AP(WE.t, WE.t.ap()[le * 1536 + 1024:le * 1536 + 1536, :].offset, [[1024, 1024], [1, 1024]]).rearrange("(k p) c -> p k c", p=128)
            for (src, dst, ncb) in ((w1v, WEB1, 4), (w2v, WEB2, 2)):
                for cb in range(ncb):
                    a = st_[n % 2]; b_ = sb_[n % 2]
                    fw.dma("sp", a[:], src[:, :, cb * 512:(cb + 1) * 512], a.b, WE.b)
                    eng = ("act", "dve")[n % 2]
                    if eng == "act":
                        fw.op("act", lambda e, a=a, b_=b_: e.activation(out=b_[:], in_=a[:], func=AF.Copy), R=[a.b], W=[b_.b])
                    else:
                        fw.op(eng, lambda e, a=a, b_=b_: e.tensor_copy(out=b_[:], in_=a[:]), R=[a.b], W=[b_.b])
                    fw.dma("sp", dst.t.ap()[le].rearrange("(k p) c -> p k c", p=128)[:, :, cb * 512:(cb + 1) * 512], b_[:], dst.b, b_.b, part=True)
                    n += 1
        fw.fence()
    with contextlib.ExitStack() as ps:
        W1 = [fw.sb(ps, "mW1_%d" % i, [128, 8, 2048], BF16) for i in range(2)]
        W2 = [fw.sb(ps, "mW2_%d" % i, [128, 8, 1024], BF16) for i in range(2)]
        wrt = self.ld(ps, "wrt%d" % l); brt = self.ld(ps, "brt%d" % l)
        b1T = self.ld(ps, "b1T%d" % l); b2r = self.ld(ps, "b2r%d" % l, 16)
        xt = fw.sb(ps, "mxt", [128, 4, 1024]); xn = fw.sb(ps, "mxn", [128, 1024]); junk = fw.sb(ps, "mjunk", [128, 1024])
        ss = fw.sb(ps, "mss", [128, 2])
        h2T = fw.sb(ps, "mh2T", [128, 8, 512], BF16)
        h2f = fw.sb(ps, "mh2f", [128, 8, 128])
        lg = fw.sb(ps, "mlg", [128, 32]); m8 = fw.sb(ps, "mm8", [128, 8]); msk = fw.sb(ps, "mmsk", [128, 32])
        nb = fw.sb(ps, "mnb", [128, 2])
        Gt = fw.sb(ps, "mG", [128, 4, 32])
        gT = fw.sb(ps, "mgT", [16, 128])
        Y = fw.sb(ps, "mY", [128, 4, 1024])
        A = fw.sb(ps, "mA", [128, 8, 512], BF16)
        g2 = [fw.sb(ps, "mg%d" % i, [128, 512]) for i in range(2)]; sg2 = [fw.sb(ps, "msg%d" % i, [128, 512]) for i in range(2)]; ln2 = [fw.sb(ps, "mln%d" % i, [128, 512]) for i in range(2)]
        mb = fw.sb(ps, "mmb", [128, 1024])
        tz = fw.sb(ps, "mtz", [128, 1024])
        pp = [fw.ps(ps, "mpp%d" % i) for i in range(8)]
        npp = 0
        ident = self.cv("ident")
        lastj = -1
        nw = 0
        for t0 in range(0, ntok, 512):
            n_ = min(512, ntok - t0); nsub = n_ // 128
            j = 0 if t0 < L else 1
            if j != lastj:
                fw.dma("sp", mb[:], self.MODBC.t.ap()[:, j * 2048 + 1024:j * 2048 + 2048], mb.b, self.MODBC.b)
                lastj = j
            fw.dma("sp", xt[:, 0:nsub], self.X.t.ap()[t0:t0 + n_].rearrange("(s p) c -> p s c", p=128), xt.b, self.X.b)
            for s_ in range(nsub):
                fw.op("act", lambda e, s_=s_: e.activation(out=junk[:], in_=xt[:, s_, :], func=AF.Square, accum_out=ss[:, 0:1]), R=[xt.b], W=[junk.b, ss.b])
                fw.op("act", lambda e: e.activation(out=ss[:, 1:2], in_=ss[:, 0:1], func=AF.Sqrt, scale=1.0 / D, bias=self.epsb[:, 0:1]), R=[ss.b], W=[ss.b])
                fw.op("dve", lambda e: e.reciprocal(out=ss[:, 1:2], in_=ss[:, 1:2]), R=[ss.b], W=[ss.b])
                fw.op("dve", lambda e, s_=s_: e.tensor_scalar(out=xn[:], in0=xt[:, s_, :], scalar1=ss[:, 1:2], scalar2=None, op0=ALU.mult), R=[xt.b, ss.b], W=[xn.b])
                for g in range(2):
                    p = pp[npp % 8]; npp += 1
                    for q in range(4):
                        k = 4 * g + q
                        fw.op("pe", lambda e, p=p, q=q, k=k: e.transpose(out=p[:, q * 128:(q + 1) * 128], in_=xn[:, k * 128:(k + 1) * 128], identity=ident), R=[xn.b, self.cstb], W=[p.b])
                    for q in range(4):
                        k = 4 * g + q
                        fw.op("dve", lambda e, p=p, q=q, k=k, j=j: e.tensor_scalar(out=h2f[:, k, :], in0=p[:, q * 128:(q + 1) * 128], scalar1=self.gs2[:, k, j:j + 1], scalar2=self.sh2[:, k, j:j + 1],
                                                                                  op0=ALU.mult, op1=ALU.add), R=[p.b, self.gs2.b, self.sh2.b], W=[h2f.b])
                fw.op("act", lambda e, s_=s_: e.activation(out=h2T[:, :, s_ * 128:(s_ + 1) * 128], in_=h2f[:], func=AF.Copy), R=[h2f.b], W=[h2T.b])
                p = pp[npp % 8]; npp += 1
                for k in range(8):
                    fw.op("pe", lambda e, p=p, k=k: e.matmul(p[:, 0:32], h2f[:, k, :], wrt[:, k * 32:(k + 1) * 32], start=(k == 0), stop=(k == 7)), R=[h2f.b, wrt.b], W=[p.b])
                fw.op("dve", lambda e, p=p: e.tensor_tensor(out=lg[:], in0=p[:, 0:32], in1=brt[:], op=ALU.add), R=[p.b, brt.b], W=[lg.b])
                fw.op("dve", lambda e: e.max(out=m8[:], in_=lg[:]), R=[lg.b], W=[m8.b])
                fw.op("dve", lambda e: e.tensor_scalar(out=msk[:], in0=lg[:], scalar1=m8[:, 3:4], scalar2=None, op0=ALU.is_ge), R=[lg.b, m8.b], W=[msk.b])
                fw.op("dve", lambda e: e.tensor_scalar(out=nb[:, 0:1], in0=m8[:, 0:1], scalar1=-1.0, scalar2=None, op0=ALU.mult), R=[m8.b], W=[nb.b])
                fw.op("act", lambda e: e.activation(out=lg[:], in_=lg[:], func=AF.Exp, bias=nb[:, 0:1]), R=[lg.b, nb.b], W=[lg.b])
                fw.op("dve", lambda e: e.tensor_tensor(out=lg[:], in0=lg[:], in1=msk[:], op=ALU.mult), R=[lg.b, msk.b], W=[lg.b])
                fw.op("dve", lambda e: e.tensor_reduce(out=nb[:, 1:2], in_=lg[:], axis=AX.X, op=ALU.add), R=[lg.b], W=[nb.b])
                fw.op("dve", lambda e: e.reciprocal(out=nb[:, 1:2], in_=nb[:, 1:2]), R=[nb.b], W=[nb.b])
                fw.op("dve", lambda e, s_=s_: e.tensor_scalar(out=Gt[:, s_, :], in0=lg[:], scalar1=nb[:, 1:2], scalar2=None, op0=ALU.mult), R=[lg.b, nb.b], W=[Gt.b])
                p = pp[npp % 8]; npp += 1
                fw.op("pe", lambda e, p=p, s_=s_: e.transpose(out=p[0:16, 0:128], in_=Gt[:, s_, 0:16], identity=ident), R=[Gt.b, self.cstb], W=[p.b])
                fw.op("dve", lambda e, p=p: e.tensor_copy(out=gT[:], in_=p[0:16, 0:128]), R=[p.b], W=[gT.b])
                for half in range(2):
                    p = pp[npp % 8]; npp += 1
                    fw.op("pe", lambda e, p=p, half=half: e.matmul(p[:], gT[:], b2r[0:16, half * 512:(half + 1) * 512], start=True, stop=True), R=[gT.b, b2r.b], W=[p.b])
                    fw.op("act", lambda e, p=p, half=half, s_=s_: e.activation(out=Y[:, s_, half * 512:(half + 1) * 512], in_=p[:], func=AF.Copy), R=[p.b], W=[Y.b])
            for le in range(16):
                w1 = W1[nw % 2]; w2 = W2[nw % 2]; nw += 1
                for cb in range(4):
                    fw.dma("sp", w1[:, :, cb * 512:(cb + 1) * 512], WEB1.t.ap()[le].rearrange("(k p) c -> p k c", p=128)[:, :, cb * 512:(cb + 1) * 512], w1.b, WEB1.b, part=(cb > 0))
                for cb in range(2):
                    fw.dma("sp", w2[:, :, cb * 512:(cb + 1) * 512], WEB2.t.ap()[le].rearrange("(k p) c -> p k c", p=128)[:, :, cb * 512:(cb + 1) * 512], w2.b, WEB2.b, part=(cb > 0))
                for jc in range(8):
                    pg = pp[npp % 8]; npp += 1
                    pl = pp[npp % 8]; npp += 1
                    g_ = g2[jc % 2]; sg = sg2[jc % 2]; ln = ln2[jc % 2]
                    for k in range(8):
                        fw.op("pe", lambda e, pg=pg, k=k, jc=jc, w1=w1, n_=n_: e.matmul(pg[:, 0:n_], w1[:, k, jc * 128:(jc + 1) * 128], h2T[:, k, 0:n_], start=(k == 0), stop=(k == 7)), R=[w1.b, h2T.b], W=[pg.b])
                    for k in range(8):
                        fw.op("pe", lambda e, pl=pl, k=k, jc=jc, w1=w1, n_=n_: e.matmul(pl[:, 0:n_], w1[:, k, 1024 + jc * 128:1024 + (jc + 1) * 128], h2T[:, k, 0:n_], start=(k == 0), stop=(k == 7)), R=[w1.b, h2T.b], W=[pl.b])
                    bg = b1T[:, le * 16 + jc:le * 16 + jc + 1]; bl = b1T[:, le * 16 + 8 + jc:le * 16 + 8 + jc + 1]
                    fw.op("dve", lambda e, pg=pg, bg=bg, n_=n_, g_=g_: e.tensor_scalar(out=g_[:, 0:n_], in0=pg[:, 0:n_], scalar1=bg, scalar2=7.0, op0=ALU.add, op1=ALU.min), R=[pg.b, b1T.b], W=[g_.b])
                    fw.op("act", lambda e, n_=n_, g_=g_, sg=sg: e.activation(out=sg[:, 0:n_], in_=g_[:, 0:n_], func=AF.Sigmoid, scale=1.702), R=[g_.b], W=[sg.b])
                    fw.op("act", lambda e, pl=pl, bl=bl, n_=n_, ln=ln: e.activation(out=ln[:, 0:n_], in_=pl[:, 0:n_], func=AF.Identity, bias=bl), R=[pl.b, b1T.b], W=[ln.b])
                    fw.op("dve", lambda e, n_=n_, ln=ln: e.tensor_scalar(out=ln[:, 0:n_], in0=ln[:, 0:n_], scalar1=7.0, scalar2=-7.0, op0=ALU.min, op1=ALU.max), R=[ln.b], W=[ln.b])
                    fw.op("dve", lambda e, n_=n_, g_=g_, sg=sg: e.tensor_tensor(out=g_[:, 0:n_], in0=g_[:, 0:n_], in1=sg[:, 0:n_], op=ALU.mult), R=[g_.b, sg.b], W=[g_.b])
                    fw.op("dve", lambda e, jc=jc, n_=n_, g_=g_, ln=ln: e.scalar_tensor_tensor(out=A[:, jc, 0:n_], in0=ln[:, 0:n_], scalar=1.0, in1=g_[:, 0:n_], op0=ALU.add, op1=ALU.mult), R=[g_.b, ln.b], W=[A.b])
                for s_ in range(nsub):
                    for half in range(2):
                        p = pp[npp % 8]; npp += 1
                        for k in range(8):
                            fw.op("pe", lambda e, p=p, k=k, s_=s_, half=half, w2=w2: e.matmul(p[:], A[:, k, s_ * 128:(s_ + 1) * 128], w2[:, k, half * 512:(half + 1) * 512], start=(k == 0), stop=(k == 7)),
                                  R=[A.b, w2.b], W=[p.b])
                        fw.op("dve", lambda e, p=p, s_=s_, half=half, le=le: e.scalar_tensor_tensor(out=Y[:, s_, half * 512:(half + 1) * 512], in0=p[:], scalar=Gt[:, s_, le:le + 1],
                                                                                                   in1=Y[:, s_, half * 512:(half + 1) * 512], op0=ALU.mult, op1=ALU.add), R=[p.b, Gt.b, Y.b], W=[Y.b])
            for s_ in range(nsub):
                fw.op("dve", lambda e, s_=s_: e.tensor_tensor(out=tz[:], in0=Y[:, s_, :], in1=mb[:], op=ALU.mult), R=[Y.b, mb.b], W=[tz.b])
                fw.op("dve", lambda e, s_=s_: e.scalar_tensor_tensor(out=tz[:], in0=xt[:, s_, :], scalar=0.5, in1=tz[:], op0=ALU.mult, op1=ALU.add), R=[xt.b, tz.b], W=[tz.b])
                fw.dma("sp", self.ARin.t.ap()[t0 + s_ * 128:t0 + (s_ + 1) * 128, :], tz[:], self.ARin.b, tz.b, part=True)
        fw.fence()
    self.allreduce_to_X(ntok)


setattr(Prog, "phase_moe", phase_moe)


_CACHE = {}


def kernel(**inputs):
    inp = {k: np.asarray(v) for k, v in inputs.items()}
    maps = host_prep(inp)
    P = Prog(stop=None, dumps=(), with_exp=True)
    nc = P.build()
    res = run_bass_kernel_spmd(nc, maps, core_ids=list(range(8)))
    out = np.stack([np.asarray(res.results[b]["out"]) for b in range(4)], 0)
    return out.astype(np.float32)
```

```python
import contextlib
import math
import numpy as np
import ml_dtypes
import concourse.bass as bass
import concourse.mybir as mybir
from concourse.bass_utils import run_bass_kernel_spmd

F32 = mybir.dt.float32
BF16 = mybir.dt.bfloat16
AF = mybir.ActivationFunctionType
ALU = mybir.AluOpType
AX = mybir.AxisListType

D = 1024
L = 8192
NCTX = 256
T = L + NCTX
DEPTH = 2
EPS = 1e-6
SUBLN_EPS = 1e-5
OFF_F, OFF_HY, OFF_Q, OFF_K, OFF_V, OFF_G = 0, 256, 1024, 1536, 2048, 2560
NMIX = 1280
NEXP_CORE = 16
PAIRS = [[0, 4], [1, 5], [2, 6], [3, 7]]
HALVES = [[0, 1, 2, 3], [4, 5, 6, 7]]


class Buf:
    __slots__ = ("name", "w", "r", "dsem")

    def __init__(self, name):
        self.name = name
        self.w = []
        self.r = []
        self.dsem = None


class Tile:
    def __init__(self, t, b):
        self.t = t
        self.b = b

    def __getitem__(self, k):
        return self.t[k]


class Op:
    __slots__ = ("id", "eng", "fn", "deps", "isdma", "signal", "wbuf")

    def __init__(self, id, eng, fn, deps, isdma, wbuf=None):
        self.id = id; self.eng = eng; self.fn = fn; self.deps = deps
        self.isdma = isdma; self.signal = isdma; self.wbuf = wbuf


class FW:
    ENG = ("pe", "act", "dve", "pool", "sp")
    NDSEM = 72

    def __init__(self, nc):
        self.nc = nc
        self.es = contextlib.ExitStack()
        self.h = {"pe": nc.tensor, "act": nc.scalar, "dve": nc.vector, "pool": nc.gpsimd, "sp": nc.sync}
        self.esem = {e: self.es.enter_context(nc.semaphore("e_" + e)) for e in self.ENG}
        self.ecnt = {e: 0 for e in self.ENG}
        self.fsem = self.es.enter_context(nc.semaphore("fence"))
        self.fcnt = 0
        self.dsems = [self.es.enter_context(nc.semaphore("d%d" % i)) for i in range(self.NDSEM)]
        self.dissued = [0] * self.NDSEM
        self.dnext = 0
        self.seen = {e: {} for e in self.ENG}
        self.done = {}
        self.pending = []
        self.bufs = []
        self.opmap = {}
        self.nid = 0
        self.ninst = 0

    def buf(self, name):
        b = Buf(name)
        self.bufs.append(b)
        return b

    def sb(self, st, name, shape, dtype=F32):
        self.nid += 1
        name = "%s_%d" % (name, self.nid)
        t = st.enter_context(self.nc.sbuf_tensor(name, shape, dtype))
        return Tile(t, self.buf(name))

    def ps(self, st, name, shape=(128, 512), dtype=F32):
        self.nid += 1
        name = "%s_%d" % (name, self.nid)
        t = st.enter_context(self.nc.psum_tensor(name, list(shape), dtype))
        return Tile(t, self.buf(name))

    def dram(self, name, shape, dtype, kind="Internal"):
        t = self.nc.dram_tensor(name, list(shape), dtype, kind=kind)
        return Tile(t, self.buf(name))

    def op(self, eng, fn, R=(), W=()):
        deps = set()
        for b in R:
            deps.update(b.w)
        for b in W:
            deps.update(b.w)
            deps.update(b.r)
        o = Op(self.nid, eng, fn, deps, False)
        self.nid += 1
        self.pending.append(o)
        for b in R:
            b.r.append(o.id)
        for b in W:
            b.w = [o.id]; b.r = []
        return o

    def dma(self, q, out_ap, in_ap, W, R, part=False, **kw):
        deps = set(R.w)
        for x in W.w:
            if part:
                p = self.opmap.get(x)
                if p is not None and p.isdma and p.wbuf is W:
                    continue
            deps.add(x)
        deps.update(W.r)
        fn = (lambda h, o=out_ap, i=in_ap, kw=kw: h.dma_start(out=o, in_=i, **kw))
        o = Op(self.nid, q, fn, deps, True, wbuf=W)
        self.nid += 1
        self.pending.append(o)
        self.opmap[o.id] = o
        R.r.append(o.id)
        if part:
            W.w = list(W.w) + [o.id]
        else:
            W.w = [o.id]
            W.r = []
        return o

    def _wait(self, eng, key, val):
        s = self.seen[eng]
        if s.get(key, 0) >= val:
            return
        s[key] = val
        if key[0] == "e":
            sem = self.esem[key[1]]
        elif key[0] == "f":
            sem = self.fsem
        else:
            sem = self.dsems[key[1]]
        self.h[eng].wait_ge(sem, val)
        self.ninst += 1

    def flush(self):
        ops = self.pending
        self.pending = []
        ids = {o.id: o for o in ops}
        for o in ops:
            for d in o.deps:
                p = ids.get(d)
                if p is not None and not p.isdma and not (p.eng == "pe" and o.eng == "pe"):
                    p.signal = True
        for o in ops:
            for d in o.deps:
                ev = self.done.get(d)
                if ev is None:
                    continue
                key, val = ev
                if key == ("e", "pe") and o.eng == "pe":
                    continue
                if key[0] == "d":
                    val = max(val, self.dissued[key[1]])
                self._wait(o.eng, key, val)
            inst = o.fn(self.h[o.eng])
            self.ninst += 1
            if o.isdma:
                b = o.wbuf
                if b.dsem is None:
                    b.dsem = self.dnext % self.NDSEM
                    self.dnext += 1
                k = b.dsem
                self.dissued[k] += 16
                inst.then_inc(self.dsems[k], 16)
                self.done[o.id] = (("d", k), self.dissued[k])
            elif o.signal:
                self.ecnt[o.eng] += 1
                inst.then_inc(self.esem[o.eng], 1)
                self.done[o.id] = (("e", o.eng), self.ecnt[o.eng])

    def fence(self):
        last = {}
        for o in self.pending:
            if not o.isdma:
                last[o.eng] = o
        for o in last.values():
            o.signal = True
        self.flush()
        for e in self.ENG:
            if self.ecnt[e] > 0:
                self._wait("sp", ("e", e), self.ecnt[e])
        for k in range(self.NDSEM):
            if self.dissued[k] > 0:
                self._wait("sp", ("d", k), self.dissued[k])
        self.fcnt += 1
        self.h["sp"].sem_inc(self.fsem, 1)
        self.ninst += 1
        for e in self.ENG:
            self._wait(e, ("f",), self.fcnt)
            for e2 in self.ENG:
                self.seen[e][("e", e2)] = self.ecnt[e2]
            for k in range(self.NDSEM):
                self.seen[e][("d", k)] = self.dissued[k]
        for b in self.bufs:
            b.w = []; b.r = []
        self.done = {}
        self.opmap = {}

    def close(self):
        self.es.close()


class Pack:
    def __init__(self):
        self.off = {}
        self.n = 0
        self.items = []

    def add(self, name, width):
        self.off[name] = (self.n, width)
        self.n += width

    def fill(self, arr, name, val):
        o, w = self.off[name]
        val = np.asarray(val, np.float32)
        val = val.reshape(val.shape[0], -1)
        assert val.shape[1] == w, (name, val.shape, w)
        arr[:val.shape[0], o:o + w] = val


def fm(v, nk):
    return np.asarray(v, np.float32).reshape(nk, 128).T


def rep(v):
    v = np.asarray(v, np.float32).reshape(1, -1)
    return np.broadcast_to(v, (128, v.shape[1]))


def make_sm_layout():
    P = Pack()
    P.add("cT", 16)
    P.add("crep", 2 * 8 * 128)
    P.add("deltabc", 128)
    for l in range(DEPTH):
        P.add("bmodT%d" % l, 48)
        P.add("bmodbc%d" % l, 2 * 1024)
        P.add("g1T%d" % l, 8)
        P.add("g2T%d" % l, 8)
        P.add("qg%d" % l, 1)
        P.add("kg%d" % l, 1)
        P.add("subg%d" % l, 1)
        P.add("lamq%d" % l, 128)
        P.add("lamk%d" % l, 128)
        P.add("wrt%d" % l, 8 * 32)
        P.add("brt%d" % l, 32)
        P.add("b1T%d" % l, 16 * 16)
        P.add("b2r%d" % l, 1024)
        P.add("cwbc%d" % l, 3 * 384)
        P.add("cbbc%d" % l, 384)
        P.add("hw1%d" % l, 64)
        P.add("hb1%d" % l, 1)
        P.add("hf1%d" % l, 1)
        P.add("hw2%d" % l, 64)
        P.add("hb2%d" % l, 1)
        P.add("hf2%d" % l, 1)
        P.add("hw3%d" % l, 512)
        P.add("hb3bc%d" % l, 512)
        P.add("hbias%d" % l, 256)
    return P


SM = make_sm_layout()


def make_cst_layout():
    P = Pack()
    P.add("ident", 128)
    P.add("ones", 128)
    P.add("bones", 128)
    P.add("rot", 128)
    P.add("ntl8192", 64)
    P.add("ntl256", 2)
    P.add("c64", 128)
    P.add("s64", 128)
    return P


CST = make_cst_layout()


def rope_tables():
    rows = L // 64
    row = np.repeat(np.arange(rows), 64).astype(np.float32)
    col = np.tile(np.arange(64), rows).astype(np.float32)
    nf = 16
    inv = (10000.0 ** (-np.arange(nf, dtype=np.float32) / nf)).astype(np.float32)
    angr = row[None, :] * inv[:, None]
    angc = col[None, :] * inv[:, None]
    ang64 = np.concatenate([angr, angr, angc, angc], 0)
    cos = np.cos(ang64).astype(np.float32)
    sin = np.sin(ang64).astype(np.float32)
    cos = np.concatenate([cos, np.ones((64, NCTX), np.float32)], 1)
    sin = np.concatenate([sin, np.zeros((64, NCTX), np.float32)], 1)
    return np.concatenate([cos, cos], 0), np.concatenate([sin, sin], 0)


def hy_emb(l):
    t = np.linspace(0.0, 1.0, l, dtype=np.float32)[:, None]
    ang = (np.float32(2.0 * math.pi / l) * np.arange(l, dtype=np.float32))[:, None]
    bands = np.linspace(1e-4, 15, 16, dtype=np.float32)[None, :]
    emb = np.concatenate([t, np.cos(bands * ang), -np.sin(bands * ang)], -1)
    return emb.T.astype(np.float32)


def hy_deltas(s):
    d = np.abs(np.linspace(math.log(1e-2) / 1.5, math.log(1e-2) / 0.3, 256, dtype=np.float32))
    return d[128 * s:128 * s + 128]


def rot_lhsT():
    R = np.zeros((128, 128), np.float32)
    for blk in range(2):
        for base in (0, 32):
            for j in range(16):
                a = blk * 64 + base + j
                b = a + 16
                R[a, b] = -1.0
                R[b, a] = 1.0
    return R.T.copy()


def shared_layout():
    off = {}
    n = 0
    for l in range(DEPTH):
        off["wg%d" % l] = (n, (1024, 3072)); n += 1024 * 3072
        off["wmod%d" % l] = (n, (1024, 6144)); n += 1024 * 6144
        off["wo%d" % l] = (n, (1024, 1024)); n += 1024 * 1024
    off["ropec"] = (n, (128, T)); n += 128 * T
    off["ropes"] = (n, (128, T)); n += 128 * T
    rows = -(-n // (512 * 2048)) * 512
    return off, rows


SHOFF, SHROWS = shared_layout()


def host_prep(inp):
    x = inp["x"]; ctx = inp["ctx"]; c = inp["c"]; c_ctx = inp["c_ctx"]
    blob = np.zeros((SHROWS * 2048,), np.float32)

    def put(name, arr):
        o, shp = SHOFF[name]
        blob[o:o + arr.size] = np.ascontiguousarray(arr, np.float32).reshape(-1)

    for l in range(DEPTH):
        put("wg%d" % l, inp["w_in"][l][:, OFF_G:])
        put("wmod%d" % l, inp["w_mod"][l])
        put("wo%d" % l, inp["w_o"][l])
    rc, rs = rope_tables()
    put("ropec", rc); put("ropes", rs)
    blob = blob.reshape(SHROWS // 512, 4, 128, 2048)

    cst = np.zeros((128, CST.n), np.float32)
    CST.fill(cst, "ident", np.eye(128, dtype=np.float32))
    CST.fill(cst, "ones", np.ones((128, 128), np.float32))
    bo = np.zeros((128, 128), np.float32); bo[:64, :64] = 1; bo[64:, 64:] = 1
    CST.fill(cst, "bones", bo)
    CST.fill(cst, "rot", rot_lhsT())
    pp = np.arange(128)[:, None]
    CST.fill(cst, "ntl8192", -((np.arange(64)[None, :] * 128 + pp) / (L - 1.0)))
    CST.fill(cst, "ntl256", -((np.arange(2)[None, :] * 128 + pp) / (NCTX - 1.0)))
    a64 = np.arange(64)
    c64 = np.cos(2 * np.pi * np.outer(a64, a64) / 64); s64 = np.sin(2 * np.pi * np.outer(a64, a64) / 64)
    z = np.zeros((64, 64))
    CST.fill(cst, "c64", np.block([[c64, z], [z, c64]]))
    CST.fill(cst, "s64", np.block([[s64, z], [z, s64]]))
    embc = np.concatenate([hy_emb(L), hy_emb(NCTX)], 1).astype(np.float32)
    fftc = fft_constants()

    maps = []
    for r in range(8):
        s, b = r // 4, r % 4
        m = {}
        m["xs"] = np.ascontiguousarray(x[b, s * 4096:(s + 1) * 4096])
        m["ctxb"] = np.ascontiguousarray(ctx[b])
        m["wsh"] = np.ascontiguousarray(blob[:, b]).reshape(SHROWS // 4, 2048)
        m["cst"] = cst
        m["embc"] = embc
        m["fftc"] = fftc
        wmix = np.zeros((DEPTH, 1024, NMIX), np.float32)
        wout = np.zeros((DEPTH, 512, 1024), np.float32)
        sm = np.zeros((128, SM.n), np.float32)
        SM.fill(sm, "cT", np.stack([fm(c[b], 8), fm(c_ctx, 8)], -1).reshape(128, 16))
        crep = np.zeros((128, 2, 8, 128), np.float32)
        crep[:, 0] = fm(c[b], 8)[:, :, None]
        crep[:, 1] = fm(c_ctx, 8)[:, :, None]
        SM.fill(sm, "crep", crep.reshape(128, -1))
        SM.fill(sm, "deltabc", rep(hy_deltas(s)))
        for l in range(DEPTH):
            w_in = inp["w_in"][l]
            cols = np.concatenate([
                np.arange(OFF_F + 128 * s, OFF_F + 128 * s + 128),
                np.arange(OFF_HY + 128 * s, OFF_HY + 128 * s + 128),
                np.arange(OFF_HY + 256 + 128 * s, OFF_HY + 256 + 128 * s + 128),
                np.arange(OFF_HY + 512 + 128 * s, OFF_HY + 512 + 128 * s + 128),
                np.arange(OFF_Q + 256 * s, OFF_Q + 256 * s + 256),
                np.arange(OFF_K + 256 * s, OFF_K + 256 * s + 256),
                np.arange(OFF_V + 256 * s, OFF_V + 256 * s + 256)])
            wmix[l] = w_in[:, cols]
            wout[l, 0:128] = inp["w_f"][l][128 * s:128 * s + 128]
            wout[l, 128:256] = inp["w_h"][l][128 * s:128 * s + 128]
            wout[l, 256:512] = inp["w_a"][l][256 * s:256 * s + 256]
            SM.fill(sm, "bmodT%d" % l, fm(inp["b_mod"][l], 48))
            bm = inp["b_mod"][l].reshape(6, 1024)
            SM.fill(sm, "bmodbc%d" % l, rep(np.concatenate([bm[2], bm[5]])))
            SM.fill(sm, "g1T%d" % l, fm(inp["norm1_g"][l], 8))
            SM.fill(sm, "g2T%d" % l, fm(inp["norm2_g"][l], 8))
            SM.fill(sm, "qg%d" % l, np.tile(inp["q_norm_g"][l], 2).reshape(128, 1))
            SM.fill(sm, "kg%d" % l, np.tile(inp["k_norm_g"][l], 2).reshape(128, 1))
            SM.fill(sm, "subg%d" % l, inp["subln_g"][l].reshape(128, 1))
            SM.fill(sm, "lamq%d" % l, rep(inp["lam_q"][l].reshape(-1)))
            SM.fill(sm, "lamk%d" % l, rep(inp["lam_k"][l].reshape(-1)))
            es_ = [16 * s + le for le in range(16)]
            perm = es_ + [e for e in range(32) if e not in es_]
            SM.fill(sm, "wrt%d" % l, inp["w_router"][l][:, perm].reshape(8, 128, 32).transpose(1, 0, 2).reshape(128, 256))
            SM.fill(sm, "brt%d" % l, rep(inp["b_router"][l][perm]))
            b1 = inp["b_e1"][l][es_]
            b1p = np.concatenate([b1[:, 0::2], b1[:, 1::2]], 1)
            SM.fill(sm, "b1T%d" % l, b1p.reshape(16, 16, 128).transpose(2, 0, 1).reshape(128, 256))
            SM.fill(sm, "b2r%d" % l, inp["b_e2"][l][es_])
            cw = inp["hy_conv_w"][l]; cb = inp["hy_conv_b"][l]
            vx = np.concatenate([np.arange(128 * s, 128 * s + 128), np.arange(256 + 128 * s, 256 + 128 * s + 128),
                                 np.arange(512 + 128 * s, 512 + 128 * s + 128)])
            SM.fill(sm, "cwbc%d" % l, rep(cw[:, vx].reshape(-1)))
            SM.fill(sm, "cbbc%d" % l, rep(cb[vx]))
            SM.fill(sm, "hw1%d" % l, inp["hy_w1"][l])
            SM.fill(sm, "hb1%d" % l, inp["hy_b1"][l].reshape(64, 1))
            SM.fill(sm, "hf1%d" % l, inp["hy_freq1"][l].reshape(64, 1))
            SM.fill(sm, "hw2%d" % l, inp["hy_w2"][l])
            SM.fill(sm, "hb2%d" % l, inp["hy_b2"][l].reshape(64, 1))
            SM.fill(sm, "hf2%d" % l, inp["hy_freq2"][l].reshape(64, 1))
            w3 = inp["hy_w3"][l].reshape(64, 2, 2, 256)[:, :, :, 128 * s:128 * s + 128].reshape(64, 512)
            b3 = inp["hy_b3"][l].reshape(2, 2, 256)[:, :, 128 * s:128 * s + 128].reshape(512)
            SM.fill(sm, "hw3%d" % l, w3)
            SM.fill(sm, "hb3bc%d" % l, rep(b3))
            SM.fill(sm, "hbias%d" % l, rep(inp["hy_bias"][l][:, 128 * s:128 * s + 128].reshape(-1)))
        m["wmix"] = wmix
        m["wout"] = wout
        m["sm"] = sm
        wexp = np.zeros((DEPTH, 6144, 2048), np.float32)
        for l in range(DEPTH):
            full = np.zeros((16, 1536, 2048), np.float32)
            for le in range(16):
                e = 16 * s + le
                w1 = inp["w_e1"][l][e]
                full[le, :1024] = np.concatenate([w1[:, 0::2], w1[:, 1::2]], 1)
                full[le, 1024:] = inp["w_e2"][l][e].reshape(512, 2048)
            wexp[l] = full.reshape(48, 4, 128, 2048)[:, b].reshape(6144, 2048)
        m["wexp"] = wexp
        maps.append(m)
    return maps


def dview(tile, off, shape):
    r, c = shape
    return bass.AP(tile.t, off, [[c, r], [1, c]])


class Prog:
    def __init__(self, stop=None, dumps=(), with_exp=True):
        self.stop = stop
        self.dumps = list(dumps)
        self.with_exp = with_exp
        nc = self.nc = bass.Bass("TRN2", target_bir_lowering=False)
        fw = self.fw = FW(nc)
        self.ncc = 0
        self.xs = fw.dram("xs", [4096, 1024], F32, "ExternalInput")
        self.ctxb = fw.dram("ctxb", [NCTX, 1024], F32, "ExternalInput")
        self.wsh = fw.dram("wsh", [SHROWS // 4, 2048], F32, "ExternalInput")
        self.cst_d = fw.dram("cst", [128, CST.n], F32, "ExternalInput")
        self.wmix = fw.dram("wmix", [DEPTH, 1024, NMIX], F32, "ExternalInput")
        self.wout = fw.dram("wout", [DEPTH, 512, 1024], F32, "ExternalInput")
        self.sm_d = fw.dram("sm", [128, SM.n], F32, "ExternalInput")
        self.embc = fw.dram("embc", [33, T], F32, "ExternalInput")
        self.fftc_d = fw.dram("fftc", [128, FFTC.n], F32, "ExternalInput")
        if with_exp:
            self.wexp = fw.dram("wexp", [DEPTH, 6144, 2048], F32, "ExternalInput")
        self.out = fw.dram("out", [L, 1024], F32, "ExternalOutput")
        self.X = fw.dram("X", [T, 1024], F32)
        self.SH = fw.dram("SH", [SHROWS, 2048], F32)
        self.QT = fw.dram("QT", [256, T], BF16)
        self.KT = fw.dram("KT", [256, T], BF16)
        self.V = fw.dram("V", [T, 256], BF16)
        self.GT = fw.dram("GT", [3072, T], BF16)
        self.PFH = fw.dram("PFH", [T + 4, 512], F32)
        self.MODBC = fw.dram("MODBC", [128, 4096], F32)
        self.FM = fw.dram("FM", [T, 2, 128], F32)
        self.HM = fw.dram("HM", [T, 128], F32)
        self.OT = fw.dram("OT", [256, T], BF16)
        self.ARin = fw.dram("ARin", [T, 1024], F32)
        self.dump_out = {}

    def coll_seq(self, items):
        fw = self.fw
        sem = fw.es.enter_context(self.nc.semaphore("cc%d" % self.ncc))
        self.ncc += 1
        fw.fence()
        for (kind, groups, in_ap, out_ap) in items:
            op = ALU.bypass if kind in ("AllGather", "AllToAll") else ALU.add
            fw.h["pool"].collective_compute(kind, op, replica_groups=groups, ins=[in_ap], outs=[out_ap]).then_inc(sem)
        for e in fw.ENG:
            fw.h[e].wait_ge(sem, len(items))

    def ld(self, ps, name, rows=128, q="sp"):
        o, w = SM.off[name]
        t = self.fw.sb(ps, "sm_" + name, [rows, w])
        kw = dict(allow_slow_non_contiguous=True) if w == 1 else {}
        self.fw.dma(q, t[:], self.sm_d.t.ap()[0:rows, o:o + w], t.b, self.sm_d.b, **kw)
        return t

    def cv(self, name):
        o, w = CST.off[name]
        return self.cst[:, o:o + w]

    def dump(self, name, tile):
        if name in self.dumps:
            t = tile.t
            d = self.fw.dram("dump_" + name, list(t.shape), t.dtype, "ExternalOutput")
            self.fw.dma("sp", d.t.ap(), t.ap(), d.b, tile.b)
            self.dump_out[name] = "dump_" + name
            self.fw.fence()

    def bigcopy(self, dst, dap, src, sap, rows, step=256):
        for r0 in range(0, rows, step):
            r1 = min(rows, r0 + step)
            self.fw.dma("sp", dap[r0:r1, :], sap[r0:r1, :], dst.b, src.b, part=True)

    def phase_gather(self):
        fw = self.fw
        xsI = fw.dram("xsI", [4096, 1024], F32)
        wshI = fw.dram("wshI", [SHROWS // 4, 2048], F32)
        self.bigcopy(xsI, xsI.t.ap(), self.xs, self.xs.t.ap(), 4096, 512)
        self.bigcopy(wshI, wshI.t.ap(), self.wsh, self.wsh.t.ap(), SHROWS // 4)
        fw.dma("sp", self.X.t.ap()[L:T, :], self.ctxb.t.ap(), self.X.b, self.ctxb.b, part=True)
        if self.with_exp:
            self.wexpI = fw.dram("wexpI", [DEPTH, 6144, 2048], F32)
            for l in range(DEPTH):
                self.bigcopy(self.wexpI, self.wexpI.t.ap()[l], self.wexp, self.wexp.t.ap()[l], 6144)
        XG = fw.dram("XG", [8, 1024, 1024], F32)
        items = [("AllGather", PAIRS, xsI.t.ap()[c * 512:(c + 1) * 512, :], XG.t.ap()[c]) for c in range(8)]
        items += [("AllGather", HALVES, wshI.t.ap()[c * 128:(c + 1) * 128, :], self.SH.t.ap()[c * 512:(c + 1) * 512, :])
                  for c in range(SHROWS // 512)]
        self.coll_seq(items)
        if self.with_exp:
            self.WEl = [fw.dram("WE%d" % l, [24576, 2048], F32) for l in range(DEPTH)]
            self.expsem = fw.es.enter_context(self.nc.semaphore("expsem"))
            for l in range(DEPTH):
                for c in range(48):
                    fw.h["pool"].collective_compute("AllGather", ALU.bypass, replica_groups=HALVES,
                                                    ins=[self.wexpI.t.ap()[l, c * 128:(c + 1) * 128, :]],
                                                    outs=[self.WEl[l].t.ap()[c * 512:(c + 1) * 512, :]]).then_inc(self.expsem)
        for c in range(8):
            for r in range(2):
                for q in range(2):
                    fw.dma("sp", self.X.t.ap()[r * 4096 + c * 512 + q * 256: r * 4096 + c * 512 + q * 256 + 256, :],
                           XG.t.ap()[c, r * 512 + q * 256: r * 512 + q * 256 + 256, :], self.X.b, XG.b, part=True)
        fw.fence()

    def phase_mod(self, l):
        fw = self.fw
        with contextlib.ExitStack() as ps:
            cT = self.ld(ps, "cT"); crep = self.ld(ps, "crep")
            bmodT = self.ld(ps, "bmodT%d" % l); bmodbc = self.ld(ps, "bmodbc%d" % l)
            g1T = self.ld(ps, "g1T%d" % l); g2T = self.ld(ps, "g2T%d" % l)
            lamq = self.ld(ps, "lamq%d" % l); lamk = self.ld(ps, "lamk%d" % l)
            for nm, dst in (("qg%d" % l, 0), ("kg%d" % l, 1), ("subg%d" % l, 2)):
                o, w = SM.off[nm]
                fw.dma("sp", self.qks[:, dst:dst + 1], self.sm_d.t.ap()[:, o:o + 1], self.qks.b, self.sm_d.b, part=True, allow_slow_non_contiguous=True)
            sc = fw.sb(ps, "sc", [128, 8, 2])
            screp = fw.sb(ps, "screp", [128, 2, 8, 128])
            mbc = fw.sb(ps, "mbc", [128, 2, 2, 1024])
            pm = fw.ps(ps, "pm")
            pbc = [fw.ps(ps, "pbc%d" % i) for i in range(2)]
            wm = [fw.sb(ps, "wm%d" % i, [128, 8, 512]) for i in range(2)]
            fw.op("act", lambda e: e.activation(out=sc[:].rearrange("p k j -> p (k j)"), in_=cT[:], func=AF.Silu),
                  R=[cT.b], W=[sc.b])
            fw.op("act", lambda e: e.activation(out=screp[:].rearrange("p j k m -> p (j k m)"), in_=crep[:], func=AF.Silu),
                  R=[crep.b], W=[screp.b])
            o, _ = SHOFF["wmod%d" % l]
            wv = dview(self.SH, o, (1024, 6144)).rearrange("(k p) c -> p k c", p=128)
            for cb in range(12):
                w = wm[cb % 2]
                fw.dma("sp", w[:], wv[:, :, cb * 512:(cb + 1) * 512], w.b, self.SH.b)
                for oc in range(4):
                    col = (cb * 4 + oc) * 2
                    for k in range(8):
                        fw.op("pe", lambda e, w=w, oc=oc, k=k, col=col: e.matmul(
                            pm[:, col:col + 2], w[:, k, oc * 128:(oc + 1) * 128], sc[:, k, :], start=(k == 0), stop=(k == 7)),
                            R=[w.b, sc.b], W=[pm.b])
                if cb in (4, 5, 10, 11):
                    which = 0 if cb < 6 else 1
                    half = cb - 4 if cb < 6 else cb - 10
                    for j in range(2):
                        pb = pbc[j]
                        for k in range(8):
                            fw.op("pe", lambda e, w=w, j=j, k=k, pb=pb: e.matmul(
                                pb[:], screp[:, j, k, :], w[:, k, :], start=(k == 0), stop=(k == 7)),
                                R=[w.b, screp.b], W=[pb.b])
                        bsl = bmodbc[:, which * 1024 + half * 512: which * 1024 + half * 512 + 512]
                        fw.op("dve", lambda e, pb=pb, j=j, which=which, half=half, bsl=bsl: e.tensor_tensor(
                            out=mbc[:, j, which, half * 512:(half + 1) * 512], in0=pb[:], in1=bsl, op=ALU.add),
                            R=[pb.b, bmodbc.b], W=[mbc.b])
            fw.dma("sp", self.MODBC.t.ap(), mbc[:].rearrange("p j w c -> p (j w c)"), self.MODBC.b, mbc.b)
            pmv = pm[:, 0:96].rearrange("p (a j) -> p a j", j=2)
            for j in range(2):
                fw.op("dve", lambda e, j=j: e.tensor_tensor(out=self.modT[:, :, j], in0=pmv[:, :, j], in1=bmodT[:], op=ALU.add),
                      R=[pm.b, bmodT.b], W=[self.modT.b])
            for (gT, gs, sh, ishift, iscale) in ((g1T, self.gs1, self.sh1, 0, 1), (g2T, self.gs2, self.sh2, 3, 4)):
                for j in range(2):
                    fw.op("dve", lambda e, j=j, gs=gs, iscale=iscale, gT=gT: e.scalar_tensor_tensor(
                        out=gs[:, :, j], in0=self.modT[:, iscale * 8:(iscale + 1) * 8, j], scalar=1.0, in1=gT[:],
                        op0=ALU.add, op1=ALU.mult), R=[self.modT.b, gT.b], W=[gs.b])
                    fw.op("dve", lambda e, j=j, sh=sh, ishift=ishift: e.tensor_copy(
                        out=sh[:, :, j], in_=self.modT[:, ishift * 8:(ishift + 1) * 8, j]), R=[self.modT.b], W=[sh.b])
            lp = fw.sb(ps, "lp", [128, 2, 64])
            le = fw.sb(ps, "le", [128, 2])
            fw.op("dve", lambda e: e.tensor_tensor(out=lp[:].rearrange("p a b -> p (a b)"), in0=lamq[:], in1=lamk[:], op=ALU.mult),
                  R=[lamq.b, lamk.b], W=[lp.b])
            fw.op("dve", lambda e: e.tensor_reduce(out=le[:], in_=lp[:], axis=AX.X, op=ALU.add), R=[lp.b], W=[le.b])
            fw.op("act", lambda e: e.activation(out=le[:], in_=le[:], func=AF.Exp), R=[le.b], W=[le.b])
            lam_init = 0.8 - 0.6 * math.exp(-0.3 * l)
            fw.op("dve", lambda e: e.tensor_tensor(out=self.lam[:, 0:1], in0=le[:, 0:1], in1=le[:, 1:2], op=ALU.subtract),
                  R=[le.b], W=[self.lam.b])
            fw.op("dve", lambda e: e.tensor_scalar(out=self.lam[:, 0:1], in0=self.lam[:, 0:1], scalar1=lam_init, scalar2=None,
                                                   op0=ALU.add), R=[self.lam.b], W=[self.lam.b])
            fw.op("dve", lambda e: e.tensor_scalar(out=self.lam[:, 1:2], in0=self.lam[:, 0:1], scalar1=-1.0, scalar2=None,
                                                   op0=ALU.mult), R=[self.lam.b], W=[self.lam.b])
            fw.fence()

    def pfh_row(self, t0):
        return t0 + 1 if t0 < L else (t0 - L) + L + 3

    def phase_proj(self, l):
        fw = self.fw
        with contextlib.ExitStack() as ps:
            Wm = fw.sb(ps, "Wm", [128, 8, NMIX], BF16)
            Wg = fw.sb(ps, "Wg", [128, 8, 3072], BF16)
            stg = [fw.sb(ps, "stg%d" % i, [128, 8, 256]) for i in range(2)]
            wmv = self.wmix.t.ap()[l].rearrange("(k p) c -> p k c", p=128)
            og, _ = SHOFF["wg%d" % l]
            wgv = dview(self.SH, og, (1024, 3072)).rearrange("(k p) c -> p k c", p=128)
            n = 0
            for (src, srcb, dst, nb) in ((wmv, self.wmix.b, Wm, NMIX // 256), (wgv, self.SH.b, Wg, 12)):
                for cb in range(nb):
                    s = stg[n % 2]; n += 1
                    fw.dma("sp", s[:], src[:, :, cb * 256:(cb + 1) * 256], s.b, srcb)
                    fw.op("act" if n % 2 else "dve",
                          (lambda e, s=s, dst=dst, cb=cb: e.activation(out=dst[:, :, cb * 256:(cb + 1) * 256], in_=s[:], func=AF.Copy))
                          if n % 2 else
                          (lambda e, s=s, dst=dst, cb=cb: e.tensor_copy(out=dst[:, :, cb * 256:(cb + 1) * 256], in_=s[:])),
                          R=[s.b], W=[dst.b])
            xt = [fw.sb(ps, "xt%d" % i, [128, 1024]) for i in range(2)]
            xn = [fw.sb(ps, "xn%d" % i, [128, 1024]) for i in range(2)]
            junk = fw.sb(ps, "junk", [128, 1024])
            ss = [fw.sb(ps, "ss%d" % i, [128, 2]) for i in range(2)]
            hT = [fw.sb(ps, "hT%d" % i, [128, 8, 512], BF16) for i in range(2)]
            rc = [fw.sb(ps, "rc%d" % i, [128, 512]) for i in range(2)]
            rs_ = [fw.sb(ps, "rs%d" % i, [128, 512]) for i in range(2)]
            sq = fw.sb(ps, "sq", [128, 512]); rq = fw.sb(ps, "rq", [128, 512]); qn = fw.sb(ps, "qn", [128, 512])
            t1 = fw.sb(ps, "t1", [128, 512]); t2 = fw.sb(ps, "t2", [128, 512])
            qo = [fw.sb(ps, "qo%d" % i, [128, 512], BF16) for i in range(2)]
            fhs = [fw.sb(ps, "fhs%d" % i, [128, 512]) for i in range(2)]
            vs = [fw.sb(ps, "vs%d" % i, [128, 256], BF16) for i in range(2)]
            gst = [fw.sb(ps, "gst%d" % i, [128, 4, 512], BF16) for i in range(2)]
            zt = fw.sb(ps, "zt", [4, 512])
            pt = [fw.ps(ps, "pt%d" % i) for i in range(2)]
            pj = [fw.ps(ps, "pj%d" % i) for i in range(3)]
            pn = [fw.ps(ps, "pn%d" % i) for i in range(2)]
            ident = self.cv("ident"); bones = self.cv("bones"); rot = self.cv("rot")
            cb_ = self.cstb
            fw.op("dve", lambda e: e.memset(zt[:], 0.0), W=[zt.b])
            for r in (0, L + 1, L + 2, T + 3):
                fw.dma("sp", self.PFH.t.ap()[r:r + 1, :], zt[0:1, :], self.PFH.b, zt.b, part=True)
            oc_, _ = SHOFF["ropec"]; os_, _ = SHOFF["ropes"]
            rcv = dview(self.SH, oc_, (128, T)); rsv = dview(self.SH, os_, (128, T))
            npj = 0
            ntile = (T + 511) // 512
            for tt in range(ntile):
                t0 = tt * 512
                n_ = min(512, T - t0)
                nsub = n_ // 128
                j = 0 if t0 < L else 1
                h = hT[tt % 2]
                for i in range(nsub):
                    x = xt[i % 2]; y = xn[i % 2]; s2 = ss[i % 2]
                    fw.dma("sp", x[:], self.X.t.ap()[t0 + i * 128:t0 + (i + 1) * 128, :], x.b, self.X.b)
                    fw.op("act", lambda e, x=x, s2=s2: e.activation(out=junk[:], in_=x[:], func=AF.Square, accum_out=s2[:, 0:1]),
                          R=[x.b], W=[junk.b, s2.b])
                    fw.op("act", lambda e, s2=s2: e.activation(out=s2[:, 1:2], in_=s2[:, 0:1], func=AF.Sqrt, scale=1.0 / D, bias=self.epsb[:, 0:1]),
                          R=[s2.b], W=[s2.b])
                    fw.op("dve", lambda e, s2=s2: e.reciprocal(out=s2[:, 1:2], in_=s2[:, 1:2]), R=[s2.b], W=[s2.b])
                    fw.op("dve", lambda e, x=x, y=y, s2=s2: e.tensor_scalar(out=y[:], in0=x[:], scalar1=s2[:, 1:2], scalar2=None, op0=ALU.mult),
                          R=[x.b, s2.b], W=[y.b])
                    for g in range(2):
                        for q in range(4):
                            k = 4 * g + q
                            fw.op("pe", lambda e, y=y, g=g, q=q, k=k: e.transpose(out=pt[g][:, q * 128:(q + 1) * 128], in_=y[:, k * 128:(k + 1) * 128], identity=ident),
                                  R=[y.b, cb_], W=[pt[g].b])
                        for q in range(4):
                            k = 4 * g + q
                            fw.op("dve", lambda e, g=g, q=q, k=k, i=i, h=h, j=j: e.tensor_scalar(
                                out=h[:, k, i * 128:(i + 1) * 128], in0=pt[g][:, q * 128:(q + 1) * 128],
                                scalar1=self.gs1[:, k, j:j + 1], scalar2=self.sh1[:, k, j:j + 1], op0=ALU.mult, op1=ALU.add),
                                R=[pt[g].b, self.gs1.b, self.sh1.b], W=[h.b])
                c_ = rc[tt % 2]; s_ = rs_[tt % 2]
                fw.dma("sp", c_[:, 0:n_], rcv[:, t0:t0 + n_], c_.b, self.SH.b)
                fw.dma("sp", s_[:, 0:n_], rsv[:, t0:t0 + n_], s_.b, self.SH.b)
                for i in range(nsub):
                    p = pj[npj % 3]; npj += 1
                    for k in range(8):
                        fw.op("pe", lambda e, p=p, k=k, i=i, h=h: e.matmul(p[:, 0:512], h[:, k, i * 128:(i + 1) * 128], Wm[:, k, 0:512], start=(k == 0), stop=(k == 7)),
                              R=[h.b, Wm.b], W=[p.b])
                    f = fhs[i % 2]
                    fw.op("act", lambda e, p=p, f=f: e.activation(out=f[:], in_=p[:, 0:512], func=AF.Copy), R=[p.b], W=[f.b])
                    r0 = self.pfh_row(t0 + i * 128)
                    fw.dma("sp", self.PFH.t.ap()[r0:r0 + 128, :], f[:], self.PFH.b, f.b, part=True)
                    p = pj[npj % 3]; npj += 1
                    for k in range(8):
                        fw.op("pe", lambda e, p=p, k=k, i=i, h=h: e.matmul(p[:, 0:256], h[:, k, i * 128:(i + 1) * 128], Wm[:, k, 1024:1280], start=(k == 0), stop=(k == 7)),
                              R=[h.b, Wm.b], W=[p.b])
                    v = vs[i % 2]
                    fw.op("act", lambda e, p=p, v=v: e.activation(out=v[:], in_=p[:, 0:256], func=AF.Copy), R=[p.b], W=[v.b])
                    fw.dma("sp", self.V.t.ap()[t0 + i * 128:t0 + (i + 1) * 128, :], v[:], self.V.b, v.b, part=True)
                for c in range(4):
                    p = pj[npj % 3]; npj += 1
                    c0 = 512 + c * 128
                    for k in range(8):
                        fw.op("pe", lambda e, p=p, k=k, h=h, n_=n_, c0=c0: e.matmul(p[:, 0:n_], Wm[:, k, c0:c0 + 128], h[:, k, 0:n_], start=(k == 0), stop=(k == 7)),
                              R=[h.b, Wm.b], W=[p.b])
                    fw.op("act", lambda e, p=p, n_=n_: e.activation(out=sq[:, 0:n_], in_=p[:, 0:n_], func=AF.Square), R=[p.b], W=[sq.b])
                    pa = pn[0]
                    fw.op("pe", lambda e, pa=pa, n_=n_: e.matmul(pa[:, 0:n_], bones, sq[:, 0:n_], start=True, stop=True), R=[sq.b, cb_], W=[pa.b])
                    fw.op("act", lambda e, pa=pa, n_=n_: e.activation(out=rq[:, 0:n_], in_=pa[:, 0:n_], func=AF.Sqrt, scale=1.0 / 64, bias=self.epsb[:, 0:1]),
                          R=[pa.b], W=[rq.b])
                    fw.op("dve", lambda e, n_=n_: e.reciprocal(out=rq[:, 0:n_], in_=rq[:, 0:n_]), R=[rq.b], W=[rq.b])
                    gi = 0 if c < 2 else 1
                    fw.op("dve", lambda e, p=p, n_=n_, gi=gi: e.scalar_tensor_tensor(out=qn[:, 0:n_], in0=p[:, 0:n_], scalar=self.qks[:, gi:gi + 1], in1=rq[:, 0:n_],
                                                                                   op0=ALU.mult, op1=ALU.mult), R=[p.b, rq.b, self.qks.b], W=[qn.b])
                    pb = pn[1]
                    fw.op("pe", lambda e, pb=pb, n_=n_: e.matmul(pb[:, 0:n_], rot, qn[:, 0:n_], start=True, stop=True), R=[qn.b, cb_], W=[pb.b])
                    fw.op("dve", lambda e, n_=n_, c_=c_: e.tensor_tensor(out=t1[:, 0:n_], in0=qn[:, 0:n_], in1=c_[:, 0:n_], op=ALU.mult), R=[qn.b, c_.b], W=[t1.b])
                    fw.op("dve", lambda e, pb=pb, n_=n_, s_=s_: e.tensor_tensor(out=t2[:, 0:n_], in0=pb[:, 0:n_], in1=s_[:, 0:n_], op=ALU.mult), R=[pb.b, s_.b], W=[t2.b])
                    o_ = qo[c % 2]
                    fw.op("dve", lambda e, n_=n_, o_=o_: e.tensor_tensor(out=o_[:, 0:n_], in0=t1[:, 0:n_], in1=t2[:, 0:n_], op=ALU.add), R=[t1.b, t2.b], W=[o_.b])
                    dstT = self.QT if c < 2 else self.KT
                    cc = c % 2
                    fw.dma("sp", dstT.t.ap()[cc * 128:(cc + 1) * 128, t0:t0 + n_], o_[:, 0:n_], dstT.b, o_.b, part=True)
                for c4 in range(6):
                    g_ = gst[c4 % 2]
                    for c in range(4):
                        cg = c4 * 4 + c
                        p = pj[npj % 3]; npj += 1
                        for k in range(8):
                            fw.op("pe", lambda e, p=p, k=k, h=h, n_=n_, cg=cg: e.matmul(p[:, 0:n_], Wg[:, k, cg * 128:(cg + 1) * 128], h[:, k, 0:n_], start=(k == 0), stop=(k == 7)),
                                  R=[h.b, Wg.b], W=[p.b])
                        fw.op("act", lambda e, p=p, n_=n_, g_=g_, c=c: e.activation(out=g_[:, c, 0:n_], in_=p[:, 0:n_], func=AF.Sigmoid), R=[p.b], W=[g_.b])
                    fw.dma("sp", self.GT.t.ap()[c4 * 512:(c4 + 1) * 512, t0:t0 + n_].rearrange("(c p) t -> p c t", p=128), g_[:, :, 0:n_], self.GT.b, g_.b, part=True)
            fw.fence()

    def build(self):
        fw = self.fw
        with contextlib.ExitStack() as st:
            self.cst = fw.sb(st, "cstt", [128, CST.n]); self.cstb = self.cst.b
            self.modT = fw.sb(st, "modT", [128, 48, 2])
            self.gs1 = fw.sb(st, "gs1", [128, 8, 2]); self.sh1 = fw.sb(st, "sh1", [128, 8, 2])
            self.gs2 = fw.sb(st, "gs2", [128, 8, 2]); self.sh2 = fw.sb(st, "sh2", [128, 8, 2])
            self.lam = fw.sb(st, "lam", [128, 2])
            self.qks = fw.sb(st, "qks", [128, 4])
            self.epsb = fw.sb(st, "epsb", [128, 2])
            fw.dma("sp", self.cst[:], self.cst_d.t.ap(), self.cst.b, self.cst_d.b)
            fw.op("dve", lambda e: e.memset(self.epsb[:, 0:1], EPS), W=[self.epsb.b])
            fw.op("dve", lambda e: e.memset(self.epsb[:, 1:2], SUBLN_EPS), W=[self.epsb.b])
            self.phase_gather()
            for l in range(DEPTH if self.stop != ("gather", 0) else 0):
                self.phase_mod(l)
                if self.stop == ("mod", l):
                    break
                self.phase_proj(l)
                if self.stop == ("proj", l):
                    break
                self.phase_seqmix(l, "lat")
                if l == 0:
                    self.phase_seqmix(l, "ctx")
                if self.stop == ("seq", l):
                    break
                self.phase_attn(l)
                if self.stop == ("attn", l):
                    break
                self.phase_outproj(l)
                if self.stop == ("outp", l):
                    break
                if self.with_exp:
                    self.phase_moe(l)
            for nm, tl in (("QT", self.QT), ("KT", self.KT), ("V", self.V), ("PFH", self.PFH), ("GT", self.GT), ("X", self.X), ("MODBC", self.MODBC), ("FM", self.FM), ("HM", self.HM), ("OT", self.OT)):
                self.dump(nm, tl)
            self.bigcopy(self.out, self.out.t.ap(), self.X, self.X.t.ap(), L, 512)
            fw.fence()
        fw.close()
        return self.nc


FFT_CFG = {"FA": (64, 64), "HB": (128, 64), "FC": (2, 2), "HD": (4, 2)}


def make_fft_layout():
    P = Pack()
    for nm in ("w128r", "w128i", "w128n"):
        P.add(nm, 128)
    for cfg, (n1, n1in) in FFT_CFG.items():
        for nm in ("w1r", "w1i", "w1n"):
            P.add(cfg + nm, n1)
        P.add(cfg + "tr", 128); P.add(cfg + "ti", 128); P.add(cfg + "tn", 128)
        if cfg in ("HB", "HD"):
            for nm in ("v1r", "v1i", "v1n"):
                P.add(cfg + nm, n1in)
    return P


FFTC = make_fft_layout()


def fft_constants():
    c = np.zeros((128, FFTC.n), np.float64)

    def put(name, val):
        o, w = FFTC.off[name]
        c[:val.shape[0], o:o + w] = val

    a = np.arange(128)
    w128 = np.exp(-2j * np.pi * np.outer(a, a) / 128)
    put("w128r", w128.real); put("w128i", w128.imag); put("w128n", -w128.imag)
    for cfg, (n1, n1in) in FFT_CFG.items():
        N = n1 * 128
        b = np.arange(n1)
        w1 = np.exp(-2j * np.pi * np.outer(b, b) / n1)
        put(cfg + "w1r", w1.real); put(cfg + "w1i", w1.imag); put(cfg + "w1n", -w1.imag)
        tw = np.exp(-2j * np.pi * np.outer(b, a) / N)
        put(cfg + "tr", tw.real); put(cfg + "ti", tw.imag); put(cfg + "tn", -tw.imag)
        if cfg in ("HB", "HD"):
            v1 = np.exp(+2j * np.pi * np.outer(b, np.arange(n1in)) / n1) / N
            put(cfg + "v1r", v1.real); put(cfg + "v1i", v1.imag); put(cfg + "v1n", -v1.imag)
    return c.astype(np.float32)


CG = 32
NCOL = 128 * CG


def _fc(self, name):
    o, w = FFTC.off[name]
    return self.fftc[:, o:o + w]


def _mm_blocks(self, outs, terms, M, K, ncols):
    fw = self.fw
    nb = 0
    for c0 in range(0, ncols, 512):
        cw = min(512, ncols - c0)
        for oi, (ot, tl) in enumerate(zip(outs, terms)):
            p = self.fps[self.nfps % len(self.fps)]; self.nfps += 1
            for ti, (lh, rt) in enumerate(tl):
                fw.op("pe", lambda e, p=p, lh=lh, rt=rt, c0=c0, cw=cw, ti=ti, n=len(tl): e.matmul(
                    p[0:M, 0:cw], lh, rt[0:K, c0:c0 + cw], start=(ti == 0), stop=(ti == n - 1)),
                    R=[rt.b, self.fftcb.b], W=[p.b])
            if nb % 2 == 0:
                fw.op("act", lambda e, p=p, ot=ot, c0=c0, cw=cw: e.activation(out=ot[0:M, c0:c0 + cw], in_=p[0:M, 0:cw], func=AF.Copy),
                      R=[p.b], W=[ot.b])
            else:
                fw.op("dve", lambda e, p=p, ot=ot, c0=c0, cw=cw: e.tensor_copy(out=ot[0:M, c0:c0 + cw], in_=p[0:M, 0:cw]),
                      R=[p.b], W=[ot.b])
            nb += 1


def _cmul(self, outr, outi, ar, ai, br, bi, P, ncols, t1, t2, bshape=None):
    fw = self.fw

    def A(t):
        return t[0:P, 0:ncols]

    def Bv(t):
        return t if bshape else t[0:P, 0:ncols]

    def V(t):
        return t[0:P, 0:ncols].rearrange("p (a c) -> p a c", c=bshape) if bshape else t[0:P, 0:ncols]
    xr = [self.fftc.b] if bshape else [br.b]
    xi = [self.fftc.b] if bshape else [bi.b]
    fw.op("dve", lambda e: e.tensor_tensor(out=V(t1), in0=V(ar), in1=Bv(br), op=ALU.mult), R=[ar.b] + xr, W=[t1.b])
    fw.op("dve", lambda e: e.tensor_tensor(out=V(t2), in0=V(ai), in1=Bv(bi), op=ALU.mult), R=[ai.b] + xi, W=[t2.b])
    fw.op("dve", lambda e: e.tensor_tensor(out=A(outr), in0=A(t1), in1=A(t2), op=ALU.subtract), R=[t1.b, t2.b], W=[outr.b])
    fw.op("dve", lambda e: e.tensor_tensor(out=V(t1), in0=V(ar), in1=Bv(bi), op=ALU.mult), R=[ar.b] + xi, W=[t1.b])
    fw.op("dve", lambda e: e.tensor_tensor(out=V(t2), in0=V(ai), in1=Bv(br), op=ALU.mult), R=[ai.b] + xr, W=[t2.b])
    fw.op("dve", lambda e: e.tensor_tensor(out=A(outi), in0=A(t1), in1=A(t2), op=ALU.add), R=[t1.b, t2.b], W=[outi.b])


def _dtrans(self, src, P1, dst):
    fw = self.fw
    sc = self.tscr[self.ntscr % len(self.tscr)]; self.ntscr += 1
    fw.dma("sp", sc.t.ap()[0:P1, :], src[0:P1, 0:NCOL], sc.b, src.b)
    v = sc.t.ap()[0:P1, :].rearrange("a (j c) -> j a c", c=CG)
    fw.dma("sp", dst[:, 0:P1 * CG].rearrange("p (a c) -> p a c", c=CG), v, dst.b, sc.b)


def _fcb(self, name):
    o, w = FFTC.off[name]
    return self.fftcb[:, o:o + w]


def _fft_fwd(self, cfg, x, Xr, Xi, W):
    n1, n1in = FFT_CFG[cfg]
    a_r, a_i, t1, t2 = W[0], W[1], W[2], W[3]
    B = self.fB
    fc = self._fc; fb = self._fcb
    w1r = fb(cfg + "w1r")[0:n1in, 0:n1]; w1i = fb(cfg + "w1i")[0:n1in, 0:n1]
    self._mm_blocks([a_r, a_i], [[(w1r, x)], [(w1i, x)]], n1, n1in, NCOL)
    tr = fc(cfg + "tr")[0:n1, :].unsqueeze(2).broadcast_to([n1, 128, CG])
    ti = fc(cfg + "ti")[0:n1, :].unsqueeze(2).broadcast_to([n1, 128, CG])
    self._cmul(B[0], B[1], a_r, a_i, tr, ti, n1, NCOL, t1, t2, bshape=CG)
    self._dtrans(B[0], n1, B[2])
    self._dtrans(B[1], n1, B[3])
    wr = fb("w128r"); wi = fb("w128i"); wn = fb("w128n")
    self._mm_blocks([Xr, Xi], [[(wr, B[2]), (wn, B[3])], [(wi, B[2]), (wr, B[3])]], 128, 128, n1 * CG)


def _fft_inv(self, cfg, Yr, Yi, y, W):
    n1, n1in = FFT_CFG[cfg]
    t1, t2 = W[2], W[3]
    B = self.fB
    fc = self._fc; fb = self._fcb
    wr = fb("w128r"); wi = fb("w128i"); wn = fb("w128n")
    self._mm_blocks([B[2], B[3]], [[(wr, Yr), (wi, Yi)], [(wn, Yr), (wr, Yi)]], 128, 128, n1 * CG)
    self._dtrans_back(B[2], n1, B[0])
    self._dtrans_back(B[3], n1, B[1])
    tr = fc(cfg + "tr")[0:n1, :].unsqueeze(2).broadcast_to([n1, 128, CG])
    tn = fc(cfg + "tn")[0:n1, :].unsqueeze(2).broadcast_to([n1, 128, CG])
    self._cmul(B[2], B[3], B[0], B[1], tr, tn, n1, NCOL, t1, t2, bshape=CG)
    v1r = fb(cfg + "v1r")[0:n1, 0:n1in]; v1n = fb(cfg + "v1n")[0:n1, 0:n1in]
    self._mm_blocks([y], [[(v1r, B[2]), (v1n, B[3])]], n1in, n1, NCOL)


def _dtrans_back(self, src, P1, dst):
    fw = self.fw
    sc = self.tscr[self.ntscr % len(self.tscr)]; self.ntscr += 1
    v = sc.t.ap()[0:P1, :].rearrange("a (j c) -> j a c", c=CG)
    fw.dma("sp", v, src[:, 0:P1 * CG].rearrange("p (a c) -> p a c", c=CG), sc.b, src.b)
    fw.dma("sp", dst[0:P1, 0:NCOL], sc.t.ap()[0:P1, :], dst.b, sc.b)


for _f in (_fc, _fcb, _mm_blocks, _cmul, _dtrans, _dtrans_back, _fft_fwd, _fft_inv):
    setattr(Prog, _f.__name__, _f)

MAGIC = 12582912.0
TWO_PI = 2.0 * math.pi


def _rr_sin(self, out, arg, tmp, P, n):
    fw = self.fw
    fw.op("dve", lambda e: e.tensor_scalar(out=tmp[0:P, 0:n], in0=arg[0:P, 0:n], scalar1=1.0 / TWO_PI, scalar2=MAGIC, op0=ALU.mult, op1=ALU.add),
          R=[arg.b], W=[tmp.b])
    fw.op("dve", lambda e: e.tensor_scalar(out=tmp[0:P, 0:n], in0=tmp[0:P, 0:n], scalar1=-MAGIC, scalar2=None, op0=ALU.add), R=[tmp.b], W=[tmp.b])
    fw.op("dve", lambda e: e.scalar_tensor_tensor(out=tmp[0:P, 0:n], in0=tmp[0:P, 0:n], scalar=-TWO_PI, in1=arg[0:P, 0:n], op0=ALU.mult, op1=ALU.add),
          R=[tmp.b, arg.b], W=[tmp.b])
    fw.op("dve", lambda e: e.tensor_scalar(out=tmp[0:P, 0:n], in0=tmp[0:P, 0:n], scalar1=3.14159, scalar2=-3.14159, op0=ALU.min, op1=ALU.max),
          R=[tmp.b], W=[tmp.b])
    fw.op("act", lambda e: e.activation(out=out[0:P, 0:n], in_=tmp[0:P, 0:n], func=AF.Sin), R=[tmp.b], W=[out.b])


def _gen_filter(self, l, ps, Lseq, ecol0, ntl, HF, sml, rn):
    fw = self.fw
    hw1, hb1, hf1, hw2, hb2, hf2, hw3, hb3, dbc = sml
    et = fw.sb(ps, "f_e", [33, 512]); arg = fw.sb(ps, "f_arg", [64, 512]); tmp = fw.sb(ps, "f_tmp", [64, 512])
    z1 = fw.sb(ps, "f_z1", [64, 512]); z2 = fw.sb(ps, "f_z2", [64, 512])
    hh = [fw.sb(ps, "f_h0", [128, 512])]
    ab = fw.sb(ps, "f_ab", [128, 512]); dec = fw.sb(ps, "f_dec", [128, 128])
    nrm = ab
    p1 = self.fps[0]; p2 = self.fps[1]; p3 = self.fps[2]; pN = self.fps[3]
    ones = self.cv("ones")
    nsub_tot = Lseq // 128
    si = 0
    for t0 in range(0, Lseq, 512):
        n = min(512, Lseq - t0)
        fw.dma("sp", et[:, 0:n], self.embc.t.ap()[:, ecol0 + t0:ecol0 + t0 + n], et.b, self.embc.b)
        fw.op("pe", lambda e, n=n: e.matmul(p1[0:64, 0:n], hw1[0:33, :], et[:, 0:n], start=True, stop=True), R=[et.b, hw1.b], W=[p1.b])
        fw.op("dve", lambda e, n=n: e.tensor_scalar(out=arg[:, 0:n], in0=p1[0:64, 0:n], scalar1=hb1[0:64, 0:1], scalar2=hf1[0:64, 0:1], op0=ALU.add, op1=ALU.mult),
              R=[p1.b, hb1.b, hf1.b], W=[arg.b])
        self._rr_sin(z1, arg, tmp, 64, n)
        fw.op("pe", lambda e, n=n: e.matmul(p2[0:64, 0:n], hw2[0:64, :], z1[:, 0:n], start=True, stop=True), R=[z1.b, hw2.b], W=[p2.b])
        fw.op("dve", lambda e, n=n: e.tensor_scalar(out=arg[:, 0:n], in0=p2[0:64, 0:n], scalar1=hb2[0:64, 0:1], scalar2=hf2[0:64, 0:1], op0=ALU.add, op1=ALU.mult),
              R=[p2.b, hb2.b, hf2.b], W=[arg.b])
        self._rr_sin(z2, arg, tmp, 64, n)
        for i in range(n // 128):
            h = hh[0]
            fw.op("pe", lambda e, i=i: e.matmul(p3[:, :], z2[:, i * 128:(i + 1) * 128], hw3[0:64, :], start=True, stop=True), R=[z2.b, hw3.b], W=[p3.b])
            fw.op("dve", lambda e, h=h: e.tensor_tensor(out=h[:], in0=p3[:], in1=hb3[:], op=ALU.add), R=[p3.b, hb3.b], W=[h.b])
            fw.op("act", lambda e, si=si: e.activation(out=dec[:], in_=dbc[:], func=AF.Exp, scale=ntl[:, si:si + 1]), R=[dbc.b, self.cstb], W=[dec.b])
            fw.op("dve", lambda e, h=h: e.tensor_tensor(out=h[:].rearrange("p (g c) -> p g c", c=128), in0=h[:].rearrange("p (g c) -> p g c", c=128),
                                                        in1=dec[:].unsqueeze(1).broadcast_to([128, 4, 128]), op=ALU.mult), R=[h.b, dec.b], W=[h.b])
            if si == 0:
                for c0 in (128, 384):
                    fw.op("dve", lambda e, h=h, c0=c0: e.memset(h[0:1, c0:c0 + 128], 0.0), W=[h.b])
            fw.op("act", lambda e, h=h: e.activation(out=ab[:], in_=h[:], func=AF.Abs), R=[h.b], W=[ab.b])
            fw.op("pe", lambda e, si=si: e.matmul(pN[:], ones, ab[:], start=(si == 0), stop=(si == nsub_tot - 1)), R=[ab.b, self.cstb], W=[pN.b])
            fw.dma("sp", HF.t.ap()[t0 + i * 128:t0 + (i + 1) * 128, :], h[:], HF.b, h.b, part=True)
            si += 1
    fw.op("dve", lambda e: e.tensor_copy(out=nrm[:], in_=pN[:]), R=[pN.b], W=[nrm.b])
    nv = nrm[:].rearrange("p (o d c) -> p o d c", o=2, d=2)
    fw.op("dve", lambda e: e.tensor_tensor(out=rn[:], in0=nv[:, :, 0, :], in1=nv[:, :, 1, :], op=ALU.add), R=[nrm.b], W=[rn.b])
    fw.op("dve", lambda e: e.reciprocal(out=rn[:], in_=rn[:]), R=[rn.b], W=[rn.b])
    return rn


def _seq_cfg(self, which):
    if which == "lat":
        return "FA", "HB", L, 1, 0, 0, "ntl8192"
    return "FC", "HD", NCTX, L + 3, L, L, "ntl256"


def phase_seqmix(self, l, which):
    fw = self.fw
    fcfg, hcfg, Lseq, prow, tbase, ecol0, ntlname = self._seq_cfg(which)
    n1f, n1fin = FFT_CFG[fcfg]
    n1h, n1hin = FFT_CFG[hcfg]
    with contextlib.ExitStack() as ps:
        self.fftc = fw.sb(ps, "fftc_t", [128, FFTC.n])
        fw.dma("sp", self.fftc[:], self.fftc_d.t.ap(), self.fftc.b, self.fftc_d.b)
        self.fps = [fw.ps(ps, "fps%d" % i) for i in range(6)]
        self.nfps = 0
        self.tscr = [fw.dram("tscr%d_%d_%s" % (i, l, which), [128, NCOL], BF16) for i in range(2)]
        self.fftcb = fw.sb(ps, "fftcb_t", [128, FFTC.n], BF16)
        fw.op("act", lambda e: e.activation(out=self.fftcb[:], in_=self.fftc[:], func=AF.Copy), R=[self.fftc.b], W=[self.fftcb.b])
        self.fB = [fw.sb(ps, "fB%d" % i, [128, NCOL], BF16) for i in range(4)]
        self.ntscr = 0
        W = [fw.sb(ps, "fw%d" % i, [128, NCOL]) for i in range(4)]
        Xr = fw.sb(ps, "fXr", [128, NCOL]); Xi = fw.sb(ps, "fXi", [128, NCOL])
        zv = fw.sb(ps, "fzv", [64, NCOL], BF16); zx = fw.sb(ps, "fzx", [64, NCOL], BF16)
        halo = fw.sb(ps, "fhalo", [64, 130, CG])

        def ldcast(src_ap, srcb, rows):
            fw.dma("sp", W[0][0:rows, :].rearrange("p (j c) -> p j c", c=CG), src_ap, W[0].b, srcb)
            fw.op("act", lambda e: e.activation(out=zv[0:rows, :], in_=W[0][0:rows, :], func=AF.Copy), R=[W[0].b], W=[zv.b])
            return zv
        for cg in range(4):
            src = self.PFH.t.ap()[prow:prow + Lseq, cg * CG:(cg + 1) * CG].rearrange("(a j) c -> a j c", j=128)
            self._fft_fwd(fcfg, ldcast(src, self.PFH.b, n1fin), Xr, Xi, W)
            for ri, X_ in ((0, Xr), (1, Xi)):
                dst = self.FM.t.ap()[tbase:tbase + Lseq, ri, cg * CG:(cg + 1) * CG].rearrange("(p a) c -> p a c", a=n1f)
                fw.dma("sp", dst, X_[:, 0:n1f * CG].rearrange("p (a c) -> p a c", c=CG), self.FM.b, X_.b, part=True)
        rn = fw.sb(ps, "f_rn", [128, 2, 128])
        with contextlib.ExitStack() as ps2:
            sml = [self.ld(ps2, "hw1%d" % l, 33), self.ld(ps2, "hb1%d" % l, 64), self.ld(ps2, "hf1%d" % l, 64),
                   self.ld(ps2, "hw2%d" % l, 64), self.ld(ps2, "hb2%d" % l, 64), self.ld(ps2, "hf2%d" % l, 64),
                   self.ld(ps2, "hw3%d" % l, 64), self.ld(ps2, "hb3bc%d" % l), self.ld(ps2, "deltabc")]
            o_, w_ = CST.off[ntlname]
            ntl = self.cst[:, o_:o_ + w_]
            HF = fw.dram("HF_%d_%s" % (l, which), [Lseq, 512], F32)
            self._gen_filter(l, ps2, Lseq, ecol0, ntl, HF, sml, rn)
            fw.fence()
        hbias = self.ld(ps, "hbias%d" % l)
        HS = fw.dram("HS_%d_%s" % (l, which), [2, 4, 2, 128, n1h * CG], F32)
        for o in range(2):
            for cg in range(4):
                nc_ = n1h * CG
                for d in range(2):
                    src = HF.t.ap()[:, o * 256 + d * 128 + cg * CG: o * 256 + d * 128 + (cg + 1) * CG].rearrange("(a j) c -> a j c", j=128)
                    self._fft_fwd(hcfg, ldcast(src, HF.b, n1hin), Xr, Xi, W)
                    if d == 0:
                        fw.dma("sp", HS.t.ap()[o, cg, 0], Xr[:, 0:nc_], HS.b, Xr.b)
                        fw.dma("sp", HS.t.ap()[o, cg, 1], Xi[:, 0:nc_], HS.b, Xi.b)
                fw.dma("sp", W[0][:, 0:nc_], HS.t.ap()[o, cg, 0], W[0].b, HS.b)
                fw.dma("sp", W[1][:, 0:nc_], HS.t.ap()[o, cg, 1], W[1].b, HS.b)
                rnb = rn[:, o, cg * CG:(cg + 1) * CG].unsqueeze(1).broadcast_to([128, n1h, CG])
                bb = hbias[:, o * 128 + cg * CG: o * 128 + (cg + 1) * CG].unsqueeze(1).broadcast_to([128, n1h, CG])

                def v3(t, nc_=nc_):
                    return t[:, 0:nc_].rearrange("p (a c) -> p a c", c=CG)
                fw.op("dve", lambda e, v3=v3: e.tensor_tensor(out=v3(W[0]), in0=v3(W[0]), in1=v3(Xr), op=ALU.add), R=[Xr.b, W[0].b], W=[W[0].b])
                fw.op("dve", lambda e, v3=v3: e.tensor_tensor(out=v3(W[1]), in0=v3(W[1]), in1=v3(Xi), op=ALU.subtract), R=[Xi.b, W[1].b], W=[W[1].b])
                fw.op("dve", lambda e, rnb=rnb, v3=v3: e.tensor_tensor(out=v3(W[0]), in0=v3(W[0]), in1=rnb, op=ALU.mult), R=[W[0].b, rn.b], W=[W[0].b])
                fw.op("dve", lambda e, rnb=rnb, v3=v3: e.tensor_tensor(out=v3(W[1]), in0=v3(W[1]), in1=rnb, op=ALU.mult), R=[W[1].b, rn.b], W=[W[1].b])
                fw.op("dve", lambda e, bb=bb, v3=v3: e.tensor_tensor(out=v3(W[0]), in0=v3(W[0]), in1=bb, op=ALU.add), R=[W[0].b, hbias.b], W=[W[0].b])
                fw.dma("sp", HS.t.ap()[o, cg, 0], W[0][:, 0:nc_], HS.b, W[0].b)
                fw.dma("sp", HS.t.ap()[o, cg, 1], W[1][:, 0:nc_], HS.b, W[1].b)
        cw = self.ld(ps, "cwbc%d" % l); cbv = self.ld(ps, "cbbc%d" % l)

        def shortconv(dst, wi, cg):
            col0 = 128 + wi * 128 + cg * CG
            base = self.PFH.t.ap()[prow - 1:prow - 1 + Lseq + 2, col0:col0 + CG]
            src = bass.AP(self.PFH.t, base.offset, [[128 * 512, n1hin], [512, 130], [1, CG]])
            fw.dma("sp", halo[0:n1hin, :, :], src, halo.b, self.PFH.b)
            d3 = dst[0:n1hin, :].rearrange("p (j c) -> p j c", c=CG)
            for j in range(3):
                wv = cw[0:n1hin, j * 384 + wi * 128 + cg * CG: j * 384 + wi * 128 + (cg + 1) * CG].unsqueeze(1).broadcast_to([n1hin, 128, CG])
                if j == 0:
                    fw.op("dve", lambda e, wv=wv: e.tensor_tensor(out=d3, in0=halo[0:n1hin, 0:128, :], in1=wv, op=ALU.mult), R=[halo.b, cw.b], W=[dst.b])
                else:
                    fw.op("dve", lambda e, wv=wv, j=j: e.tensor_tensor(out=W[3][0:n1hin, :].rearrange("p (j c) -> p j c", c=CG), in0=halo[0:n1hin, j:j + 128, :], in1=wv, op=ALU.mult),
                          R=[halo.b, cw.b], W=[W[3].b])
                    fw.op("dve", lambda e: e.tensor_tensor(out=dst[0:n1hin, :], in0=dst[0:n1hin, :], in1=W[3][0:n1hin, :], op=ALU.add), R=[dst.b, W[3].b], W=[dst.b])
            bv = cbv[0:n1hin, wi * 128 + cg * CG: wi * 128 + (cg + 1) * CG].unsqueeze(1).broadcast_to([n1hin, 128, CG])
            fw.op("dve", lambda e, bv=bv: e.tensor_tensor(out=d3, in0=d3, in1=bv, op=ALU.add), R=[dst.b, cbv.b], W=[dst.b])

        for cg in range(4):
            shortconv(zv, 0, cg)
            shortconv(zx, 1, cg)
            for o in range(2):
                src_t = zv if o == 0 else zx
                self._fft_fwd(hcfg, src_t, Xr, Xi, W)
                nc_ = n1h * CG
                fw.dma("sp", W[0][:, 0:nc_], HS.t.ap()[o, cg, 0], W[0].b, HS.b)
                fw.dma("sp", W[1][:, 0:nc_], HS.t.ap()[o, cg, 1], W[1].b, HS.b)
                self._cmul(self.fB[0], self.fB[1], Xr, Xi, W[0], W[1], 128, nc_, W[2], W[3])
                self._fft_inv(hcfg, self.fB[0], self.fB[1], W[0], W)
                if o == 0:
                    fw.op("dve", lambda e: e.tensor_tensor(out=zx[0:n1hin, :], in0=zx[0:n1hin, :], in1=W[0][0:n1hin, :], op=ALU.mult), R=[zx.b, W[0].b], W=[zx.b])
                else:
                    shortconv(W[1], 2, cg)
                    fw.op("dve", lambda e: e.tensor_tensor(out=W[0][0:n1hin, :], in0=W[0][0:n1hin, :], in1=W[1][0:n1hin, :], op=ALU.mult), R=[W[0].b, W[1].b], W=[W[0].b])
                    dst = self.HM.t.ap()[tbase:tbase + Lseq, cg * CG:(cg + 1) * CG].rearrange("(a j) c -> a j c", j=128)
                    fw.dma("sp", dst, W[0][0:n1hin, :].rearrange("p (j c) -> p j c", c=CG), self.HM.b, W[0].b, part=True)
        fw.fence()


def halo_flat(halo):
    return Tile(halo.t, halo.b)


for _f in (_rr_sin, _gen_filter, _seq_cfg, phase_seqmix):
    setattr(Prog, _f.__name__, _f)


def phase_attn(self, l):
    fw = self.fw
    lam_init = 0.8 - 0.6 * math.exp(-0.3 * l)
    with contextlib.ExitStack() as ps:
        KTs = [fw.sb(ps, "aKT%d" % h, [128, T], BF16) for h in range(2)]
        Vs = fw.sb(ps, "aV", [128, T // 128, 256], BF16)
        onesb = fw.sb(ps, "aones", [128, 128], BF16)
        Qs = [fw.sb(ps, "aQ%d" % i, [128, 512], BF16) for i in range(2)]
        pts = [fw.sb(ps, "apt%d" % i, [128, 512], BF16) for i in range(4)]
        r0 = fw.sb(ps, "ar0", [128, 512]); r1 = fw.sb(ps, "ar1", [128, 512])
        a0 = fw.sb(ps, "aa0", [128, 512]); a1 = fw.sb(ps, "aa1", [128, 512])
        sq = fw.sb(ps, "asq", [128, 512])
        ob = [fw.sb(ps, "aob%d" % i, [128, 512], BF16) for i in range(2)]
        pss = [fw.ps(ps, "aps%d" % i) for i in range(3)]
        po = [fw.ps(ps, "apo%d" % i) for i in range(2)]
        pz = [fw.ps(ps, "apz%d" % i) for i in range(2)]
        ones = self.cv("ones")
        fw.op("dve", lambda e: e.tensor_copy(out=onesb[:], in_=ones), R=[self.cstb], W=[onesb.b])
        for h in range(2):
            fw.dma("sp", KTs[h][:], self.KT.t.ap()[h * 128:(h + 1) * 128, :], KTs[h].b, self.KT.b)
        fw.dma("sp", Vs[:], self.V.t.ap().rearrange("(ch p) c -> p ch c", p=128), Vs.b, self.V.b)
        qtiles = [(q0, 512, list(range(T // 128))) for q0 in range(0, L, 512)]
        if l == 0:
            qtiles.append((L, NCTX, [L // 128, L // 128 + 1]))
        nq = 0; npt = 0; nps = 0
        for (q0, n_, kcs) in qtiles:
            for h in range(2):
                Q = Qs[nq % 2]; nq += 1
                fw.dma("sp", Q[:, 0:n_], self.QT.t.ap()[h * 128:(h + 1) * 128, q0:q0 + n_], Q.b, self.QT.b)
                LOOK = 2
                for m in range(2):
                    nk = len(kcs)
                    ptq = {}
                    for kk in range(nk + LOOK):
                        if kk < nk:
                            kc = kcs[kk]
                            s_ = pss[nps % 3]; nps += 1
                            fw.op("pe", lambda e, s_=s_, h=h, m=m, kc=kc, Q=Q, n_=n_: e.matmul(
                                s_[:, 0:n_], KTs[h][m * 64:(m + 1) * 64, kc * 128:(kc + 1) * 128], Q[m * 64:(m + 1) * 64, 0:n_], start=True, stop=True),
                                R=[KTs[h].b, Q.b], W=[s_.b])
                            pt = pts[npt % 4]; npt += 1
                            fw.op("act", lambda e, s_=s_, pt=pt, n_=n_: e.activation(out=pt[:, 0:n_], in_=s_[:, 0:n_], func=AF.Exp, scale=0.125),
                                  R=[s_.b], W=[pt.b])
                            ptq[kk] = pt
                        ki = kk - LOOK
                        if ki >= 0:
                            kc = kcs[ki]; pt = ptq.pop(ki)
                            fw.op("pe", lambda e, pt=pt, h=h, m=m, kc=kc, ki=ki, n_=n_, nk=nk: e.matmul(
                                po[m][:, 0:n_], Vs[:, kc, h * 128:(h + 1) * 128], pt[:, 0:n_], start=(ki == 0), stop=(ki == nk - 1)),
                                R=[Vs.b, pt.b], W=[po[m].b])
                            fw.op("pe", lambda e, pt=pt, m=m, ki=ki, n_=n_, nk=nk: e.matmul(
                                pz[m][:, 0:n_], onesb[:], pt[:, 0:n_], start=(ki == 0), stop=(ki == nk - 1)),
                                R=[onesb.b, pt.b], W=[pz[m].b])
                fw.op("dve", lambda e, n_=n_: e.reciprocal(out=r0[:, 0:n_], in_=pz[0][:, 0:n_]), R=[pz[0].b], W=[r0.b])
                fw.op("dve", lambda e, n_=n_: e.reciprocal(out=r1[:, 0:n_], in_=pz[1][:, 0:n_]), R=[pz[1].b], W=[r1.b])
                fw.op("dve", lambda e, n_=n_: e.tensor_tensor(out=a0[:, 0:n_], in0=po[0][:, 0:n_], in1=r0[:, 0:n_], op=ALU.mult), R=[po[0].b, r0.b], W=[a0.b])
                fw.op("dve", lambda e, n_=n_: e.tensor_tensor(out=a1[:, 0:n_], in0=po[1][:, 0:n_], in1=r1[:, 0:n_], op=ALU.mult), R=[po[1].b, r1.b], W=[a1.b])
                fw.op("dve", lambda e, n_=n_: e.scalar_tensor_tensor(out=a0[:, 0:n_], in0=a1[:, 0:n_], scalar=self.lam[:, 1:2], in1=a0[:, 0:n_], op0=ALU.mult, op1=ALU.add),
                      R=[a0.b, a1.b, self.lam.b], W=[a0.b])
                fw.op("act", lambda e, n_=n_: e.activation(out=sq[:, 0:n_], in_=a0[:, 0:n_], func=AF.Square), R=[a0.b], W=[sq.b])
                s_ = pss[nps % 3]; nps += 1
                fw.op("pe", lambda e, s_=s_, n_=n_: e.matmul(s_[:, 0:n_], ones, sq[:, 0:n_], start=True, stop=True), R=[sq.b, self.cstb], W=[s_.b])
                fw.op("act", lambda e, s_=s_, n_=n_: e.activation(out=r0[:, 0:n_], in_=s_[:, 0:n_], func=AF.Sqrt, scale=1.0 / 128, bias=self.epsb[:, 1:2]), R=[s_.b], W=[r0.b])
                fw.op("dve", lambda e, n_=n_: e.reciprocal(out=r0[:, 0:n_], in_=r0[:, 0:n_]), R=[r0.b], W=[r0.b])
                fw.op("dve", lambda e, n_=n_: e.scalar_tensor_tensor(out=a0[:, 0:n_], in0=a0[:, 0:n_], scalar=self.qks[:, 2:3], in1=r0[:, 0:n_], op0=ALU.mult, op1=ALU.mult),
                      R=[a0.b, r0.b, self.qks.b], W=[a0.b])
                o_ = ob[nq % 2]
                fw.op("dve", lambda e, n_=n_, o_=o_: e.tensor_scalar(out=o_[:, 0:n_], in0=a0[:, 0:n_], scalar1=(1.0 - lam_init), scalar2=None, op0=ALU.mult), R=[a0.b], W=[o_.b])
                fw.dma("sp", self.OT.t.ap()[h * 128:(h + 1) * 128, q0:q0 + n_], o_[:, 0:n_], self.OT.b, o_.b, part=True)
        fw.fence()


def allreduce_to_X(self, nrows):
    items = []
    for r0 in range(0, nrows, 1024):
        r1 = min(nrows, r0 + 1024)
        items.append(("AllReduce", PAIRS, self.ARin.t.ap()[r0:r1, :], self.X.t.ap()[r0:r1, :]))
    self.coll_seq(items)
    self.fw.fence()


def phase_outproj(self, l):
    fw = self.fw
    ntok = T if l == 0 else L
    with contextlib.ExitStack() as ps:
        Wfc = [fw.sb(ps, "oWc%d" % i, [128, 1024], BF16) for i in range(2)]
        Wfs = [fw.sb(ps, "oWs%d" % i, [128, 1024], BF16) for i in range(2)]
        Wh = fw.sb(ps, "oWh", [128, 1024], BF16)
        Wa = fw.sb(ps, "oWa", [128, 2, 1024], BF16)
        Wo = fw.sb(ps, "oWo", [128, 8, 1024], BF16)
        stg = fw.sb(ps, "ostg", [128, 8, 1024])
        wf32 = fw.sb(ps, "owf", [128, 1024])
        pp = [fw.ps(ps, "opp%d" % i) for i in range(7)]
        npp = 0
        c64 = self.cv("c64"); s64 = self.cv("s64"); ident = self.cv("ident")
        wov = self.wout.t.ap()[l]
        fw.dma("sp", wf32[:], wov[0:128, :], wf32.b, self.wout.b)
        for (cm, dsts) in ((c64, Wfc), (s64, Wfs)):
            for half in range(2):
                p = pp[npp % 7]; npp += 1
                fw.op("pe", lambda e, p=p, cm=cm, half=half: e.matmul(p[:], cm, wf32[:, half * 512:(half + 1) * 512], start=True, stop=True), R=[wf32.b, self.cstb], W=[p.b])
                for i, Ls in enumerate((L, NCTX)):
                    fw.op("act", lambda e, p=p, d=dsts[i], half=half, Ls=Ls: e.activation(out=d[:, half * 512:(half + 1) * 512], in_=p[:], func=AF.Copy, scale=1.0 / math.sqrt(64.0 * Ls)),
                          R=[p.b], W=[dsts[i].b])
        fw.dma("sp", stg[:, 0, :], wov[128:256, :], stg.b, self.wout.b)
        fw.op("dve", lambda e: e.tensor_copy(out=Wh[:], in_=stg[:, 0, :]), R=[stg.b], W=[Wh.b])
        fw.dma("sp", stg[:, 0:2, :], wov[256:512, :].rearrange("(k p) c -> p k c", p=128), stg.b, self.wout.b)
        fw.op("dve", lambda e: e.tensor_copy(out=Wa[:], in_=stg[:, 0:2, :]), R=[stg.b], W=[Wa.b])
        oo, _ = SHOFF["wo%d" % l]
        fw.dma("sp", stg[:], dview(self.SH, oo, (1024, 1024)).rearrange("(k p) c -> p k c", p=128), stg.b, self.SH.b)
        fw.op("act", lambda e: e.activation(out=Wo[:], in_=stg[:], func=AF.Copy), R=[stg.b], W=[Wo.b])
        fm = fw.sb(ps, "ofm", [128, 4, 2, 128]); hm = fw.sb(ps, "ohm", [128, 4, 128])
        FrT = fw.sb(ps, "oFr", [128, 512], BF16); FiT = fw.sb(ps, "oFi", [128, 512], BF16); HT = fw.sb(ps, "oHT", [128, 512], BF16)
        OTt = fw.sb(ps, "oOT", [128, 2, 512], BF16)
        G = fw.sb(ps, "oG", [128, 24, 512], BF16)
        xt = fw.sb(ps, "oxt", [128, 4, 1024])
        mb = fw.sb(ps, "omb", [128, 1024])
        m1 = fw.sb(ps, "om1", [128, 512]); m2 = fw.sb(ps, "om2", [128, 512])
        M = fw.sb(ps, "oM", [128, 8, 512], BF16)
        tz = fw.sb(ps, "otz", [128, 512]); ar = [fw.sb(ps, "oar%d" % i, [128, 512]) for i in range(2)]
        lastj = -1
        nar = 0
        for t0 in range(0, ntok, 512):
            n_ = min(512, ntok - t0); nsub = n_ // 128
            j = 0 if t0 < L else 1
            if j != lastj:
                fw.dma("sp", mb[:], self.MODBC.t.ap()[:, j * 2048:j * 2048 + 1024], mb.b, self.MODBC.b)
                lastj = j
            fw.dma("sp", fm[:, 0:nsub], self.FM.t.ap()[t0:t0 + n_].rearrange("(s p) r c -> p s r c", p=128), fm.b, self.FM.b)
            fw.dma("sp", hm[:, 0:nsub], self.HM.t.ap()[t0:t0 + n_].rearrange("(s p) c -> p s c", p=128), hm.b, self.HM.b)
            fw.dma("sp", OTt[:, :, 0:n_], self.OT.t.ap()[:, t0:t0 + n_].rearrange("(k p) t -> p k t", p=128), OTt.b, self.OT.b)
            fw.dma("sp", G[:, :, 0:n_], self.GT.t.ap()[:, t0:t0 + n_].rearrange("(k p) t -> p k t", p=128), G.b, self.GT.b)
            fw.dma("sp", xt[:, 0:nsub], self.X.t.ap()[t0:t0 + n_].rearrange("(s p) c -> p s c", p=128), xt.b, self.X.b)
            for (srcf, dstT) in ((lambda s_: fm[:, s_, 0, :], FrT), (lambda s_: fm[:, s_, 1, :], FiT), (lambda s_: hm[:, s_, :], HT)):
                p = pp[npp % 7]; npp += 1
                srcb = fm.b if dstT is not HT else hm.b
                for s_ in range(nsub):
                    fw.op("pe", lambda e, p=p, s_=s_, srcf=srcf: e.transpose(out=p[:, s_ * 128:(s_ + 1) * 128], in_=srcf(s_), identity=ident), R=[srcb, self.cstb], W=[p.b])
                fw.op("act", lambda e, p=p, dstT=dstT, n_=n_: e.activation(out=dstT[:, 0:n_], in_=p[:, 0:n_], func=AF.Copy), R=[p.b], W=[dstT.b])
            for fc in range(8):
                fsl = slice(fc * 128, (fc + 1) * 128)
                pf = pp[npp % 7]; npp += 1
                fw.op("pe", lambda e, pf=pf, fsl=fsl, n_=n_, j=j: e.matmul(pf[:, 0:n_], Wfc[j][:, fsl], FrT[:, 0:n_], start=True, stop=False), R=[Wfc[j].b, FrT.b], W=[pf.b])
                fw.op("pe", lambda e, pf=pf, fsl=fsl, n_=n_, j=j: e.matmul(pf[:, 0:n_], Wfs[j][:, fsl], FiT[:, 0:n_], start=False, stop=True), R=[Wfs[j].b, FiT.b], W=[pf.b])
                ph = pp[npp % 7]; npp += 1
                fw.op("pe", lambda e, ph=ph, fsl=fsl, n_=n_: e.matmul(ph[:, 0:n_], Wh[:, fsl], HT[:, 0:n_], start=True, stop=True), R=[Wh.b, HT.b], W=[ph.b])
                pa = pp[npp % 7]; npp += 1
                for k in range(2):
                    fw.op("pe", lambda e, pa=pa, fsl=fsl, n_=n_, k=k: e.matmul(pa[:, 0:n_], Wa[:, k, fsl], OTt[:, k, 0:n_], start=(k == 0), stop=(k == 1)), R=[Wa.b, OTt.b], W=[pa.b])
                fw.op("dve", lambda e, pf=pf, fc=fc, n_=n_: e.tensor_tensor(out=m1[:, 0:n_], in0=pf[:, 0:n_], in1=G[:, fc, 0:n_], op=ALU.mult), R=[pf.b, G.b], W=[m1.b])
                fw.op("dve", lambda e, ph=ph, fc=fc, n_=n_: e.tensor_tensor(out=m2[:, 0:n_], in0=ph[:, 0:n_], in1=G[:, 8 + fc, 0:n_], op=ALU.mult), R=[ph.b, G.b], W=[m2.b])
                fw.op("dve", lambda e, n_=n_: e.tensor_tensor(out=m1[:, 0:n_], in0=m1[:, 0:n_], in1=m2[:, 0:n_], op=ALU.add), R=[m1.b, m2.b], W=[m1.b])
                fw.op("dve", lambda e, pa=pa, fc=fc, n_=n_: e.tensor_tensor(out=m2[:, 0:n_], in0=pa[:, 0:n_], in1=G[:, 16 + fc, 0:n_], op=ALU.mult), R=[pa.b, G.b], W=[m2.b])
                fw.op("dve", lambda e, fc=fc, n_=n_: e.tensor_tensor(out=M[:, fc, 0:n_], in0=m1[:, 0:n_], in1=m2[:, 0:n_], op=ALU.add), R=[m1.b, m2.b], W=[M.b])
            for s_ in range(nsub):
                for half in range(2):
                    p = pp[npp % 7]; npp += 1
                    for k in range(8):
                        fw.op("pe", lambda e, p=p, k=k, s_=s_, half=half: e.matmul(p[:], M[:, k, s_ * 128:(s_ + 1) * 128], Wo[:, k, half * 512:(half + 1) * 512], start=(k == 0), stop=(k == 7)),
                              R=[M.b, Wo.b], W=[p.b])
                    fw.op("dve", lambda e, p=p, half=half: e.tensor_tensor(out=tz[:], in0=p[:], in1=mb[:, half * 512:(half + 1) * 512], op=ALU.mult), R=[p.b, mb.b], W=[tz.b])
                    a_ = ar[nar % 2]; nar += 1
                    fw.op("dve", lambda e, a_=a_, s_=s_, half=half: e.scalar_tensor_tensor(out=a_[:], in0=xt[:, s_, half * 512:(half + 1) * 512], scalar=0.5, in1=tz[:], op0=ALU.mult, op1=ALU.add),
                          R=[xt.b, tz.b], W=[a_.b])
                    fw.dma("sp", self.ARin.t.ap()[t0 + s_ * 128:t0 + (s_ + 1) * 128, half * 512:(half + 1) * 512], a_[:], self.ARin.b, a_.b, part=True)
        fw.fence()
    self.allreduce_to_X(ntok)


for _f in (phase_attn, allreduce_to_X, phase_outproj):
    setattr(Prog, _f.__name__, _f)


def phase_moe(self, l):
    fw = self.fw
    ntok = T if l == 0 else L
    WEB1 = fw.dram("WEB1_%d" % l, [16, 1024, 2048], BF16)
    WEB2 = fw.dram("WEB2_%d" % l, [16, 1024, 1024], BF16)
    fw.fence()
    for e_ in fw.ENG:
        fw.h[e_].wait_ge(self.expsem, 48 * (l + 1))
    with contextlib.ExitStack() as ps:
        st_ = [fw.sb(ps, "mst%d" % i, [128, 8, 512]) for i in range(2)]
        sb_ = [fw.sb(ps, "msb%d" % i, [128, 8, 512], BF16) for i in range(2)]
        n = 0
        for le in range(16):
            WE = self.WEl[l]
            w1v = WE.t.ap()[le * 1536:le * 1536 + 1024, :].rearrange("(k p) c -> p k c", p=128)
            w2v = bass.AP(WE.t, WE.t.ap()[le * 1536 + 1024:le * 1536 + 1536, :].offset, [[1024, 1024], [1, 1024]]).rearrange("(k p) c -> p k c", p=128)
            for (src, dst, ncb) in ((w1v, WEB1, 4), (w2v, WEB2, 2)):
                for cb in range(ncb):
                    a = st_[n % 2]; b_ = sb_[n % 2]
                    fw.dma("sp", a[:], src[:, :, cb * 512:(cb + 1) * 512], a.b, WE.b)
                    eng = ("act", "dve")[n % 2]
                    if eng == "act":
                        fw.op("act", lambda e, a=a, b_=b_: e.activation(out=b_[:], in_=a[:], func=AF.Copy), R=[a.b], W=[b_.b])
                    else:
                        fw.op(eng, lambda e, a=a, b_=b_: e.tensor_copy(out=b_[:], in_=a[:]), R=[a.b], W=[b_.b])
                    fw.dma("sp", dst.t.ap()[le].rearrange("(k p) c -> p k c", p=128)[:, :, cb * 512:(cb + 1) * 512], b_[:], dst.b, b_.b, part=True)
                    n += 1
        fw.fence()
    with contextlib.ExitStack() as ps:
        W1 = [fw.sb(ps, "mW1_%d" % i, [128, 8, 2048], BF16) for i in range(2)]
        W2 = [fw.sb(ps, "mW2_%d" % i, [128, 8, 1024], BF16) for i in range(2)]
        wrt = self.ld(ps, "wrt%d" % l); brt = self.ld(ps, "brt%d" % l)
        b1T = self.ld(ps, "b1T%d" % l); b2r = self.ld(ps, "b2r%d" % l, 16)
        xt = fw.sb(ps, "mxt", [128, 4, 1024]); xn = fw.sb(ps, "mxn", [128, 1024]); junk = fw.sb(ps, "mjunk", [128, 1024])
        ss = fw.sb(ps, "mss", [128, 2])
        h2T = fw.sb(ps, "mh2T", [128, 8, 512], BF16)
        h2f = fw.sb(ps, "mh2f", [128, 8, 128])
        lg = fw.sb(ps, "mlg", [128, 32]); m8 = fw.sb(ps, "mm8", [128, 8]); msk = fw.sb(ps, "mmsk", [128, 32])
        nb = fw.sb(ps, "mnb", [128, 2])
        Gt = fw.sb(ps, "mG", [128, 4, 32])
        gT = fw.sb(ps, "mgT", [16, 128])
        Y = fw.sb(ps, "mY", [128, 4, 1024])
        A = fw.sb(ps, "mA", [128, 8, 512], BF16)
        g2 = [fw.sb(ps, "mg%d" % i, [128, 512]) for i in range(2)]; sg2 = [fw.sb(ps, "msg%d" % i, [128, 512]) for i in range(2)]; ln2 = [fw.sb(ps, "mln%d" % i, [128, 512]) for i in range(2)]
        mb = fw.sb(ps, "mmb", [128, 1024])
        tz = fw.sb(ps, "mtz", [128, 1024])
        pp = [fw.ps(ps, "mpp%d" % i) for i in range(8)]
        npp = 0
        ident = self.cv("ident")
        lastj = -1
        nw = 0
        for t0 in range(0, ntok, 512):
            n_ = min(512, ntok - t0); nsub = n_ // 128
            j = 0 if t0 < L else 1
            if j != lastj:
                fw.dma("sp", mb[:], self.MODBC.t.ap()[:, j * 2048 + 1024:j * 2048 + 2048], mb.b, self.MODBC.b)
                lastj = j
            fw.dma("sp", xt[:, 0:nsub], self.X.t.ap()[t0:t0 + n_].rearrange("(s p) c -> p s c", p=128), xt.b, self.X.b)
            for s_ in range(nsub):
                fw.op("act", lambda e, s_=s_: e.activation(out=junk[:], in_=xt[:, s_, :], func=AF.Square, accum_out=ss[:, 0:1]), R=[xt.b], W=[junk.b, ss.b])
                fw.op("act", lambda e: e.activation(out=ss[:, 1:2], in_=ss[:, 0:1], func=AF.Sqrt, scale=1.0 / D, bias=self.epsb[:, 0:1]), R=[ss.b], W=[ss.b])
                fw.op("dve", lambda e: e.reciprocal(out=ss[:, 1:2], in_=ss[:, 1:2]), R=[ss.b], W=[ss.b])
                fw.op("dve", lambda e, s_=s_: e.tensor_scalar(out=xn[:], in0=xt[:, s_, :], scalar1=ss[:, 1:2], scalar2=None, op0=ALU.mult), R=[xt.b, ss.b], W=[xn.b])
                for g in range(2):
                    p = pp[npp % 8]; npp += 1
                    for q in range(4):
                        k = 4 * g + q
                        fw.op("pe", lambda e, p=p, q=q, k=k: e.transpose(out=p[:, q * 128:(q + 1) * 128], in_=xn[:, k * 128:(k + 1) * 128], identity=ident), R=[xn.b, self.cstb], W=[p.b])
                    for q in range(4):
                        k = 4 * g + q
                        fw.op("dve", lambda e, p=p, q=q, k=k, j=j: e.tensor_scalar(out=h2f[:, k, :], in0=p[:, q * 128:(q + 1) * 128], scalar1=self.gs2[:, k, j:j + 1], scalar2=self.sh2[:, k, j:j + 1],
                                                                                  op0=ALU.mult, op1=ALU.add), R=[p.b, self.gs2.b, self.sh2.b], W=[h2f.b])
                fw.op("act", lambda e, s_=s_: e.activation(out=h2T[:, :, s_ * 128:(s_ + 1) * 128], in_=h2f[:], func=AF.Copy), R=[h2f.b], W=[h2T.b])
                p = pp[npp % 8]; npp += 1
                for k in range(8):
                    fw.op("pe", lambda e, p=p, k=k: e.matmul(p[:, 0:32], h2f[:, k, :], wrt[:, k * 32:(k + 1) * 32], start=(k == 0), stop=(k == 7)), R=[h2f.b, wrt.b], W=[p.b])
                fw.op("dve", lambda e, p=p: e.tensor_tensor(out=lg[:], in0=p[:, 0:32], in1=brt[:], op=ALU.add), R=[p.b, brt.b], W=[lg.b])
                fw.op("dve", lambda e: e.max(out=m8[:], in_=lg[:]), R=[lg.b], W=[m8.b])
                fw.op("dve", lambda e: e.tensor_scalar(out=msk[:], in0=lg[:], scalar1=m8[:, 3:4], scalar2=None, op0=ALU.is_ge), R=[lg.b, m8.b], W=[msk.b])
                fw.op("dve", lambda e: e.tensor_scalar(out=nb[:, 0:1], in0=m8[:, 0:1], scalar1=-1.0, scalar2=None, op0=ALU.mult), R=[m8.b], W=[nb.b])
                fw.op("act", lambda e: e.activation(out=lg[:], in_=lg[:], func=AF.Exp, bias=nb[:, 0:1]), R=[lg.b, nb.b], W=[lg.b])
                fw.op("dve", lambda e: e.tensor_tensor(out=lg[:], in0=lg[:], in1=msk[:], op=ALU.mult), R=[lg.b, msk.b], W=[lg.b])
                fw.op("dve", lambda e: e.tensor_reduce(out=nb[:, 1:2], in_=lg[:], axis=AX.X, op=ALU.add), R=[lg.b], W=[nb.b])
                fw.op("dve", lambda e: e.reciprocal(out=nb[:, 1:2], in_=nb[:, 1:2]), R=[nb.b], W=[nb.b])
                fw.op("dve", lambda e, s_=s_: e.tensor_scalar(out=Gt[:, s_, :], in0=lg[:], scalar1=nb[:, 1:2], scalar2=None, op0=ALU.mult), R=[lg.b, nb.b], W=[Gt.b])
                p = pp[npp % 8]; npp += 1
                fw.op("pe", lambda e, p=p, s_=s_: e.transpose(out=p[0:16, 0:128], in_=Gt[:, s_, 0:16], identity=ident), R=[Gt.b, self.cstb], W=[p.b])
                fw.op("dve", lambda e, p=p: e.tensor_copy(out=gT[:], in_=p[0:16, 0:128]), R=[p.b], W=[gT.b])
                for half in range(2):
                    p = pp[npp % 8]; npp += 1
                    fw.op("pe", lambda e, p=p, half=half: e.matmul(p[:], gT[:], b2r[0:16, half * 512:(half + 1) * 512], start=True, stop=True), R=[gT.b, b2r.b], W=[p.b])
                    fw.op("act", lambda e, p=p, half=half, s_=s_: e.activation(out=Y[:, s_, half * 512:(half + 1) * 512], in_=p[:], func=AF.Copy), R=[p.b], W=[Y.b])
            for le in range(16):
                w1 = W1[nw % 2]; w2 = W2[nw % 2]; nw += 1
                for cb in range(4):
                    fw.dma("sp", w1[:, :, cb * 512:(cb + 1) * 512], WEB1.t.ap()[le].rearrange("(k p) c -> p k c", p=128)[:, :, cb * 512:(cb + 1) * 512], w1.b, WEB1.b, part=(cb > 0))
                for cb in range(2):
                    fw.dma("sp", w2[:, :, cb * 512:(cb + 1) * 512], WEB2.t.ap()[le].rearrange("(k p) c -> p k c", p=128)[:, :, cb * 512:(cb + 1) * 512], w2.b, WEB2.b, part=(cb > 0))
                for jc in range(8):
                    pg = pp[npp % 8]; npp += 1
                    pl = pp[npp % 8]; npp += 1
                    g_ = g2[jc % 2]; sg = sg2[jc % 2]; ln = ln2[jc % 2]
                    for k in range(8):
                        fw.op("pe", lambda e, pg=pg, k=k, jc=jc, w1=w1, n_=n_: e.matmul(pg[:, 0:n_], w1[:, k, jc * 128:(jc + 1) * 128], h2T[:, k, 0:n_], start=(k == 0), stop=(k == 7)), R=[w1.b, h2T.b], W=[pg.b])
                    for k in range(8):
                        fw.op("pe", lambda e, pl=pl, k=k, jc=jc, w1=w1, n_=n_: e.matmul(pl[:, 0:n_], w1[:, k, 1024 + jc * 128:1024 + (jc + 1) * 128], h2T[:, k, 0:n_], start=(k == 0), stop=(k == 7)), R=[w1.b, h2T.b], W=[pl.b])
                    bg = b1T[:, le * 16 + jc:le * 16 + jc + 1]; bl = b1T[:, le * 16 + 8 + jc:le * 16 + 8 + jc + 1]
                    fw.op("dve", lambda e, pg=pg, bg=bg, n_=n_, g_=g_: e.tensor_scalar(out=g_[:, 0:n_], in0=pg[:, 0:n_], scalar1=bg, scalar2=7.0, op0=ALU.add, op1=ALU.min), R=[pg.b, b1T.b], W=[g_.b])
                    fw.op("act", lambda e, n_=n_, g_=g_, sg=sg: e.activation(out=sg[:, 0:n_], in_=g_[:, 0:n_], func=AF.Sigmoid, scale=1.702), R=[g_.b], W=[sg.b])
                    fw.op("act", lambda e, pl=pl, bl=bl, n_=n_, ln=ln: e.activation(out=ln[:, 0:n_], in_=pl[:, 0:n_], func=AF.Identity, bias=bl), R=[pl.b, b1T.b], W=[ln.b])
                    fw.op("dve", lambda e, n_=n_, ln=ln: e.tensor_scalar(out=ln[:, 0:n_], in0=ln[:, 0:n_], scalar1=7.0, scalar2=-7.0, op0=ALU.min, op1=ALU.max), R=[ln.b], W=[ln.b])
                    fw.op("dve", lambda e, n_=n_, g_=g_, sg=sg: e.tensor_tensor(out=g_[:, 0:n_], in0=g_[:, 0:n_], in1=sg[:, 0:n_], op=ALU.mult), R=[g_.b, sg.b], W=[g_.b])
                    fw.op("dve", lambda e, jc=jc, n_=n_, g_=g_, ln=ln: e.scalar_tensor_tensor(out=A[:, jc, 0:n_], in0=ln[:, 0:n_], scalar=1.0, in1=g_[:, 0:n_], op0=ALU.add, op1=ALU.mult), R=[g_.b, ln.b], W=[A.b])
                for s_ in range(nsub):
                    for half in range(2):
                        p = pp[npp % 8]; npp += 1
                        for k in range(8):
                            fw.op("pe", lambda e, p=p, k=k, s_=s_, half=half, w2=w2: e.matmul(p[:], A[:, k, s_ * 128:(s_ + 1) * 128], w2[:, k, half * 512:(half + 1) * 512], start=(k == 0), stop=(k == 7)),
                                  R=[A.b, w2.b], W=[p.b])
                        fw.op("dve", lambda e, p=p, s_=s_, half=half, le=le: e.scalar_tensor_tensor(out=Y[:, s_, half * 512:(half + 1) * 512], in0=p[:], scalar=Gt[:, s_, le:le + 1],
                                                                                                   in1=Y[:, s_, half * 512:(half + 1) * 512], op0=ALU.mult, op1=ALU.add), R=[p.b, Gt.b, Y.b], W=[Y.b])
            for s_ in range(nsub):
                fw.op("dve", lambda e, s_=s_: e.tensor_tensor(out=tz[:], in0=Y[:, s_, :], in1=mb[:], op=ALU.mult), R=[Y.b, mb.b], W=[tz.b])
                fw.op("dve", lambda e, s_=s_: e.scalar_tensor_tensor(out=tz[:], in0=xt[:, s_, :], scalar=0.5, in1=tz[:], op0=ALU.mult, op1=ALU.add), R=[xt.b, tz.b], W=[tz.b])
                fw.dma("sp", self.ARin.t.ap()[t0 + s_ * 128:t0 + (s_ + 1) * 128, :], tz[:], self.ARin.b, tz.b, part=True)
        fw.fence()
    self.allreduce_to_X(ntok)


setattr(Prog, "phase_moe", phase_moe)


_CACHE = {}


def kernel(**inputs):
    inp = {k: np.asarray(v) for k, v in inputs.items()}
    maps = host_prep(inp)
    P = Prog(stop=None, dumps=(), with_exp=True)
    nc = P.build()
    res = run_bass_kernel_spmd(nc, maps, core_ids=list(range(8)))
    out = np.stack([np.asarray(res.results[b]["out"]) for b in range(4)], 0)
    return out.astype(np.float32)
```

```python
import contextlib
import math
import numpy as np
import ml_dtypes
import concourse.bass as bass
import concourse.mybir as mybir
from concourse.bass_utils import run_bass_kernel_spmd

F32 = mybir.dt.float32
BF16 = mybir.dt.bfloat16
AF = mybir.ActivationFunctionType
ALU = mybir.AluOpType
AX = mybir.AxisListType

D = 1024
L = 8192
NCTX = 256
T = L + NCTX
DEPTH = 2
EPS = 1e-6
SUBLN_EPS = 1e-5
OFF_F, OFF_HY, OFF_Q, OFF_K, OFF_V, OFF_G = 0, 256, 1024, 1536, 2048, 2560
NMIX = 1280
NEXP_CORE = 16
PAIRS = [[0, 4], [1, 5], [2, 6], [3, 7]]
HALVES = [[0, 1, 2, 3], [4, 5, 6, 7]]


class Buf:
    __slots__ = ("name", "w", "r", "dsem")

    def __init__(self, name):
        self.name = name
        self.w = []
        self.r = []
        self.dsem = None


class Tile:
    def __init__(self, t, b):
        self.t = t
        self.b = b

    def __getitem__(self, k):
        return self.t[k]


class Op:
    __slots__ = ("id", "eng", "fn", "deps", "isdma", "signal", "wbuf")

    def __init__(self, id, eng, fn, deps, isdma, wbuf=None):
        self.id = id; self.eng = eng; self.fn = fn; self.deps = deps
        self.isdma = isdma; self.signal = isdma; self.wbuf = wbuf


class FW:
    ENG = ("pe", "act", "dve", "pool", "sp")
    NDSEM = 72

    def __init__(self, nc):
        self.nc = nc
        self.es = contextlib.ExitStack()
        self.h = {"pe": nc.tensor, "act": nc.scalar, "dve": nc.vector, "pool": nc.gpsimd, "sp": nc.sync}
        self.esem = {e: self.es.enter_context(nc.semaphore("e_" + e)) for e in self.ENG}
        self.ecnt = {e: 0 for e in self.ENG}
        self.fsem = self.es.enter_context(nc.semaphore("fence"))
        self.fcnt = 0
        self.dsems = [self.es.enter_context(nc.semaphore("d%d" % i)) for i in range(self.NDSEM)]
        self.dissued = [0] * self.NDSEM
        self.dnext = 0
        self.seen = {e: {} for e in self.ENG}
        self.done = {}
        self.pending = []
        self.bufs = []
        self.opmap = {}
        self.nid = 0
        self.ninst = 0

    def buf(self, name):
        b = Buf(name)
        self.bufs.append(b)
        return b

    def sb(self, st, name, shape, dtype=F32):
        self.nid += 1
        name = "%s_%d" % (name, self.nid)
        t = st.enter_context(self.nc.sbuf_tensor(name, shape, dtype))
        return Tile(t, self.buf(name))

    def ps(self, st, name, shape=(128, 512), dtype=F32):
        self.nid += 1
        name = "%s_%d" % (name, self.nid)
        t = st.enter_context(self.nc.psum_tensor(name, list(shape), dtype))
        return Tile(t, self.buf(name))

    def dram(self, name, shape, dtype, kind="Internal"):
        t = self.nc.dram_tensor(name, list(shape), dtype, kind=kind)
        return Tile(t, self.buf(name))

    def op(self, eng, fn, R=(), W=()):
        deps = set()
        for b in R:
            deps.update(b.w)
        for b in W:
            deps.update(b.w)
            deps.update(b.r)
        o = Op(self.nid, eng, fn, deps, False)
        self.nid += 1
        self.pending.append(o)
        for b in R:
            b.r.append(o.id)
        for b in W:
            b.w = [o.id]; b.r = []
        return o

    def dma(self, q, out_ap, in_ap, W, R, part=False, **kw):
        deps = set(R.w)
        for x in W.w:
            if part:
                p = self.opmap.get(x)
                if p is not None and p.isdma and p.wbuf is W:
                    continue
            deps.add(x)
        deps.update(W.r)
        fn = (lambda h, o=out_ap, i=in_ap, kw=kw: h.dma_start(out=o, in_=i, **kw))
        o = Op(self.nid, q, fn, deps, True, wbuf=W)
        self.nid += 1
        self.pending.append(o)
        self.opmap[o.id] = o
        R.r.append(o.id)
        if part:
            W.w = list(W.w) + [o.id]
        else:
            W.w = [o.id]
            W.r = []
        return o

    def _wait(self, eng, key, val):
        s = self.seen[eng]
        if s.get(key, 0) >= val:
            return
        s[key] = val
        if key[0] == "e":
            sem = self.esem[key[1]]
        elif key[0] == "f":
            sem = self.fsem
        else:
            sem = self.dsems[key[1]]
        self.h[eng].wait_ge(sem, val)
        self.ninst += 1

    def flush(self):
        ops = self.pending
        self.pending = []
        ids = {o.id: o for o in ops}
        for o in ops:
            for d in o.deps:
                p = ids.get(d)
                if p is not None and not p.isdma and not (p.eng == "pe" and o.eng == "pe"):
                    p.signal = True
        for o in ops:
            for d in o.deps:
                ev = self.done.get(d)
                if ev is None:
                    continue
                key, val = ev
                if key == ("e", "pe") and o.eng == "pe":
                    continue
                if key[0] == "d":
                    val = max(val, self.dissued[key[1]])
                self._wait(o.eng, key, val)
            inst = o.fn(self.h[o.eng])
            self.ninst += 1
            if o.isdma:
                b = o.wbuf
                if b.dsem is None:
                    b.dsem = self.dnext % self.NDSEM
                    self.dnext += 1
                k = b.dsem
                self.dissued[k] += 16
                inst.then_inc(self.dsems[k], 16)
                self.done[o.id] = (("d", k), self.dissued[k])
            elif o.signal:
                self.ecnt[o.eng] += 1
                inst.then_inc(self.esem[o.eng], 1)
                self.done[o.id] = (("e", o.eng), self.ecnt[o.eng])

    def fence(self):
        last = {}
        for o in self.pending:
            if not o.isdma:
                last[o.eng] = o
        for o in last.values():
            o.signal = True
        self.flush()
        for e in self.ENG:
            if self.ecnt[e] > 0:
                self._wait("sp", ("e", e), self.ecnt[e])
        for k in range(self.NDSEM):
            if self.dissued[k] > 0:
                self._wait("sp", ("d", k), self.dissued[k])
        self.fcnt += 1
        self.h["sp"].sem_inc(self.fsem, 1)
        self.ninst += 1
        for e in self.ENG:
            self._wait(e, ("f",), self.fcnt)
            for e2 in self.ENG:
                self.seen[e][("e", e2)] = self.ecnt[e2]
            for k in range(self.NDSEM):
                self.seen[e][("d", k)] = self.dissued[k]
        for b in self.bufs:
            b.w = []; b.r = []
        self.done = {}
        self.opmap = {}

    def close(self):
        self.es.close()


class Pack:
    def __init__(self):
        self.off = {}
        self.n = 0
        self.items = []

    def add(self, name, width):
        self.off[name] = (self.n, width)
        self.n += width

    def fill(self, arr, name, val):
        o, w = self.off[name]
        val = np.asarray(val, np.float32)
        val = val.reshape(val.shape[0], -1)
        assert val.shape[1] == w, (name, val.shape, w)
        arr[:val.shape[0], o:o + w] = val


def fm(v, nk):
    return np.asarray(v, np.float32).reshape(nk, 128).T


def rep(v):
    v = np.asarray(v, np.float32).reshape(1, -1)
    return np.broadcast_to(v, (128, v.shape[1]))


def make_sm_layout():
    P = Pack()
    P.add("cT", 16)
    P.add("crep", 2 * 8 * 128)
    P.add("deltabc", 128)
    for l in range(DEPTH):
        P.add("bmodT%d" % l, 48)
        P.add("bmodbc%d" % l, 2 * 1024)
        P.add("g1T%d" % l, 8)
        P.add("g2T%d" % l, 8)
        P.add("qg%d" % l, 1)
        P.add("kg%d" % l, 1)
        P.add("subg%d" % l, 1)
        P.add("lamq%d" % l, 128)
        P.add("lamk%d" % l, 128)
        P.add("wrt%d" % l, 8 * 32)
        P.add("brt%d" % l, 32)
        P.add("b1T%d" % l, 16 * 16)
        P.add("b2r%d" % l, 1024)
        P.add("cwbc%d" % l, 3 * 384)
        P.add("cbbc%d" % l, 384)
        P.add("hw1%d" % l, 64)
        P.add("hb1%d" % l, 1)
        P.add("hf1%d" % l, 1)
        P.add("hw2%d" % l, 64)
        P.add("hb2%d" % l, 1)
        P.add("hf2%d" % l, 1)
        P.add("hw3%d" % l, 512)
        P.add("hb3bc%d" % l, 512)
        P.add("hbias%d" % l, 256)
    return P


SM = make_sm_layout()


def make_cst_layout():
    P = Pack()
    P.add("ident", 128)
    P.add("ones", 128)
    P.add("bones", 128)
    P.add("rot", 128)
    P.add("ntl8192", 64)
    P.add("ntl256", 2)
    P.add("c64", 128)
    P.add("s64", 128)
    return P


CST = make_cst_layout()


def rope_tables():
    rows = L // 64
    row = np.repeat(np.arange(rows), 64).astype(np.float32)
    col = np.tile(np.arange(64), rows).astype(np.float32)
    nf = 16
    inv = (10000.0 ** (-np.arange(nf, dtype=np.float32) / nf)).astype(np.float32)
    angr = row[None, :] * inv[:, None]
    angc = col[None, :] * inv[:, None]
    ang64 = np.concatenate([angr, angr, angc, angc], 0)
    cos = np.cos(ang64).astype(np.float32)
    sin = np.sin(ang64).astype(np.float32)
    cos = np.concatenate([cos, np.ones((64, NCTX), np.float32)], 1)
    sin = np.concatenate([sin, np.zeros((64, NCTX), np.float32)], 1)
    return np.concatenate([cos, cos], 0), np.concatenate([sin, sin], 0)


def hy_emb(l):
    t = np.linspace(0.0, 1.0, l, dtype=np.float32)[:, None]
    ang = (np.float32(2.0 * math.pi / l) * np.arange(l, dtype=np.float32))[:, None]
    bands = np.linspace(1e-4, 15, 16, dtype=np.float32)[None, :]
    emb = np.concatenate([t, np.cos(bands * ang), -np.sin(bands * ang)], -1)
    return emb.T.astype(np.float32)


def hy_deltas(s):
    d = np.abs(np.linspace(math.log(1e-2) / 1.5, math.log(1e-2) / 0.3, 256, dtype=np.float32))
    return d[128 * s:128 * s + 128]


def rot_lhsT():
    R = np.zeros((128, 128), np.float32)
    for blk in range(2):
        for base in (0, 32):
            for j in range(16):
                a = blk * 64 + base + j
                b = a + 16
                R[a, b] = -1.0
                R[b, a] = 1.0
    return R.T.copy()


def shared_layout():
    off = {}
    n = 0
    for l in range(DEPTH):
        off["wg%d" % l] = (n, (1024, 3072)); n += 1024 * 3072
        off["wmod%d" % l] = (n, (1024, 6144)); n += 1024 * 6144
        off["wo%d" % l] = (n, (1024, 1024)); n += 1024 * 1024
    off["ropec"] = (n, (128, T)); n += 128 * T
    off["ropes"] = (n, (128, T)); n += 128 * T
    rows = -(-n // (512 * 2048)) * 512
    return off, rows


SHOFF, SHROWS = shared_layout()


def host_prep(inp):
    x = inp["x"]; ctx = inp["ctx"]; c = inp["c"]; c_ctx = inp["c_ctx"]
    blob = np.zeros((SHROWS * 2048,), np.float32)

    def put(name, arr):
        o, shp = SHOFF[name]
        blob[o:o + arr.size] = np.ascontiguousarray(arr, np.float32).reshape(-1)

    for l in range(DEPTH):
        put("wg%d" % l, inp["w_in"][l][:, OFF_G:])
        put("wmod%d" % l, inp["w_mod"][l])
        put("wo%d" % l, inp["w_o"][l])
    rc, rs = rope_tables()
    put("ropec", rc); put("ropes", rs)
    blob = blob.reshape(SHROWS // 512, 4, 128, 2048)

    cst = np.zeros((128, CST.n), np.float32)
    CST.fill(cst, "ident", np.eye(128, dtype=np.float32))
    CST.fill(cst, "ones", np.ones((128, 128), np.float32))
    bo = np.zeros((128, 128), np.float32); bo[:64, :64] = 1; bo[64:, 64:] = 1
    CST.fill(cst, "bones", bo)
    CST.fill(cst, "rot", rot_lhsT())
    pp = np.arange(128)[:, None]
    CST.fill(cst, "ntl8192", -((np.arange(64)[None, :] * 128 + pp) / (L - 1.0)))
    CST.fill(cst, "ntl256", -((np.arange(2)[None, :] * 128 + pp) / (NCTX - 1.0)))
    a64 = np.arange(64)
    c64 = np.cos(2 * np.pi * np.outer(a64, a64) / 64); s64 = np.sin(2 * np.pi * np.outer(a64, a64) / 64)
    z = np.zeros((64, 64))
    CST.fill(cst, "c64", np.block([[c64, z], [z, c64]]))
    CST.fill(cst, "s64", np.block([[s64, z], [z, s64]]))
    embc = np.concatenate([hy_emb(L), hy_emb(NCTX)], 1).astype(np.float32)
    fftc = fft_constants()

    maps = []
    for r in range(8):
        s, b = r // 4, r % 4
        m = {}
        m["xs"] = np.ascontiguousarray(x[b, s * 4096:(s + 1) * 4096])
        m["ctxb"] = np.ascontiguousarray(ctx[b])
        m["wsh"] = np.ascontiguousarray(blob[:, b]).reshape(SHROWS // 4, 2048)
        m["cst"] = cst
        m["embc"] = embc
        m["fftc"] = fftc
        wmix = np.zeros((DEPTH, 1024, NMIX), np.float32)
        wout = np.zeros((DEPTH, 512, 1024), np.float32)
        sm = np.zeros((128, SM.n), np.float32)
        SM.fill(sm, "cT", np.stack([fm(c[b], 8), fm(c_ctx, 8)], -1).reshape(128, 16))
        crep = np.zeros((128, 2, 8, 128), np.float32)
        crep[:, 0] = fm(c[b], 8)[:, :, None]
        crep[:, 1] = fm(c_ctx, 8)[:, :, None]
        SM.fill(sm, "crep", crep.reshape(128, -1))
        SM.fill(sm, "deltabc", rep(hy_deltas(s)))
        for l in range(DEPTH):
            w_in = inp["w_in"][l]
            cols = np.concatenate([
                np.arange(OFF_F + 128 * s, OFF_F + 128 * s + 128),
                np.arange(OFF_HY + 128 * s, OFF_HY + 128 * s + 128),
                np.arange(OFF_HY + 256 + 128 * s, OFF_HY + 256 + 128 * s + 128),
                np.arange(OFF_HY + 512 + 128 * s, OFF_HY + 512 + 128 * s + 128),
                np.arange(OFF_Q + 256 * s, OFF_Q + 256 * s + 256),
                np.arange(OFF_K + 256 * s, OFF_K + 256 * s + 256),
                np.arange(OFF_V + 256 * s, OFF_V + 256 * s + 256)])
            wmix[l] = w_in[:, cols]
            wout[l, 0:128] = inp["w_f"][l][128 * s:128 * s + 128]
            wout[l, 128:256] = inp["w_h"][l][128 * s:128 * s + 128]
            wout[l, 256:512] = inp["w_a"][l][256 * s:256 * s + 256]
            SM.fill(sm, "bmodT%d" % l, fm(inp["b_mod"][l], 48))
            bm = inp["b_mod"][l].reshape(6, 1024)
            SM.fill(sm, "bmodbc%d" % l, rep(np.concatenate([bm[2], bm[5]])))
            SM.fill(sm, "g1T%d" % l, fm(inp["norm1_g"][l], 8))
            SM.fill(sm, "g2T%d" % l, fm(inp["norm2_g"][l], 8))
            SM.fill(sm, "qg%d" % l, np.tile(inp["q_norm_g"][l], 2).reshape(128, 1))
            SM.fill(sm, "kg%d" % l, np.tile(inp["k_norm_g"][l], 2).reshape(128, 1))
            SM.fill(sm, "subg%d" % l, inp["subln_g"][l].reshape(128, 1))
            SM.fill(sm, "lamq%d" % l, rep(inp["lam_q"][l].reshape(-1)))
            SM.fill(sm, "lamk%d" % l, rep(inp["lam_k"][l].reshape(-1)))
            es_ = [16 * s + le for le in range(16)]
            perm = es_ + [e for e in range(32) if e not in es_]
            SM.fill(sm, "wrt%d" % l, inp["w_router"][l][:, perm].reshape(8, 128, 32).transpose(1, 0, 2).reshape(128, 256))
            SM.fill(sm, "brt%d" % l, rep(inp["b_router"][l][perm]))
            b1 = inp["b_e1"][l][es_]
            b1p = np.concatenate([b1[:, 0::2], b1[:, 1::2]], 1)
            SM.fill(sm, "b1T%d" % l, b1p.reshape(16, 16, 128).transpose(2, 0, 1).reshape(128, 256))
            SM.fill(sm, "b2r%d" % l, inp["b_e2"][l][es_])
            cw = inp["hy_conv_w"][l]; cb = inp["hy_conv_b"][l]
            vx = np.concatenate([np.arange(128 * s, 128 * s + 128), np.arange(256 + 128 * s, 256 + 128 * s + 128),
                                 np.arange(512 + 128 * s, 512 + 128 * s + 128)])
            SM.fill(sm, "cwbc%d" % l, rep(cw[:, vx].reshape(-1)))
            SM.fill(sm, "cbbc%d" % l, rep(cb[vx]))
            SM.fill(sm, "hw1%d" % l, inp["hy_w1"][l])
            SM.fill(sm, "hb1%d" % l, inp["hy_b1"][l].reshape(64, 1))
            SM.fill(sm, "hf1%d" % l, inp["hy_freq1"][l].reshape(64, 1))
            SM.fill(sm, "hw2%d" % l, inp["hy_w2"][l])
            SM.fill(sm, "hb2%d" % l, inp["hy_b2"][l].reshape(64, 1))
            SM.fill(sm, "hf2%d" % l, inp["hy_freq2"][l].reshape(64, 1))
            w3 = inp["hy_w3"][l].reshape(64, 2, 2, 256)[:, :, :, 128 * s:128 * s + 128].reshape(64, 512)
            b3 = inp["hy_b3"][l].reshape(2, 2, 256)[:, :, 128 * s:128 * s + 128].reshape(512)
            SM.fill(sm, "hw3%d" % l, w3)
            SM.fill(sm, "hb3bc%d" % l, rep(b3))
            SM.fill(sm, "hbias%d" % l, rep(inp["hy_bias"][l][:, 128 * s:128 * s + 128].reshape(-1)))
        m["wmix"] = wmix
        m["wout"] = wout
        m["sm"] = sm
        wexp = np.zeros((DEPTH, 6144, 2048), np.float32)
        for l in range(DEPTH):
            full = np.zeros((16, 1536, 2048), np.float32)
            for le in range(16):
                e = 16 * s + le
                w1 = inp["w_e1"][l][e]
                full[le, :1024] = np.concatenate([w1[:, 0::2], w1[:, 1::2]], 1)
                full[le, 1024:] = inp["w_e2"][l][e].reshape(512, 2048)
            wexp[l] = full.reshape(48, 4, 128, 2048)[:, b].reshape(6144, 2048)
        m["wexp"] = wexp
        maps.append(m)
    return maps


def dview(tile, off, shape):
    r, c = shape
    return bass.AP(tile.t, off, [[c, r], [1, c]])


class Prog:
    def __init__(self, stop=None, dumps=(), with_exp=True):
        self.stop = stop
        self.dumps = list(dumps)
        self.with_exp = with_exp
        nc = self.nc = bass.Bass("TRN2", target_bir_lowering=False)
        fw = self.fw = FW(nc)
        self.ncc = 0
        self.WEB = {}
        self.xs = fw.dram("xs", [4096, 1024], F32, "ExternalInput")
        self.ctxb = fw.dram("ctxb", [NCTX, 1024], F32, "ExternalInput")
        self.wsh = fw.dram("wsh", [SHROWS // 4, 2048], F32, "ExternalInput")
        self.cst_d = fw.dram("cst", [128, CST.n], F32, "ExternalInput")
        self.wmix = fw.dram("wmix", [DEPTH, 1024, NMIX], F32, "ExternalInput")
        self.wout = fw.dram("wout", [DEPTH, 512, 1024], F32, "ExternalInput")
        self.sm_d = fw.dram("sm", [128, SM.n], F32, "ExternalInput")
        self.embc = fw.dram("embc", [33, T], F32, "ExternalInput")
        self.fftc_d = fw.dram("fftc", [128, FFTC.n], F32, "ExternalInput")
        if with_exp:
            self.wexp = fw.dram("wexp", [DEPTH, 6144, 2048], F32, "ExternalInput")
        self.out = fw.dram("out", [L, 1024], F32, "ExternalOutput")
        self.X = fw.dram("X", [T, 1024], F32)
        self.SH = fw.dram("SH", [SHROWS, 2048], F32)
        self.QT = fw.dram("QT", [256, T], BF16)
        self.KT = fw.dram("KT", [256, T], BF16)
        self.V = fw.dram("V", [T, 256], BF16)
        self.GT = fw.dram("GT", [3072, T], BF16)
        self.PFH = fw.dram("PFH", [T + 4, 512], F32)
        self.MODBC = fw.dram("MODBC", [128, 4096], F32)
        self.FM = fw.dram("FM", [T, 2, 128], F32)
        self.HM = fw.dram("HM", [T, 128], F32)
        self.OT = fw.dram("OT", [256, T], BF16)
        self.ARin = fw.dram("ARin", [T, 1024], F32)
        self.dump_out = {}

    def coll_seq(self, items):
        fw = self.fw
        sem = fw.es.enter_context(self.nc.semaphore("cc%d" % self.ncc))
        self.ncc += 1
        fw.fence()
        for (kind, groups, in_ap, out_ap) in items:
            op = ALU.bypass if kind in ("AllGather", "AllToAll") else ALU.add
            fw.h["pool"].collective_compute(kind, op, replica_groups=groups, ins=[in_ap], outs=[out_ap]).then_inc(sem)
        for e in fw.ENG:
            fw.h[e].wait_ge(sem, len(items))

    def ld(self, ps, name, rows=128, q="sp"):
        o, w = SM.off[name]
        t = self.fw.sb(ps, "sm_" + name, [rows, w])
        kw = dict(allow_slow_non_contiguous=True) if w == 1 else {}
        self.fw.dma(q, t[:], self.sm_d.t.ap()[0:rows, o:o + w], t.b, self.sm_d.b, **kw)
        return t

    def cv(self, name):
        o, w = CST.off[name]
        return self.cst[:, o:o + w]

    def dump(self, name, tile):
        if name in self.dumps:
            t = tile.t
            d = self.fw.dram("dump_" + name, list(t.shape), t.dtype, "ExternalOutput")
            self.fw.dma("sp", d.t.ap(), t.ap(), d.b, tile.b)
            self.dump_out[name] = "dump_" + name
            self.fw.fence()

    def bigcopy(self, dst, dap, src, sap, rows, step=256):
        for r0 in range(0, rows, step):
            r1 = min(rows, r0 + step)
            self.fw.dma("sp", dap[r0:r1, :], sap[r0:r1, :], dst.b, src.b, part=True)

    def phase_gather(self):
        fw = self.fw
        xsI = fw.dram("xsI", [4096, 1024], F32)
        wshI = fw.dram("wshI", [SHROWS // 4, 2048], F32)
        self.bigcopy(xsI, xsI.t.ap(), self.xs, self.xs.t.ap(), 4096, 512)
        self.bigcopy(wshI, wshI.t.ap(), self.wsh, self.wsh.t.ap(), SHROWS // 4)
        fw.dma("sp", self.X.t.ap()[L:T, :], self.ctxb.t.ap(), self.X.b, self.ctxb.b, part=True)
        if self.with_exp:
            self.wexpI = fw.dram("wexpI", [DEPTH, 6144, 2048], F32)
            for l in range(DEPTH):
                self.bigcopy(self.wexpI, self.wexpI.t.ap()[l], self.wexp, self.wexp.t.ap()[l], 6144)
        XG = fw.dram("XG", [8, 1024, 1024], F32)
        items = [("AllGather", PAIRS, xsI.t.ap()[c * 512:(c + 1) * 512, :], XG.t.ap()[c]) for c in range(8)]
        items += [("AllGather", HALVES, wshI.t.ap()[c * 128:(c + 1) * 128, :], self.SH.t.ap()[c * 512:(c + 1) * 512, :])
                  for c in range(SHROWS // 512)]
        self.coll_seq(items)
        if self.with_exp:
            self.WEl = [fw.dram("WE%d" % l, [24576, 2048], F32) for l in range(DEPTH)]
            self.expsem = fw.es.enter_context(self.nc.semaphore("expsem"))
            for l in range(DEPTH):
                for c in range(48):
                    fw.h["pool"].collective_compute("AllGather", ALU.bypass, replica_groups=HALVES,
                                                    ins=[self.wexpI.t.ap()[l, c * 128:(c + 1) * 128, :]],
                                                    outs=[self.WEl[l].t.ap()[c * 512:(c + 1) * 512, :]]).then_inc(self.expsem)
        for c in range(8):
            for r in range(2):
                for q in range(2):
                    fw.dma("sp", self.X.t.ap()[r * 4096 + c * 512 + q * 256: r * 4096 + c * 512 + q * 256 + 256, :],
                           XG.t.ap()[c, r * 512 + q * 256: r * 512 + q * 256 + 256, :], self.X.b, XG.b, part=True)
        fw.fence()

    def phase_mod(self, l):
        fw = self.fw
        with contextlib.ExitStack() as ps:
            cT = self.ld(ps, "cT"); crep = self.ld(ps, "crep")
            bmodT = self.ld(ps, "bmodT%d" % l); bmodbc = self.ld(ps, "bmodbc%d" % l)
            g1T = self.ld(ps, "g1T%d" % l); g2T = self.ld(ps, "g2T%d" % l)
            lamq = self.ld(ps, "lamq%d" % l); lamk = self.ld(ps, "lamk%d" % l)
            for nm, dst in (("qg%d" % l, 0), ("kg%d" % l, 1), ("subg%d" % l, 2)):
                o, w = SM.off[nm]
                fw.dma("sp", self.qks[:, dst:dst + 1], self.sm_d.t.ap()[:, o:o + 1], self.qks.b, self.sm_d.b, part=True, allow_slow_non_contiguous=True)
            sc = fw.sb(ps, "sc", [128, 8, 2])
            screp = fw.sb(ps, "screp", [128, 2, 8, 128])
            mbc = fw.sb(ps, "mbc", [128, 2, 2, 1024])
            pm = fw.ps(ps, "pm")
            pbc = [fw.ps(ps, "pbc%d" % i) for i in range(2)]
            wm = [fw.sb(ps, "wm%d" % i, [128, 8, 512]) for i in range(2)]
            fw.op("act", lambda e: e.activation(out=sc[:].rearrange("p k j -> p (k j)"), in_=cT[:], func=AF.Silu),
                  R=[cT.b], W=[sc.b])
            fw.op("act", lambda e: e.activation(out=screp[:].rearrange("p j k m -> p (j k m)"), in_=crep[:], func=AF.Silu),
                  R=[crep.b], W=[screp.b])
            o, _ = SHOFF["wmod%d" % l]
            wv = dview(self.SH, o, (1024, 6144)).rearrange("(k p) c -> p k c", p=128)
            for cb in range(12):
                w = wm[cb % 2]
                fw.dma("sp", w[:], wv[:, :, cb * 512:(cb + 1) * 512], w.b, self.SH.b)
                for oc in range(4):
                    col = (cb * 4 + oc) * 2
                    for k in range(8):
                        fw.op("pe", lambda e, w=w, oc=oc, k=k, col=col: e.matmul(
                            pm[:, col:col + 2], w[:, k, oc * 128:(oc + 1) * 128], sc[:, k, :], start=(k == 0), stop=(k == 7)),
                            R=[w.b, sc.b], W=[pm.b])
                if cb in (4, 5, 10, 11):
                    which = 0 if cb < 6 else 1
                    half = cb - 4 if cb < 6 else cb - 10
                    for j in range(2):
                        pb = pbc[j]
                        for k in range(8):
                            fw.op("pe", lambda e, w=w, j=j, k=k, pb=pb: e.matmul(
                                pb[:], screp[:, j, k, :], w[:, k, :], start=(k == 0), stop=(k == 7)),
                                R=[w.b, screp.b], W=[pb.b])
                        bsl = bmodbc[:, which * 1024 + half * 512: which * 1024 + half * 512 + 512]
                        fw.op("dve", lambda e, pb=pb, j=j, which=which, half=half, bsl=bsl: e.tensor_tensor(
                            out=mbc[:, j, which, half * 512:(half + 1) * 512], in0=pb[:], in1=bsl, op=ALU.add),
                            R=[pb.b, bmodbc.b], W=[mbc.b])
            fw.dma("sp", self.MODBC.t.ap(), mbc[:].rearrange("p j w c -> p (j w c)"), self.MODBC.b, mbc.b)
            pmv = pm[:, 0:96].rearrange("p (a j) -> p a j", j=2)
            for j in range(2):
                fw.op("dve", lambda e, j=j: e.tensor_tensor(out=self.modT[:, :, j], in0=pmv[:, :, j], in1=bmodT[:], op=ALU.add),
                      R=[pm.b, bmodT.b], W=[self.modT.b])
            for (gT, gs, sh, ishift, iscale) in ((g1T, self.gs1, self.sh1, 0, 1), (g2T, self.gs2, self.sh2, 3, 4)):
                for j in range(2):
                    fw.op("dve", lambda e, j=j, gs=gs, iscale=iscale, gT=gT: e.scalar_tensor_tensor(
                        out=gs[:, :, j], in0=self.modT[:, iscale * 8:(iscale + 1) * 8, j], scalar=1.0, in1=gT[:],
                        op0=ALU.add, op1=ALU.mult), R=[self.modT.b, gT.b], W=[gs.b])
                    fw.op("dve", lambda e, j=j, sh=sh, ishift=ishift: e.tensor_copy(
                        out=sh[:, :, j], in_=self.modT[:, ishift * 8:(ishift + 1) * 8, j]), R=[self.modT.b], W=[sh.b])
            lp = fw.sb(ps, "lp", [128, 2, 64])
            le = fw.sb(ps, "le", [128, 2])
            fw.op("dve", lambda e: e.tensor_tensor(out=lp[:].rearrange("p a b -> p (a b)"), in0=lamq[:], in1=lamk[:], op=ALU.mult),
                  R=[lamq.b, lamk.b], W=[lp.b])
            fw.op("dve", lambda e: e.tensor_reduce(out=le[:], in_=lp[:], axis=AX.X, op=ALU.add), R=[lp.b], W=[le.b])
            fw.op("act", lambda e: e.activation(out=le[:], in_=le[:], func=AF.Exp), R=[le.b], W=[le.b])
            lam_init = 0.8 - 0.6 * math.exp(-0.3 * l)
            fw.op("dve", lambda e: e.tensor_tensor(out=self.lam[:, 0:1], in0=le[:, 0:1], in1=le[:, 1:2], op=ALU.subtract),
                  R=[le.b], W=[self.lam.b])
            fw.op("dve", lambda e: e.tensor_scalar(out=self.lam[:, 0:1], in0=self.lam[:, 0:1], scalar1=lam_init, scalar2=None,
                                                   op0=ALU.add), R=[self.lam.b], W=[self.lam.b])
            fw.op("dve", lambda e: e.tensor_scalar(out=self.lam[:, 1:2], in0=self.lam[:, 0:1], scalar1=-1.0, scalar2=None,
                                                   op0=ALU.mult), R=[self.lam.b], W=[self.lam.b])
            fw.fence()

    def pfh_row(self, t0):
        return t0 + 1 if t0 < L else (t0 - L) + L + 3

    def phase_proj(self, l):
        fw = self.fw
        with contextlib.ExitStack() as ps:
            Wm = fw.sb(ps, "Wm", [128, 8, NMIX], BF16)
            Wg = fw.sb(ps, "Wg", [128, 8, 3072], BF16)
            stg = [fw.sb(ps, "stg%d" % i, [128, 8, 256]) for i in range(2)]
            wmv = self.wmix.t.ap()[l].rearrange("(k p) c -> p k c", p=128)
            og, _ = SHOFF["wg%d" % l]
            wgv = dview(self.SH, og, (1024, 3072)).rearrange("(k p) c -> p k c", p=128)
            n = 0
            for (src, srcb, dst, nb) in ((wmv, self.wmix.b, Wm, NMIX // 256), (wgv, self.SH.b, Wg, 12)):
                for cb in range(nb):
                    s = stg[n % 2]; n += 1
                    fw.dma("sp", s[:], src[:, :, cb * 256:(cb + 1) * 256], s.b, srcb)
                    fw.op("act" if n % 2 else "dve",
                          (lambda e, s=s, dst=dst, cb=cb: e.activation(out=dst[:, :, cb * 256:(cb + 1) * 256], in_=s[:], func=AF.Copy))
                          if n % 2 else
                          (lambda e, s=s, dst=dst, cb=cb: e.tensor_copy(out=dst[:, :, cb * 256:(cb + 1) * 256], in_=s[:])),
                          R=[s.b], W=[dst.b])
            xt = [fw.sb(ps, "xt%d" % i, [128, 1024]) for i in range(2)]
            xn = [fw.sb(ps, "xn%d" % i, [128, 1024]) for i in range(2)]
            junk = fw.sb(ps, "junk", [128, 1024])
            ss = [fw.sb(ps, "ss%d" % i, [128, 2]) for i in range(2)]
            hT = [fw.sb(ps, "hT%d" % i, [128, 8, 512], BF16) for i in range(2)]
            rc = [fw.sb(ps, "rc%d" % i, [128, 512]) for i in range(2)]
            rs_ = [fw.sb(ps, "rs%d" % i, [128, 512]) for i in range(2)]
            sq = fw.sb(ps, "sq", [128, 512]); rq = fw.sb(ps, "rq", [128, 512]); qn = fw.sb(ps, "qn", [128, 512])
            t1 = fw.sb(ps, "t1", [128, 512]); t2 = fw.sb(ps, "t2", [128, 512])
            qo = [fw.sb(ps, "qo%d" % i, [128, 512], BF16) for i in range(2)]
            fhs = [fw.sb(ps, "fhs%d" % i, [128, 512]) for i in range(2)]
            vs = [fw.sb(ps, "vs%d" % i, [128, 256], BF16) for i in range(2)]
            gst = [fw.sb(ps, "gst%d" % i, [128, 4, 512], BF16) for i in range(2)]
            zt = fw.sb(ps, "zt", [4, 512])
            pt = [fw.ps(ps, "pt%d" % i) for i in range(2)]
            pj = [fw.ps(ps, "pj%d" % i) for i in range(3)]
            pn = [fw.ps(ps, "pn%d" % i) for i in range(2)]
            ident = self.cv("ident"); bones = self.cv("bones"); rot = self.cv("rot")
            cb_ = self.cstb
            fw.op("dve", lambda e: e.memset(zt[:], 0.0), W=[zt.b])
            for r in (0, L + 1, L + 2, T + 3):
                fw.dma("sp", self.PFH.t.ap()[r:r + 1, :], zt[0:1, :], self.PFH.b, zt.b, part=True)
            oc_, _ = SHOFF["ropec"]; os_, _ = SHOFF["ropes"]
            rcv = dview(self.SH, oc_, (128, T)); rsv = dview(self.SH, os_, (128, T))
            npj = 0
            ntile = (T + 511) // 512
            for tt in range(ntile):
                t0 = tt * 512
                n_ = min(512, T - t0)
                nsub = n_ // 128
                j = 0 if t0 < L else 1
                h = hT[tt % 2]
                for i in range(nsub):
                    x = xt[i % 2]; y = xn[i % 2]; s2 = ss[i % 2]
                    fw.dma("sp", x[:], self.X.t.ap()[t0 + i * 128:t0 + (i + 1) * 128, :], x.b, self.X.b)
                    fw.op("act", lambda e, x=x, s2=s2: e.activation(out=junk[:], in_=x[:], func=AF.Square, accum_out=s2[:, 0:1]),
                          R=[x.b], W=[junk.b, s2.b])
                    fw.op("act", lambda e, s2=s2: e.activation(out=s2[:, 1:2], in_=s2[:, 0:1], func=AF.Sqrt, scale=1.0 / D, bias=self.epsb[:, 0:1]),
                          R=[s2.b], W=[s2.b])
                    fw.op("dve", lambda e, s2=s2: e.reciprocal(out=s2[:, 1:2], in_=s2[:, 1:2]), R=[s2.b], W=[s2.b])
                    fw.op("dve", lambda e, x=x, y=y, s2=s2: e.tensor_scalar(out=y[:], in0=x[:], scalar1=s2[:, 1:2], scalar2=None, op0=ALU.mult),
                          R=[x.b, s2.b], W=[y.b])
                    for g in range(2):
                        for q in range(4):
                            k = 4 * g + q
                            fw.op("pe", lambda e, y=y, g=g, q=q, k=k: e.transpose(out=pt[g][:, q * 128:(q + 1) * 128], in_=y[:, k * 128:(k + 1) * 128], identity=ident),
                                  R=[y.b, cb_], W=[pt[g].b])
                        for q in range(4):
                            k = 4 * g + q
                            fw.op("dve", lambda e, g=g, q=q, k=k, i=i, h=h, j=j: e.tensor_scalar(
                                out=h[:, k, i * 128:(i + 1) * 128], in0=pt[g][:, q * 128:(q + 1) * 128],
                                scalar1=self.gs1[:, k, j:j + 1], scalar2=self.sh1[:, k, j:j + 1], op0=ALU.mult, op1=ALU.add),
                                R=[pt[g].b, self.gs1.b, self.sh1.b], W=[h.b])
                c_ = rc[tt % 2]; s_ = rs_[tt % 2]
                fw.dma("sp", c_[:, 0:n_], rcv[:, t0:t0 + n_], c_.b, self.SH.b)
                fw.dma("sp", s_[:, 0:n_], rsv[:, t0:t0 + n_], s_.b, self.SH.b)
                for i in range(nsub):
                    p = pj[npj % 3]; npj += 1
                    for k in range(8):
                        fw.op("pe", lambda e, p=p, k=k, i=i, h=h: e.matmul(p[:, 0:512], h[:, k, i * 128:(i + 1) * 128], Wm[:, k, 0:512], start=(k == 0), stop=(k == 7)),
                              R=[h.b, Wm.b], W=[p.b])
                    f = fhs[i % 2]
                    fw.op("act", lambda e, p=p, f=f: e.activation(out=f[:], in_=p[:, 0:512], func=AF.Copy), R=[p.b], W=[f.b])
                    r0 = self.pfh_row(t0 + i * 128)
                    fw.dma("sp", self.PFH.t.ap()[r0:r0 + 128, :], f[:], self.PFH.b, f.b, part=True)
                    p = pj[npj % 3]; npj += 1
                    for k in range(8):
                        fw.op("pe", lambda e, p=p, k=k, i=i, h=h: e.matmul(p[:, 0:256], h[:, k, i * 128:(i + 1) * 128], Wm[:, k, 1024:1280], start=(k == 0), stop=(k == 7)),
                              R=[h.b, Wm.b], W=[p.b])
                    v = vs[i % 2]
                    fw.op("act", lambda e, p=p, v=v: e.activation(out=v[:], in_=p[:, 0:256], func=AF.Copy), R=[p.b], W=[v.b])
                    fw.dma("sp", self.V.t.ap()[t0 + i * 128:t0 + (i + 1) * 128, :], v[:], self.V.b, v.b, part=True)
                for c in range(4):
                    p = pj[npj % 3]; npj += 1
                    c0 = 512 + c * 128
                    for k in range(8):
                        fw.op("pe", lambda e, p=p, k=k, h=h, n_=n_, c0=c0: e.matmul(p[:, 0:n_], Wm[:, k, c0:c0 + 128], h[:, k, 0:n_], start=(k == 0), stop=(k == 7)),
                              R=[h.b, Wm.b], W=[p.b])
                    fw.op("act", lambda e, p=p, n_=n_: e.activation(out=sq[:, 0:n_], in_=p[:, 0:n_], func=AF.Square), R=[p.b], W=[sq.b])
                    pa = pn[0]
                    fw.op("pe", lambda e, pa=pa, n_=n_: e.matmul(pa[:, 0:n_], bones, sq[:, 0:n_], start=True, stop=True), R=[sq.b, cb_], W=[pa.b])
                    fw.op("act", lambda e, pa=pa, n_=n_: e.activation(out=rq[:, 0:n_], in_=pa[:, 0:n_], func=AF.Sqrt, scale=1.0 / 64, bias=self.epsb[:, 0:1]),
                          R=[pa.b], W=[rq.b])
                    fw.op("dve", lambda e, n_=n_: e.reciprocal(out=rq[:, 0:n_], in_=rq[:, 0:n_]), R=[rq.b], W=[rq.b])
                    gi = 0 if c < 2 else 1
                    fw.op("dve", lambda e, p=p, n_=n_, gi=gi: e.scalar_tensor_tensor(out=qn[:, 0:n_], in0=p[:, 0:n_], scalar=self.qks[:, gi:gi + 1], in1=rq[:, 0:n_],
                                                                                   op0=ALU.mult, op1=ALU.mult), R=[p.b, rq.b, self.qks.b], W=[qn.b])
                    pb = pn[1]
                    fw.op("pe", lambda e, pb=pb, n_=n_: e.matmul(pb[:, 0:n_], rot, qn[:, 0:n_], start=True, stop=True), R=[qn.b, cb_], W=[pb.b])
                    fw.op("dve", lambda e, n_=n_, c_=c_: e.tensor_tensor(out=t1[:, 0:n_], in0=qn[:, 0:n_], in1=c_[:, 0:n_], op=ALU.mult), R=[qn.b, c_.b], W=[t1.b])
                    fw.op("dve", lambda e, pb=pb, n_=n_, s_=s_: e.tensor_tensor(out=t2[:, 0:n_], in0=pb[:, 0:n_], in1=s_[:, 0:n_], op=ALU.mult), R=[pb.b, s_.b], W=[t2.b])
                    o_ = qo[c % 2]
                    fw.op("dve", lambda e, n_=n_, o_=o_: e.tensor_tensor(out=o_[:, 0:n_], in0=t1[:, 0:n_], in1=t2[:, 0:n_], op=ALU.add), R=[t1.b, t2.b], W=[o_.b])
                    dstT = self.QT if c < 2 else self.KT
                    cc = c % 2
                    fw.dma("sp", dstT.t.ap()[cc * 128:(cc + 1) * 128, t0:t0 + n_], o_[:, 0:n_], dstT.b, o_.b, part=True)
                for c4 in range(6):
                    g_ = gst[c4 % 2]
                    for c in range(4):
                        cg = c4 * 4 + c
                        p = pj[npj % 3]; npj += 1
                        for k in range(8):
                            fw.op("pe", lambda e, p=p, k=k, h=h, n_=n_, cg=cg: e.matmul(p[:, 0:n_], Wg[:, k, cg * 128:(cg + 1) * 128], h[:, k, 0:n_], start=(k == 0), stop=(k == 7)),
                                  R=[h.b, Wg.b], W=[p.b])
                        fw.op("act", lambda e, p=p, n_=n_, g_=g_, c=c: e.activation(out=g_[:, c, 0:n_], in_=p[:, 0:n_], func=AF.Sigmoid), R=[p.b], W=[g_.b])
                    fw.dma("sp", self.GT.t.ap()[c4 * 512:(c4 + 1) * 512, t0:t0 + n_].rearrange("(c p) t -> p c t", p=128), g_[:, :, 0:n_], self.GT.b, g_.b, part=True)
            fw.fence()

    def build(self):
        fw = self.fw
        with contextlib.ExitStack() as st:
            self.cst = fw.sb(st, "cstt", [128, CST.n]); self.cstb = self.cst.b
            self.modT = fw.sb(st, "modT", [128, 48, 2])
            self.gs1 = fw.sb(st, "gs1", [128, 8, 2]); self.sh1 = fw.sb(st, "sh1", [128, 8, 2])
            self.gs2 = fw.sb(st, "gs2", [128, 8, 2]); self.sh2 = fw.sb(st, "sh2", [128, 8, 2])
            self.lam = fw.sb(st, "lam", [128, 2])
            self.qks = fw.sb(st, "qks", [128, 4])
            self.epsb = fw.sb(st, "epsb", [128, 2])
            fw.dma("sp", self.cst[:], self.cst_d.t.ap(), self.cst.b, self.cst_d.b)
            fw.op("dve", lambda e: e.memset(self.epsb[:, 0:1], EPS), W=[self.epsb.b])
            fw.op("dve", lambda e: e.memset(self.epsb[:, 1:2], SUBLN_EPS), W=[self.epsb.b])
            self.phase_gather()
            for l in range(DEPTH if self.stop != ("gather", 0) else 0):
                self.phase_mod(l)
                if self.stop == ("mod", l):
                    break
                self.phase_proj(l)
                if self.stop == ("proj", l):
                    break
                self.phase_seqmix(l, "lat")
                if l == 0:
                    self.phase_seqmix(l, "ctx")
                if self.stop == ("seq", l):
                    break
                self.phase_attn(l)
                if self.stop == ("attn", l):
                    break
                self.phase_outproj(l)
                if self.stop == ("outp", l):
                    break
                if self.with_exp:
                    self.phase_moe(l)
            for nm, tl in (("QT", self.QT), ("KT", self.KT), ("V", self.V), ("PFH", self.PFH), ("GT", self.GT), ("X", self.X), ("MODBC", self.MODBC), ("FM", self.FM), ("HM", self.HM), ("OT", self.OT)):
                self.dump(nm, tl)
            self.bigcopy(self.out, self.out.t.ap(), self.X, self.X.t.ap(), L, 512)
            fw.fence()
        fw.close()
        return self.nc


FFT_CFG = {"FA": (64, 64), "HB": (128, 64), "FC": (2, 2), "HD": (4, 2)}


def make_fft_layout():
    P = Pack()
    for nm in ("w128r", "w128i", "w128n"):
        P.add(nm, 128)
    for cfg, (n1, n1in) in FFT_CFG.items():
        for nm in ("w1r", "w1i", "w1n"):
            P.add(cfg + nm, n1)
        P.add(cfg + "tr", 128); P.add(cfg + "ti", 128); P.add(cfg + "tn", 128)
        if cfg in ("HB", "HD"):
            for nm in ("v1r", "v1i", "v1n"):
                P.add(cfg + nm, n1in)
    return P


FFTC = make_fft_layout()


def fft_constants():
    c = np.zeros((128, FFTC.n), np.float64)

    def put(name, val):
        o, w = FFTC.off[name]
        c[:val.shape[0], o:o + w] = val

    a = np.arange(128)
    w128 = np.exp(-2j * np.pi * np.outer(a, a) / 128)
    put("w128r", w128.real); put("w128i", w128.imag); put("w128n", -w128.imag)
    for cfg, (n1, n1in) in FFT_CFG.items():
        N = n1 * 128
        b = np.arange(n1)
        w1 = np.exp(-2j * np.pi * np.outer(b, b) / n1)
        put(cfg + "w1r", w1.real); put(cfg + "w1i", w1.imag); put(cfg + "w1n", -w1.imag)
        tw = np.exp(-2j * np.pi * np.outer(b, a) / N)
        put(cfg + "tr", tw.real); put(cfg + "ti", tw.imag); put(cfg + "tn", -tw.imag)
        if cfg in ("HB", "HD"):
            v1 = np.exp(+2j * np.pi * np.outer(b, np.arange(n1in)) / n1) / N
            put(cfg + "v1r", v1.real); put(cfg + "v1i", v1.imag); put(cfg + "v1n", -v1.imag)
    return c.astype(np.float32)


CG = 32
NCOL = 128 * CG


def _fc(self, name):
    o, w = FFTC.off[name]
    return self.fftc[:, o:o + w]


def _mm_blocks(self, outs, terms, M, K, ncols):
    fw = self.fw
    nb = 0
    for c0 in range(0, ncols, 512):
        cw = min(512, ncols - c0)
        for oi, (ot, tl) in enumerate(zip(outs, terms)):
            p = self.fps[self.nfps % len(self.fps)]; self.nfps += 1
            for ti, (lh, rt) in enumerate(tl):
                fw.op("pe", lambda e, p=p, lh=lh, rt=rt, c0=c0, cw=cw, ti=ti, n=len(tl): e.matmul(
                    p[0:M, 0:cw], lh, rt[0:K, c0:c0 + cw], start=(ti == 0), stop=(ti == n - 1)),
                    R=[rt.b, self.fftcb.b], W=[p.b])
            if nb % 2 == 0:
                fw.op("act", lambda e, p=p, ot=ot, c0=c0, cw=cw: e.activation(out=ot[0:M, c0:c0 + cw], in_=p[0:M, 0:cw], func=AF.Copy),
                      R=[p.b], W=[ot.b])
            else:
                fw.op("dve", lambda e, p=p, ot=ot, c0=c0, cw=cw: e.tensor_copy(out=ot[0:M, c0:c0 + cw], in_=p[0:M, 0:cw]),
                      R=[p.b], W=[ot.b])
            nb += 1


def _cmul(self, outr, outi, ar, ai, br, bi, P, ncols, t1, t2, bshape=None):
    fw = self.fw

    def A(t):
        return t[0:P, 0:ncols]

    def Bv(t):
        return t if bshape else t[0:P, 0:ncols]

    def V(t):
        return t[0:P, 0:ncols].rearrange("p (a c) -> p a c", c=bshape) if bshape else t[0:P, 0:ncols]
    xr = [self.fftc.b] if bshape else [br.b]
    xi = [self.fftc.b] if bshape else [bi.b]
    fw.op("dve", lambda e: e.tensor_tensor(out=V(t1), in0=V(ar), in1=Bv(br), op=ALU.mult), R=[ar.b] + xr, W=[t1.b])
    fw.op("dve", lambda e: e.tensor_tensor(out=V(t2), in0=V(ai), in1=Bv(bi), op=ALU.mult), R=[ai.b] + xi, W=[t2.b])
    fw.op("dve", lambda e: e.tensor_tensor(out=A(outr), in0=A(t1), in1=A(t2), op=ALU.subtract), R=[t1.b, t2.b], W=[outr.b])
    fw.op("dve", lambda e: e.tensor_tensor(out=V(t1), in0=V(ar), in1=Bv(bi), op=ALU.mult), R=[ar.b] + xi, W=[t1.b])
    fw.op("dve", lambda e: e.tensor_tensor(out=V(t2), in0=V(ai), in1=Bv(br), op=ALU.mult), R=[ai.b] + xr, W=[t2.b])
    fw.op("dve", lambda e: e.tensor_tensor(out=A(outi), in0=A(t1), in1=A(t2), op=ALU.add), R=[t1.b, t2.b], W=[outi.b])


def _dtrans(self, src, P1, dst):
    fw = self.fw
    sc = self.tscr[self.ntscr % len(self.tscr)]; self.ntscr += 1
    fw.dma("sp", sc.t.ap()[0:P1, :], src[0:P1, 0:NCOL], sc.b, src.b)
    v = sc.t.ap()[0:P1, :].rearrange("a (j c) -> j a c", c=CG)
    fw.dma("sp", dst[:, 0:P1 * CG].rearrange("p (a c) -> p a c", c=CG), v, dst.b, sc.b)


def _fcb(self, name):
    o, w = FFTC.off[name]
    return self.fftcb[:, o:o + w]


def _fft_fwd(self, cfg, x, Xr, Xi, W):
    n1, n1in = FFT_CFG[cfg]
    a_r, a_i, t1, t2 = W[0], W[1], W[2], W[3]
    B = self.fB
    fc = self._fc; fb = self._fcb
    w1r = fb(cfg + "w1r")[0:n1in, 0:n1]; w1i = fb(cfg + "w1i")[0:n1in, 0:n1]
    self._mm_blocks([a_r, a_i], [[(w1r, x)], [(w1i, x)]], n1, n1in, NCOL)
    tr = fc(cfg + "tr")[0:n1, :].unsqueeze(2).broadcast_to([n1, 128, CG])
    ti = fc(cfg + "ti")[0:n1, :].unsqueeze(2).broadcast_to([n1, 128, CG])
    self._cmul(B[0], B[1], a_r, a_i, tr, ti, n1, NCOL, t1, t2, bshape=CG)
    self._dtrans(B[0], n1, B[2])
    self._dtrans(B[1], n1, B[3])
    wr = fb("w128r"); wi = fb("w128i"); wn = fb("w128n")
    self._mm_blocks([Xr, Xi], [[(wr, B[2]), (wn, B[3])], [(wi, B[2]), (wr, B[3])]], 128, 128, n1 * CG)


def _fft_inv(self, cfg, Yr, Yi, y, W):
    n1, n1in = FFT_CFG[cfg]
    t1, t2 = W[2], W[3]
    B = self.fB
    fc = self._fc; fb = self._fcb
    wr = fb("w128r"); wi = fb("w128i"); wn = fb("w128n")
    self._mm_blocks([B[2], B[3]], [[(wr, Yr), (wi, Yi)], [(wn, Yr), (wr, Yi)]], 128, 128, n1 * CG)
    self._dtrans_back(B[2], n1, B[0])
    self._dtrans_back(B[3], n1, B[1])
    tr = fc(cfg + "tr")[0:n1, :].unsqueeze(2).broadcast_to([n1, 128, CG])
    tn = fc(cfg + "tn")[0:n1, :].unsqueeze(2).broadcast_to([n1, 128, CG])
    self._cmul(B[2], B[3], B[0], B[1], tr, tn, n1, NCOL, t1, t2, bshape=CG)
    v1r = fb(cfg + "v1r")[0:n1, 0:n1in]; v1n = fb(cfg + "v1n")[0:n1, 0:n1in]
    self._mm_blocks([y], [[(v1r, B[2]), (v1n, B[3])]], n1in, n1, NCOL)


def _dtrans_back(self, src, P1, dst):
    fw = self.fw
    sc = self.tscr[self.ntscr % len(self.tscr)]; self.ntscr += 1
    v = sc.t.ap()[0:P1, :].rearrange("a (j c) -> j a c", c=CG)
    fw.dma("sp", v, src[:, 0:P1 * CG].rearrange("p (a c) -> p a c", c=CG), sc.b, src.b)
    fw.dma("sp", dst[0:P1, 0:NCOL], sc.t.ap()[0:P1, :], dst.b, sc.b)


for _f in (_fc, _fcb, _mm_blocks, _cmul, _dtrans, _dtrans_back, _fft_fwd, _fft_inv):
    setattr(Prog, _f.__name__, _f)

MAGIC = 12582912.0
TWO_PI = 2.0 * math.pi


def _rr_sin(self, out, arg, tmp, P, n):
    fw = self.fw
    fw.op("dve", lambda e: e.tensor_scalar(out=tmp[0:P, 0:n], in0=arg[0:P, 0:n], scalar1=1.0 / TWO_PI, scalar2=MAGIC, op0=ALU.mult, op1=ALU.add),
          R=[arg.b], W=[tmp.b])
    fw.op("dve", lambda e: e.tensor_scalar(out=tmp[0:P, 0:n], in0=tmp[0:P, 0:n], scalar1=-MAGIC, scalar2=None, op0=ALU.add), R=[tmp.b], W=[tmp.b])
    fw.op("dve", lambda e: e.scalar_tensor_tensor(out=tmp[0:P, 0:n], in0=tmp[0:P, 0:n], scalar=-TWO_PI, in1=arg[0:P, 0:n], op0=ALU.mult, op1=ALU.add),
          R=[tmp.b, arg.b], W=[tmp.b])
    fw.op("dve", lambda e: e.tensor_scalar(out=tmp[0:P, 0:n], in0=tmp[0:P, 0:n], scalar1=3.14159, scalar2=-3.14159, op0=ALU.min, op1=ALU.max),
          R=[tmp.b], W=[tmp.b])
    fw.op("act", lambda e: e.activation(out=out[0:P, 0:n], in_=tmp[0:P, 0:n], func=AF.Sin), R=[tmp.b], W=[out.b])


def _gen_filter(self, l, ps, Lseq, ecol0, ntl, HF, sml, rn):
    fw = self.fw
    hw1, hb1, hf1, hw2, hb2, hf2, hw3, hb3, dbc = sml
    et = fw.sb(ps, "f_e", [33, 512]); arg = fw.sb(ps, "f_arg", [64, 512]); tmp = fw.sb(ps, "f_tmp", [64, 512])
    z1 = fw.sb(ps, "f_z1", [64, 512]); z2 = fw.sb(ps, "f_z2", [64, 512])
    hh = [fw.sb(ps, "f_h0", [128, 512])]
    ab = fw.sb(ps, "f_ab", [128, 512]); dec = fw.sb(ps, "f_dec", [128, 128])
    nrm = ab
    p1 = self.fps[0]; p2 = self.fps[1]; p3 = self.fps[2]; pN = self.fps[3]
    ones = self.cv("ones")
    nsub_tot = Lseq // 128
    si = 0
    for t0 in range(0, Lseq, 512):
        n = min(512, Lseq - t0)
        fw.dma("sp", et[:, 0:n], self.embc.t.ap()[:, ecol0 + t0:ecol0 + t0 + n], et.b, self.embc.b)
        fw.op("pe", lambda e, n=n: e.matmul(p1[0:64, 0:n], hw1[0:33, :], et[:, 0:n], start=True, stop=True), R=[et.b, hw1.b], W=[p1.b])
        fw.op("dve", lambda e, n=n: e.tensor_scalar(out=arg[:, 0:n], in0=p1[0:64, 0:n], scalar1=hb1[0:64, 0:1], scalar2=hf1[0:64, 0:1], op0=ALU.add, op1=ALU.mult),
              R=[p1.b, hb1.b, hf1.b], W=[arg.b])
        self._rr_sin(z1, arg, tmp, 64, n)
        fw.op("pe", lambda e, n=n: e.matmul(p2[0:64, 0:n], hw2[0:64, :], z1[:, 0:n], start=True, stop=True), R=[z1.b, hw2.b], W=[p2.b])
        fw.op("dve", lambda e, n=n: e.tensor_scalar(out=arg[:, 0:n], in0=p2[0:64, 0:n], scalar1=hb2[0:64, 0:1], scalar2=hf2[0:64, 0:1], op0=ALU.add, op1=ALU.mult),
              R=[p2.b, hb2.b, hf2.b], W=[arg.b])
        self._rr_sin(z2, arg, tmp, 64, n)
        for i in range(n // 128):
            h = hh[0]
            fw.op("pe", lambda e, i=i: e.matmul(p3[:, :], z2[:, i * 128:(i + 1) * 128], hw3[0:64, :], start=True, stop=True), R=[z2.b, hw3.b], W=[p3.b])
            fw.op("dve", lambda e, h=h: e.tensor_tensor(out=h[:], in0=p3[:], in1=hb3[:], op=ALU.add), R=[p3.b, hb3.b], W=[h.b])
            fw.op("act", lambda e, si=si: e.activation(out=dec[:], in_=dbc[:], func=AF.Exp, scale=ntl[:, si:si + 1]), R=[dbc.b, self.cstb], W=[dec.b])
            fw.op("dve", lambda e, h=h: e.tensor_tensor(out=h[:].rearrange("p (g c) -> p g c", c=128), in0=h[:].rearrange("p (g c) -> p g c", c=128),
                                                        in1=dec[:].unsqueeze(1).broadcast_to([128, 4, 128]), op=ALU.mult), R=[h.b, dec.b], W=[h.b])
            if si == 0:
                for c0 in (128, 384):
                    fw.op("dve", lambda e, h=h, c0=c0: e.memset(h[0:1, c0:c0 + 128], 0.0), W=[h.b])
            fw.op("act", lambda e, h=h: e.activation(out=ab[:], in_=h[:], func=AF.Abs), R=[h.b], W=[ab.b])
            fw.op("pe", lambda e, si=si: e.matmul(pN[:], ones, ab[:], start=(si == 0), stop=(si == nsub_tot - 1)), R=[ab.b, self.cstb], W=[pN.b])
            fw.dma("sp", HF.t.ap()[t0 + i * 128:t0 + (i + 1) * 128, :], h[:], HF.b, h.b, part=True)
            si += 1
    fw.op("dve", lambda e: e.tensor_copy(out=nrm[:], in_=pN[:]), R=[pN.b], W=[nrm.b])
    nv = nrm[:].rearrange("p (o d c) -> p o d c", o=2, d=2)
    fw.op("dve", lambda e: e.tensor_tensor(out=rn[:], in0=nv[:, :, 0, :], in1=nv[:, :, 1, :], op=ALU.add), R=[nrm.b], W=[rn.b])
    fw.op("dve", lambda e: e.reciprocal(out=rn[:], in_=rn[:]), R=[rn.b], W=[rn.b])
    return rn


def _seq_cfg(self, which):
    if which == "lat":
        return "FA", "HB", L, 1, 0, 0, "ntl8192"
    return "FC", "HD", NCTX, L + 3, L, L, "ntl256"


def phase_seqmix(self, l, which):
    fw = self.fw
    fcfg, hcfg, Lseq, prow, tbase, ecol0, ntlname = self._seq_cfg(which)
    n1f, n1fin = FFT_CFG[fcfg]
    n1h, n1hin = FFT_CFG[hcfg]
    with contextlib.ExitStack() as ps:
        self.fftc = fw.sb(ps, "fftc_t", [128, FFTC.n])
        fw.dma("sp", self.fftc[:], self.fftc_d.t.ap(), self.fftc.b, self.fftc_d.b)
        self.fps = [fw.ps(ps, "fps%d" % i) for i in range(6)]
        self.nfps = 0
        self.tscr = [fw.dram("tscr%d_%d_%s" % (i, l, which), [128, NCOL], BF16) for i in range(2)]
        self.fftcb = fw.sb(ps, "fftcb_t", [128, FFTC.n], BF16)
        fw.op("act", lambda e: e.activation(out=self.fftcb[:], in_=self.fftc[:], func=AF.Copy), R=[self.fftc.b], W=[self.fftcb.b])
        self.fB = [fw.sb(ps, "fB%d" % i, [128, NCOL], BF16) for i in range(4)]
        self.ntscr = 0
        W = [fw.sb(ps, "fw%d" % i, [128, NCOL]) for i in range(4)]
        Xr = fw.sb(ps, "fXr", [128, NCOL]); Xi = fw.sb(ps, "fXi", [128, NCOL])
        zv = fw.sb(ps, "fzv", [64, NCOL], BF16); zx = fw.sb(ps, "fzx", [64, NCOL], BF16)
        halo = fw.sb(ps, "fhalo", [64, 130, CG])

        def ldcast(src_ap, srcb, rows):
            fw.dma("sp", W[0][0:rows, :].rearrange("p (j c) -> p j c", c=CG), src_ap, W[0].b, srcb)
            fw.op("act", lambda e: e.activation(out=zv[0:rows, :], in_=W[0][0:rows, :], func=AF.Copy), R=[W[0].b], W=[zv.b])
            return zv
        for cg in range(4):
            src = self.PFH.t.ap()[prow:prow + Lseq, cg * CG:(cg + 1) * CG].rearrange("(a j) c -> a j c", j=128)
            self._fft_fwd(fcfg, ldcast(src, self.PFH.b, n1fin), Xr, Xi, W)
            for ri, X_ in ((0, Xr), (1, Xi)):
                dst = self.FM.t.ap()[tbase:tbase + Lseq, ri, cg * CG:(cg + 1) * CG].rearrange("(p a) c -> p a c", a=n1f)
                fw.dma("sp", dst, X_[:, 0:n1f * CG].rearrange("p (a c) -> p a c", c=CG), self.FM.b, X_.b, part=True)
        rn = fw.sb(ps, "f_rn", [128, 2, 128])
        with contextlib.ExitStack() as ps2:
            sml = [self.ld(ps2, "hw1%d" % l, 33), self.ld(ps2, "hb1%d" % l, 64), self.ld(ps2, "hf1%d" % l, 64),
                   self.ld(ps2, "hw2%d" % l, 64), self.ld(ps2, "hb2%d" % l, 64), self.ld(ps2, "hf2%d" % l, 64),
                   self.ld(ps2, "hw3%d" % l, 64), self.ld(ps2, "hb3bc%d" % l), self.ld(ps2, "deltabc")]
            o_, w_ = CST.off[ntlname]
            ntl = self.cst[:, o_:o_ + w_]
            HF = fw.dram("HF_%d_%s" % (l, which), [Lseq, 512], F32)
            self._gen_filter(l, ps2, Lseq, ecol0, ntl, HF, sml, rn)
            fw.fence()
        hbias = self.ld(ps, "hbias%d" % l)
        HS = fw.dram("HS_%d_%s" % (l, which), [2, 4, 2, 128, n1h * CG], F32)
        for o in range(2):
            for cg in range(4):
                nc_ = n1h * CG
                for d in range(2):
                    src = HF.t.ap()[:, o * 256 + d * 128 + cg * CG: o * 256 + d * 128 + (cg + 1) * CG].rearrange("(a j) c -> a j c", j=128)
                    self._fft_fwd(hcfg, ldcast(src, HF.b, n1hin), Xr, Xi, W)
                    if d == 0:
                        fw.dma("sp", HS.t.ap()[o, cg, 0], Xr[:, 0:nc_], HS.b, Xr.b)
                        fw.dma("sp", HS.t.ap()[o, cg, 1], Xi[:, 0:nc_], HS.b, Xi.b)
                fw.dma("sp", W[0][:, 0:nc_], HS.t.ap()[o, cg, 0], W[0].b, HS.b)
                fw.dma("sp", W[1][:, 0:nc_], HS.t.ap()[o, cg, 1], W[1].b, HS.b)
                rnb = rn[:, o, cg * CG:(cg + 1) * CG].unsqueeze(1).broadcast_to([128, n1h, CG])
                bb = hbias[:, o * 128 + cg * CG: o * 128 + (cg + 1) * CG].unsqueeze(1).broadcast_to([128, n1h, CG])

                def v3(t, nc_=nc_):
                    return t[:, 0:nc_].rearrange("p (a c) -> p a c", c=CG)
                fw.op("dve", lambda e, v3=v3: e.tensor_tensor(out=v3(W[0]), in0=v3(W[0]), in1=v3(Xr), op=ALU.add), R=[Xr.b, W[0].b], W=[W[0].b])
                fw.op("dve", lambda e, v3=v3: e.tensor_tensor(out=v3(W[1]), in0=v3(W[1]), in1=v3(Xi), op=ALU.subtract), R=[Xi.b, W[1].b], W=[W[1].b])
                fw.op("dve", lambda e, rnb=rnb, v3=v3: e.tensor_tensor(out=v3(W[0]), in0=v3(W[0]), in1=rnb, op=ALU.mult), R=[W[0].b, rn.b], W=[W[0].b])
                fw.op("dve", lambda e, rnb=rnb, v3=v3: e.tensor_tensor(out=v3(W[1]), in0=v3(W[1]), in1=rnb, op=ALU.mult), R=[W[1].b, rn.b], W=[W[1].b])
                fw.op("dve", lambda e, bb=bb, v3=v3: e.tensor_tensor(out=v3(W[0]), in0=v3(W[0]), in1=bb, op=ALU.add), R=[W[0].b, hbias.b], W=[W[0].b])
                fw.dma("sp", HS.t.ap()[o, cg, 0], W[0][:, 0:nc_], HS.b, W[0].b)
                fw.dma("sp", HS.t.ap()[o, cg, 1], W[1][:, 0:nc_], HS.b, W[1].b)
        cw = self.ld(ps, "cwbc%d" % l); cbv = self.ld(ps, "cbbc%d" % l)

        def shortconv(dst, wi, cg):
            col0 = 128 + wi * 128 + cg * CG
            base = self.PFH.t.ap()[prow - 1:prow - 1 + Lseq + 2, col0:col0 + CG]
            src = bass.AP(self.PFH.t, base.offset, [[128 * 512, n1hin], [512, 130], [1, CG]])
            fw.dma("sp", halo[0:n1hin, :, :], src, halo.b, self.PFH.b)
            d3 = dst[0:n1hin, :].rearrange("p (j c) -> p j c", c=CG)
            for j in range(3):
                wv = cw[0:n1hin, j * 384 + wi * 128 + cg * CG: j * 384 + wi * 128 + (cg + 1) * CG].unsqueeze(1).broadcast_to([n1hin, 128, CG])
                if j == 0:
                    fw.op("dve", lambda e, wv=wv: e.tensor_tensor(out=d3, in0=halo[0:n1hin, 0:128, :], in1=wv, op=ALU.mult), R=[halo.b, cw.b], W=[dst.b])
                else:
                    fw.op("dve", lambda e, wv=wv, j=j: e.tensor_tensor(out=W[3][0:n1hin, :].rearrange("p (j c) -> p j c", c=CG), in0=halo[0:n1hin, j:j + 128, :], in1=wv, op=ALU.mult),
                          R=[halo.b, cw.b], W=[W[3].b])
                    fw.op("dve", lambda e: e.tensor_tensor(out=dst[0:n1hin, :], in0=dst[0:n1hin, :], in1=W[3][0:n1hin, :], op=ALU.add), R=[dst.b, W[3].b], W=[dst.b])
            bv = cbv[0:n1hin, wi * 128 + cg * CG: wi * 128 + (cg + 1) * CG].unsqueeze(1).broadcast_to([n1hin, 128, CG])
            fw.op("dve", lambda e, bv=bv: e.tensor_tensor(out=d3, in0=d3, in1=bv, op=ALU.add), R=[dst.b, cbv.b], W=[dst.b])

        for cg in range(4):
            shortconv(zv, 0, cg)
            shortconv(zx, 1, cg)
            for o in range(2):
                src_t = zv if o == 0 else zx
                self._fft_fwd(hcfg, src_t, Xr, Xi, W)
                nc_ = n1h * CG
                fw.dma("sp", W[0][:, 0:nc_], HS.t.ap()[o, cg, 0], W[0].b, HS.b)
                fw.dma("sp", W[1][:, 0:nc_], HS.t.ap()[o, cg, 1], W[1].b, HS.b)
                self._cmul(self.fB[0], self.fB[1], Xr, Xi, W[0], W[1], 128, nc_, W[2], W[3])
                self._fft_inv(hcfg, self.fB[0], self.fB[1], W[0], W)
                if o == 0:
                    fw.op("dve", lambda e: e.tensor_tensor(out=zx[0:n1hin, :], in0=zx[0:n1hin, :], in1=W[0][0:n1hin, :], op=ALU.mult), R=[zx.b, W[0].b], W=[zx.b])
                else:
                    shortconv(W[1], 2, cg)
                    fw.op("dve", lambda e: e.tensor_tensor(out=W[0][0:n1hin, :], in0=W[0][0:n1hin, :], in1=W[1][0:n1hin, :], op=ALU.mult), R=[W[0].b, W[1].b], W=[W[0].b])
                    dst = self.HM.t.ap()[tbase:tbase + Lseq, cg * CG:(cg + 1) * CG].rearrange("(a j) c -> a j c", j=128)
                    fw.dma("sp", dst, W[0][0:n1hin, :].rearrange("p (j c) -> p j c", c=CG), self.HM.b, W[0].b, part=True)
        fw.fence()


def halo_flat(halo):
    return Tile(halo.t, halo.b)


for _f in (_rr_sin, _gen_filter, _seq_cfg, phase_seqmix):
    setattr(Prog, _f.__name__, _f)


def phase_attn(self, l):
    fw = self.fw
    lam_init = 0.8 - 0.6 * math.exp(-0.3 * l)
    with contextlib.ExitStack() as ps:
        KTs = [fw.sb(ps, "aKT%d" % h, [128, T], BF16) for h in range(2)]
        Vs = fw.sb(ps, "aV", [128, T // 128, 256], BF16)
        onesb = fw.sb(ps, "aones", [128, 128], BF16)
        Qs = [fw.sb(ps, "aQ%d" % i, [128, 512], BF16) for i in range(2)]
        pts = [fw.sb(ps, "apt%d" % i, [128, 512], BF16) for i in range(4)]
        r0 = fw.sb(ps, "ar0", [128, 512]); r1 = fw.sb(ps, "ar1", [128, 512])
        a0 = fw.sb(ps, "aa0", [128, 512]); a1 = fw.sb(ps, "aa1", [128, 512])
        sq = fw.sb(ps, "asq", [128, 512])
        ob = [fw.sb(ps, "aob%d" % i, [128, 512], BF16) for i in range(2)]
        pss = [fw.ps(ps, "aps%d" % i) for i in range(3)]
        po = [fw.ps(ps, "apo%d" % i) for i in range(2)]
        pz = [fw.ps(ps, "apz%d" % i) for i in range(2)]
        ones = self.cv("ones")
        pc = None
        if self.with_exp:
            fw.h["sp"].wait_ge(self.expsem, 48 * (l + 1))
            pst = [fw.sb(ps, "pcst%d" % i, [128, 8, 512]) for i in range(2)]
            psb = [fw.sb(ps, "pcsb%d" % i, [128, 8, 512], BF16) for i in range(2)]
            pc = self.precast_steps(l, pst, psb)
        fw.op("dve", lambda e: e.tensor_copy(out=onesb[:], in_=ones), R=[self.cstb], W=[onesb.b])
        for h in range(2):
            fw.dma("sp", KTs[h][:], self.KT.t.ap()[h * 128:(h + 1) * 128, :], KTs[h].b, self.KT.b)
        fw.dma("sp", Vs[:], self.V.t.ap().rearrange("(ch p) c -> p ch c", p=128), Vs.b, self.V.b)
        qtiles = [(q0, 512, list(range(T // 128))) for q0 in range(0, L, 512)]
        if l == 0:
            qtiles.append((L, NCTX, [L // 128, L // 128 + 1]))
        nq = 0; npt = 0; nps = 0
        for (q0, n_, kcs) in qtiles:
            for h in range(2):
                Q = Qs[nq % 2]; nq += 1
                fw.dma("sp", Q[:, 0:n_], self.QT.t.ap()[h * 128:(h + 1) * 128, q0:q0 + n_], Q.b, self.QT.b)
                LOOK = 2
                for m in range(2):
                    nk = len(kcs)
                    ptq = {}
                    for kk in range(nk + LOOK):
                        if kk < nk:
                            kc = kcs[kk]
                            s_ = pss[nps % 3]; nps += 1
                            fw.op("pe", lambda e, s_=s_, h=h, m=m, kc=kc, Q=Q, n_=n_: e.matmul(
                                s_[:, 0:n_], KTs[h][m * 64:(m + 1) * 64, kc * 128:(kc + 1) * 128], Q[m * 64:(m + 1) * 64, 0:n_], start=True, stop=True),
                                R=[KTs[h].b, Q.b], W=[s_.b])
                            pt = pts[npt % 4]; npt += 1
                            fw.op("act", lambda e, s_=s_, pt=pt, n_=n_: e.activation(out=pt[:, 0:n_], in_=s_[:, 0:n_], func=AF.Exp, scale=0.125),
                                  R=[s_.b], W=[pt.b])
                            ptq[kk] = pt
                        ki = kk - LOOK
                        if ki >= 0:
                            kc = kcs[ki]; pt = ptq.pop(ki)
                            fw.op("pe", lambda e, pt=pt, h=h, m=m, kc=kc, ki=ki, n_=n_, nk=nk: e.matmul(
                                po[m][:, 0:n_], Vs[:, kc, h * 128:(h + 1) * 128], pt[:, 0:n_], start=(ki == 0), stop=(ki == nk - 1)),
                                R=[Vs.b, pt.b], W=[po[m].b])
                            fw.op("pe", lambda e, pt=pt, m=m, ki=ki, n_=n_, nk=nk: e.matmul(
                                pz[m][:, 0:n_], onesb[:], pt[:, 0:n_], start=(ki == 0), stop=(ki == nk - 1)),
                                R=[onesb.b, pt.b], W=[pz[m].b])
                fw.op("dve", lambda e, n_=n_: e.reciprocal(out=r0[:, 0:n_], in_=pz[0][:, 0:n_]), R=[pz[0].b], W=[r0.b])
                fw.op("dve", lambda e, n_=n_: e.reciprocal(out=r1[:, 0:n_], in_=pz[1][:, 0:n_]), R=[pz[1].b], W=[r1.b])
                fw.op("dve", lambda e, n_=n_: e.tensor_tensor(out=a0[:, 0:n_], in0=po[0][:, 0:n_], in1=r0[:, 0:n_], op=ALU.mult), R=[po[0].b, r0.b], W=[a0.b])
                fw.op("dve", lambda e, n_=n_: e.tensor_tensor(out=a1[:, 0:n_], in0=po[1][:, 0:n_], in1=r1[:, 0:n_], op=ALU.mult), R=[po[1].b, r1.b], W=[a1.b])
                fw.op("dve", lambda e, n_=n_: e.scalar_tensor_tensor(out=a0[:, 0:n_], in0=a1[:, 0:n_], scalar=self.lam[:, 1:2], in1=a0[:, 0:n_], op0=ALU.mult, op1=ALU.add),
                      R=[a0.b, a1.b, self.lam.b], W=[a0.b])
                fw.op("act", lambda e, n_=n_: e.activation(out=sq[:, 0:n_], in_=a0[:, 0:n_], func=AF.Square), R=[a0.b], W=[sq.b])
                s_ = pss[nps % 3]; nps += 1
                fw.op("pe", lambda e, s_=s_, n_=n_: e.matmul(s_[:, 0:n_], ones, sq[:, 0:n_], start=True, stop=True), R=[sq.b, self.cstb], W=[s_.b])
                fw.op("act", lambda e, s_=s_, n_=n_: e.activation(out=r0[:, 0:n_], in_=s_[:, 0:n_], func=AF.Sqrt, scale=1.0 / 128, bias=self.epsb[:, 1:2]), R=[s_.b], W=[r0.b])
                fw.op("dve", lambda e, n_=n_: e.reciprocal(out=r0[:, 0:n_], in_=r0[:, 0:n_]), R=[r0.b], W=[r0.b])
                fw.op("dve", lambda e, n_=n_: e.scalar_tensor_tensor(out=a0[:, 0:n_], in0=a0[:, 0:n_], scalar=self.qks[:, 2:3], in1=r0[:, 0:n_], op0=ALU.mult, op1=ALU.mult),
                      R=[a0.b, r0.b, self.qks.b], W=[a0.b])
                o_ = ob[nq % 2]
                fw.op("dve", lambda e, n_=n_, o_=o_: e.tensor_scalar(out=o_[:, 0:n_], in0=a0[:, 0:n_], scalar1=(1.0 - lam_init), scalar2=None, op0=ALU.mult), R=[a0.b], W=[o_.b])
                fw.dma("sp", self.OT.t.ap()[h * 128:(h + 1) * 128, q0:q0 + n_], o_[:, 0:n_], self.OT.b, o_.b, part=True)
                if pc is not None:
                    for _ in range(3):
                        next(pc, None)
        if pc is not None:
            for _ in pc:
                pass
        fw.fence()


def allreduce_to_X(self, nrows):
    items = []
    for r0 in range(0, nrows, 1024):
        r1 = min(nrows, r0 + 1024)
        items.append(("AllReduce", PAIRS, self.ARin.t.ap()[r0:r1, :], self.X.t.ap()[r0:r1, :]))
    self.coll_seq(items)
    self.fw.fence()


def phase_outproj(self, l):
    fw = self.fw
    ntok = T if l == 0 else L
    with contextlib.ExitStack() as ps:
        Wfc = [fw.sb(ps, "oWc%d" % i, [128, 1024], BF16) for i in range(2)]
        Wfs = [fw.sb(ps, "oWs%d" % i, [128, 1024], BF16) for i in range(2)]
        Wh = fw.sb(ps, "oWh", [128, 1024], BF16)
        Wa = fw.sb(ps, "oWa", [128, 2, 1024], BF16)
        Wo = fw.sb(ps, "oWo", [128, 8, 1024], BF16)
        stg = fw.sb(ps, "ostg", [128, 8, 1024])
        wf32 = fw.sb(ps, "owf", [128, 1024])
        pp = [fw.ps(ps, "opp%d" % i) for i in range(7)]
        npp = 0
        c64 = self.cv("c64"); s64 = self.cv("s64"); ident = self.cv("ident")
        wov = self.wout.t.ap()[l]
        fw.dma("sp", wf32[:], wov[0:128, :], wf32.b, self.wout.b)
        for (cm, dsts) in ((c64, Wfc), (s64, Wfs)):
            for half in range(2):
                p = pp[npp % 7]; npp += 1
                fw.op("pe", lambda e, p=p, cm=cm, half=half: e.matmul(p[:], cm, wf32[:, half * 512:(half + 1) * 512], start=True, stop=True), R=[wf32.b, self.cstb], W=[p.b])
                for i, Ls in enumerate((L, NCTX)):
                    fw.op("act", lambda e, p=p, d=dsts[i], half=half, Ls=Ls: e.activation(out=d[:, half * 512:(half + 1) * 512], in_=p[:], func=AF.Copy, scale=1.0 / math.sqrt(64.0 * Ls)),
                          R=[p.b], W=[dsts[i].b])
        fw.dma("sp", stg[:, 0, :], wov[128:256, :], stg.b, self.wout.b)
        fw.op("dve", lambda e: e.tensor_copy(out=Wh[:], in_=stg[:, 0, :]), R=[stg.b], W=[Wh.b])
        fw.dma("sp", stg[:, 0:2, :], wov[256:512, :].rearrange("(k p) c -> p k c", p=128), stg.b, self.wout.b)
        fw.op("dve", lambda e: e.tensor_copy(out=Wa[:], in_=stg[:, 0:2, :]), R=[stg.b], W=[Wa.b])
        oo, _ = SHOFF["wo%d" % l]
        fw.dma("sp", stg[:], dview(self.SH, oo, (1024, 1024)).rearrange("(k p) c -> p k c", p=128), stg.b, self.SH.b)
        fw.op("act", lambda e: e.activation(out=Wo[:], in_=stg[:], func=AF.Copy), R=[stg.b], W=[Wo.b])
        fm = fw.sb(ps, "ofm", [128, 4, 2, 128]); hm = fw.sb(ps, "ohm", [128, 4, 128])
        FrT = fw.sb(ps, "oFr", [128, 512], BF16); FiT = fw.sb(ps, "oFi", [128, 512], BF16); HT = fw.sb(ps, "oHT", [128, 512], BF16)
        OTt = fw.sb(ps, "oOT", [128, 2, 512], BF16)
        G = fw.sb(ps, "oG", [128, 24, 512], BF16)
        xt = fw.sb(ps, "oxt", [128, 4, 1024])
        mb = fw.sb(ps, "omb", [128, 1024])
        m1 = fw.sb(ps, "om1", [128, 512]); m2 = fw.sb(ps, "om2", [128, 512])
        M = fw.sb(ps, "oM", [128, 8, 512], BF16)
        tz = fw.sb(ps, "otz", [128, 512]); ar = [fw.sb(ps, "oar%d" % i, [128, 512]) for i in range(2)]
        lastj = -1
        nar = 0
        for t0 in range(0, ntok, 512):
            n_ = min(512, ntok - t0); nsub = n_ // 128
            j = 0 if t0 < L else 1
            if j != lastj:
                fw.dma("sp", mb[:], self.MODBC.t.ap()[:, j * 2048:j * 2048 + 1024], mb.b, self.MODBC.b)
                lastj = j
            fw.dma("sp", fm[:, 0:nsub], self.FM.t.ap()[t0:t0 + n_].rearrange("(s p) r c -> p s r c", p=128), fm.b, self.FM.b)
            fw.dma("sp", hm[:, 0:nsub], self.HM.t.ap()[t0:t0 + n_].rearrange("(s p) c -> p s c", p=128), hm.b, self.HM.b)
            fw.dma("sp", OTt[:, :, 0:n_], self.OT.t.ap()[:, t0:t0 + n_].rearrange("(k p) t -> p k t", p=128), OTt.b, self.OT.b)
            fw.dma("sp", G[:, :, 0:n_], self.GT.t.ap()[:, t0:t0 + n_].rearrange("(k p) t -> p k t", p=128), G.b, self.GT.b)
            fw.dma("sp", xt[:, 0:nsub], self.X.t.ap()[t0:t0 + n_].rearrange("(s p) c -> p s c", p=128), xt.b, self.X.b)
            for (srcf, dstT) in ((lambda s_: fm[:, s_, 0, :], FrT), (lambda s_: fm[:, s_, 1, :], FiT), (lambda s_: hm[:, s_, :], HT)):
                p = pp[npp % 7]; npp += 1
                srcb = fm.b if dstT is not HT else hm.b
                for s_ in range(nsub):
                    fw.op("pe", lambda e, p=p, s_=s_, srcf=srcf: e.transpose(out=p[:, s_ * 128:(s_ + 1) * 128], in_=srcf(s_), identity=ident), R=[srcb, self.cstb], W=[p.b])
                fw.op("act", lambda e, p=p, dstT=dstT, n_=n_: e.activation(out=dstT[:, 0:n_], in_=p[:, 0:n_], func=AF.Copy), R=[p.b], W=[dstT.b])
            for fc in range(8):
                fsl = slice(fc * 128, (fc + 1) * 128)
                pf = pp[npp % 7]; npp += 1
                fw.op("pe", lambda e, pf=pf, fsl=fsl, n_=n_, j=j: e.matmul(pf[:, 0:n_], Wfc[j][:, fsl], FrT[:, 0:n_], start=True, stop=False), R=[Wfc[j].b, FrT.b], W=[pf.b])
                fw.op("pe", lambda e, pf=pf, fsl=fsl, n_=n_, j=j: e.matmul(pf[:, 0:n_], Wfs[j][:, fsl], FiT[:, 0:n_], start=False, stop=True), R=[Wfs[j].b, FiT.b], W=[pf.b])
                ph = pp[npp % 7]; npp += 1
                fw.op("pe", lambda e, ph=ph, fsl=fsl, n_=n_: e.matmul(ph[:, 0:n_], Wh[:, fsl], HT[:, 0:n_], start=True, stop=True), R=[Wh.b, HT.b], W=[ph.b])
                pa = pp[npp % 7]; npp += 1
                for k in range(2):
                    fw.op("pe", lambda e, pa=pa, fsl=fsl, n_=n_, k=k: e.matmul(pa[:, 0:n_], Wa[:, k, fsl], OTt[:, k, 0:n_], start=(k == 0), stop=(k == 1)), R=[Wa.b, OTt.b], W=[pa.b])
                fw.op("dve", lambda e, pf=pf, fc=fc, n_=n_: e.tensor_tensor(out=m1[:, 0:n_], in0=pf[:, 0:n_], in1=G[:, fc, 0:n_], op=ALU.mult), R=[pf.b, G.b], W=[m1.b])
                fw.op("dve", lambda e, ph=ph, fc=fc, n_=n_: e.tensor_tensor(out=m2[:, 0:n_], in0=ph[:, 0:n_], in1=G[:, 8 + fc, 0:n_], op=ALU.mult), R=[ph.b, G.b], W=[m2.b])
                fw.op("dve", lambda e, n_=n_: e.tensor_tensor(out=m1[:, 0:n_], in0=m1[:, 0:n_], in1=m2[:, 0:n_], op=ALU.add), R=[m1.b, m2.b], W=[m1.b])
                fw.op("dve", lambda e, pa=pa, fc=fc, n_=n_: e.tensor_tensor(out=m2[:, 0:n_], in0=pa[:, 0:n_], in1=G[:, 16 + fc, 0:n_], op=ALU.mult), R=[pa.b, G.b], W=[m2.b])
                fw.op("dve", lambda e, fc=fc, n_=n_: e.tensor_tensor(out=M[:, fc, 0:n_], in0=m1[:, 0:n_], in1=m2[:, 0:n_], op=ALU.add), R=[m1.b, m2.b], W=[M.b])
            for s_ in range(nsub):
                for half in range(2):
                    p = pp[npp % 7]; npp += 1
                    for k in range(8):
                        fw.op("pe", lambda e, p=p, k=k, s_=s_, half=half: e.matmul(p[:], M[:, k, s_ * 128:(s_ + 1) * 128], Wo[:, k, half * 512:(half + 1) * 512], start=(k == 0), stop=(k == 7)),
                              R=[M.b, Wo.b], W=[p.b])
                    fw.op("dve", lambda e, p=p, half=half: e.tensor_tensor(out=tz[:], in0=p[:], in1=mb[:, half * 512:(half + 1) * 512], op=ALU.mult), R=[p.b, mb.b], W=[tz.b])
                    a_ = ar[nar % 2]; nar += 1
                    fw.op("dve", lambda e, a_=a_, s_=s_, half=half: e.scalar_tensor_tensor(out=a_[:], in0=xt[:, s_, half * 512:(half + 1) * 512], scalar=0.5, in1=tz[:], op0=ALU.mult, op1=ALU.add),
                          R=[xt.b, tz.b], W=[a_.b])
                    fw.dma("sp", self.ARin.t.ap()[t0 + s_ * 128:t0 + (s_ + 1) * 128, half * 512:(half + 1) * 512], a_[:], self.ARin.b, a_.b, part=True)
        fw.fence()
    self.allreduce_to_X(ntok)


for _f in (phase_attn, allreduce_to_X, phase_outproj):
    setattr(Prog, _f.__name__, _f)


def phase_moe(self, l):
    fw = self.fw
    ntok = T if l == 0 else L
    WEB1, WEB2 = self.WEB[l]
    fw.fence()
    with contextlib.ExitStack() as ps:
        W1 = [fw.sb(ps, "mW1_%d" % i, [128, 8, 2048], BF16) for i in range(2)]
        W2 = [fw.sb(ps, "mW2_%d" % i, [128, 8, 1024], BF16) for i in range(2)]
        wrt = self.ld(ps, "wrt%d" % l); brt = self.ld(ps, "brt%d" % l)
        b1T = self.ld(ps, "b1T%d" % l); b2r = self.ld(ps, "b2r%d" % l, 16)
        xt = fw.sb(ps, "mxt", [128, 4, 1024]); xn = fw.sb(ps, "mxn", [128, 1024]); junk = fw.sb(ps, "mjunk", [128, 1024])
        ss = fw.sb(ps, "mss", [128, 2])
        h2T = fw.sb(ps, "mh2T", [128, 8, 512], BF16)
        h2f = fw.sb(ps, "mh2f", [128, 8, 128])
        lg = fw.sb(ps, "mlg", [128, 32]); m8 = fw.sb(ps, "mm8", [128, 8]); msk = fw.sb(ps, "mmsk", [128, 32])
        nb = fw.sb(ps, "mnb", [128, 2])
        Gt = fw.sb(ps, "mG", [128, 4, 32])
        gT = fw.sb(ps, "mgT", [16, 128])
        Y = fw.sb(ps, "mY", [128, 4, 1024])
        A = fw.sb(ps, "mA", [128, 8, 512], BF16)
        g2 = [fw.sb(ps, "mg%d" % i, [128, 512]) for i in range(2)]; sg2 = [fw.sb(ps, "msg%d" % i, [128, 512]) for i in range(2)]; ln2 = [fw.sb(ps, "mln%d" % i, [128, 512]) for i in range(2)]
        mb = fw.sb(ps, "mmb", [128, 1024])
        tz = fw.sb(ps, "mtz", [128, 1024])
        pp = [fw.ps(ps, "mpp%d" % i) for i in range(8)]
        npp = 0
        ident = self.cv("ident")
        lastj = -1
        nw = 0
        for t0 in range(0, ntok, 512):
            n_ = min(512, ntok - t0); nsub = n_ // 128
            j = 0 if t0 < L else 1
            if j != lastj:
                fw.dma("sp", mb[:], self.MODBC.t.ap()[:, j * 2048 + 1024:j * 2048 + 2048], mb.b, self.MODBC.b)
                lastj = j
            fw.dma("sp", xt[:, 0:nsub], self.X.t.ap()[t0:t0 + n_].rearrange("(s p) c -> p s c", p=128), xt.b, self.X.b)
            for s_ in range(nsub):
                fw.op("act", lambda e, s_=s_: e.activation(out=junk[:], in_=xt[:, s_, :], func=AF.Square, accum_out=ss[:, 0:1]), R=[xt.b], W=[junk.b, ss.b])
                fw.op("act", lambda e: e.activation(out=ss[:, 1:2], in_=ss[:, 0:1], func=AF.Sqrt, scale=1.0 / D, bias=self.epsb[:, 0:1]), R=[ss.b], W=[ss.b])
                fw.op("dve", lambda e: e.reciprocal(out=ss[:, 1:2], in_=ss[:, 1:2]), R=[ss.b], W=[ss.b])
                fw.op("dve", lambda e, s_=s_: e.tensor_scalar(out=xn[:], in0=xt[:, s_, :], scalar1=ss[:, 1:2], scalar2=None, op0=ALU.mult), R=[xt.b, ss.b], W=[xn.b])
                for g in range(2):
                    p = pp[npp % 8]; npp += 1
                    for q in range(4):
                        k = 4 * g + q
                        fw.op("pe", lambda e, p=p, q=q, k=k: e.transpose(out=p[:, q * 128:(q + 1) * 128], in_=xn[:, k * 128:(k + 1) * 128], identity=ident), R=[xn.b, self.cstb], W=[p.b])
                    for q in range(4):
                        k = 4 * g + q
                        fw.op("dve", lambda e, p=p, q=q, k=k, j=j: e.tensor_scalar(out=h2f[:, k, :], in0=p[:, q * 128:(q + 1) * 128], scalar1=self.gs2[:, k, j:j + 1], scalar2=self.sh2[:, k, j:j + 1],
                                                                                  op0=ALU.mult, op1=ALU.add), R=[p.b, self.gs2.b, self.sh2.b], W=[h2f.b])
                fw.op("act", lambda e, s_=s_: e.activation(out=h2T[:, :, s_ * 128:(s_ + 1) * 128], in_=h2f[:], func=AF.Copy), R=[h2f.b], W=[h2T.b])
                p = pp[npp % 8]; npp += 1
                for k in range(8):
                    fw.op("pe", lambda e, p=p, k=k: e.matmul(p[:, 0:32], h2f[:, k, :], wrt[:, k * 32:(k + 1) * 32], start=(k == 0), stop=(k == 7)), R=[h2f.b, wrt.b], W=[p.b])
                fw.op("dve", lambda e, p=p: e.tensor_tensor(out=lg[:], in0=p[:, 0:32], in1=brt[:], op=ALU.add), R=[p.b, brt.b], W=[lg.b])
                fw.op("dve", lambda e: e.max(out=m8[:], in_=lg[:]), R=[lg.b], W=[m8.b])
                fw.op("dve", lambda e: e.tensor_scalar(out=msk[:], in0=lg[:], scalar1=m8[:, 3:4], scalar2=None, op0=ALU.is_ge), R=[lg.b, m8.b], W=[msk.b])
                fw.op("dve", lambda e: e.tensor_scalar(out=nb[:, 0:1], in0=m8[:, 0:1], scalar1=-1.0, scalar2=None, op0=ALU.mult), R=[m8.b], W=[nb.b])
                fw.op("act", lambda e: e.activation(out=lg[:], in_=lg[:], func=AF.Exp, bias=nb[:, 0:1]), R=[lg.b, nb.b], W=[lg.b])
                fw.op("dve", lambda e: e.tensor_tensor(out=lg[:], in0=lg[:], in1=msk[:], op=ALU.mult), R=[lg.b, msk.b], W=[lg.b])
                fw.op("dve", lambda e: e.tensor_reduce(out=nb[:, 1:2], in_=lg[:], axis=AX.X, op=ALU.add), R=[lg.b], W=[nb.b])
                fw.op("dve", lambda e: e.reciprocal(out=nb[:, 1:2], in_=nb[:, 1:2]), R=[nb.b], W=[nb.b])
                fw.op("dve", lambda e, s_=s_: e.tensor_scalar(out=Gt[:, s_, :], in0=lg[:], scalar1=nb[:, 1:2], scalar2=None, op0=ALU.mult), R=[lg.b, nb.b], W=[Gt.b])
                p = pp[npp % 8]; npp += 1
                fw.op("pe", lambda e, p=p, s_=s_: e.transpose(out=p[0:16, 0:128], in_=Gt[:, s_, 0:16], identity=ident), R=[Gt.b, self.cstb], W=[p.b])
                fw.op("dve", lambda e, p=p: e.tensor_copy(out=gT[:], in_=p[0:16, 0:128]), R=[p.b], W=[gT.b])
                for half in range(2):
                    p = pp[npp % 8]; npp += 1
                    fw.op("pe", lambda e, p=p, half=half: e.matmul(p[:], gT[:], b2r[0:16, half * 512:(half + 1) * 512], start=True, stop=True), R=[gT.b, b2r.b], W=[p.b])
                    fw.op("act", lambda e, p=p, half=half, s_=s_: e.activation(out=Y[:, s_, half * 512:(half + 1) * 512], in_=p[:], func=AF.Copy), R=[p.b], W=[Y.b])
            for le in range(16):
                w1 = W1[nw % 2]; w2 = W2[nw % 2]; nw += 1
                for cb in range(4):
                    fw.dma("sp", w1[:, :, cb * 512:(cb + 1) * 512], WEB1.t.ap()[le].rearrange("(k p) c -> p k c", p=128)[:, :, cb * 512:(cb + 1) * 512], w1.b, WEB1.b, part=(cb > 0))
                for cb in range(2):
                    fw.dma("sp", w2[:, :, cb * 512:(cb + 1) * 512], WEB2.t.ap()[le].rearrange("(k p) c -> p k c", p=128)[:, :, cb * 512:(cb + 1) * 512], w2.b, WEB2.b, part=(cb > 0))
                for jc in range(8):
                    pg = pp[npp % 8]; npp += 1
                    pl = pp[npp % 8]; npp += 1
                    g_ = g2[jc % 2]; sg = sg2[jc % 2]; ln = ln2[jc % 2]
                    for k in range(8):
                        fw.op("pe", lambda e, pg=pg, k=k, jc=jc, w1=w1, n_=n_: e.matmul(pg[:, 0:n_], w1[:, k, jc * 128:(jc + 1) * 128], h2T[:, k, 0:n_], start=(k == 0), stop=(k == 7)), R=[w1.b, h2T.b], W=[pg.b])
                    for k in range(8):
                        fw.op("pe", lambda e, pl=pl, k=k, jc=jc, w1=w1, n_=n_: e.matmul(pl[:, 0:n_], w1[:, k, 1024 + jc * 128:1024 + (jc + 1) * 128], h2T[:, k, 0:n_], start=(k == 0), stop=(k == 7)), R=[w1.b, h2T.b], W=[pl.b])
                    bg = b1T[:, le * 16 + jc:le * 16 + jc + 1]; bl = b1T[:, le * 16 + 8 + jc:le * 16 + 8 + jc + 1]
                    fw.op("dve", lambda e, pg=pg, bg=bg, n_=n_, g_=g_: e.tensor_scalar(out=g_[:, 0:n_], in0=pg[:, 0:n_], scalar1=bg, scalar2=7.0, op0=ALU.add, op1=ALU.min), R=[pg.b, b1T.b], W=[g_.b])
                    fw.op("act", lambda e, n_=n_, g_=g_, sg=sg: e.activation(out=sg[:, 0:n_], in_=g_[:, 0:n_], func=AF.Sigmoid, scale=1.702), R=[g_.b], W=[sg.b])
                    fw.op("act", lambda e, pl=pl, bl=bl, n_=n_, ln=ln: e.activation(out=ln[:, 0:n_], in_=pl[:, 0:n_], func=AF.Identity, bias=bl), R=[pl.b, b1T.b], W=[ln.b])
                    fw.op("dve", lambda e, n_=n_, ln=ln: e.tensor_scalar(out=ln[:, 0:n_], in0=ln[:, 0:n_], scalar1=7.0, scalar2=-7.0, op0=ALU.min, op1=ALU.max), R=[ln.b], W=[ln.b])
                    fw.op("dve", lambda e, n_=n_, g_=g_, sg=sg: e.tensor_tensor(out=g_[:, 0:n_], in0=g_[:, 0:n_], in1=sg[:, 0:n_], op=ALU.mult), R=[g_.b, sg.b], W=[g_.b])
                    fw.op("dve", lambda e, jc=jc, n_=n_, g_=g_, ln=ln: e.scalar_tensor_tensor(out=A[:, jc, 0:n_], in0=ln[:, 0:n_], scalar=1.0, in1=g_[:, 0:n_], op0=ALU.add, op1=ALU.mult), R=[g_.b, ln.b], W=[A.b])
                for s_ in range(nsub):
                    for half in range(2):
                        p = pp[npp % 8]; npp += 1
                        for k in range(8):
                            fw.op("pe", lambda e, p=p, k=k, s_=s_, half=half, w2=w2: e.matmul(p[:], A[:, k, s_ * 128:(s_ + 1) * 128], w2[:, k, half * 512:(half + 1) * 512], start=(k == 0), stop=(k == 7)),
                                  R=[A.b, w2.b], W=[p.b])
                        fw.op("dve", lambda e, p=p, s_=s_, half=half, le=le: e.scalar_tensor_tensor(out=Y[:, s_, half * 512:(half + 1) * 512], in0=p[:], scalar=Gt[:, s_, le:le + 1],
                                                                                                   in1=Y[:, s_, half * 512:(half + 1) * 512], op0=ALU.mult, op1=ALU.add), R=[p.b, Gt.b, Y.b], W=[Y.b])
            for s_ in range(nsub):
                fw.op("dve", lambda e, s_=s_: e.tensor_tensor(out=tz[:], in0=Y[:, s_, :], in1=mb[:], op=ALU.mult), R=[Y.b, mb.b], W=[tz.b])
                fw.op("dve", lambda e, s_=s_: e.scalar_tensor_tensor(out=tz[:], in0=xt[:, s_, :], scalar=0.5, in1=tz[:], op0=ALU.mult, op1=ALU.add), R=[xt.b, tz.b], W=[tz.b])
                fw.dma("sp", self.ARin.t.ap()[t0 + s_ * 128:t0 + (s_ + 1) * 128, :], tz[:], self.ARin.b, tz.b, part=True)
        fw.fence()
    self.allreduce_to_X(ntok)


setattr(Prog, "phase_moe", phase_moe)

def precast_steps(self, l, st_, sb_):
    fw = self.fw
    WEB1 = fw.dram("WEB1_%d" % l, [16, 1024, 2048], BF16)
    WEB2 = fw.dram("WEB2_%d" % l, [16, 1024, 1024], BF16)
    self.WEB[l] = (WEB1, WEB2)
    WE = self.WEl[l]
    n = 0
    for le in range(16):
        w1v = WE.t.ap()[le * 1536:le * 1536 + 1024, :].rearrange("(k p) c -> p k c", p=128)
        w2v = bass.AP(WE.t, WE.t.ap()[le * 1536 + 1024:le * 1536 + 1536, :].offset, [[1024, 1024], [1, 1024]]).rearrange("(k p) c -> p k c", p=128)
        for (src, dst, ncb) in ((w1v, WEB1, 4), (w2v, WEB2, 2)):
            for cb in range(ncb):
                a = st_[n % 2]; b_ = sb_[n % 2]
                fw.dma("sp", a[:], src[:, :, cb * 512:(cb + 1) * 512], a.b, WE.b)
                fw.op("dve", lambda e, a=a, b_=b_: e.tensor_copy(out=b_[:], in_=a[:]), R=[a.b], W=[b_.b])
                fw.dma("sp", dst.t.ap()[le].rearrange("(k p) c -> p k c", p=128)[:, :, cb * 512:(cb + 1) * 512], b_[:], dst.b, b_.b, part=True)
                n += 1
                yield n


setattr(Prog, "precast_steps", precast_steps)


_CACHE = {}


def kernel(**inputs):
    inp = {k: np.asarray(v) for k, v in inputs.items()}
    maps = host_prep(inp)
    P = Prog(stop=None, dumps=(), with_exp=True)
    nc = P.build()
    res = run_bass_kernel_spmd(nc, maps, core_ids=list(range(8)))
    out = np.stack([np.asarray(res.results[b]["out"]) for b in range(4)], 0)
    return out.astype(np.float32)
```

```python
import contextlib
import math
import numpy as np
import ml_dtypes
import concourse.bass as bass
import concourse.mybir as mybir
from concourse.bass_utils import run_bass_kernel_spmd

F32 = mybir.dt.float32
BF16 = mybir.dt.bfloat16
AF = mybir.ActivationFunctionType
ALU = mybir.AluOpType
AX = mybir.AxisListType

D = 1024
L = 8192
NCTX = 256
T = L + NCTX
DEPTH = 2
EPS = 1e-6
SUBLN_EPS = 1e-5
OFF_F, OFF_HY, OFF_Q, OFF_K, OFF_V, OFF_G = 0, 256, 1024, 1536, 2048, 2560
NMIX = 1280
NEXP_CORE = 16
PAIRS = [[0, 4], [1, 5], [2, 6], [3, 7]]
HALVES = [[0, 1, 2, 3], [4, 5, 6, 7]]


class Buf:
    __slots__ = ("name", "w", "r", "dsem")

    def __init__(self, name):
        self.name = name
        self.w = []
        self.r = []
        self.dsem = None


class Tile:
    def __init__(self, t, b):
        self.t = t
        self.b = b

    def __getitem__(self, k):
        return self.t[k]


class Op:
    __slots__ = ("id", "eng", "fn", "deps", "isdma", "signal", "wbuf")

    def __init__(self, id, eng, fn, deps, isdma, wbuf=None):
        self.id = id; self.eng = eng; self.fn = fn; self.deps = deps
        self.isdma = isdma; self.signal = isdma; self.wbuf = wbuf


class FW:
    ENG = ("pe", "act", "dve", "pool", "sp")
    NDSEM = 72

    def __init__(self, nc):
        self.nc = nc
        self.es = contextlib.ExitStack()
        self.h = {"pe": nc.tensor, "act": nc.scalar, "dve": nc.vector, "pool": nc.gpsimd, "sp": nc.sync}
        self.esem = {e: self.es.enter_context(nc.semaphore("e_" + e)) for e in self.ENG}
        self.ecnt = {e: 0 for e in self.ENG}
        self.fsem = self.es.enter_context(nc.semaphore("fence"))
        self.fcnt = 0
        self.dsems = [self.es.enter_context(nc.semaphore("d%d" % i)) for i in range(self.NDSEM)]
        self.dissued = [0] * self.NDSEM
        self.dnext = 0
        self.seen = {e: {} for e in self.ENG}
        self.done = {}
        self.pending = []
        self.bufs = []
        self.opmap = {}
        self.nid = 0
        self.ninst = 0

    def buf(self, name):
        b = Buf(name)
        self.bufs.append(b)
        return b

    def sb(self, st, name, shape, dtype=F32):
        self.nid += 1
        name = "%s_%d" % (name, self.nid)
        t = st.enter_context(self.nc.sbuf_tensor(name, shape, dtype))
        return Tile(t, self.buf(name))

    def ps(self, st, name, shape=(128, 512), dtype=F32):
        self.nid += 1
        name = "%s_%d" % (name, self.nid)
        t = st.enter_context(self.nc.psum_tensor(name, list(shape), dtype))
        return Tile(t, self.buf(name))

    def dram(self, name, shape, dtype, kind="Internal"):
        t = self.nc.dram_tensor(name, list(shape), dtype, kind=kind)
        return Tile(t, self.buf(name))

    def op(self, eng, fn, R=(), W=()):
        deps = set()
        for b in R:
            deps.update(b.w)
        for b in W:
            deps.update(b.w)
            deps.update(b.r)
        o = Op(self.nid, eng, fn, deps, False)
        self.nid += 1
        self.pending.append(o)
        for b in R:
            b.r.append(o.id)
        for b in W:
            b.w = [o.id]; b.r = []
        return o

    def dma(self, q, out_ap, in_ap, W, R, part=False, **kw):
        deps = set(R.w)
        for x in W.w:
            if part:
                p = self.opmap.get(x)
                if p is not None and p.isdma and p.wbuf is W:
                    continue
            deps.add(x)
        deps.update(W.r)
        fn = (lambda h, o=out_ap, i=in_ap, kw=kw: h.dma_start(out=o, in_=i, **kw))
        o = Op(self.nid, q, fn, deps, True, wbuf=W)
        self.nid += 1
        self.pending.append(o)
        self.opmap[o.id] = o
        R.r.append(o.id)
        if part:
            W.w = list(W.w) + [o.id]
        else:
            W.w = [o.id]
            W.r = []
        return o

    def _wait(self, eng, key, val):
        s = self.seen[eng]
        if s.get(key, 0) >= val:
            return
        s[key] = val
        if key[0] == "e":
            sem = self.esem[key[1]]
        elif key[0] == "f":
            sem = self.fsem
        else:
            sem = self.dsems[key[1]]
        self.h[eng].wait_ge(sem, val)
        self.ninst += 1

    def flush(self):
        ops = self.pending
        self.pending = []
        ids = {o.id: o for o in ops}
        for o in ops:
            for d in o.deps:
                p = ids.get(d)
                if p is not None and not p.isdma and not (p.eng == "pe" and o.eng == "pe"):
                    p.signal = True
        for o in ops:
            for d in o.deps:
                ev = self.done.get(d)
                if ev is None:
                    continue
                key, val = ev
                if key == ("e", "pe") and o.eng == "pe":
                    continue
                if key[0] == "d":
                    val = max(val, self.dissued[key[1]])
                self._wait(o.eng, key, val)
            inst = o.fn(self.h[o.eng])
            self.ninst += 1
            if o.isdma:
                b = o.wbuf
                if b.dsem is None:
                    b.dsem = self.dnext % self.NDSEM
                    self.dnext += 1
                k = b.dsem
                self.dissued[k] += 16
                inst.then_inc(self.dsems[k], 16)
                self.done[o.id] = (("d", k), self.dissued[k])
            elif o.signal:
                self.ecnt[o.eng] += 1
                inst.then_inc(self.esem[o.eng], 1)
                self.done[o.id] = (("e", o.eng), self.ecnt[o.eng])

    def fence(self):
        last = {}
        for o in self.pending:
            if not o.isdma:
                last[o.eng] = o
        for o in last.values():
            o.signal = True
        self.flush()
        for e in self.ENG:
            if self.ecnt[e] > 0:
                self._wait("sp", ("e", e), self.ecnt[e])
        for k in range(self.NDSEM):
            if self.dissued[k] > 0:
                self._wait("sp", ("d", k), self.dissued[k])
        self.fcnt += 1
        self.h["sp"].sem_inc(self.fsem, 1)
        self.ninst += 1
        for e in self.ENG:
            self._wait(e, ("f",), self.fcnt)
            for e2 in self.ENG:
                self.seen[e][("e", e2)] = self.ecnt[e2]
            for k in range(self.NDSEM):
                self.seen[e][("d", k)] = self.dissued[k]
        for b in self.bufs:
            b.w = []; b.r = []
        self.done = {}
        self.opmap = {}

    def close(self):
        self.es.close()


class Pack:
    def __init__(self):
        self.off = {}
        self.n = 0
        self.items = []

    def add(self, name, width):
        self.off[name] = (self.n, width)
        self.n += width

    def fill(self, arr, name, val):
        o, w = self.off[name]
        val = np.asarray(val, np.float32)
        val = val.reshape(val.shape[0], -1)
        assert val.shape[1] == w, (name, val.shape, w)
        arr[:val.shape[0], o:o + w] = val


def fm(v, nk):
    return np.asarray(v, np.float32).reshape(nk, 128).T


def rep(v):
    v = np.asarray(v, np.float32).reshape(1, -1)
    return np.broadcast_to(v, (128, v.shape[1]))


def make_sm_layout():
    P = Pack()
    P.add("cT", 16)
    P.add("crep", 2 * 8 * 128)
    P.add("deltabc", 128)
    for l in range(DEPTH):
        P.add("bmodT%d" % l, 48)
        P.add("bmodbc%d" % l, 2 * 1024)
        P.add("g1T%d" % l, 8)
        P.add("g2T%d" % l, 8)
        P.add("qg%d" % l, 1)
        P.add("kg%d" % l, 1)
        P.add("subg%d" % l, 1)
        P.add("lamq%d" % l, 128)
        P.add("lamk%d" % l, 128)
        P.add("wrt%d" % l, 8 * 32)
        P.add("brt%d" % l, 32)
        P.add("b1T%d" % l, 16 * 16)
        P.add("b2r%d" % l, 1024)
        P.add("cwbc%d" % l, 3 * 384)
        P.add("cbbc%d" % l, 384)
        P.add("hw1%d" % l, 64)
        P.add("hb1%d" % l, 1)
        P.add("hf1%d" % l, 1)
        P.add("hw2%d" % l, 64)
        P.add("hb2%d" % l, 1)
        P.add("hf2%d" % l, 1)
        P.add("hw3%d" % l, 512)
        P.add("hb3bc%d" % l, 512)
        P.add("hbias%d" % l, 256)
    return P


SM = make_sm_layout()


def make_cst_layout():
    P = Pack()
    P.add("ident", 128)
    P.add("ones", 128)
    P.add("bones", 128)
    P.add("rot", 128)
    P.add("ntl8192", 64)
    P.add("ntl256", 2)
    P.add("c64", 128)
    P.add("s64", 128)
    return P


CST = make_cst_layout()


def rope_tables():
    rows = L // 64
    row = np.repeat(np.arange(rows), 64).astype(np.float32)
    col = np.tile(np.arange(64), rows).astype(np.float32)
    nf = 16
    inv = (10000.0 ** (-np.arange(nf, dtype=np.float32) / nf)).astype(np.float32)
    angr = row[None, :] * inv[:, None]
    angc = col[None, :] * inv[:, None]
    ang64 = np.concatenate([angr, angr, angc, angc], 0)
    cos = np.cos(ang64).astype(np.float32)
    sin = np.sin(ang64).astype(np.float32)
    cos = np.concatenate([cos, np.ones((64, NCTX), np.float32)], 1)
    sin = np.concatenate([sin, np.zeros((64, NCTX), np.float32)], 1)
    return np.concatenate([cos, cos], 0), np.concatenate([sin, sin], 0)


def hy_emb(l):
    t = np.linspace(0.0, 1.0, l, dtype=np.float32)[:, None]
    ang = (np.float32(2.0 * math.pi / l) * np.arange(l, dtype=np.float32))[:, None]
    bands = np.linspace(1e-4, 15, 16, dtype=np.float32)[None, :]
    emb = np.concatenate([t, np.cos(bands * ang), -np.sin(bands * ang)], -1)
    return emb.T.astype(np.float32)


def hy_deltas(s):
    d = np.abs(np.linspace(math.log(1e-2) / 1.5, math.log(1e-2) / 0.3, 256, dtype=np.float32))
    return d[128 * s:128 * s + 128]


def rot_lhsT():
    R = np.zeros((128, 128), np.float32)
    for blk in range(2):
        for base in (0, 32):
            for j in range(16):
                a = blk * 64 + base + j
                b = a + 16
                R[a, b] = -1.0
                R[b, a] = 1.0
    return R.T.copy()


def shared_layout():
    off = {}
    n = 0
    for l in range(DEPTH):
        off["wg%d" % l] = (n, (1024, 3072)); n += 1024 * 3072
        off["wmod%d" % l] = (n, (1024, 6144)); n += 1024 * 6144
        off["wo%d" % l] = (n, (1024, 1024)); n += 1024 * 1024
    off["ropec"] = (n, (128, T)); n += 128 * T
    off["ropes"] = (n, (128, T)); n += 128 * T
    rows = -(-n // (512 * 2048)) * 512
    return off, rows


SHOFF, SHROWS = shared_layout()


def host_prep(inp):
    x = inp["x"]; ctx = inp["ctx"]; c = inp["c"]; c_ctx = inp["c_ctx"]
    blob = np.zeros((SHROWS * 2048,), np.float32)

    def put(name, arr):
        o, shp = SHOFF[name]
        blob[o:o + arr.size] = np.ascontiguousarray(arr, np.float32).reshape(-1)

    for l in range(DEPTH):
        put("wg%d" % l, inp["w_in"][l][:, OFF_G:])
        put("wmod%d" % l, inp["w_mod"][l])
        put("wo%d" % l, inp["w_o"][l])
    rc, rs = rope_tables()
    put("ropec", rc); put("ropes", rs)
    blob = blob.reshape(SHROWS // 512, 4, 128, 2048)

    cst = np.zeros((128, CST.n), np.float32)
    CST.fill(cst, "ident", np.eye(128, dtype=np.float32))
    CST.fill(cst, "ones", np.ones((128, 128), np.float32))
    bo = np.zeros((128, 128), np.float32); bo[:64, :64] = 1; bo[64:, 64:] = 1
    CST.fill(cst, "bones", bo)
    CST.fill(cst, "rot", rot_lhsT())
    pp = np.arange(128)[:, None]
    CST.fill(cst, "ntl8192", -((np.arange(64)[None, :] * 128 + pp) / (L - 1.0)))
    CST.fill(cst, "ntl256", -((np.arange(2)[None, :] * 128 + pp) / (NCTX - 1.0)))
    a64 = np.arange(64)
    c64 = np.cos(2 * np.pi * np.outer(a64, a64) / 64); s64 = np.sin(2 * np.pi * np.outer(a64, a64) / 64)
    z = np.zeros((64, 64))
    CST.fill(cst, "c64", np.block([[c64, z], [z, c64]]))
    CST.fill(cst, "s64", np.block([[s64, z], [z, s64]]))
    embc = np.concatenate([hy_emb(L), hy_emb(NCTX)], 1).astype(np.float32)
    fftc = fft_constants()

    maps = []
    for r in range(8):
        s, b = r // 4, r % 4
        m = {}
        m["xs"] = np.ascontiguousarray(x[b, s * 4096:(s + 1) * 4096])
        m["ctxb"] = np.ascontiguousarray(ctx[b])
        m["wsh"] = np.ascontiguousarray(blob[:, b]).reshape(SHROWS // 4, 2048)
        m["cst"] = cst
        m["embc"] = embc
        m["fftc"] = fftc
        wmix = np.zeros((DEPTH, 1024, NMIX), np.float32)
        wout = np.zeros((DEPTH, 512, 1024), np.float32)
        sm = np.zeros((128, SM.n), np.float32)
        SM.fill(sm, "cT", np.stack([fm(c[b], 8), fm(c_ctx, 8)], -1).reshape(128, 16))
        crep = np.zeros((128, 2, 8, 128), np.float32)
        crep[:, 0] = fm(c[b], 8)[:, :, None]
        crep[:, 1] = fm(c_ctx, 8)[:, :, None]
        SM.fill(sm, "crep", crep.reshape(128, -1))
        SM.fill(sm, "deltabc", rep(hy_deltas(s)))
        for l in range(DEPTH):
            w_in = inp["w_in"][l]
            cols = np.concatenate([
                np.arange(OFF_F + 128 * s, OFF_F + 128 * s + 128),
                np.arange(OFF_HY + 128 * s, OFF_HY + 128 * s + 128),
                np.arange(OFF_HY + 256 + 128 * s, OFF_HY + 256 + 128 * s + 128),
                np.arange(OFF_HY + 512 + 128 * s, OFF_HY + 512 + 128 * s + 128),
                np.arange(OFF_Q + 256 * s, OFF_Q + 256 * s + 256),
                np.arange(OFF_K + 256 * s, OFF_K + 256 * s + 256),
                np.arange(OFF_V + 256 * s, OFF_V + 256 * s + 256)])
            wmix[l] = w_in[:, cols]
            wout[l, 0:128] = inp["w_f"][l][128 * s:128 * s + 128]
            wout[l, 128:256] = inp["w_h"][l][128 * s:128 * s + 128]
            wout[l, 256:512] = inp["w_a"][l][256 * s:256 * s + 256]
            SM.fill(sm, "bmodT%d" % l, fm(inp["b_mod"][l], 48))
            bm = inp["b_mod"][l].reshape(6, 1024)
            SM.fill(sm, "bmodbc%d" % l, rep(np.concatenate([bm[2], bm[5]])))
            SM.fill(sm, "g1T%d" % l, fm(inp["norm1_g"][l], 8))
            SM.fill(sm, "g2T%d" % l, fm(inp["norm2_g"][l], 8))
            SM.fill(sm, "qg%d" % l, np.tile(inp["q_norm_g"][l], 2).reshape(128, 1))
            SM.fill(sm, "kg%d" % l, np.tile(inp["k_norm_g"][l], 2).reshape(128, 1))
            SM.fill(sm, "subg%d" % l, inp["subln_g"][l].reshape(128, 1))
            SM.fill(sm, "lamq%d" % l, rep(inp["lam_q"][l].reshape(-1)))
            SM.fill(sm, "lamk%d" % l, rep(inp["lam_k"][l].reshape(-1)))
            es_ = [16 * s + le for le in range(16)]
            perm = es_ + [e for e in range(32) if e not in es_]
            SM.fill(sm, "wrt%d" % l, inp["w_router"][l][:, perm].reshape(8, 128, 32).transpose(1, 0, 2).reshape(128, 256))
            SM.fill(sm, "brt%d" % l, rep(inp["b_router"][l][perm]))
            b1 = inp["b_e1"][l][es_]
            b1p = np.concatenate([b1[:, 0::2], b1[:, 1::2]], 1)
            SM.fill(sm, "b1T%d" % l, b1p.reshape(16, 16, 128).transpose(2, 0, 1).reshape(128, 256))
            SM.fill(sm, "b2r%d" % l, inp["b_e2"][l][es_])
            cw = inp["hy_conv_w"][l]; cb = inp["hy_conv_b"][l]
            vx = np.concatenate([np.arange(128 * s, 128 * s + 128), np.arange(256 + 128 * s, 256 + 128 * s + 128),
                                 np.arange(512 + 128 * s, 512 + 128 * s + 128)])
            SM.fill(sm, "cwbc%d" % l, rep(cw[:, vx].reshape(-1)))
            SM.fill(sm, "cbbc%d" % l, rep(cb[vx]))
            SM.fill(sm, "hw1%d" % l, inp["hy_w1"][l])
            SM.fill(sm, "hb1%d" % l, inp["hy_b1"][l].reshape(64, 1))
            SM.fill(sm, "hf1%d" % l, inp["hy_freq1"][l].reshape(64, 1))
            SM.fill(sm, "hw2%d" % l, inp["hy_w2"][l])
            SM.fill(sm, "hb2%d" % l, inp["hy_b2"][l].reshape(64, 1))
            SM.fill(sm, "hf2%d" % l, inp["hy_freq2"][l].reshape(64, 1))
            w3 = inp["hy_w3"][l].reshape(64, 2, 2, 256)[:, :, :, 128 * s:128 * s + 128].reshape(64, 512)
            b3 = inp["hy_b3"][l].reshape(2, 2, 256)[:, :, 128 * s:128 * s + 128].reshape(512)
            SM.fill(sm, "hw3%d" % l, w3)
            SM.fill(sm, "hb3bc%d" % l, rep(b3))
            SM.fill(sm, "hbias%d" % l, rep(inp["hy_bias"][l][:, 128 * s:128 * s + 128].reshape(-1)))
        m["wmix"] = wmix
        m["wout"] = wout
        m["sm"] = sm
        wexp = np.zeros((DEPTH, 6144, 2048), np.float32)
        for l in range(DEPTH):
            full = np.zeros((16, 1536, 2048), np.float32)
            for le in range(16):
                e = 16 * s + le
                w1 = inp["w_e1"][l][e]
                full[le, :1024] = np.concatenate([w1[:, 0::2], w1[:, 1::2]], 1)
                full[le, 1024:] = inp["w_e2"][l][e].reshape(512, 2048)
            wexp[l] = full.reshape(48, 4, 128, 2048)[:, b].reshape(6144, 2048)
        m["wexp"] = wexp
        maps.append(m)
    return maps


def dview(tile, off, shape):
    r, c = shape
    return bass.AP(tile.t, off, [[c, r], [1, c]])


class Prog:
    def __init__(self, stop=None, dumps=(), with_exp=True):
        self.stop = stop
        self.dumps = list(dumps)
        self.with_exp = with_exp
        nc = self.nc = bass.Bass("TRN2", target_bir_lowering=False)
        fw = self.fw = FW(nc)
        self.ncc = 0
        self.WEB = {}
        self.xs = fw.dram("xs", [4096, 1024], F32, "ExternalInput")
        self.ctxb = fw.dram("ctxb", [NCTX, 1024], F32, "ExternalInput")
        self.wsh = fw.dram("wsh", [SHROWS // 4, 2048], F32, "ExternalInput")
        self.cst_d = fw.dram("cst", [128, CST.n], F32, "ExternalInput")
        self.wmix = fw.dram("wmix", [DEPTH, 1024, NMIX], F32, "ExternalInput")
        self.wout = fw.dram("wout", [DEPTH, 512, 1024], F32, "ExternalInput")
        self.sm_d = fw.dram("sm", [128, SM.n], F32, "ExternalInput")
        self.embc = fw.dram("embc", [33, T], F32, "ExternalInput")
        self.fftc_d = fw.dram("fftc", [128, FFTC.n], F32, "ExternalInput")
        if with_exp:
            self.wexp = fw.dram("wexp", [DEPTH, 6144, 2048], F32, "ExternalInput")
        self.out = fw.dram("out", [L, 1024], F32, "ExternalOutput")
        self.X = fw.dram("X", [T, 1024], F32)
        self.SH = fw.dram("SH", [SHROWS, 2048], F32)
        self.QT = fw.dram("QT", [256, T], BF16)
        self.KT = fw.dram("KT", [256, T], BF16)
        self.V = fw.dram("V", [T, 256], BF16)
        self.GT = fw.dram("GT", [3072, T], BF16)
        self.PFH = fw.dram("PFH", [T + 4, 512], F32)
        self.MODBC = fw.dram("MODBC", [128, 4096], F32)
        self.FM = fw.dram("FM", [T, 2, 128], F32)
        self.HM = fw.dram("HM", [T, 128], F32)
        self.OT = fw.dram("OT", [256, T], BF16)
        self.ARin = fw.dram("ARin", [T, 1024], F32)
        self.dump_out = {}

    def coll_seq(self, items):
        fw = self.fw
        sem = fw.es.enter_context(self.nc.semaphore("cc%d" % self.ncc))
        self.ncc += 1
        fw.fence()
        for (kind, groups, in_ap, out_ap) in items:
            op = ALU.bypass if kind in ("AllGather", "AllToAll") else ALU.add
            fw.h["pool"].collective_compute(kind, op, replica_groups=groups, ins=[in_ap], outs=[out_ap]).then_inc(sem)
        for e in fw.ENG:
            fw.h[e].wait_ge(sem, len(items))

    def ld(self, ps, name, rows=128, q="sp"):
        o, w = SM.off[name]
        t = self.fw.sb(ps, "sm_" + name, [rows, w])
        kw = dict(allow_slow_non_contiguous=True) if w == 1 else {}
        self.fw.dma(q, t[:], self.sm_d.t.ap()[0:rows, o:o + w], t.b, self.sm_d.b, **kw)
        return t

    def cv(self, name):
        o, w = CST.off[name]
        return self.cst[:, o:o + w]

    def dump(self, name, tile):
        if name in self.dumps:
            t = tile.t
            d = self.fw.dram("dump_" + name, list(t.shape), t.dtype, "ExternalOutput")
            self.fw.dma("sp", d.t.ap(), t.ap(), d.b, tile.b)
            self.dump_out[name] = "dump_" + name
            self.fw.fence()

    def issue_exp_gather(self, l):
        for c in range(48):
            self.fw.h["pool"].collective_compute("AllGather", ALU.bypass, replica_groups=HALVES,
                                                 ins=[self.wexpI.t.ap()[l, c * 128:(c + 1) * 128, :]],
                                                 outs=[self.WEl[l].t.ap()[c * 512:(c + 1) * 512, :]]).then_inc(self.expsem)

    def bigcopy(self, dst, dap, src, sap, rows, step=256):
        for r0 in range(0, rows, step):
            r1 = min(rows, r0 + step)
            self.fw.dma("sp", dap[r0:r1, :], sap[r0:r1, :], dst.b, src.b, part=True)

    def phase_gather(self):
        fw = self.fw
        xsI = fw.dram("xsI", [4096, 1024], F32)
        wshI = fw.dram("wshI", [SHROWS // 4, 2048], F32)
        self.bigcopy(xsI, xsI.t.ap(), self.xs, self.xs.t.ap(), 4096, 512)
        self.bigcopy(wshI, wshI.t.ap(), self.wsh, self.wsh.t.ap(), SHROWS // 4)
        fw.dma("sp", self.X.t.ap()[L:T, :], self.ctxb.t.ap(), self.X.b, self.ctxb.b, part=True)
        if self.with_exp:
            self.wexpI = fw.dram("wexpI", [DEPTH, 6144, 2048], F32)
            for l in range(DEPTH):
                self.bigcopy(self.wexpI, self.wexpI.t.ap()[l], self.wexp, self.wexp.t.ap()[l], 6144)
        XG = fw.dram("XG", [8, 1024, 1024], F32)
        items = [("AllGather", PAIRS, xsI.t.ap()[c * 512:(c + 1) * 512, :], XG.t.ap()[c]) for c in range(8)]
        items += [("AllGather", HALVES, wshI.t.ap()[c * 128:(c + 1) * 128, :], self.SH.t.ap()[c * 512:(c + 1) * 512, :])
                  for c in range(SHROWS // 512)]
        self.coll_seq(items)
        if self.with_exp:
            self.WEl = [fw.dram("WE%d" % l, [24576, 2048], F32) for l in range(DEPTH)]
            self.expsem = fw.es.enter_context(self.nc.semaphore("expsem"))
            self.issue_exp_gather(0)
        for c in range(8):
            for r in range(2):
                for q in range(2):
                    fw.dma("sp", self.X.t.ap()[r * 4096 + c * 512 + q * 256: r * 4096 + c * 512 + q * 256 + 256, :],
                           XG.t.ap()[c, r * 512 + q * 256: r * 512 + q * 256 + 256, :], self.X.b, XG.b, part=True)
        fw.fence()

    def phase_mod(self, l):
        fw = self.fw
        with contextlib.ExitStack() as ps:
            cT = self.ld(ps, "cT"); crep = self.ld(ps, "crep")
            bmodT = self.ld(ps, "bmodT%d" % l); bmodbc = self.ld(ps, "bmodbc%d" % l)
            g1T = self.ld(ps, "g1T%d" % l); g2T = self.ld(ps, "g2T%d" % l)
            lamq = self.ld(ps, "lamq%d" % l); lamk = self.ld(ps, "lamk%d" % l)
            for nm, dst in (("qg%d" % l, 0), ("kg%d" % l, 1), ("subg%d" % l, 2)):
                o, w = SM.off[nm]
                fw.dma("sp", self.qks[:, dst:dst + 1], self.sm_d.t.ap()[:, o:o + 1], self.qks.b, self.sm_d.b, part=True, allow_slow_non_contiguous=True)
            sc = fw.sb(ps, "sc", [128, 8, 2])
            screp = fw.sb(ps, "screp", [128, 2, 8, 128])
            mbc = fw.sb(ps, "mbc", [128, 2, 2, 1024])
            pm = fw.ps(ps, "pm")
            pbc = [fw.ps(ps, "pbc%d" % i) for i in range(2)]
            wm = [fw.sb(ps, "wm%d" % i, [128, 8, 512]) for i in range(2)]
            fw.op("act", lambda e: e.activation(out=sc[:].rearrange("p k j -> p (k j)"), in_=cT[:], func=AF.Silu),
                  R=[cT.b], W=[sc.b])
            fw.op("act", lambda e: e.activation(out=screp[:].rearrange("p j k m -> p (j k m)"), in_=crep[:], func=AF.Silu),
                  R=[crep.b], W=[screp.b])
            o, _ = SHOFF["wmod%d" % l]
            wv = dview(self.SH, o, (1024, 6144)).rearrange("(k p) c -> p k c", p=128)
            for cb in range(12):
                w = wm[cb % 2]
                fw.dma("sp", w[:], wv[:, :, cb * 512:(cb + 1) * 512], w.b, self.SH.b)
                for oc in range(4):
                    col = (cb * 4 + oc) * 2
                    for k in range(8):
                        fw.op("pe", lambda e, w=w, oc=oc, k=k, col=col: e.matmul(
                            pm[:, col:col + 2], w[:, k, oc * 128:(oc + 1) * 128], sc[:, k, :], start=(k == 0), stop=(k == 7)),
                            R=[w.b, sc.b], W=[pm.b])
                if cb in (4, 5, 10, 11):
                    which = 0 if cb < 6 else 1
                    half = cb - 4 if cb < 6 else cb - 10
                    for j in range(2):
                        pb = pbc[j]
                        for k in range(8):
                            fw.op("pe", lambda e, w=w, j=j, k=k, pb=pb: e.matmul(
                                pb[:], screp[:, j, k, :], w[:, k, :], start=(k == 0), stop=(k == 7)),
                                R=[w.b, screp.b], W=[pb.b])
                        bsl = bmodbc[:, which * 1024 + half * 512: which * 1024 + half * 512 + 512]
                        fw.op("dve", lambda e, pb=pb, j=j, which=which, half=half, bsl=bsl: e.tensor_tensor(
                            out=mbc[:, j, which, half * 512:(half + 1) * 512], in0=pb[:], in1=bsl, op=ALU.add),
                            R=[pb.b, bmodbc.b], W=[mbc.b])
            fw.dma("sp", self.MODBC.t.ap(), mbc[:].rearrange("p j w c -> p (j w c)"), self.MODBC.b, mbc.b)
            pmv = pm[:, 0:96].rearrange("p (a j) -> p a j", j=2)
            for j in range(2):
                fw.op("dve", lambda e, j=j: e.tensor_tensor(out=self.modT[:, :, j], in0=pmv[:, :, j], in1=bmodT[:], op=ALU.add),
                      R=[pm.b, bmodT.b], W=[self.modT.b])
            for (gT, gs, sh, ishift, iscale) in ((g1T, self.gs1, self.sh1, 0, 1), (g2T, self.gs2, self.sh2, 3, 4)):
                for j in range(2):
                    fw.op("dve", lambda e, j=j, gs=gs, iscale=iscale, gT=gT: e.scalar_tensor_tensor(
                        out=gs[:, :, j], in0=self.modT[:, iscale * 8:(iscale + 1) * 8, j], scalar=1.0, in1=gT[:],
                        op0=ALU.add, op1=ALU.mult), R=[self.modT.b, gT.b], W=[gs.b])
                    fw.op("dve", lambda e, j=j, sh=sh, ishift=ishift: e.tensor_copy(
                        out=sh[:, :, j], in_=self.modT[:, ishift * 8:(ishift + 1) * 8, j]), R=[self.modT.b], W=[sh.b])
            lp = fw.sb(ps, "lp", [128, 2, 64])
            le = fw.sb(ps, "le", [128, 2])
            fw.op("dve", lambda e: e.tensor_tensor(out=lp[:].rearrange("p a b -> p (a b)"), in0=lamq[:], in1=lamk[:], op=ALU.mult),
                  R=[lamq.b, lamk.b], W=[lp.b])
            fw.op("dve", lambda e: e.tensor_reduce(out=le[:], in_=lp[:], axis=AX.X, op=ALU.add), R=[lp.b], W=[le.b])
            fw.op("act", lambda e: e.activation(out=le[:], in_=le[:], func=AF.Exp), R=[le.b], W=[le.b])
            lam_init = 0.8 - 0.6 * math.exp(-0.3 * l)
            fw.op("dve", lambda e: e.tensor_tensor(out=self.lam[:, 0:1], in0=le[:, 0:1], in1=le[:, 1:2], op=ALU.subtract),
                  R=[le.b], W=[self.lam.b])
            fw.op("dve", lambda e: e.tensor_scalar(out=self.lam[:, 0:1], in0=self.lam[:, 0:1], scalar1=lam_init, scalar2=None,
                                                   op0=ALU.add), R=[self.lam.b], W=[self.lam.b])
            fw.op("dve", lambda e: e.tensor_scalar(out=self.lam[:, 1:2], in0=self.lam[:, 0:1], scalar1=-1.0, scalar2=None,
                                                   op0=ALU.mult), R=[self.lam.b], W=[self.lam.b])
            fw.fence()

    def pfh_row(self, t0):
        return t0 + 1 if t0 < L else (t0 - L) + L + 3

    def phase_proj(self, l):
        fw = self.fw
        with contextlib.ExitStack() as ps:
            Wm = fw.sb(ps, "Wm", [128, 8, NMIX], BF16)
            Wg = fw.sb(ps, "Wg", [128, 8, 3072], BF16)
            stg = [fw.sb(ps, "stg%d" % i, [128, 8, 256]) for i in range(2)]
            wmv = self.wmix.t.ap()[l].rearrange("(k p) c -> p k c", p=128)
            og, _ = SHOFF["wg%d" % l]
            wgv = dview(self.SH, og, (1024, 3072)).rearrange("(k p) c -> p k c", p=128)
            n = 0
            for (src, srcb, dst, nb) in ((wmv, self.wmix.b, Wm, NMIX // 256), (wgv, self.SH.b, Wg, 12)):
                for cb in range(nb):
                    s = stg[n % 2]; n += 1
                    fw.dma("sp", s[:], src[:, :, cb * 256:(cb + 1) * 256], s.b, srcb)
                    fw.op("act" if n % 2 else "dve",
                          (lambda e, s=s, dst=dst, cb=cb: e.activation(out=dst[:, :, cb * 256:(cb + 1) * 256], in_=s[:], func=AF.Copy))
                          if n % 2 else
                          (lambda e, s=s, dst=dst, cb=cb: e.tensor_copy(out=dst[:, :, cb * 256:(cb + 1) * 256], in_=s[:])),
                          R=[s.b], W=[dst.b])
            xt = [fw.sb(ps, "xt%d" % i, [128, 1024]) for i in range(2)]
            xn = [fw.sb(ps, "xn%d" % i, [128, 1024]) for i in range(2)]
            junk = fw.sb(ps, "junk", [128, 1024])
            ss = [fw.sb(ps, "ss%d" % i, [128, 2]) for i in range(2)]
            hT = [fw.sb(ps, "hT%d" % i, [128, 8, 512], BF16) for i in range(2)]
            rc = [fw.sb(ps, "rc%d" % i, [128, 512]) for i in range(2)]
            rs_ = [fw.sb(ps, "rs%d" % i, [128, 512]) for i in range(2)]
            sq = fw.sb(ps, "sq", [128, 512]); rq = fw.sb(ps, "rq", [128, 512]); qn = fw.sb(ps, "qn", [128, 512])
            t1 = fw.sb(ps, "t1", [128, 512]); t2 = fw.sb(ps, "t2", [128, 512])
            qo = [fw.sb(ps, "qo%d" % i, [128, 512], BF16) for i in range(2)]
            fhs = [fw.sb(ps, "fhs%d" % i, [128, 512]) for i in range(2)]
            vs = [fw.sb(ps, "vs%d" % i, [128, 256], BF16) for i in range(2)]
            gst = [fw.sb(ps, "gst%d" % i, [128, 4, 512], BF16) for i in range(2)]
            zt = fw.sb(ps, "zt", [4, 512])
            pt = [fw.ps(ps, "pt%d" % i) for i in range(2)]
            pj = [fw.ps(ps, "pj%d" % i) for i in range(3)]
            pn = [fw.ps(ps, "pn%d" % i) for i in range(2)]
            ident = self.cv("ident"); bones = self.cv("bones"); rot = self.cv("rot")
            cb_ = self.cstb
            fw.op("dve", lambda e: e.memset(zt[:], 0.0), W=[zt.b])
            for r in (0, L + 1, L + 2, T + 3):
                fw.dma("sp", self.PFH.t.ap()[r:r + 1, :], zt[0:1, :], self.PFH.b, zt.b, part=True)
            oc_, _ = SHOFF["ropec"]; os_, _ = SHOFF["ropes"]
            rcv = dview(self.SH, oc_, (128, T)); rsv = dview(self.SH, os_, (128, T))
            npj = 0
            ntile = (T + 511) // 512
            for tt in range(ntile):
                t0 = tt * 512
                n_ = min(512, T - t0)
                nsub = n_ // 128
                j = 0 if t0 < L else 1
                h = hT[tt % 2]
                for i in range(nsub):
                    x = xt[i % 2]; y = xn[i % 2]; s2 = ss[i % 2]
                    fw.dma("sp", x[:], self.X.t.ap()[t0 + i * 128:t0 + (i + 1) * 128, :], x.b, self.X.b)
                    fw.op("act", lambda e, x=x, s2=s2: e.activation(out=junk[:], in_=x[:], func=AF.Square, accum_out=s2[:, 0:1]),
                          R=[x.b], W=[junk.b, s2.b])
                    fw.op("act", lambda e, s2=s2: e.activation(out=s2[:, 1:2], in_=s2[:, 0:1], func=AF.Sqrt, scale=1.0 / D, bias=self.epsb[:, 0:1]),
                          R=[s2.b], W=[s2.b])
                    fw.op("dve", lambda e, s2=s2: e.reciprocal(out=s2[:, 1:2], in_=s2[:, 1:2]), R=[s2.b], W=[s2.b])
                    fw.op("dve", lambda e, x=x, y=y, s2=s2: e.tensor_scalar(out=y[:], in0=x[:], scalar1=s2[:, 1:2], scalar2=None, op0=ALU.mult),
                          R=[x.b, s2.b], W=[y.b])
                    for g in range(2):
                        for q in range(4):
                            k = 4 * g + q
                            fw.op("pe", lambda e, y=y, g=g, q=q, k=k: e.transpose(out=pt[g][:, q * 128:(q + 1) * 128], in_=y[:, k * 128:(k + 1) * 128], identity=ident),
                                  R=[y.b, cb_], W=[pt[g].b])
                        for q in range(4):
                            k = 4 * g + q
                            fw.op("dve", lambda e, g=g, q=q, k=k, i=i, h=h, j=j: e.tensor_scalar(
                                out=h[:, k, i * 128:(i + 1) * 128], in0=pt[g][:, q * 128:(q + 1) * 128],
                                scalar1=self.gs1[:, k, j:j + 1], scalar2=self.sh1[:, k, j:j + 1], op0=ALU.mult, op1=ALU.add),
                                R=[pt[g].b, self.gs1.b, self.sh1.b], W=[h.b])
                c_ = rc[tt % 2]; s_ = rs_[tt % 2]
                fw.dma("sp", c_[:, 0:n_], rcv[:, t0:t0 + n_], c_.b, self.SH.b)
                fw.dma("sp", s_[:, 0:n_], rsv[:, t0:t0 + n_], s_.b, self.SH.b)
                for i in range(nsub):
                    p = pj[npj % 3]; npj += 1
                    for k in range(8):
                        fw.op("pe", lambda e, p=p, k=k, i=i, h=h: e.matmul(p[:, 0:512], h[:, k, i * 128:(i + 1) * 128], Wm[:, k, 0:512], start=(k == 0), stop=(k == 7)),
                              R=[h.b, Wm.b], W=[p.b])
                    f = fhs[i % 2]
                    fw.op("act", lambda e, p=p, f=f: e.activation(out=f[:], in_=p[:, 0:512], func=AF.Copy), R=[p.b], W=[f.b])
                    r0 = self.pfh_row(t0 + i * 128)
                    fw.dma("sp", self.PFH.t.ap()[r0:r0 + 128, :], f[:], self.PFH.b, f.b, part=True)
                    p = pj[npj % 3]; npj += 1
                    for k in range(8):
                        fw.op("pe", lambda e, p=p, k=k, i=i, h=h: e.matmul(p[:, 0:256], h[:, k, i * 128:(i + 1) * 128], Wm[:, k, 1024:1280], start=(k == 0), stop=(k == 7)),
                              R=[h.b, Wm.b], W=[p.b])
                    v = vs[i % 2]
                    fw.op("act", lambda e, p=p, v=v: e.activation(out=v[:], in_=p[:, 0:256], func=AF.Copy), R=[p.b], W=[v.b])
                    fw.dma("sp", self.V.t.ap()[t0 + i * 128:t0 + (i + 1) * 128, :], v[:], self.V.b, v.b, part=True)
                for c in range(4):
                    p = pj[npj % 3]; npj += 1
                    c0 = 512 + c * 128
                    for k in range(8):
                        fw.op("pe", lambda e, p=p, k=k, h=h, n_=n_, c0=c0: e.matmul(p[:, 0:n_], Wm[:, k, c0:c0 + 128], h[:, k, 0:n_], start=(k == 0), stop=(k == 7)),
                              R=[h.b, Wm.b], W=[p.b])
                    fw.op("act", lambda e, p=p, n_=n_: e.activation(out=sq[:, 0:n_], in_=p[:, 0:n_], func=AF.Square), R=[p.b], W=[sq.b])
                    pa = pn[0]
                    fw.op("pe", lambda e, pa=pa, n_=n_: e.matmul(pa[:, 0:n_], bones, sq[:, 0:n_], start=True, stop=True), R=[sq.b, cb_], W=[pa.b])
                    fw.op("act", lambda e, pa=pa, n_=n_: e.activation(out=rq[:, 0:n_], in_=pa[:, 0:n_], func=AF.Sqrt, scale=1.0 / 64, bias=self.epsb[:, 0:1]),
                          R=[pa.b], W=[rq.b])
                    fw.op("dve", lambda e, n_=n_: e.reciprocal(out=rq[:, 0:n_], in_=rq[:, 0:n_]), R=[rq.b], W=[rq.b])
                    gi = 0 if c < 2 else 1
                    fw.op("dve", lambda e, p=p, n_=n_, gi=gi: e.scalar_tensor_tensor(out=qn[:, 0:n_], in0=p[:, 0:n_], scalar=self.qks[:, gi:gi + 1], in1=rq[:, 0:n_],
                                                                                   op0=ALU.mult, op1=ALU.mult), R=[p.b, rq.b, self.qks.b], W=[qn.b])
                    pb = pn[1]
                    fw.op("pe", lambda e, pb=pb, n_=n_: e.matmul(pb[:, 0:n_], rot, qn[:, 0:n_], start=True, stop=True), R=[qn.b, cb_], W=[pb.b])
                    fw.op("dve", lambda e, n_=n_, c_=c_: e.tensor_tensor(out=t1[:, 0:n_], in0=qn[:, 0:n_], in1=c_[:, 0:n_], op=ALU.mult), R=[qn.b, c_.b], W=[t1.b])
                    fw.op("dve", lambda e, pb=pb, n_=n_, s_=s_: e.tensor_tensor(out=t2[:, 0:n_], in0=pb[:, 0:n_], in1=s_[:, 0:n_], op=ALU.mult), R=[pb.b, s_.b], W=[t2.b])
                    o_ = qo[c % 2]
                    fw.op("dve", lambda e, n_=n_, o_=o_: e.tensor_tensor(out=o_[:, 0:n_], in0=t1[:, 0:n_], in1=t2[:, 0:n_], op=ALU.add), R=[t1.b, t2.b], W=[o_.b])
                    dstT = self.QT if c < 2 else self.KT
                    cc = c % 2
                    fw.dma("sp", dstT.t.ap()[cc * 128:(cc + 1) * 128, t0:t0 + n_], o_[:, 0:n_], dstT.b, o_.b, part=True)
                for c4 in range(6):
                    g_ = gst[c4 % 2]
                    for c in range(4):
                        cg = c4 * 4 + c
                        p = pj[npj % 3]; npj += 1
                        for k in range(8):
                            fw.op("pe", lambda e, p=p, k=k, h=h, n_=n_, cg=cg: e.matmul(p[:, 0:n_], Wg[:, k, cg * 128:(cg + 1) * 128], h[:, k, 0:n_], start=(k == 0), stop=(k == 7)),
                                  R=[h.b, Wg.b], W=[p.b])
                        fw.op("act", lambda e, p=p, n_=n_, g_=g_, c=c: e.activation(out=g_[:, c, 0:n_], in_=p[:, 0:n_], func=AF.Sigmoid), R=[p.b], W=[g_.b])
                    fw.dma("sp", self.GT.t.ap()[c4 * 512:(c4 + 1) * 512, t0:t0 + n_].rearrange("(c p) t -> p c t", p=128), g_[:, :, 0:n_], self.GT.b, g_.b, part=True)
            fw.fence()

    def build(self):
        fw = self.fw
        with contextlib.ExitStack() as st:
            self.cst = fw.sb(st, "cstt", [128, CST.n]); self.cstb = self.cst.b
            self.modT = fw.sb(st, "modT", [128, 48, 2])
            self.gs1 = fw.sb(st, "gs1", [128, 8, 2]); self.sh1 = fw.sb(st, "sh1", [128, 8, 2])
            self.gs2 = fw.sb(st, "gs2", [128, 8, 2]); self.sh2 = fw.sb(st, "sh2", [128, 8, 2])
            self.lam = fw.sb(st, "lam", [128, 2])
            self.qks = fw.sb(st, "qks", [128, 4])
            self.epsb = fw.sb(st, "epsb", [128, 2])
            fw.dma("sp", self.cst[:], self.cst_d.t.ap(), self.cst.b, self.cst_d.b)
            fw.op("dve", lambda e: e.memset(self.epsb[:, 0:1], EPS), W=[self.epsb.b])
            fw.op("dve", lambda e: e.memset(self.epsb[:, 1:2], SUBLN_EPS), W=[self.epsb.b])
            self.phase_gather()
            for l in range(DEPTH if self.stop != ("gather", 0) else 0):
                self.phase_mod(l)
                if self.stop == ("mod", l):
                    break
                self.phase_proj(l)
                if self.stop == ("proj", l):
                    break
                self.phase_seqmix(l, "lat")
                if l == 0:
                    self.phase_seqmix(l, "ctx")
                if self.stop == ("seq", l):
                    break
                self.phase_attn(l)
                if self.stop == ("attn", l):
                    break
                self.phase_outproj(l)
                if self.stop == ("outp", l):
                    break
                if self.with_exp:
                    self.phase_moe(l)
            for nm, tl in (("QT", self.QT), ("KT", self.KT), ("V", self.V), ("PFH", self.PFH), ("GT", self.GT), ("X", self.X), ("MODBC", self.MODBC), ("FM", self.FM), ("HM", self.HM), ("OT", self.OT)):
                self.dump(nm, tl)
            self.bigcopy(self.out, self.out.t.ap(), self.X, self.X.t.ap(), L, 512)
            fw.fence()
        fw.close()
        return self.nc


FFT_CFG = {"FA": (64, 64), "HB": (128, 64), "FC": (2, 2), "HD": (4, 2)}


def make_fft_layout():
    P = Pack()
    for nm in ("w128r", "w128i", "w128n"):
        P.add(nm, 128)
    for cfg, (n1, n1in) in FFT_CFG.items():
        for nm in ("w1r", "w1i", "w1n"):
            P.add(cfg + nm, n1)
        P.add(cfg + "tr", 128); P.add(cfg + "ti", 128); P.add(cfg + "tn", 128)
        if cfg in ("HB", "HD"):
            for nm in ("v1r", "v1i", "v1n"):
                P.add(cfg + nm, n1in)
    return P


FFTC = make_fft_layout()


def fft_constants():
    c = np.zeros((128, FFTC.n), np.float64)

    def put(name, val):
        o, w = FFTC.off[name]
        c[:val.shape[0], o:o + w] = val

    a = np.arange(128)
    w128 = np.exp(-2j * np.pi * np.outer(a, a) / 128)
    put("w128r", w128.real); put("w128i", w128.imag); put("w128n", -w128.imag)
    for cfg, (n1, n1in) in FFT_CFG.items():
        N = n1 * 128
        b = np.arange(n1)
        w1 = np.exp(-2j * np.pi * np.outer(b, b) / n1)
        put(cfg + "w1r", w1.real); put(cfg + "w1i", w1.imag); put(cfg + "w1n", -w1.imag)
        tw = np.exp(-2j * np.pi * np.outer(b, a) / N)
        put(cfg + "tr", tw.real); put(cfg + "ti", tw.imag); put(cfg + "tn", -tw.imag)
        if cfg in ("HB", "HD"):
            v1 = np.exp(+2j * np.pi * np.outer(b, np.arange(n1in)) / n1) / N
            put(cfg + "v1r", v1.real); put(cfg + "v1i", v1.imag); put(cfg + "v1n", -v1.imag)
    return c.astype(np.float32)


CG = 32
NCOL = 128 * CG


def _fc(self, name):
    o, w = FFTC.off[name]
    return self.fftc[:, o:o + w]


def _mm_blocks(self, outs, terms, M, K, ncols):
    fw = self.fw
    nb = 0
    for c0 in range(0, ncols, 512):
        cw = min(512, ncols - c0)
        for oi, (ot, tl) in enumerate(zip(outs, terms)):
            p = self.fps[self.nfps % len(self.fps)]; self.nfps += 1
            for ti, (lh, rt) in enumerate(tl):
                fw.op("pe", lambda e, p=p, lh=lh, rt=rt, c0=c0, cw=cw, ti=ti, n=len(tl): e.matmul(
                    p[0:M, 0:cw], lh, rt[0:K, c0:c0 + cw], start=(ti == 0), stop=(ti == n - 1)),
                    R=[rt.b, self.fftcb.b], W=[p.b])
            if nb % 2 == 0:
                fw.op("act", lambda e, p=p, ot=ot, c0=c0, cw=cw: e.activation(out=ot[0:M, c0:c0 + cw], in_=p[0:M, 0:cw], func=AF.Copy),
                      R=[p.b], W=[ot.b])
            else:
                fw.op("dve", lambda e, p=p, ot=ot, c0=c0, cw=cw: e.tensor_copy(out=ot[0:M, c0:c0 + cw], in_=p[0:M, 0:cw]),
                      R=[p.b], W=[ot.b])
            nb += 1


def _cmul(self, outr, outi, ar, ai, br, bi, P, ncols, t1, t2, bshape=None):
    fw = self.fw

    def A(t):
        return t[0:P, 0:ncols]

    def Bv(t):
        return t if bshape else t[0:P, 0:ncols]

    def V(t):
        return t[0:P, 0:ncols].rearrange("p (a c) -> p a c", c=bshape) if bshape else t[0:P, 0:ncols]
    xr = [self.fftc.b] if bshape else [br.b]
    xi = [self.fftc.b] if bshape else [bi.b]
    fw.op("dve", lambda e: e.tensor_tensor(out=V(t1), in0=V(ar), in1=Bv(br), op=ALU.mult), R=[ar.b] + xr, W=[t1.b])
    fw.op("dve", lambda e: e.tensor_tensor(out=V(t2), in0=V(ai), in1=Bv(bi), op=ALU.mult), R=[ai.b] + xi, W=[t2.b])
    fw.op("dve", lambda e: e.tensor_tensor(out=A(outr), in0=A(t1), in1=A(t2), op=ALU.subtract), R=[t1.b, t2.b], W=[outr.b])
    fw.op("dve", lambda e: e.tensor_tensor(out=V(t1), in0=V(ar), in1=Bv(bi), op=ALU.mult), R=[ar.b] + xi, W=[t1.b])
    fw.op("dve", lambda e: e.tensor_tensor(out=V(t2), in0=V(ai), in1=Bv(br), op=ALU.mult), R=[ai.b] + xr, W=[t2.b])
    fw.op("dve", lambda e: e.tensor_tensor(out=A(outi), in0=A(t1), in1=A(t2), op=ALU.add), R=[t1.b, t2.b], W=[outi.b])


def _dtrans(self, src, P1, dst):
    fw = self.fw
    sc = self.tscr[self.ntscr % len(self.tscr)]; self.ntscr += 1
    fw.dma("sp", sc.t.ap()[0:P1, :], src[0:P1, 0:NCOL], sc.b, src.b)
    v = sc.t.ap()[0:P1, :].rearrange("a (j c) -> j a c", c=CG)
    fw.dma("sp", dst[:, 0:P1 * CG].rearrange("p (a c) -> p a c", c=CG), v, dst.b, sc.b)


def _fcb(self, name):
    o, w = FFTC.off[name]
    return self.fftcb[:, o:o + w]


def _fft_fwd(self, cfg, x, Xr, Xi, W):
    n1, n1in = FFT_CFG[cfg]
    a_r, a_i, t1, t2 = W[0], W[1], W[2], W[3]
    B = self.fB
    fc = self._fc; fb = self._fcb
    w1r = fb(cfg + "w1r")[0:n1in, 0:n1]; w1i = fb(cfg + "w1i")[0:n1in, 0:n1]
    self._mm_blocks([a_r, a_i], [[(w1r, x)], [(w1i, x)]], n1, n1in, NCOL)
    tr = fc(cfg + "tr")[0:n1, :].unsqueeze(2).broadcast_to([n1, 128, CG])
    ti = fc(cfg + "ti")[0:n1, :].unsqueeze(2).broadcast_to([n1, 128, CG])
    self._cmul(B[0], B[1], a_r, a_i, tr, ti, n1, NCOL, t1, t2, bshape=CG)
    self._dtrans(B[0], n1, B[2])
    self._dtrans(B[1], n1, B[3])
    wr = fb("w128r"); wi = fb("w128i"); wn = fb("w128n")
    self._mm_blocks([Xr, Xi], [[(wr, B[2]), (wn, B[3])], [(wi, B[2]), (wr, B[3])]], 128, 128, n1 * CG)


def _fft_inv(self, cfg, Yr, Yi, y, W):
    n1, n1in = FFT_CFG[cfg]
    t1, t2 = W[2], W[3]
    B = self.fB
    fc = self._fc; fb = self._fcb
    wr = fb("w128r"); wi = fb("w128i"); wn = fb("w128n")
    self._mm_blocks([B[2], B[3]], [[(wr, Yr), (wi, Yi)], [(wn, Yr), (wr, Yi)]], 128, 128, n1 * CG)
    self._dtrans_back(B[2], n1, B[0])
    self._dtrans_back(B[3], n1, B[1])
    tr = fc(cfg + "tr")[0:n1, :].unsqueeze(2).broadcast_to([n1, 128, CG])
    tn = fc(cfg + "tn")[0:n1, :].unsqueeze(2).broadcast_to([n1, 128, CG])
    self._cmul(B[2], B[3], B[0], B[1], tr, tn, n1, NCOL, t1, t2, bshape=CG)
    v1r = fb(cfg + "v1r")[0:n1, 0:n1in]; v1n = fb(cfg + "v1n")[0:n1, 0:n1in]
    self._mm_blocks([y], [[(v1r, B[2]), (v1n, B[3])]], n1in, n1, NCOL)


def _dtrans_back(self, src, P1, dst):
    fw = self.fw
    sc = self.tscr[self.ntscr % len(self.tscr)]; self.ntscr += 1
    v = sc.t.ap()[0:P1, :].rearrange("a (j c) -> j a c", c=CG)
    fw.dma("sp", v, src[:, 0:P1 * CG].rearrange("p (a c) -> p a c", c=CG), sc.b, src.b)
    fw.dma("sp", dst[0:P1, 0:NCOL], sc.t.ap()[0:P1, :], dst.b, sc.b)


for _f in (_fc, _fcb, _mm_blocks, _cmul, _dtrans, _dtrans_back, _fft_fwd, _fft_inv):
    setattr(Prog, _f.__name__, _f)

MAGIC = 12582912.0
TWO_PI = 2.0 * math.pi


def _rr_sin(self, out, arg, tmp, P, n):
    fw = self.fw
    fw.op("dve", lambda e: e.tensor_scalar(out=tmp[0:P, 0:n], in0=arg[0:P, 0:n], scalar1=1.0 / TWO_PI, scalar2=MAGIC, op0=ALU.mult, op1=ALU.add),
          R=[arg.b], W=[tmp.b])
    fw.op("dve", lambda e: e.tensor_scalar(out=tmp[0:P, 0:n], in0=tmp[0:P, 0:n], scalar1=-MAGIC, scalar2=None, op0=ALU.add), R=[tmp.b], W=[tmp.b])
    fw.op("dve", lambda e: e.scalar_tensor_tensor(out=tmp[0:P, 0:n], in0=tmp[0:P, 0:n], scalar=-TWO_PI, in1=arg[0:P, 0:n], op0=ALU.mult, op1=ALU.add),
          R=[tmp.b, arg.b], W=[tmp.b])
    fw.op("dve", lambda e: e.tensor_scalar(out=tmp[0:P, 0:n], in0=tmp[0:P, 0:n], scalar1=3.14159, scalar2=-3.14159, op0=ALU.min, op1=ALU.max),
          R=[tmp.b], W=[tmp.b])
    fw.op("act", lambda e: e.activation(out=out[0:P, 0:n], in_=tmp[0:P, 0:n], func=AF.Sin), R=[tmp.b], W=[out.b])


def _gen_filter(self, l, ps, Lseq, ecol0, ntl, HF, sml, rn):
    fw = self.fw
    hw1, hb1, hf1, hw2, hb2, hf2, hw3, hb3, dbc = sml
    et = fw.sb(ps, "f_e", [33, 512]); arg = fw.sb(ps, "f_arg", [64, 512]); tmp = fw.sb(ps, "f_tmp", [64, 512])
    z1 = fw.sb(ps, "f_z1", [64, 512]); z2 = fw.sb(ps, "f_z2", [64, 512])
    hh = [fw.sb(ps, "f_h0", [128, 512])]
    ab = fw.sb(ps, "f_ab", [128, 512]); dec = fw.sb(ps, "f_dec", [128, 128])
    nrm = ab
    p1 = self.fps[0]; p2 = self.fps[1]; p3 = self.fps[2]; pN = self.fps[3]
    ones = self.cv("ones")
    nsub_tot = Lseq // 128
    si = 0
    for t0 in range(0, Lseq, 512):
        n = min(512, Lseq - t0)
        fw.dma("sp", et[:, 0:n], self.embc.t.ap()[:, ecol0 + t0:ecol0 + t0 + n], et.b, self.embc.b)
        fw.op("pe", lambda e, n=n: e.matmul(p1[0:64, 0:n], hw1[0:33, :], et[:, 0:n], start=True, stop=True), R=[et.b, hw1.b], W=[p1.b])
        fw.op("dve", lambda e, n=n: e.tensor_scalar(out=arg[:, 0:n], in0=p1[0:64, 0:n], scalar1=hb1[0:64, 0:1], scalar2=hf1[0:64, 0:1], op0=ALU.add, op1=ALU.mult),
              R=[p1.b, hb1.b, hf1.b], W=[arg.b])
        self._rr_sin(z1, arg, tmp, 64, n)
        fw.op("pe", lambda e, n=n: e.matmul(p2[0:64, 0:n], hw2[0:64, :], z1[:, 0:n], start=True, stop=True), R=[z1.b, hw2.b], W=[p2.b])
        fw.op("dve", lambda e, n=n: e.tensor_scalar(out=arg[:, 0:n], in0=p2[0:64, 0:n], scalar1=hb2[0:64, 0:1], scalar2=hf2[0:64, 0:1], op0=ALU.add, op1=ALU.mult),
              R=[p2.b, hb2.b, hf2.b], W=[arg.b])
        self._rr_sin(z2, arg, tmp, 64, n)
        for i in range(n // 128):
            h = hh[0]
            fw.op("pe", lambda e, i=i: e.matmul(p3[:, :], z2[:, i * 128:(i + 1) * 128], hw3[0:64, :], start=True, stop=True), R=[z2.b, hw3.b], W=[p3.b])
            fw.op("dve", lambda e, h=h: e.tensor_tensor(out=h[:], in0=p3[:], in1=hb3[:], op=ALU.add), R=[p3.b, hb3.b], W=[h.b])
            fw.op("act", lambda e, si=si: e.activation(out=dec[:], in_=dbc[:], func=AF.Exp, scale=ntl[:, si:si + 1]), R=[dbc.b, self.cstb], W=[dec.b])
            fw.op("dve", lambda e, h=h: e.tensor_tensor(out=h[:].rearrange("p (g c) -> p g c", c=128), in0=h[:].rearrange("p (g c) -> p g c", c=128),
                                                        in1=dec[:].unsqueeze(1).broadcast_to([128, 4, 128]), op=ALU.mult), R=[h.b, dec.b], W=[h.b])
            if si == 0:
                for c0 in (128, 384):
                    fw.op("dve", lambda e, h=h, c0=c0: e.memset(h[0:1, c0:c0 + 128], 0.0), W=[h.b])
            fw.op("act", lambda e, h=h: e.activation(out=ab[:], in_=h[:], func=AF.Abs), R=[h.b], W=[ab.b])
            fw.op("pe", lambda e, si=si: e.matmul(pN[:], ones, ab[:], start=(si == 0), stop=(si == nsub_tot - 1)), R=[ab.b, self.cstb], W=[pN.b])
            fw.dma("sp", HF.t.ap()[t0 + i * 128:t0 + (i + 1) * 128, :], h[:], HF.b, h.b, part=True)
            si += 1
    fw.op("dve", lambda e: e.tensor_copy(out=nrm[:], in_=pN[:]), R=[pN.b], W=[nrm.b])
    nv = nrm[:].rearrange("p (o d c) -> p o d c", o=2, d=2)
    fw.op("dve", lambda e: e.tensor_tensor(out=rn[:], in0=nv[:, :, 0, :], in1=nv[:, :, 1, :], op=ALU.add), R=[nrm.b], W=[rn.b])
    fw.op("dve", lambda e: e.reciprocal(out=rn[:], in_=rn[:]), R=[rn.b], W=[rn.b])
    return rn


def _seq_cfg(self, which):
    if which == "lat":
        return "FA", "HB", L, 1, 0, 0, "ntl8192"
    return "FC", "HD", NCTX, L + 3, L, L, "ntl256"


def phase_seqmix(self, l, which):
    fw = self.fw
    fcfg, hcfg, Lseq, prow, tbase, ecol0, ntlname = self._seq_cfg(which)
    n1f, n1fin = FFT_CFG[fcfg]
    n1h, n1hin = FFT_CFG[hcfg]
    with contextlib.ExitStack() as ps:
        self.fftc = fw.sb(ps, "fftc_t", [128, FFTC.n])
        fw.dma("sp", self.fftc[:], self.fftc_d.t.ap(), self.fftc.b, self.fftc_d.b)
        self.fps = [fw.ps(ps, "fps%d" % i) for i in range(6)]
        self.nfps = 0
        self.tscr = [fw.dram("tscr%d_%d_%s" % (i, l, which), [128, NCOL], BF16) for i in range(2)]
        self.fftcb = fw.sb(ps, "fftcb_t", [128, FFTC.n], BF16)
        fw.op("act", lambda e: e.activation(out=self.fftcb[:], in_=self.fftc[:], func=AF.Copy), R=[self.fftc.b], W=[self.fftcb.b])
        self.fB = [fw.sb(ps, "fB%d" % i, [128, NCOL], BF16) for i in range(4)]
        self.ntscr = 0
        W = [fw.sb(ps, "fw%d" % i, [128, NCOL]) for i in range(4)]
        Xr = fw.sb(ps, "fXr", [128, NCOL]); Xi = fw.sb(ps, "fXi", [128, NCOL])
        zv = fw.sb(ps, "fzv", [64, NCOL], BF16); zx = fw.sb(ps, "fzx", [64, NCOL], BF16)
        halo = fw.sb(ps, "fhalo", [64, 130, CG])

        def ldcast(src_ap, srcb, rows):
            fw.dma("sp", W[0][0:rows, :].rearrange("p (j c) -> p j c", c=CG), src_ap, W[0].b, srcb)
            fw.op("act", lambda e: e.activation(out=zv[0:rows, :], in_=W[0][0:rows, :], func=AF.Copy), R=[W[0].b], W=[zv.b])
            return zv
        for cg in range(4):
            src = self.PFH.t.ap()[prow:prow + Lseq, cg * CG:(cg + 1) * CG].rearrange("(a j) c -> a j c", j=128)
            self._fft_fwd(fcfg, ldcast(src, self.PFH.b, n1fin), Xr, Xi, W)
            for ri, X_ in ((0, Xr), (1, Xi)):
                dst = self.FM.t.ap()[tbase:tbase + Lseq, ri, cg * CG:(cg + 1) * CG].rearrange("(p a) c -> p a c", a=n1f)
                fw.dma("sp", dst, X_[:, 0:n1f * CG].rearrange("p (a c) -> p a c", c=CG), self.FM.b, X_.b, part=True)
        rn = fw.sb(ps, "f_rn", [128, 2, 128])
        with contextlib.ExitStack() as ps2:
            sml = [self.ld(ps2, "hw1%d" % l, 33), self.ld(ps2, "hb1%d" % l, 64), self.ld(ps2, "hf1%d" % l, 64),
                   self.ld(ps2, "hw2%d" % l, 64), self.ld(ps2, "hb2%d" % l, 64), self.ld(ps2, "hf2%d" % l, 64),
                   self.ld(ps2, "hw3%d" % l, 64), self.ld(ps2, "hb3bc%d" % l), self.ld(ps2, "deltabc")]
            o_, w_ = CST.off[ntlname]
            ntl = self.cst[:, o_:o_ + w_]
            HF = fw.dram("HF_%d_%s" % (l, which), [Lseq, 512], F32)
            self._gen_filter(l, ps2, Lseq, ecol0, ntl, HF, sml, rn)
            fw.fence()
        hbias = self.ld(ps, "hbias%d" % l)
        HS = fw.dram("HS_%d_%s" % (l, which), [2, 4, 2, 128, n1h * CG], F32)
        for o in range(2):
            for cg in range(4):
                nc_ = n1h * CG
                for d in range(2):
                    src = HF.t.ap()[:, o * 256 + d * 128 + cg * CG: o * 256 + d * 128 + (cg + 1) * CG].rearrange("(a j) c -> a j c", j=128)
                    self._fft_fwd(hcfg, ldcast(src, HF.b, n1hin), Xr, Xi, W)
                    if d == 0:
                        fw.dma("sp", HS.t.ap()[o, cg, 0], Xr[:, 0:nc_], HS.b, Xr.b)
                        fw.dma("sp", HS.t.ap()[o, cg, 1], Xi[:, 0:nc_], HS.b, Xi.b)
                fw.dma("sp", W[0][:, 0:nc_], HS.t.ap()[o, cg, 0], W[0].b, HS.b)
                fw.dma("sp", W[1][:, 0:nc_], HS.t.ap()[o, cg, 1], W[1].b, HS.b)
                rnb = rn[:, o, cg * CG:(cg + 1) * CG].unsqueeze(1).broadcast_to([128, n1h, CG])
                bb = hbias[:, o * 128 + cg * CG: o * 128 + (cg + 1) * CG].unsqueeze(1).broadcast_to([128, n1h, CG])

                def v3(t, nc_=nc_):
                    return t[:, 0:nc_].rearrange("p (a c) -> p a c", c=CG)
                fw.op("dve", lambda e, v3=v3: e.tensor_tensor(out=v3(W[0]), in0=v3(W[0]), in1=v3(Xr), op=ALU.add), R=[Xr.b, W[0].b], W=[W[0].b])
                fw.op("dve", lambda e, v3=v3: e.tensor_tensor(out=v3(W[1]), in0=v3(W[1]), in1=v3(Xi), op=ALU.subtract), R=[Xi.b, W[1].b], W=[W[1].b])
                fw.op("dve", lambda e, rnb=rnb, v3=v3: e.tensor_tensor(out=v3(W[0]), in0=v3(W[0]), in1=rnb, op=ALU.mult), R=[W[0].b, rn.b], W=[W[0].b])
                fw.op("dve", lambda e, rnb=rnb, v3=v3: e.tensor_tensor(out=v3(W[1]), in0=v3(W[1]), in1=rnb, op=ALU.mult), R=[W[1].b, rn.b], W=[W[1].b])
                fw.op("dve", lambda e, bb=bb, v3=v3: e.tensor_tensor(out=v3(W[0]), in0=v3(W[0]), in1=bb, op=ALU.add), R=[W[0].b, hbias.b], W=[W[0].b])
                fw.dma("sp", HS.t.ap()[o, cg, 0], W[0][:, 0:nc_], HS.b, W[0].b)
                fw.dma("sp", HS.t.ap()[o, cg, 1], W[1][:, 0:nc_], HS.b, W[1].b)
        cw = self.ld(ps, "cwbc%d" % l); cbv = self.ld(ps, "cbbc%d" % l)

        def shortconv(dst, wi, cg):
            col0 = 128 + wi * 128 + cg * CG
            base = self.PFH.t.ap()[prow - 1:prow - 1 + Lseq + 2, col0:col0 + CG]
            src = bass.AP(self.PFH.t, base.offset, [[128 * 512, n1hin], [512, 130], [1, CG]])
            fw.dma("sp", halo[0:n1hin, :, :], src, halo.b, self.PFH.b)
            d3 = dst[0:n1hin, :].rearrange("p (j c) -> p j c", c=CG)
            for j in range(3):
                wv = cw[0:n1hin, j * 384 + wi * 128 + cg * CG: j * 384 + wi * 128 + (cg + 1) * CG].unsqueeze(1).broadcast_to([n1hin, 128, CG])
                if j == 0:
                    fw.op("dve", lambda e, wv=wv: e.tensor_tensor(out=d3, in0=halo[0:n1hin, 0:128, :], in1=wv, op=ALU.mult), R=[halo.b, cw.b], W=[dst.b])
                else:
                    fw.op("dve", lambda e, wv=wv, j=j: e.tensor_tensor(out=W[3][0:n1hin, :].rearrange("p (j c) -> p j c", c=CG), in0=halo[0:n1hin, j:j + 128, :], in1=wv, op=ALU.mult),
                          R=[halo.b, cw.b], W=[W[3].b])
                    fw.op("dve", lambda e: e.tensor_tensor(out=dst[0:n1hin, :], in0=dst[0:n1hin, :], in1=W[3][0:n1hin, :], op=ALU.add), R=[dst.b, W[3].b], W=[dst.b])
            bv = cbv[0:n1hin, wi * 128 + cg * CG: wi * 128 + (cg + 1) * CG].unsqueeze(1).broadcast_to([n1hin, 128, CG])
            fw.op("dve", lambda e, bv=bv: e.tensor_tensor(out=d3, in0=d3, in1=bv, op=ALU.add), R=[dst.b, cbv.b], W=[dst.b])

        for cg in range(4):
            shortconv(zv, 0, cg)
            shortconv(zx, 1, cg)
            for o in range(2):
                src_t = zv if o == 0 else zx
                self._fft_fwd(hcfg, src_t, Xr, Xi, W)
                nc_ = n1h * CG
                fw.dma("sp", W[0][:, 0:nc_], HS.t.ap()[o, cg, 0], W[0].b, HS.b)
                fw.dma("sp", W[1][:, 0:nc_], HS.t.ap()[o, cg, 1], W[1].b, HS.b)
                self._cmul(self.fB[0], self.fB[1], Xr, Xi, W[0], W[1], 128, nc_, W[2], W[3])
                self._fft_inv(hcfg, self.fB[0], self.fB[1], W[0], W)
                if o == 0:
                    fw.op("dve", lambda e: e.tensor_tensor(out=zx[0:n1hin, :], in0=zx[0:n1hin, :], in1=W[0][0:n1hin, :], op=ALU.mult), R=[zx.b, W[0].b], W=[zx.b])
                else:
                    shortconv(W[1], 2, cg)
                    fw.op("dve", lambda e: e.tensor_tensor(out=W[0][0:n1hin, :], in0=W[0][0:n1hin, :], in1=W[1][0:n1hin, :], op=ALU.mult), R=[W[0].b, W[1].b], W=[W[0].b])
                    dst = self.HM.t.ap()[tbase:tbase + Lseq, cg * CG:(cg + 1) * CG].rearrange("(a j) c -> a j c", j=128)
                    fw.dma("sp", dst, W[0][0:n1hin, :].rearrange("p (j c) -> p j c", c=CG), self.HM.b, W[0].b, part=True)
        fw.fence()


def halo_flat(halo):
    return Tile(halo.t, halo.b)


for _f in (_rr_sin, _gen_filter, _seq_cfg, phase_seqmix):
    setattr(Prog, _f.__name__, _f)


def phase_attn(self, l):
    fw = self.fw
    lam_init = 0.8 - 0.6 * math.exp(-0.3 * l)
    with contextlib.ExitStack() as ps:
        KTs = [fw.sb(ps, "aKT%d" % h, [128, T], BF16) for h in range(2)]
        Vs = fw.sb(ps, "aV", [128, T // 128, 256], BF16)
        onesb = fw.sb(ps, "aones", [128, 128], BF16)
        Qs = [fw.sb(ps, "aQ%d" % i, [128, 512], BF16) for i in range(2)]
        pts = [fw.sb(ps, "apt%d" % i, [128, 512], BF16) for i in range(5)]
        r0 = fw.sb(ps, "ar0", [128, 512]); r1 = fw.sb(ps, "ar1", [128, 512])
        a0 = fw.sb(ps, "aa0", [128, 512]); a1 = fw.sb(ps, "aa1", [128, 512])
        sq = fw.sb(ps, "asq", [128, 512])
        ob = [fw.sb(ps, "aob%d" % i, [128, 512], BF16) for i in range(2)]
        pss = [fw.ps(ps, "aps%d" % i) for i in range(4)]
        po = [fw.ps(ps, "apo%d" % i) for i in range(2)]
        pz = [fw.ps(ps, "apz%d" % i) for i in range(2)]
        ones = self.cv("ones")
        pc = None
        if self.with_exp:
            fw.h["sp"].wait_ge(self.expsem, 48 * (l + 1))
            pst = [fw.sb(ps, "pcst%d" % i, [128, 8, 512]) for i in range(2)]
            psb = [fw.sb(ps, "pcsb%d" % i, [128, 8, 512], BF16) for i in range(2)]
            pc = self.precast_steps(l, pst, psb)
        fw.op("dve", lambda e: e.tensor_copy(out=onesb[:], in_=ones), R=[self.cstb], W=[onesb.b])
        for h in range(2):
            fw.dma("sp", KTs[h][:], self.KT.t.ap()[h * 128:(h + 1) * 128, :], KTs[h].b, self.KT.b)
        fw.dma("sp", Vs[:], self.V.t.ap().rearrange("(ch p) c -> p ch c", p=128), Vs.b, self.V.b)
        qtiles = [(q0, 512, list(range(T // 128))) for q0 in range(0, L, 512)]
        if l == 0:
            qtiles.append((L, NCTX, [L // 128, L // 128 + 1]))
        nq = 0; npt = 0; nps = 0
        for (q0, n_, kcs) in qtiles:
            for h in range(2):
                Q = Qs[nq % 2]; nq += 1
                fw.dma("sp", Q[:, 0:n_], self.QT.t.ap()[h * 128:(h + 1) * 128, q0:q0 + n_], Q.b, self.QT.b)
                LOOK = 3
                for m in range(2):
                    nk = len(kcs)
                    ptq = {}
                    for kk in range(nk + LOOK):
                        if kk < nk:
                            kc = kcs[kk]
                            s_ = pss[nps % 4]; nps += 1
                            fw.op("pe", lambda e, s_=s_, h=h, m=m, kc=kc, Q=Q, n_=n_: e.matmul(
                                s_[:, 0:n_], KTs[h][m * 64:(m + 1) * 64, kc * 128:(kc + 1) * 128], Q[m * 64:(m + 1) * 64, 0:n_], start=True, stop=True),
                                R=[KTs[h].b, Q.b], W=[s_.b])
                            pt = pts[npt % 5]; npt += 1
                            fw.op("act", lambda e, s_=s_, pt=pt, n_=n_: e.activation(out=pt[:, 0:n_], in_=s_[:, 0:n_], func=AF.Exp, scale=0.125),
                                  R=[s_.b], W=[pt.b])
                            ptq[kk] = pt
                        ki = kk - LOOK
                        if ki >= 0:
                            kc = kcs[ki]; pt = ptq.pop(ki)
                            fw.op("pe", lambda e, pt=pt, h=h, m=m, kc=kc, ki=ki, n_=n_, nk=nk: e.matmul(
                                po[m][:, 0:n_], Vs[:, kc, h * 128:(h + 1) * 128], pt[:, 0:n_], start=(ki == 0), stop=(ki == nk - 1)),
                                R=[Vs.b, pt.b], W=[po[m].b])
                            fw.op("pe", lambda e, pt=pt, m=m, ki=ki, n_=n_, nk=nk: e.matmul(
                                pz[m][:, 0:n_], onesb[:], pt[:, 0:n_], start=(ki == 0), stop=(ki == nk - 1)),
                                R=[onesb.b, pt.b], W=[pz[m].b])
                fw.op("dve", lambda e, n_=n_: e.reciprocal(out=r0[:, 0:n_], in_=pz[0][:, 0:n_]), R=[pz[0].b], W=[r0.b])
                fw.op("dve", lambda e, n_=n_: e.reciprocal(out=r1[:, 0:n_], in_=pz[1][:, 0:n_]), R=[pz[1].b], W=[r1.b])
                fw.op("dve", lambda e, n_=n_: e.tensor_tensor(out=a0[:, 0:n_], in0=po[0][:, 0:n_], in1=r0[:, 0:n_], op=ALU.mult), R=[po[0].b, r0.b], W=[a0.b])
                fw.op("dve", lambda e, n_=n_: e.tensor_tensor(out=a1[:, 0:n_], in0=po[1][:, 0:n_], in1=r1[:, 0:n_], op=ALU.mult), R=[po[1].b, r1.b], W=[a1.b])
                fw.op("dve", lambda e, n_=n_: e.scalar_tensor_tensor(out=a0[:, 0:n_], in0=a1[:, 0:n_], scalar=self.lam[:, 1:2], in1=a0[:, 0:n_], op0=ALU.mult, op1=ALU.add),
                      R=[a0.b, a1.b, self.lam.b], W=[a0.b])
                fw.op("act", lambda e, n_=n_: e.activation(out=sq[:, 0:n_], in_=a0[:, 0:n_], func=AF.Square), R=[a0.b], W=[sq.b])
                s_ = pss[nps % 4]; nps += 1
                fw.op("pe", lambda e, s_=s_, n_=n_: e.matmul(s_[:, 0:n_], ones, sq[:, 0:n_], start=True, stop=True), R=[sq.b, self.cstb], W=[s_.b])
                fw.op("act", lambda e, s_=s_, n_=n_: e.activation(out=r0[:, 0:n_], in_=s_[:, 0:n_], func=AF.Sqrt, scale=1.0 / 128, bias=self.epsb[:, 1:2]), R=[s_.b], W=[r0.b])
                fw.op("dve", lambda e, n_=n_: e.reciprocal(out=r0[:, 0:n_], in_=r0[:, 0:n_]), R=[r0.b], W=[r0.b])
                fw.op("dve", lambda e, n_=n_: e.scalar_tensor_tensor(out=a0[:, 0:n_], in0=a0[:, 0:n_], scalar=self.qks[:, 2:3], in1=r0[:, 0:n_], op0=ALU.mult, op1=ALU.mult),
                      R=[a0.b, r0.b, self.qks.b], W=[a0.b])
                o_ = ob[nq % 2]
                fw.op("dve", lambda e, n_=n_, o_=o_: e.tensor_scalar(out=o_[:, 0:n_], in0=a0[:, 0:n_], scalar1=(1.0 - lam_init), scalar2=None, op0=ALU.mult), R=[a0.b], W=[o_.b])
                fw.dma("sp", self.OT.t.ap()[h * 128:(h + 1) * 128, q0:q0 + n_], o_[:, 0:n_], self.OT.b, o_.b, part=True)
                if pc is not None:
                    for _ in range(3):
                        next(pc, None)
        if pc is not None:
            for _ in pc:
                pass
        fw.fence()


def allreduce_to_X(self, nrows):
    items = []
    for r0 in range(0, nrows, 1024):
        r1 = min(nrows, r0 + 1024)
        items.append(("AllReduce", PAIRS, self.ARin.t.ap()[r0:r1, :], self.X.t.ap()[r0:r1, :]))
    self.coll_seq(items)
    self.fw.fence()


def phase_outproj(self, l):
    fw = self.fw
    ntok = T if l == 0 else L
    with contextlib.ExitStack() as ps:
        Wfc = [fw.sb(ps, "oWc%d" % i, [128, 1024], BF16) for i in range(2)]
        Wfs = [fw.sb(ps, "oWs%d" % i, [128, 1024], BF16) for i in range(2)]
        Wh = fw.sb(ps, "oWh", [128, 1024], BF16)
        Wa = fw.sb(ps, "oWa", [128, 2, 1024], BF16)
        Wo = fw.sb(ps, "oWo", [128, 8, 1024], BF16)
        stg = fw.sb(ps, "ostg", [128, 8, 1024])
        wf32 = fw.sb(ps, "owf", [128, 1024])
        pp = [fw.ps(ps, "opp%d" % i) for i in range(7)]
        npp = 0
        c64 = self.cv("c64"); s64 = self.cv("s64"); ident = self.cv("ident")
        wov = self.wout.t.ap()[l]
        fw.dma("sp", wf32[:], wov[0:128, :], wf32.b, self.wout.b)
        for (cm, dsts) in ((c64, Wfc), (s64, Wfs)):
            for half in range(2):
                p = pp[npp % 7]; npp += 1
                fw.op("pe", lambda e, p=p, cm=cm, half=half: e.matmul(p[:], cm, wf32[:, half * 512:(half + 1) * 512], start=True, stop=True), R=[wf32.b, self.cstb], W=[p.b])
                for i, Ls in enumerate((L, NCTX)):
                    fw.op("act", lambda e, p=p, d=dsts[i], half=half, Ls=Ls: e.activation(out=d[:, half * 512:(half + 1) * 512], in_=p[:], func=AF.Copy, scale=1.0 / math.sqrt(64.0 * Ls)),
                          R=[p.b], W=[dsts[i].b])
        fw.dma("sp", stg[:, 0, :], wov[128:256, :], stg.b, self.wout.b)
        fw.op("dve", lambda e: e.tensor_copy(out=Wh[:], in_=stg[:, 0, :]), R=[stg.b], W=[Wh.b])
        fw.dma("sp", stg[:, 0:2, :], wov[256:512, :].rearrange("(k p) c -> p k c", p=128), stg.b, self.wout.b)
        fw.op("dve", lambda e: e.tensor_copy(out=Wa[:], in_=stg[:, 0:2, :]), R=[stg.b], W=[Wa.b])
        oo, _ = SHOFF["wo%d" % l]
        fw.dma("sp", stg[:], dview(self.SH, oo, (1024, 1024)).rearrange("(k p) c -> p k c", p=128), stg.b, self.SH.b)
        fw.op("act", lambda e: e.activation(out=Wo[:], in_=stg[:], func=AF.Copy), R=[stg.b], W=[Wo.b])
        fm = fw.sb(ps, "ofm", [128, 4, 2, 128]); hm = fw.sb(ps, "ohm", [128, 4, 128])
        FrT = fw.sb(ps, "oFr", [128, 512], BF16); FiT = fw.sb(ps, "oFi", [128, 512], BF16); HT = fw.sb(ps, "oHT", [128, 512], BF16)
        OTt = fw.sb(ps, "oOT", [128, 2, 512], BF16)
        G = fw.sb(ps, "oG", [128, 24, 512], BF16)
        xt = fw.sb(ps, "oxt", [128, 4, 1024])
        mb = fw.sb(ps, "omb", [128, 1024])
        m1 = fw.sb(ps, "om1", [128, 512]); m2 = fw.sb(ps, "om2", [128, 512])
        M = fw.sb(ps, "oM", [128, 8, 512], BF16)
        tz = fw.sb(ps, "otz", [128, 512]); ar = [fw.sb(ps, "oar%d" % i, [128, 512]) for i in range(2)]
        lastj = -1
        nar = 0
        for t0 in range(0, ntok, 512):
            n_ = min(512, ntok - t0); nsub = n_ // 128
            j = 0 if t0 < L else 1
            if j != lastj:
                fw.dma("sp", mb[:], self.MODBC.t.ap()[:, j * 2048:j * 2048 + 1024], mb.b, self.MODBC.b)
                lastj = j
            fw.dma("sp", fm[:, 0:nsub], self.FM.t.ap()[t0:t0 + n_].rearrange("(s p) r c -> p s r c", p=128), fm.b, self.FM.b)
            fw.dma("sp", hm[:, 0:nsub], self.HM.t.ap()[t0:t0 + n_].rearrange("(s p) c -> p s c", p=128), hm.b, self.HM.b)
            fw.dma("sp", OTt[:, :, 0:n_], self.OT.t.ap()[:, t0:t0 + n_].rearrange("(k p) t -> p k t", p=128), OTt.b, self.OT.b)
            fw.dma("sp", G[:, :, 0:n_], self.GT.t.ap()[:, t0:t0 + n_].rearrange("(k p) t -> p k t", p=128), G.b, self.GT.b)
            fw.dma("sp", xt[:, 0:nsub], self.X.t.ap()[t0:t0 + n_].rearrange("(s p) c -> p s c", p=128), xt.b, self.X.b)
            for (srcf, dstT) in ((lambda s_: fm[:, s_, 0, :], FrT), (lambda s_: fm[:, s_, 1, :], FiT), (lambda s_: hm[:, s_, :], HT)):
                p = pp[npp % 7]; npp += 1
                srcb = fm.b if dstT is not HT else hm.b
                for s_ in range(nsub):
                    fw.op("pe", lambda e, p=p, s_=s_, srcf=srcf: e.transpose(out=p[:, s_ * 128:(s_ + 1) * 128], in_=srcf(s_), identity=ident), R=[srcb, self.cstb], W=[p.b])
                fw.op("act", lambda e, p=p, dstT=dstT, n_=n_: e.activation(out=dstT[:, 0:n_], in_=p[:, 0:n_], func=AF.Copy), R=[p.b], W=[dstT.b])
            for fc in range(8):
                fsl = slice(fc * 128, (fc + 1) * 128)
                pf = pp[npp % 7]; npp += 1
                fw.op("pe", lambda e, pf=pf, fsl=fsl, n_=n_, j=j: e.matmul(pf[:, 0:n_], Wfc[j][:, fsl], FrT[:, 0:n_], start=True, stop=False), R=[Wfc[j].b, FrT.b], W=[pf.b])
                fw.op("pe", lambda e, pf=pf, fsl=fsl, n_=n_, j=j: e.matmul(pf[:, 0:n_], Wfs[j][:, fsl], FiT[:, 0:n_], start=False, stop=True), R=[Wfs[j].b, FiT.b], W=[pf.b])
                ph = pp[npp % 7]; npp += 1
                fw.op("pe", lambda e, ph=ph, fsl=fsl, n_=n_: e.matmul(ph[:, 0:n_], Wh[:, fsl], HT[:, 0:n_], start=True, stop=True), R=[Wh.b, HT.b], W=[ph.b])
                pa = pp[npp % 7]; npp += 1
                for k in range(2):
                    fw.op("pe", lambda e, pa=pa, fsl=fsl, n_=n_, k=k: e.matmul(pa[:, 0:n_], Wa[:, k, fsl], OTt[:, k, 0:n_], start=(k == 0), stop=(k == 1)), R=[Wa.b, OTt.b], W=[pa.b])
                fw.op("dve", lambda e, pf=pf, fc=fc, n_=n_: e.tensor_tensor(out=m1[:, 0:n_], in0=pf[:, 0:n_], in1=G[:, fc, 0:n_], op=ALU.mult), R=[pf.b, G.b], W=[m1.b])
                fw.op("dve", lambda e, ph=ph, fc=fc, n_=n_: e.tensor_tensor(out=m2[:, 0:n_], in0=ph[:, 0:n_], in1=G[:, 8 + fc, 0:n_], op=ALU.mult), R=[ph.b, G.b], W=[m2.b])
                fw.op("dve", lambda e, n_=n_: e.tensor_tensor(out=m1[:, 0:n_], in0=m1[:, 0:n_], in1=m2[:, 0:n_], op=ALU.add), R=[m1.b, m2.b], W=[m1.b])
                fw.op("dve", lambda e, pa=pa, fc=fc, n_=n_: e.tensor_tensor(out=m2[:, 0:n_], in0=pa[:, 0:n_], in1=G[:, 16 + fc, 0:n_], op=ALU.mult), R=[pa.b, G.b], W=[m2.b])
                fw.op("dve", lambda e, fc=fc, n_=n_: e.tensor_tensor(out=M[:, fc, 0:n_], in0=m1[:, 0:n_], in1=m2[:, 0:n_], op=ALU.add), R=[m1.b, m2.b], W=[M.b])
            for s_ in range(nsub):
                for half in range(2):
                    p = pp[npp % 7]; npp += 1
                    for k in range(8):
                        fw.op("pe", lambda e, p=p, k=k, s_=s_, half=half: e.matmul(p[:], M[:, k, s_ * 128:(s_ + 1) * 128], Wo[:, k, half * 512:(half + 1) * 512], start=(k == 0), stop=(k == 7)),
                              R=[M.b, Wo.b], W=[p.b])
                    fw.op("dve", lambda e, p=p, half=half: e.tensor_tensor(out=tz[:], in0=p[:], in1=mb[:, half * 512:(half + 1) * 512], op=ALU.mult), R=[p.b, mb.b], W=[tz.b])
                    a_ = ar[nar % 2]; nar += 1
                    fw.op("dve", lambda e, a_=a_, s_=s_, half=half: e.scalar_tensor_tensor(out=a_[:], in0=xt[:, s_, half * 512:(half + 1) * 512], scalar=0.5, in1=tz[:], op0=ALU.mult, op1=ALU.add),
                          R=[xt.b, tz.b], W=[a_.b])
                    fw.dma("sp", self.ARin.t.ap()[t0 + s_ * 128:t0 + (s_ + 1) * 128, half * 512:(half + 1) * 512], a_[:], self.ARin.b, a_.b, part=True)
        fw.fence()
    self.allreduce_to_X(ntok)


for _f in (phase_attn, allreduce_to_X, phase_outproj):
    setattr(Prog, _f.__name__, _f)


def phase_moe(self, l):
    fw = self.fw
    ntok = T if l == 0 else L
    WEB1, WEB2 = self.WEB[l]
    fw.fence()
    if l == 0 and DEPTH > 1:
        self.issue_exp_gather(1)
    with contextlib.ExitStack() as ps:
        W1 = [fw.sb(ps, "mW1_%d" % i, [128, 8, 2048], BF16) for i in range(2)]
        W2 = [fw.sb(ps, "mW2_%d" % i, [128, 8, 1024], BF16) for i in range(2)]
        wrt = self.ld(ps, "wrt%d" % l); brt = self.ld(ps, "brt%d" % l)
        b1T = self.ld(ps, "b1T%d" % l); b2r = self.ld(ps, "b2r%d" % l, 16)
        xt = fw.sb(ps, "mxt", [128, 4, 1024]); xn = fw.sb(ps, "mxn", [128, 1024]); junk = fw.sb(ps, "mjunk", [128, 1024])
        ss = fw.sb(ps, "mss", [128, 2])
        h2T = fw.sb(ps, "mh2T", [128, 8, 512], BF16)
        h2f = fw.sb(ps, "mh2f", [128, 8, 128])
        lg = fw.sb(ps, "mlg", [128, 32]); m8 = fw.sb(ps, "mm8", [128, 8]); msk = fw.sb(ps, "mmsk", [128, 32])
        nb = fw.sb(ps, "mnb", [128, 2])
        Gt = fw.sb(ps, "mG", [128, 4, 32])
        gT = fw.sb(ps, "mgT", [16, 128])
        Y = fw.sb(ps, "mY", [128, 4, 1024])
        A = fw.sb(ps, "mA", [128, 8, 512], BF16)
        g2 = [fw.sb(ps, "mg%d" % i, [128, 512]) for i in range(2)]; sg2 = [fw.sb(ps, "msg%d" % i, [128, 512]) for i in range(2)]; ln2 = [fw.sb(ps, "mln%d" % i, [128, 512]) for i in range(2)]
        mb = fw.sb(ps, "mmb", [128, 1024])
        tz = fw.sb(ps, "mtz", [128, 1024])
        pp = [fw.ps(ps, "mpp%d" % i) for i in range(8)]
        npp = 0
        ident = self.cv("ident")
        lastj = -1
        nw = 0
        for t0 in range(0, ntok, 512):
            n_ = min(512, ntok - t0); nsub = n_ // 128
            j = 0 if t0 < L else 1
            if j != lastj:
                fw.dma("sp", mb[:], self.MODBC.t.ap()[:, j * 2048 + 1024:j * 2048 + 2048], mb.b, self.MODBC.b)
                lastj = j
            fw.dma("sp", xt[:, 0:nsub], self.X.t.ap()[t0:t0 + n_].rearrange("(s p) c -> p s c", p=128), xt.b, self.X.b)
            for s_ in range(nsub):
                fw.op("act", lambda e, s_=s_: e.activation(out=junk[:], in_=xt[:, s_, :], func=AF.Square, accum_out=ss[:, 0:1]), R=[xt.b], W=[junk.b, ss.b])
                fw.op("act", lambda e: e.activation(out=ss[:, 1:2], in_=ss[:, 0:1], func=AF.Sqrt, scale=1.0 / D, bias=self.epsb[:, 0:1]), R=[ss.b], W=[ss.b])
                fw.op("dve", lambda e: e.reciprocal(out=ss[:, 1:2], in_=ss[:, 1:2]), R=[ss.b], W=[ss.b])
                fw.op("dve", lambda e, s_=s_: e.tensor_scalar(out=xn[:], in0=xt[:, s_, :], scalar1=ss[:, 1:2], scalar2=None, op0=ALU.mult), R=[xt.b, ss.b], W=[xn.b])
                for g in range(2):
                    p = pp[npp % 8]; npp += 1
                    for q in range(4):
                        k = 4 * g + q
                        fw.op("pe", lambda e, p=p, q=q, k=k: e.transpose(out=p[:, q * 128:(q + 1) * 128], in_=xn[:, k * 128:(k + 1) * 128], identity=ident), R=[xn.b, self.cstb], W=[p.b])
                    for q in range(4):
                        k = 4 * g + q
                        fw.op("dve", lambda e, p=p, q=q, k=k, j=j: e.tensor_scalar(out=h2f[:, k, :], in0=p[:, q * 128:(q + 1) * 128], scalar1=self.gs2[:, k, j:j + 1], scalar2=self.sh2[:, k, j:j + 1],
                                                                                  op0=ALU.mult, op1=ALU.add), R=[p.b, self.gs2.b, self.sh2.b], W=[h2f.b])
                fw.op("act", lambda e, s_=s_: e.activation(out=h2T[:, :, s_ * 128:(s_ + 1) * 128], in_=h2f[:], func=AF.Copy), R=[h2f.b], W=[h2T.b])
                p = pp[npp % 8]; npp += 1
                for k in range(8):
                    fw.op("pe", lambda e, p=p, k=k: e.matmul(p[:, 0:32], h2f[:, k, :], wrt[:, k * 32:(k + 1) * 32], start=(k == 0), stop=(k == 7)), R=[h2f.b, wrt.b], W=[p.b])
                fw.op("dve", lambda e, p=p: e.tensor_tensor(out=lg[:], in0=p[:, 0:32], in1=brt[:], op=ALU.add), R=[p.b, brt.b], W=[lg.b])
                fw.op("dve", lambda e: e.max(out=m8[:], in_=lg[:]), R=[lg.b], W=[m8.b])
                fw.op("dve", lambda e: e.tensor_scalar(out=msk[:], in0=lg[:], scalar1=m8[:, 3:4], scalar2=None, op0=ALU.is_ge), R=[lg.b, m8.b], W=[msk.b])
                fw.op("dve", lambda e: e.tensor_scalar(out=nb[:, 0:1], in0=m8[:, 0:1], scalar1=-1.0, scalar2=None, op0=ALU.mult), R=[m8.b], W=[nb.b])
                fw.op("act", lambda e: e.activation(out=lg[:], in_=lg[:], func=AF.Exp, bias=nb[:, 0:1]), R=[lg.b, nb.b], W=[lg.b])
                fw.op("dve", lambda e: e.tensor_tensor(out=lg[:], in0=lg[:], in1=msk[:], op=ALU.mult), R=[lg.b, msk.b], W=[lg.b])
                fw.op("dve", lambda e: e.tensor_reduce(out=nb[:, 1:2], in_=lg[:], axis=AX.X, op=ALU.add), R=[lg.b], W=[nb.b])
                fw.op("dve", lambda e: e.reciprocal(out=nb[:, 1:2], in_=nb[:, 1:2]), R=[nb.b], W=[nb.b])
                fw.op("dve", lambda e, s_=s_: e.tensor_scalar(out=Gt[:, s_, :], in0=lg[:], scalar1=nb[:, 1:2], scalar2=None, op0=ALU.mult), R=[lg.b, nb.b], W=[Gt.b])
                p = pp[npp % 8]; npp += 1
                fw.op("pe", lambda e, p=p, s_=s_: e.transpose(out=p[0:16, 0:128], in_=Gt[:, s_, 0:16], identity=ident), R=[Gt.b, self.cstb], W=[p.b])
                fw.op("dve", lambda e, p=p: e.tensor_copy(out=gT[:], in_=p[0:16, 0:128]), R=[p.b], W=[gT.b])
                for half in range(2):
                    p = pp[npp % 8]; npp += 1
                    fw.op("pe", lambda e, p=p, half=half: e.matmul(p[:], gT[:], b2r[0:16, half * 512:(half + 1) * 512], start=True, stop=True), R=[gT.b, b2r.b], W=[p.b])
                    fw.op("act", lambda e, p=p, half=half, s_=s_: e.activation(out=Y[:, s_, half * 512:(half + 1) * 512], in_=p[:], func=AF.Copy), R=[p.b], W=[Y.b])
            for le in range(16):
                w1 = W1[nw % 2]; w2 = W2[nw % 2]; nw += 1
                for cb in range(4):
                    fw.dma("sp", w1[:, :, cb * 512:(cb + 1) * 512], WEB1.t.ap()[le].rearrange("(k p) c -> p k c", p=128)[:, :, cb * 512:(cb + 1) * 512], w1.b, WEB1.b, part=(cb > 0))
                for cb in range(2):
                    fw.dma("sp", w2[:, :, cb * 512:(cb + 1) * 512], WEB2.t.ap()[le].rearrange("(k p) c -> p k c", p=128)[:, :, cb * 512:(cb + 1) * 512], w2.b, WEB2.b, part=(cb > 0))
                for jc in range(8):
                    pg = pp[npp % 8]; npp += 1
                    pl = pp[npp % 8]; npp += 1
                    g_ = g2[jc % 2]; sg = sg2[jc % 2]; ln = ln2[jc % 2]
                    for k in range(8):
                        fw.op("pe", lambda e, pg=pg, k=k, jc=jc, w1=w1, n_=n_: e.matmul(pg[:, 0:n_], w1[:, k, jc * 128:(jc + 1) * 128], h2T[:, k, 0:n_], start=(k == 0), stop=(k == 7)), R=[w1.b, h2T.b], W=[pg.b])
                    for k in range(8):
                        fw.op("pe", lambda e, pl=pl, k=k, jc=jc, w1=w1, n_=n_: e.matmul(pl[:, 0:n_], w1[:, k, 1024 + jc * 128:1024 + (jc + 1) * 128], h2T[:, k, 0:n_], start=(k == 0), stop=(k == 7)), R=[w1.b, h2T.b], W=[pl.b])
                    bg = b1T[:, le * 16 + jc:le * 16 + jc + 1]; bl = b1T[:, le * 16 + 8 + jc:le * 16 + 8 + jc + 1]
                    fw.op("dve", lambda e, pg=pg, bg=bg, n_=n_, g_=g_: e.tensor_scalar(out=g_[:, 0:n_], in0=pg[:, 0:n_], scalar1=bg, scalar2=7.0, op0=ALU.add, op1=ALU.min), R=[pg.b, b1T.b], W=[g_.b])
                    fw.op("act", lambda e, n_=n_, g_=g_, sg=sg: e.activation(out=sg[:, 0:n_], in_=g_[:, 0:n_], func=AF.Sigmoid, scale=1.702), R=[g_.b], W=[sg.b])
                    fw.op("act", lambda e, pl=pl, bl=bl, n_=n_, ln=ln: e.activation(out=ln[:, 0:n_], in_=pl[:, 0:n_], func=AF.Identity, bias=bl), R=[pl.b, b1T.b], W=[ln.b])
                    fw.op("dve", lambda e, n_=n_, ln=ln: e.tensor_scalar(out=ln[:, 0:n_], in0=ln[:, 0:n_], scalar1=7.0, scalar2=-7.0, op0=ALU.min, op1=ALU.max), R=[ln.b], W=[ln.b])
                    fw.op("dve", lambda e, n_=n_, g_=g_, sg=sg: e.tensor_tensor(out=g_[:, 0:n_], in0=g_[:, 0:n_], in1=sg[:, 0:n_], op=ALU.mult), R=[g_.b, sg.b], W=[g_.b])
                    fw.op("dve", lambda e, jc=jc, n_=n_, g_=g_, ln=ln: e.scalar_tensor_tensor(out=A[:, jc, 0:n_], in0=ln[:, 0:n_], scalar=1.0, in1=g_[:, 0:n_], op0=ALU.add, op1=ALU.mult), R=[g_.b, ln.b], W=[A.b])
                for s_ in range(nsub):
                    for half in range(2):
                        p = pp[npp % 8]; npp += 1
                        for k in range(8):
                            fw.op("pe", lambda e, p=p, k=k, s_=s_, half=half, w2=w2: e.matmul(p[:], A[:, k, s_ * 128:(s_ + 1) * 128], w2[:, k, half * 512:(half + 1) * 512], start=(k == 0), stop=(k == 7)),
                                  R=[A.b, w2.b], W=[p.b])
                        fw.op("dve", lambda e, p=p, s_=s_, half=half, le=le: e.scalar_tensor_tensor(out=Y[:, s_, half * 512:(half + 1) * 512], in0=p[:], scalar=Gt[:, s_, le:le + 1],
                                                                                                   in1=Y[:, s_, half * 512:(half + 1) * 512], op0=ALU.mult, op1=ALU.add), R=[p.b, Gt.b, Y.b], W=[Y.b])
            for s_ in range(nsub):
                fw.op("dve", lambda e, s_=s_: e.tensor_tensor(out=tz[:], in0=Y[:, s_, :], in1=mb[:], op=ALU.mult), R=[Y.b, mb.b], W=[tz.b])
                fw.op("dve", lambda e, s_=s_: e.scalar_tensor_tensor(out=tz[:], in0=xt[:, s_, :], scalar=0.5, in1=tz[:], op0=ALU.mult, op1=ALU.add), R=[xt.b, tz.b], W=[tz.b])
                fw.dma("sp", self.ARin.t.ap()[t0 + s_ * 128:t0 + (s_ + 1) * 128, :], tz[:], self.ARin.b, tz.b, part=True)
        fw.fence()
    self.allreduce_to_X(ntok)


setattr(Prog, "phase_moe", phase_moe)

def precast_steps(self, l, st_, sb_):
    fw = self.fw
    WEB1 = fw.dram("WEB1_%d" % l, [16, 1024, 2048], BF16)
    WEB2 = fw.dram("WEB2_%d" % l, [16, 1024, 1024], BF16)
    self.WEB[l] = (WEB1, WEB2)
    WE = self.WEl[l]
    n = 0
    for le in range(16):
        w1v = WE.t.ap()[le * 1536:le * 1536 + 1024, :].rearrange("(k p) c -> p k c", p=128)
        w2v = bass.AP(WE.t, WE.t.ap()[le * 1536 + 1024:le * 1536 + 1536, :].offset, [[1024, 1024], [1, 1024]]).rearrange("(k p) c -> p k c", p=128)
        for (src, dst, ncb) in ((w1v, WEB1, 4), (w2v, WEB2, 2)):
            for cb in range(ncb):
                a = st_[n % 2]; b_ = sb_[n % 2]
                fw.dma("sp", a[:], src[:, :, cb * 512:(cb + 1) * 512], a.b, WE.b)
                fw.op("dve", lambda e, a=a, b_=b_: e.tensor_copy(out=b_[:], in_=a[:]), R=[a.b], W=[b_.b])
                fw.dma("sp", dst.t.ap()[le].rearrange("(k p) c -> p k c", p=128)[:, :, cb * 512:(cb + 1) * 512], b_[:], dst.b, b_.b, part=True)
                n += 1
                yield n


setattr(Prog, "precast_steps", precast_steps)


_CACHE = {}


def kernel(**inputs):
    inp = {k: np.asarray(v) for k, v in inputs.items()}
    maps = host_prep(inp)
    P = Prog(stop=None, dumps=(), with_exp=True)
    nc = P.build()
    res = run_bass_kernel_spmd(nc, maps, core_ids=list(range(8)))
    out = np.stack([np.asarray(res.results[b]["out"]) for b in range(4)], 0)
    return out.astype(np.float32)
```
